# Optimizing a Trainium2 kernel written in Bass

```python
import jax, jax.numpy as jnp
from jax import lax
import numpy as np

D_MODEL = 2048
BATCH = 4
SEQ = 4096
DEPTH = 2

GRID_W = 64
CTX_LEN = 256
D_MIX = D_MODEL
RET_HEADS = 4
RET_DK = 128
RET_DV = 256
RET_QK_W = RET_HEADS * RET_DK
RET_W = RET_HEADS * RET_DV
RET_CHUNK = 128
CONV_W = 512
CONV_K = 31
NA_HEADS = 4
NA_DH = 128
NA_W = NA_HEADS * NA_DH
NA_ROWS = 8
NA_COLS = 16
D_FF = 5632
FFN_K = 3
D_IN = 2 * RET_QK_W + 2 * RET_W + 2 * CONV_W + 3 * NA_W
ROPE_BASE = 10000.0
EPS = 1e-6

kernel_name = "hybrid_retention_conformer_natten_dit"


def _rmsnorm(x, g):
    xf = x.astype(jnp.float32)
    y = xf * lax.rsqrt(jnp.mean(xf * xf, axis=-1, keepdims=True) + EPS)
    return (y * g.astype(jnp.float32)).astype(x.dtype)


def _layernorm(x, g, b):
    xf = x.astype(jnp.float32)
    mu = jnp.mean(xf, axis=-1, keepdims=True)
    var = jnp.mean(jnp.square(xf - mu), axis=-1, keepdims=True)
    y = (xf - mu) * lax.rsqrt(var + EPS)
    return (y * g.astype(jnp.float32) + b.astype(jnp.float32)).astype(x.dtype)


def _adaln(cond, w, b):
    m = jax.nn.silu(cond) @ w + b
    return jnp.split(m, 6, axis=-1)


def _modulate(h, shift, scale):
    return h * (1 + scale) + shift


def _depthwise_conv(x, w, b):
    y = lax.conv_general_dilated(x, w[:, None, :].astype(x.dtype), window_strides=(1,), padding="SAME",
                                 dimension_numbers=("NWC", "WIO", "NWC"), feature_group_count=x.shape[-1])
    return y + b.astype(x.dtype)


def _axial_rope(t):
    L, dh = t.shape[1], t.shape[-1]
    half = dh // 2
    nf = half // 2
    pos = jnp.arange(L)
    row = (pos // GRID_W).astype(jnp.float32)
    col = (pos % GRID_W).astype(jnp.float32)
    inv = ROPE_BASE ** (-jnp.arange(nf, dtype=jnp.float32) / nf)

    def rot(xa, p):
        ang = p[:, None] * inv[None, :]
        cos = jnp.cos(ang)[None, :, None, :]
        sin = jnp.sin(ang)[None, :, None, :]
        x1, x2 = xa[..., :nf], xa[..., nf:]
        return jnp.concatenate([x1 * cos - x2 * sin, x2 * cos + x1 * sin], axis=-1)

    return jnp.concatenate([rot(t[..., :half], row), rot(t[..., half:], col)], axis=-1)


def _retention_chunkwise(q, k, v, log_gamma, state0):
    b, L, h, _ = q.shape
    dv = v.shape[-1]
    n = L // RET_CHUNK

    def chunks(t):
        return t.reshape(b, n, RET_CHUNK, h, t.shape[-1]).transpose(1, 0, 3, 2, 4)

    idx = jnp.arange(RET_CHUNK, dtype=jnp.float32)
    diff = idx[:, None] - idx[None, :]
    lower = diff >= 0
    decay_in = jnp.where(lower, jnp.exp(jnp.where(lower, diff, 0.0) * log_gamma[:, None, None]), 0.0)
    xi = jnp.exp((idx + 1.0) * log_gamma[:, None])[None, :, :, None]
    zeta = jnp.exp((RET_CHUNK - 1.0 - idx) * log_gamma[:, None])[None, :, :, None]
    g_chunk = jnp.exp(RET_CHUNK * log_gamma)[None, :, None, None]

    def step(state, blk):
        qc, kc, vc = blk
        inner = jnp.einsum("bhid,bhjd->bhij", qc, kc) * decay_in[None]
        out = jnp.einsum("bhij,bhjv->bhiv", inner, vc) + jnp.einsum("bhid,bhdv->bhiv", qc, state) * xi
        state = state * g_chunk + jnp.einsum("bhjd,bhjv->bhdv", kc * zeta, vc)
        return state, out

    state, out = lax.scan(step, state0, (chunks(q), chunks(k), chunks(v)))
    return out.transpose(1, 0, 3, 2, 4).reshape(b, L, h, dv), state


def _bidirectional_retention(q_l, k_l, v_l, q_c, k_c, v_c, log_gamma):
    b = q_l.shape[0]
    zeros = jnp.zeros((b, RET_HEADS, RET_DK, RET_DV), jnp.float32)
    flip = lambda t: jnp.flip(t, axis=1)
    oc_f, s_f = _retention_chunkwise(q_c, k_c, v_c, log_gamma[0], zeros)
    ol_f, _ = _retention_chunkwise(q_l, k_l, v_l, log_gamma[0], s_f)
    oc_b, s_b = _retention_chunkwise(flip(q_c), flip(k_c), flip(v_c), log_gamma[1], zeros)
    ol_b, _ = _retention_chunkwise(flip(q_l), flip(k_l), flip(v_l), log_gamma[1], s_b)
    return ol_f + flip(ol_b), oc_f + flip(oc_b)


def _gated_group_norm(o, gate, g):
    b, L, h, dv = o.shape
    mu = jnp.mean(o, axis=-1, keepdims=True)
    var = jnp.mean(jnp.square(o - mu), axis=-1, keepdims=True)
    y = ((o - mu) * lax.rsqrt(var + EPS)).reshape(b, L, h * dv) * g.astype(jnp.float32)
    return y.astype(gate.dtype) * jax.nn.silu(gate)


def _retention_group(lq, lk, lv, lg, cq, ck, cv, cg, decay_logits, gn_g, with_ctx):
    def heads(t, d):
        return t.astype(jnp.float32).reshape(t.shape[0], t.shape[1], RET_HEADS, d)

    scale = RET_DK ** -0.5
    q_l = _axial_rope(heads(lq, RET_DK)) * scale
    k_l = _axial_rope(heads(lk, RET_DK))
    q_c = heads(cq, RET_DK) * scale
    k_c = heads(ck, RET_DK)
    log_gamma = jax.nn.log_sigmoid(decay_logits.astype(jnp.float32))
    o_l, o_c = _bidirectional_retention(q_l, k_l, heads(lv, RET_DV), q_c, k_c, heads(cv, RET_DV), log_gamma)
    out_l = _gated_group_norm(o_l, lg, gn_g)
    out_c = _gated_group_norm(o_c, cg, gn_g) if with_ctx else None
    return out_l, out_c


def _conv_group(a, b, dw_w, dw_b, ln_g, ln_b, pw):
    u = a * jax.nn.sigmoid(b)
    u = _depthwise_conv(u, dw_w, dw_b)
    u = jax.nn.silu(_layernorm(u, ln_g, ln_b))
    return u @ pw


def _na_latent(q, k, v, k_c, v_c, rpb):
    b, L, h, d = q.shape
    rows_n = L // GRID_W
    kh = min(NA_ROWS, rows_n)
    rows = jnp.arange(rows_n)
    cols = jnp.arange(GRID_W)
    key_rows = jnp.clip(rows - kh // 2, 0, rows_n - kh)[:, None] + jnp.arange(kh)[None, :]
    col_start = jnp.clip(cols - NA_COLS // 2, 0, GRID_W - NA_COLS)
    col_in = (cols[None, :] >= col_start[:, None]) & (cols[None, :] < col_start[:, None] + NA_COLS)
    row_off = key_rows - rows[:, None] + NA_ROWS - 1
    col_off = jnp.clip(cols[None, :] - cols[:, None] + NA_COLS - 1, 0, 2 * NA_COLS - 2)
    bias = rpb.astype(jnp.float32)[:, row_off[:, None, :, None], col_off[None, :, None, :]]
    qg = q.reshape(b, rows_n, GRID_W, h, d) * (NA_DH ** -0.5)
    kg = k.reshape(b, rows_n, GRID_W, h, d)[:, key_rows]
    vg = v.reshape(b, rows_n, GRID_W, h, d)[:, key_rows]
    s_lat = jnp.einsum("brqhd,brkwhd->bhrqkw", qg, kg).astype(jnp.float32) + bias[None]
    s_lat = jnp.where(col_in[:, None, :], s_lat, -jnp.inf)
    s_ctx = jnp.einsum("brqhd,bchd->bhrqc", qg, k_c).astype(jnp.float32)
    n_lat = kh * GRID_W
    p = jax.nn.softmax(jnp.concatenate([s_lat.reshape(b, h, rows_n, GRID_W, n_lat), s_ctx], axis=-1), axis=-1)
    p = p.astype(v.dtype)
    p_lat = p[..., :n_lat].reshape(b, h, rows_n, GRID_W, kh, GRID_W)
    out = jnp.einsum("bhrqkw,brkwhd->brqhd", p_lat, vg) + jnp.einsum("bhrqc,bchd->brqhd", p[..., n_lat:], v_c)
    return out.reshape(b, L, h * d)


def _na_context(q_c, k_c, v_c):
    b, lc, h, d = q_c.shape
    s = jnp.einsum("bqhd,bkhd->bhqk", q_c * (NA_DH ** -0.5), k_c).astype(jnp.float32)
    p = jax.nn.softmax(s, axis=-1).astype(v_c.dtype)
    return jnp.einsum("bhqk,bkhd->bqhd", p, v_c).reshape(b, lc, h * d)


def _token_mixers(p_l, p_c, decay_logits, gn_g, dw_w, dw_b, ln_g, ln_b, pw, rpb, with_ctx):
    sizes = [RET_QK_W, RET_QK_W, RET_W, RET_W, CONV_W, CONV_W, NA_W, NA_W, NA_W]
    cuts = [int(s) for s in np.cumsum(sizes)[:-1]]
    lq, lk, lv, lg, la, lb, nq, nk, nv = jnp.split(p_l, cuts, axis=-1)
    cq, ck, cv, cg, ca, cb, cnq, cnk, cnv = jnp.split(p_c, cuts, axis=-1)

    def na_heads(t):
        return t.reshape(t.shape[0], t.shape[1], NA_HEADS, NA_DH)

    ret_l, ret_c = _retention_group(lq, lk, lv, lg, cq, ck, cv, cg, decay_logits, gn_g, with_ctx)
    conv_l = _conv_group(la, lb, dw_w, dw_b, ln_g, ln_b, pw)
    k_c, v_c = na_heads(cnk), na_heads(cnv)
    na_l = _na_latent(na_heads(nq), na_heads(nk), na_heads(nv), k_c, v_c, rpb)
    out_l = jnp.concatenate([ret_l, conv_l, na_l], axis=-1)
    if not with_ctx:
        return out_l, None
    conv_c = _conv_group(ca, cb, dw_w, dw_b, ln_g, ln_b, pw)
    na_c = _na_context(na_heads(cnq), k_c, v_c)
    out_c = jnp.concatenate([ret_c, conv_c, na_c], axis=-1)
    return out_l, out_c


def _conv_ffn(h, up, dw_w, dw_b, down):
    u = _depthwise_conv(h @ up, dw_w, dw_b)
    val, gate = jnp.split(u, 2, axis=-1)
    return (jax.nn.silu(gate) * val) @ down


def setup_inputs(seed: int = 0) -> dict:
    key = jax.random.key(seed)
    ks = jax.random.split(key, 24)
    f32 = jnp.float32

    def nrm(k, shape, scale):
        return jax.random.normal(k, shape, f32) * scale

    base_decay = np.log(2.0 ** (5 + np.arange(RET_HEADS)) - 1.0)
    return {
        "x": nrm(ks[0], (BATCH, SEQ, D_MODEL), 1.0),
        "c": nrm(ks[1], (BATCH, D_MODEL), 1.0),
        "ctx": nrm(ks[2], (BATCH, CTX_LEN, D_MODEL), 1.0),
        "c_ctx": nrm(ks[3], (D_MODEL,), 1.0),
        "w_ada": nrm(ks[4], (DEPTH, D_MODEL, 6 * D_MODEL), 0.5 * D_MODEL ** -0.5),
        "b_ada": nrm(ks[5], (DEPTH, 6 * D_MODEL), 0.02),
        "norm1_g": 1.0 + nrm(ks[6], (DEPTH, D_MODEL), 0.02),
        "w_in": nrm(ks[7], (DEPTH, D_MODEL, D_IN), D_MODEL ** -0.5),
        "ret_decay": jnp.asarray(base_decay, f32)[None, None, :] + nrm(ks[8], (DEPTH, 2, RET_HEADS), 0.05),
        "ret_gn_g": 1.0 + nrm(ks[9], (DEPTH, RET_W), 0.02),
        "conv_dw_w": nrm(ks[10], (DEPTH, CONV_K, CONV_W), CONV_K ** -0.5),
        "conv_dw_b": nrm(ks[11], (DEPTH, CONV_W), 0.02),
        "conv_ln_g": 1.0 + nrm(ks[12], (DEPTH, CONV_W), 0.02),
        "conv_ln_b": nrm(ks[13], (DEPTH, CONV_W), 0.02),
        "conv_pw": nrm(ks[14], (DEPTH, CONV_W, CONV_W), CONV_W ** -0.5),
        "na_rpb": nrm(ks[15], (DEPTH, NA_HEADS, 2 * NA_ROWS - 1, 2 * NA_COLS - 1), 0.05),
        "w_out": nrm(ks[16], (DEPTH, D_MIX, D_MODEL), D_MIX ** -0.5),
        "norm2_g": 1.0 + nrm(ks[17], (DEPTH, D_MODEL), 0.02),
        "ffn_up": nrm(ks[18], (DEPTH, D_MODEL, 2 * D_FF), D_MODEL ** -0.5),
        "ffn_dw_w": nrm(ks[19], (DEPTH, FFN_K, 2 * D_FF), FFN_K ** -0.5),
        "ffn_dw_b": nrm(ks[20], (DEPTH, 2 * D_FF), 0.02),
        "ffn_down": nrm(ks[21], (DEPTH, D_FF, D_MODEL), D_FF ** -0.5),
        "final_g": 1.0 + nrm(ks[22], (D_MODEL,), 0.02),
    }


def reference(x, c, ctx, c_ctx, w_ada, b_ada, norm1_g, w_in, ret_decay, ret_gn_g, conv_dw_w, conv_dw_b,
              conv_ln_g, conv_ln_b, conv_pw, na_rpb, w_out, norm2_g, ffn_up, ffn_dw_w, ffn_dw_b, ffn_down, final_g):
    h_ctx = ctx
    for l in range(DEPTH):
        last = l == DEPTH - 1
        sh1, sc1, g1, sh2, sc2, g2 = [t[:, None, :] for t in _adaln(c, w_ada[l], b_ada[l])]
        csh1, csc1, cg1, csh2, csc2, cg2 = _adaln(c_ctx, w_ada[l], b_ada[l])
        hl = _modulate(_rmsnorm(x, norm1_g[l]), sh1, sc1)
        hc = _modulate(_rmsnorm(h_ctx, norm1_g[l]), csh1, csc1)
        mix_l, mix_c = _token_mixers(hl @ w_in[l], hc @ w_in[l], ret_decay[l], ret_gn_g[l], conv_dw_w[l],
                                     conv_dw_b[l], conv_ln_g[l], conv_ln_b[l], conv_pw[l], na_rpb[l],
                                     with_ctx=not last)
        x = x + g1 * (mix_l @ w_out[l])
        hl2 = _modulate(_rmsnorm(x, norm2_g[l]), sh2, sc2)
        x = x + g2 * _conv_ffn(hl2, ffn_up[l], ffn_dw_w[l], ffn_dw_b[l], ffn_down[l])
        if not last:
            h_ctx = h_ctx + cg1 * (mix_c @ w_out[l])
            hc2 = _modulate(_rmsnorm(h_ctx, norm2_g[l]), csh2, csc2)
            h_ctx = h_ctx + cg2 * _conv_ffn(hc2, ffn_up[l], ffn_dw_w[l], ffn_dw_b[l], ffn_down[l])
    return _rmsnorm(x, final_g)
```

```python
import numpy as np
import ml_dtypes
import concourse.bass as bass
import concourse.mybir as mybir
from concourse.bass_utils import run_bass_kernel_spmd

F32 = mybir.dt.float32
BF16 = mybir.dt.bfloat16
AF = mybir.ActivationFunctionType
ALU = mybir.AluOpType
AX = mybir.AxisListType

COMPUTE = ("pe", "act", "dve", "pool")
SAME_ENGINE_WAIT = True


class Res:
    __slots__ = ("name", "w", "r", "lsem", "ssem")

    def __init__(self, name):
        self.name = name
        self.w = {}
        self.r = {}
        self.lsem = None
        self.ssem = None


class _Cap:
    def __init__(self):
        self.call = None

    def __getattr__(self, name):
        def f(*a, **k):
            self.call = (name, a, k)
            return self
        return f


class Rec:
    __slots__ = ("waits", "fn", "inc")

    def __init__(self, waits, fn, inc):
        self.waits = waits
        if fn is not None:
            cap = _Cap()
            fn(cap)
            fn = cap.call
            assert fn is not None
        self.fn = fn
        self.inc = inc


class Sched:
    def __init__(self, nc):
        self.nc = nc
        self.prog = {e: [] for e in ("pe", "act", "dve", "pool", "sp")}
        self.sems = {}
        self.cnt = {}
        self.known = {e: {} for e in self.prog}
        self.last = {e: None for e in COMPUTE}
        self.pending = {e: False for e in COMPUTE}
        self.free_dma_sems = []
        self.live_dma_sems = []
        self.nsem = 0
        self.nobarrier = set()
        self.live_res = []
        for e in COMPUTE:
            self._mk("E_" + e)

    def _mk(self, key):
        self.sems[key] = self.nc.alloc_semaphore(key)
        self.cnt[key] = 0
        self.nsem += 1
        return key

    def dma_sem(self):
        if self.free_dma_sems:
            k = self.free_dma_sems.pop()
        else:
            k = self._mk("D%d" % self.nsem)
        self.live_dma_sems.append(k)
        return k

    def _force(self, key):
        if key.startswith("E_"):
            e = key[2:]
            if self.pending[e]:
                rec = self.last[e]
                assert rec.inc is None
                rec.inc = (key, 1)
                self.cnt[key] += 1
                self.pending[e] = False

    def _waits(self, eng, deps):
        out = []
        kn = self.known[eng]
        for key, val in deps.items():
            if key == "E_" + eng:
                if eng == "pe" or not SAME_ENGINE_WAIT:
                    continue
            if kn.get(key, 0) >= val:
                continue
            self._force(key)
            assert self.cnt[key] >= val, (key, self.cnt[key], val)
            kn[key] = val
            out.append((key, val))
        return out

    @staticmethod
    def _merge(d, s):
        for k, v in s.items():
            if d.get(k, 0) < v:
                d[k] = v

    def _deps(self, reads, writes):
        deps = {}
        for r in reads:
            self._merge(deps, r.w)
        for w in writes:
            self._merge(deps, w.w)
            self._merge(deps, w.r)
        return deps

    def op(self, eng, fn, reads=(), writes=()):
        deps = self._deps(reads, writes)
        waits = self._waits(eng, deps)
        key = "E_" + eng
        rec = Rec(waits, fn, None)
        self.prog[eng].append(rec)
        self.last[eng] = rec
        self.pending[eng] = True
        tok = {key: self.cnt[key] + 1}
        for r in reads:
            self._merge(r.r, tok)
        for w in writes:
            w.w = dict(tok)
            w.r = {}

    def dma(self, queue, fn, reads=(), writes=(), sem=None):
        deps = self._deps(reads, writes)
        waits = self._waits(queue, deps)
        if sem is None:
            if writes:
                w0 = writes[0]
                if w0.lsem is None:
                    w0.lsem = self.dma_sem()
                    self.live_res.append(w0)
                sem = w0.lsem
            else:
                r0 = reads[0]
                if r0.ssem is None:
                    r0.ssem = self.dma_sem()
                    self.live_res.append(r0)
                sem = r0.ssem
        self.cnt[sem] += 16
        tok = {sem: self.cnt[sem]}
        rec = Rec(waits, fn, (sem, 16))
        self.prog[queue].append(rec)
        if queue in COMPUTE:
            pass
        for r in reads:
            self._merge(r.r, tok)
        for w in writes:
            w.w = dict(tok)
            w.r = {}

    def barrier(self, recycle=True, final=False):
        for e in COMPUTE:
            self._force("E_" + e)
        allk = {k: v for k, v in self.cnt.items() if v > 0 and (final or k not in self.nobarrier)}
        for eng in self.prog:
            waits = self._waits_all(eng, allk)
            if waits:
                self.prog[eng].append(Rec(waits, None, None))
        if recycle:
            self.free_dma_sems.extend(self.live_dma_sems)
            self.live_dma_sems = []
            for r_ in self.live_res:
                r_.lsem = None
                r_.ssem = None
            self.live_res = []

    def _waits_all(self, eng, allk):
        out = []
        kn = self.known[eng]
        for key, val in allk.items():
            if key == "E_" + eng:
                continue
            if kn.get(key, 0) >= val:
                continue
            kn[key] = val
            out.append((key, val))
        return out

    def replay(self, eng, e):
        for rec in self.prog[eng]:
            for key, val in rec.waits:
                e.wait_ge(self.sems[key], val)
            if rec.fn is None:
                continue
            name, a, k = rec.fn
            ins = getattr(e, name)(*a, **k)
            if rec.inc is not None:
                ins.then_inc(self.sems[rec.inc[0]], rec.inc[1])

    def run_block(self):
        nc = self.nc
        self.barrier(recycle=False, final=True)
        with nc.Block() as block:
            @block.tensor
            def _(e):
                self.replay("pe", e)

            @block.scalar
            def _(e):
                self.replay("act", e)

            @block.vector
            def _(e):
                self.replay("dve", e)

            @block.gpsimd
            def _(e):
                self.replay("pool", e)

            @block.sync
            def _(e):
                self.replay("sp", e)


class Arena:
    def __init__(self, t, nwords, base=0):
        self.t = t
        self.n = base + nwords
        self.off = base
        self.base = base

    def sub(self, nwords):
        assert self.off + nwords <= self.n, ("arena overflow(sub)", self.off, nwords, self.n)
        a = Arena(self.t, nwords, self.off)
        self.off += nwords
        return a

    def reset(self):
        self.off = 0

    def f32(self, n, parts=128):
        assert self.off + n <= self.n, ("arena overflow", self.off, n, self.n)
        ap = self.t[0:parts, self.off:self.off + n]
        self.off += n
        return ap

    def bf16(self, n, parts=128):
        w = (n + 1) // 2
        assert self.off + w <= self.n, ("arena overflow", self.off, w, self.n)
        ap = self.t[0:parts, self.off:self.off + w].bitcast(BF16)
        self.off += w
        return ap[:, 0:n]


D = 2048
DIN = 5632
DFF = 5632
WCOLS = DIN + 1024
EPS = 1e-6
GRID_W = 64
NEG = -30000.0
CAST_BARRIER = False
import os
NA_OLDPS = bool(int(os.environ.get('NA_OLDPS', '0')))


def host_consts(L):
    c = {}
    c["ident"] = np.eye(128, dtype=np.float32)
    pos = np.arange(L)
    row = (pos // GRID_W).astype(np.float32)
    col = (pos % GRID_W).astype(np.float32)
    nf = 32
    inv = (10000.0 ** (-np.arange(nf, dtype=np.float32) / nf)).astype(np.float32)
    f = np.arange(128)
    p = np.where((f // 64)[:, None] == 0, row[None, :], col[None, :]).astype(np.float32)
    ang = (p * inv[f % 32][:, None]).astype(np.float32)
    sign = np.where((f % 64) < 32, -1.0, 1.0).astype(np.float32)[:, None]
    C = np.cos(ang).astype(np.float32)
    Sg = (sign * np.sin(ang)).astype(np.float32)
    sc = np.float32(128 ** -0.5)
    c["rope"] = np.stack([C * sc, Sg * sc, C, Sg]).astype(np.float32)
    i = np.arange(128)
    jj, ii = np.meshgrid(i, i, indexing="ij")
    dec = np.stack([np.maximum(ii - jj, 0), (ii >= jj), np.maximum(jj - ii, 0), (jj >= ii)]).astype(np.float32)
    c["dec"] = dec
    xirow = np.stack([np.tile((i + 1)[None, :], (128, 1)), np.tile((128 - i)[None, :], (128, 1))]).astype(np.float32)
    c["xirow"] = xirow
    c["zcol"] = np.stack([127 - i, i], axis=1).astype(np.float32)
    R = L // GRID_W
    types = na_types(R)
    nam = np.zeros((5, 128, 576), np.float32)
    cols = np.arange(64)
    cs = np.clip(cols - 8, 0, 64 - 16)
    band = (cols[None, :] >= cs[:, None]) & (cols[None, :] < cs[:, None] + 16)
    for ti, (m, lo) in enumerate(types):
        for qr in range(2):
            r = 2 * m + qr
            w0 = int(np.clip(r - 4, 0, R - 8))
            for kidx in range(9):
                kr = lo + kidx
                ok = (w0 <= kr < w0 + 8)
                blk = np.where(band, 0.0, NEG) if ok else np.full((64, 64), NEG)
                nam[ti, qr * 64:(qr + 1) * 64, kidx * 64:(kidx + 1) * 64] = blk
    c["nam"] = nam
    return c


def na_types(R):
    M = R // 2
    return [(0, 0), (1, 0), (2, 0), (M - 2, R - 9), (M - 1, R - 9)]


def na_type_of(m, R):
    M = R // 2
    if m == 0:
        return 0, 0
    if m == 1:
        return 1, 0
    if m == M - 2:
        return 3, R - 9
    if m == M - 1:
        return 4, R - 9
    return 2, 2 * m - 4


def fm(v, nch):
    s = v.shape[:-1]
    return np.ascontiguousarray(np.moveaxis(v.reshape(s + (nch, 128)), -1, -2))


def host_layout(inp, b, L):
    f32 = np.float32
    o = {}
    o["x"] = np.ascontiguousarray(inp["x"][b], f32)
    o["ctx"] = np.ascontiguousarray(inp["ctx"][b], f32)
    cv = np.stack([inp["c"][b], inp["c_ctx"]], axis=1).astype(f32)
    o["cT"] = np.ascontiguousarray(cv.reshape(16, 128, 2).transpose(1, 0, 2))
    for k in ("w_ada", "b_ada", "w_in", "w_out", "ffn_up", "ffn_down", "conv_pw", "final_g", "na_rpb"):
        o[k] = np.ascontiguousarray(inp[k], f32)
    o["ret_decay"] = np.ascontiguousarray(inp["ret_decay"].reshape(-1, 8), f32)
    o["gng"] = np.ascontiguousarray(inp["ret_gn_g"], f32)
    o["n1g"] = fm(inp["norm1_g"].astype(f32), 16)
    o["n2g"] = fm(inp["norm2_g"].astype(f32), 16)
    o["cdw"] = np.ascontiguousarray(fm(inp["conv_dw_w"].astype(f32), 4).transpose(0, 2, 3, 1))
    o["cdb"] = fm(inp["conv_dw_b"].astype(f32), 4)
    o["lng"] = fm(inp["conv_ln_g"].astype(f32), 4)
    o["lnb"] = fm(inp["conv_ln_b"].astype(f32), 4)
    o["fdw"] = np.ascontiguousarray(fm(inp["ffn_dw_w"].astype(f32), 88).transpose(0, 2, 3, 1))
    o["fdb"] = fm(inp["ffn_dw_b"].astype(f32), 88)
    o.update(host_consts(L))
    return o


class K:
    pass


def build(L, CTX, NL=2, dbg=False, upto=99):
    T = L + CTX
    R = L // GRID_W
    nc = bass.Bass("TRN2", target_bir_lowering=False)
    g = K()

    def din(name, shape, dt=F32):
        return nc.dram_tensor(name, list(shape), dt, kind="ExternalInput").ap()

    def dscr(name, shape, dt):
        return nc.dram_tensor(name, list(shape), dt, kind="ExternalOutput" if dbg else "Internal").ap()

    x_in = din("x", [L, D]); ctx_in = din("ctx", [CTX, D]); cT = din("cT", [128, 16, 2])
    w_ada = din("w_ada", [NL, D, 6 * D]); b_ada = din("b_ada", [NL, 6 * D]); w_in = din("w_in", [NL, D, DIN])
    w_out = din("w_out", [NL, D, D]); ffn_up = din("ffn_up", [NL, D, 2 * DFF]); ffn_down = din("ffn_down", [NL, DFF, D])
    conv_pw = din("conv_pw", [NL, 512, 512]); final_g = din("final_g", [D]); na_rpb = din("na_rpb", [NL, 4, 15, 31])
    ret_decay = din("ret_decay", [NL, 8]); gng = din("gng", [NL, 1024])
    n1g = din("n1g", [NL, 128, 16]); n2g = din("n2g", [NL, 128, 16])
    cdw = din("cdw", [NL, 128, 4, 31]); cdb = din("cdb", [NL, 128, 4]); lng = din("lng", [NL, 128, 4]); lnb = din("lnb", [NL, 128, 4])
    fdw = din("fdw", [NL, 128, 88, 3]); fdb = din("fdb", [NL, 128, 88])
    ident_d = din("ident", [128, 128]); rope_d = din("rope", [4, 128, L]); dec_d = din("dec", [4, 128, 128])
    xirow_d = din("xirow", [2, 128, 128]); zcol_d = din("zcol", [128, 2]); nam_d = din("nam", [5, 128, 576])
    out_d = nc.dram_tensor("out", [L, D], F32, kind="ExternalOutput").ap()

    winb = dscr("winb", [NL, D, WCOLS], BF16); woutb = dscr("woutb", [NL, D, D], BF16)
    upb = dscr("upb", [NL, D, 2 * DFF], BF16); downb = dscr("downb", [NL, DFF, D], BF16)
    pwb = dscr("pwb", [NL, 512, 512], BF16); wadab = dscr("wadab", [NL, D, 6 * D], BF16)
    mods_d = dscr("mods", [2, 6 * D], F32)
    qT_d = dscr("qT", [512, T], BF16); kT_d = dscr("kT", [512, T], BF16); v_d = dscr("v", [T, 1024], BF16)
    sg_d = dscr("sg", [T, 1024], F32); uT_d = dscr("uT", [512, T], F32)
    nqT_d = dscr("nqT", [512, T], BF16); nkT_d = dscr("nkT", [512, T], BF16); nv_d = dscr("nv", [T, 512], BF16)
    of_d = dscr("of", [T, 1024], F32); mixT_d = dscr("mixT", [D, T], BF16)
    xa_d = dscr("xa", [T, D], F32); toep_d = dscr("toep", [4, 64, 17, 64], F32)

    S = Sched(nc)
    NARENA = 52900
    import contextlib
    es = contextlib.ExitStack()
    arena_t = es.enter_context(nc.sbuf_tensor("arena", [128, NARENA], F32))
    psum_t = es.enter_context(nc.psum_tensor("psum", [128, 4096], F32))
    A = Arena(arena_t, NARENA)
    PS = [psum_t[:, i * 512:(i + 1) * 512] for i in range(8)]
    RPS = [Res("ps%d" % i) for i in range(8)]

    ident = A.f32(128); Rid = Res("ident")
    identb = A.bf16(128)
    modT = A.f32(192); RmodT = Res("modT")
    G1T = A.f32(32); G2T = A.f32(32); RG = Res("G")
    ztile = A.f32(2176); Rzt = Res("zt")
    pers_mark = A.off

    S.dma("sp", lambda e: e.dma_start(out=ident, in_=ident_d), writes=[Rid])
    S.op("dve", lambda e: e.tensor_copy(out=identb, in_=ident), reads=[Rid], writes=[Rid])

    RC = {}

    def cast(dst, src, rows, step, key):
        if key not in RC:
            sem = S._mk("C_" + key)
            S.nobarrier.add(sem)
            RC[key] = (Res("cast_" + key), sem)
        rc, sem = RC[key]
        for r0 in range(0, rows, step):
            S.dma("pool", lambda e, r0=r0: e.dma_start(out=dst[r0:r0 + step], in_=src[r0:r0 + step]), sem=sem)
        rc.w = {sem: S.cnt[sem]}

    def issue_casts(l):
        cast(winb[l][:, 0:DIN], w_in[l], D, 256, "win%d" % l)
        cast(pwb[l], conv_pw[l], 512, 512, "pw%d" % l)
        cast(woutb[l], w_out[l], D, 512, "wout%d" % l)
        cast(upb[l], ffn_up[l], D, 128, "up%d" % l)
        cast(downb[l], ffn_down[l], DFF, 512, "down%d" % l)

    def RCW(key):
        return RC[key][0]

    def phase_mods(l):
        A.off = pers_mark
        sT = A.f32(32); RsT = Res("sT")
        sTs = A.f32(32)
        bada2 = A.f32(6 * D, parts=2); Rb = Res("bada2")
        m = A.f32(6 * D, parts=2); Rm = Res("m")
        ng = A.f32(32); Rng = Res("ng")
        wt = [A.f32(16 * 512) for _ in range(2)]; Rwt = [Res("wt%d" % i) for i in range(2)]
        S.dma("sp", lambda e: e.dma_start(out=sT, in_=cT.rearrange("p k m -> p (k m)")), writes=[RsT])
        S.dma("sp", lambda e: e.dma_start(out=bada2, in_=b_ada[l:l + 1, :].broadcast_to([2, 6 * D])), writes=[Rb])
        S.dma("sp", lambda e: e.dma_start(out=ng[:, 0:16], in_=n1g[l]), writes=[Rng])
        S.dma("sp", lambda e: e.dma_start(out=ng[:, 16:32], in_=n2g[l]), writes=[Rng])
        S.op("act", lambda e: e.activation(out=sTs, in_=sT, func=AF.Silu), reads=[RsT], writes=[RsT])
        for nb in range(24):
            sl = nb % 2
            S.dma("sp", lambda e, nb=nb, sl=sl: e.dma_start(
                out=wt[sl].rearrange("p (k n) -> p k n", k=16),
                in_=w_ada[l][:, nb * 512:(nb + 1) * 512].rearrange("(k p) n -> p k n", p=128)), writes=[Rwt[sl]])
            for k in range(16):
                S.op("pe", lambda e, nb=nb, sl=sl, k=k: e.matmul(PS[sl][0:2, :], lhsT=sTs[:, 2 * k:2 * k + 2],
                                                               rhs=wt[sl][:, k * 512:(k + 1) * 512], start=(k == 0), stop=(k == 15)),
                     reads=[RsT, Rwt[sl]], writes=[RPS[sl]])
            S.op("dve", lambda e, nb=nb, sl=sl: e.tensor_tensor(out=m[:, nb * 512:(nb + 1) * 512], in0=PS[sl][0:2, :],
                                                                in1=bada2[:, nb * 512:(nb + 1) * 512], op=ALU.add),
                 reads=[RPS[sl], Rb], writes=[Rm])
        for s0 in (1, 4):
            S.op("dve", lambda e, s0=s0: e.tensor_scalar_add(out=m[:, s0 * D:(s0 + 1) * D], in0=m[:, s0 * D:(s0 + 1) * D], scalar1=1.0),
                 reads=[Rm], writes=[Rm])
        S.dma("pool", lambda e: e.dma_start(out=mods_d, in_=m), reads=[Rm])
        for j in range(96):
            S.op("pe", lambda e, j=j: e.transpose(PS[2][:, 2 * j:2 * j + 2], m[0:2, j * 128:(j + 1) * 128], ident[0:2, 0:2]),
                 reads=[Rm, Rid], writes=[RPS[2]])
        S.op("dve", lambda e: e.tensor_copy(out=modT, in_=PS[2][:, 0:192]), reads=[RPS[2]], writes=[RmodT])
        m3 = modT.rearrange("p (j m) -> p j m", m=2)
        for cond in range(2):
            S.op("dve", lambda e, cond=cond: e.tensor_tensor(out=G1T.rearrange("p (c m) -> p c m", m=2)[:, :, cond],
                                                             in0=ng[:, 0:16], in1=m3[:, 16:32, cond], op=ALU.mult),
                 reads=[RmodT, Rng], writes=[RG])
            S.op("dve", lambda e, cond=cond: e.tensor_tensor(out=G2T.rearrange("p (c m) -> p c m", m=2)[:, :, cond],
                                                             in0=ng[:, 16:32], in1=m3[:, 64:80, cond], op=ALU.mult),
                 reads=[RmodT, Rng], writes=[RG])
        S.barrier()

    def modcol(sec, c, cond):
        j = sec * 16 + c
        return modT[:, 2 * j + cond:2 * j + cond + 1]

    def phase_inproj(l, xlat, xctx):
        A.off = pers_mark
        rp = A.f32(4 * 512); Rrp = Res("rp")
        xs = [A.f32(D) for _ in range(4)]; Rxs = [Res("xs%d" % i) for i in range(4)]
        hT = A.bf16(16 * 512); RhT = Res("hT")
        junk = A.bf16(D); Rjunk = Res("junk")
        wb = [A.bf16(16 * 512) for _ in range(4)]; Rwb = [Res("wb%d" % i) for i in range(4)]
        stb = [A.bf16(512) for _ in range(3)]; Rstb = [Res("stb%d" % i) for i in range(3)]
        stf = [A.f32(512) for _ in range(3)]; Rstf = [Res("stf%d" % i) for i in range(3)]
        tmp = [A.f32(512) for _ in range(4)]; Rtmp = [Res("tmp%d" % i) for i in range(4)]
        ss = A.f32(8); Rss = Res("ss")
        cnt = {"w": 0, "sb": 0, "sf": 0, "tmp": 0, "ps": 0}

        def nxt(key, n):
            v = cnt[key]; cnt[key] = (v + 1) % n; return v

        def load_w(cb):
            sl = nxt("w", 4)
            S.dma("sp", lambda e: e.dma_start(out=wb[sl].rearrange("p (k n) -> p k n", k=16),
                                              in_=winb[l][:, cb * 512:(cb + 1) * 512].rearrange("(k p) n -> p k n", p=128)),
                  writes=[Rwb[sl]], reads=[RCW("win%d" % l)])
            return wb[sl], Rwb[sl]

        def psb():
            i = 2 + nxt("ps", 6)
            return PS[i], RPS[i]

        tiles = [(0, t0, min(512, L - t0)) for t0 in range(0, L, 512)] + [(1, t0, min(512, CTX - t0)) for t0 in range(0, CTX, 512)]
        for (isctx, t0, n) in tiles:
            cond = isctx
            nb = n // 128
            src = xctx if isctx else xlat
            tok0 = L + t0 if isctx else t0
            for tb in range(nb):
                S.dma("sp", lambda e, tb=tb: e.dma_start(out=xs[tb], in_=src[t0 + tb * 128:t0 + (tb + 1) * 128, :]), writes=[Rxs[tb]])
            if not isctx:
                S.dma("sp", lambda e: e.dma_start(out=rp.rearrange("p (a t) -> p a t", a=4)[:, :, 0:n],
                                                  in_=rope_d[:, :, t0:t0 + n].rearrange("a p t -> p a t")), writes=[Rrp])
            for tb in range(nb):
                S.op("act", lambda e, tb=tb: e.activation(out=junk, in_=xs[tb], func=AF.Square, accum_out=ss[:, tb:tb + 1]),
                     reads=[Rxs[tb]], writes=[Rjunk, Rss])
            S.op("act", lambda e: e.activation(out=ss[:, 4:4 + nb], in_=ss[:, 0:nb], func=AF.Sqrt, scale=1.0 / D, bias=EPS),
                 reads=[Rss], writes=[Rss])
            S.op("dve", lambda e: e.reciprocal(out=ss[:, 4:4 + nb], in_=ss[:, 4:4 + nb]), reads=[Rss], writes=[Rss])
            for tb in range(nb):
                S.op("dve", lambda e, tb=tb: e.tensor_scalar(out=xs[tb], in0=xs[tb], scalar1=ss[:, 4 + tb:5 + tb], scalar2=None, op0=ALU.mult),
                     reads=[Rxs[tb], Rss], writes=[Rxs[tb]])
            for c in range(16):
                bk = c % 2
                for tb in range(nb):
                    S.op("pe", lambda e, tb=tb, c=c, bk=bk: e.transpose(PS[bk][:, tb * 128:(tb + 1) * 128], xs[tb][:, c * 128:(c + 1) * 128], ident),
                         reads=[Rxs[tb], Rid], writes=[RPS[bk]])
                if c % 2 == 0:
                    S.op("dve", lambda e, c=c, bk=bk: e.tensor_scalar(out=hT[:, c * 512:c * 512 + n], in0=PS[bk][:, 0:n],
                                                                      scalar1=G1T[:, 2 * c + cond:2 * c + cond + 1], scalar2=modcol(0, c, cond),
                                                                      op0=ALU.mult, op1=ALU.add),
                         reads=[RPS[bk], RG, RmodT], writes=[RhT])
                else:
                    S.op("act", lambda e, c=c, bk=bk: e.activation(out=hT[:, c * 512:c * 512 + n], in_=PS[bk][:, 0:n], func=AF.Identity,
                                                                   scale=G1T[:, 2 * c + cond:2 * c + cond + 1], bias=modcol(0, c, cond)),
                         reads=[RPS[bk], RG, RmodT], writes=[RhT])

            def fm_chunk(wa, Rwa, j):
                ps, Rps = psb()
                for k in range(16):
                    S.op("pe", lambda e, k=k: e.matmul(ps[:, 0:n], lhsT=wa[:, k * 512 + j * 128:k * 512 + (j + 1) * 128],
                                                       rhs=hT[:, k * 512:k * 512 + n], start=(k == 0), stop=(k == 15)),
                         reads=[Rwa, RhT], writes=[Rps])
                return ps, Rps

            def tm_block(wa, Rwa, tb):
                ps, Rps = psb()
                for k in range(16):
                    S.op("pe", lambda e, k=k: e.matmul(ps[:, :], lhsT=hT[:, k * 512 + tb * 128:k * 512 + (tb + 1) * 128],
                                                       rhs=wa[:, k * 512:(k + 1) * 512], start=(k == 0), stop=(k == 15)),
                         reads=[Rwa, RhT], writes=[Rps])
                return ps, Rps

            def store_fm(dst, j, stage, Rst):
                S.dma("pool", lambda e: e.dma_start(out=dst[j * 128:(j + 1) * 128, tok0:tok0 + n], in_=stage[:, 0:n]), reads=[Rst])

            def store_tm(dst, tb, c0, stage, Rst):
                S.dma("pool", lambda e: e.dma_start(out=dst[tok0 + tb * 128:tok0 + (tb + 1) * 128, c0:c0 + 512], in_=stage[:, 0:512]), reads=[Rst])

            for qi, (cb, cbp, dst) in enumerate(((0, 11, qT_d), (1, 12, kT_d))):
                wa, Rwa = load_w(cb)
                if not isctx:
                    psl = nxt("w", 4)
                    wp, Rwp = wb[psl], Rwb[psl]
                    for bb in range(2):
                        S.op("pool", lambda e: e.tensor_copy(out=wp.rearrange("p (a b e) -> p a b e", b=2, e=32)[:, :, 1 - bb, :],
                                                             in_=wa.rearrange("p (a b e) -> p a b e", b=2, e=32)[:, :, bb, :]),
                             reads=[Rwa], writes=[Rwp])
                for j in range(4):
                    pa, Rpa = fm_chunk(wa, Rwa, j)
                    sb = nxt("sb", 3)
                    if not isctx:
                        pb, Rpb = fm_chunk(wp, Rwp, j)
                        t1 = nxt("tmp", 4); t2 = nxt("tmp", 4)
                        S.op("dve", lambda e, t1=t1, pa=pa: e.tensor_tensor(out=tmp[t1][:, 0:n], in0=pa[:, 0:n], in1=rp[:, (2 * qi) * 512:(2 * qi) * 512 + n], op=ALU.mult),
                             reads=[Rpa, Rrp], writes=[Rtmp[t1]])
                        S.op("dve", lambda e, t2=t2, pb=pb: e.tensor_tensor(out=tmp[t2][:, 0:n], in0=pb[:, 0:n], in1=rp[:, (2 * qi + 1) * 512:(2 * qi + 1) * 512 + n], op=ALU.mult),
                             reads=[Rpb, Rrp], writes=[Rtmp[t2]])
                        S.op("pool", lambda e, t1=t1, t2=t2, sb=sb: e.tensor_tensor(out=stb[sb][:, 0:n], in0=tmp[t1][:, 0:n], in1=tmp[t2][:, 0:n], op=ALU.add),
                             reads=[Rtmp[t1], Rtmp[t2]], writes=[Rstb[sb]])
                    else:
                        S.op("act", lambda e, sb=sb, pa=pa: e.activation(out=stb[sb][:, 0:n], in_=pa[:, 0:n], func=AF.Copy,
                                                                         scale=(128 ** -0.5 if qi == 0 else 1.0)),
                             reads=[Rpa], writes=[Rstb[sb]])
                    store_fm(dst, j, stb[sb], Rstb[sb])
            for half in range(2):
                wa, Rwa = load_w(2 + half)
                for tb in range(nb):
                    ps, Rps = tm_block(wa, Rwa, tb)
                    sb = nxt("sb", 3)
                    S.op("act", lambda e, sb=sb, ps=ps: e.activation(out=stb[sb], in_=ps, func=AF.Copy), reads=[Rps], writes=[Rstb[sb]])
                    store_tm(v_d, tb, half * 512, stb[sb], Rstb[sb])
            for half in range(2):
                wa, Rwa = load_w(4 + half)
                for tb in range(nb):
                    ps, Rps = tm_block(wa, Rwa, tb)
                    sf = nxt("sf", 3)
                    S.op("act", lambda e, sf=sf, ps=ps: e.activation(out=stf[sf], in_=ps, func=AF.Silu), reads=[Rps], writes=[Rstf[sf]])
                    store_tm(sg_d, tb, half * 512, stf[sf], Rstf[sf])
            wa, Rwa = load_w(6)
            wp, Rwp = load_w(7)
            for j in range(4):
                pa, Rpa = fm_chunk(wa, Rwa, j)
                pb, Rpb = fm_chunk(wp, Rwp, j)
                t1 = nxt("tmp", 4); sf = nxt("sf", 3)
                S.op("act", lambda e, t1=t1, pb=pb: e.activation(out=tmp[t1][:, 0:n], in_=pb[:, 0:n], func=AF.Sigmoid), reads=[Rpb], writes=[Rtmp[t1]])
                S.op("dve", lambda e, t1=t1, pa=pa, sf=sf: e.tensor_tensor(out=stf[sf][:, 0:n], in0=pa[:, 0:n], in1=tmp[t1][:, 0:n], op=ALU.mult),
                     reads=[Rpa, Rtmp[t1]], writes=[Rstf[sf]])
                store_fm(uT_d, j, stf[sf], Rstf[sf])
            for cb, dst, scl in ((8, nqT_d, 128 ** -0.5), (9, nkT_d, 1.0)):
                wa, Rwa = load_w(cb)
                for j in range(4):
                    pa, Rpa = fm_chunk(wa, Rwa, j)
                    sb = nxt("sb", 3)
                    S.op("act", lambda e, sb=sb, pa=pa, scl=scl: e.activation(out=stb[sb][:, 0:n], in_=pa[:, 0:n], func=AF.Copy, scale=scl),
                         reads=[Rpa], writes=[Rstb[sb]])
                    store_fm(dst, j, stb[sb], Rstb[sb])
            wa, Rwa = load_w(10)
            for tb in range(nb):
                ps, Rps = tm_block(wa, Rwa, tb)
                sb = nxt("sb", 3)
                S.op("dve", lambda e, sb=sb, ps=ps: e.tensor_copy(out=stb[sb], in_=ps), reads=[Rps], writes=[Rstb[sb]])
                store_tm(nv_d, tb, 0, stb[sb], Rstb[sb])
        S.barrier()

    PSB = [p_.bitcast(BF16) for p_ in PS]

    def phase_ret(l, with_ctx, A):
        RB0 = Res("ret_b0"); RB1 = Res("ret_b1"); RP_m = Res("rp_m")
        rd = A.f32(8); lg = A.f32(8); Rlg = Res("lg")
        cdt = A.f32(4 * 128); xir = A.f32(2 * 128); zc = A.f32(2); Rc = Res("retconst")
        DT = A.f32(8 * 128); XI = A.f32(8 * 128); zg = A.f32(16); Rtab = Res("rettab")
        gngt = A.f32(1024); Rgn = Res("gngt")
        Sf = [A.f32(256) for _ in range(4)]; Sb = [A.bf16(256) for _ in range(4)]; RS = [Res("S%d" % h) for h in range(4)]
        RSb = [Res("Sb%d" % h) for h in range(4)]
        qc = [A.bf16(512) for _ in range(2)]; Rqc = [Res("qc%d" % i) for i in range(2)]
        kc = [A.bf16(512) for _ in range(2)]; Rkc = [Res("kc%d" % i) for i in range(2)]
        vc = [A.bf16(1024) for _ in range(2)]; Rvc = [Res("vc%d" % i) for i in range(2)]
        ofc = [A.f32(1024) for _ in range(2)]; Rofc = [Res("ofc%d" % i) for i in range(2)]
        sgc = [A.f32(1024) for _ in range(2)]; Rsgc = [Res("sgc%d" % i) for i in range(2)]
        ob = [A.f32(1024) for _ in range(2)]; Rob = [Res("ob%d" % i) for i in range(2)]
        innb = [A.bf16(128) for _ in range(2)]; Rinnb = [Res("innb%d" % i) for i in range(2)]
        qx = [A.bf16(128) for _ in range(2)]; Rqx = [Res("qx%d" % i) for i in range(2)]
        kz = [A.bf16(128) for _ in range(2)]; Rkz = [Res("kz%d" % i) for i in range(2)]
        ybf = A.bf16(1024); Rybf = Res("ybf")
        mst = [A.bf16(1024) for _ in range(2)]; Rmst = [Res("mst%d" % i) for i in range(2)]
        st = A.f32(16); Rst = Res("gnstat")
        junk = A.f32(256); Rjunk = Res("junk")

        S.dma("sp", lambda e: e.dma_start(out=rd, in_=ret_decay[l:l + 1, :].broadcast_to([128, 8])), writes=[Rlg])
        S.dma("sp", lambda e: e.dma_start(out=cdt.rearrange("p (a i) -> p a i", a=4), in_=dec_d.rearrange("a p i -> p a i")), writes=[Rc])
        S.dma("sp", lambda e: e.dma_start(out=xir.rearrange("p (a i) -> p a i", a=2), in_=xirow_d.rearrange("a p i -> p a i")), writes=[Rc])
        S.dma("sp", lambda e: e.dma_start(out=zc, in_=zcol_d), writes=[Rc])
        S.dma("sp", lambda e: e.dma_start(out=gngt, in_=gng[l:l + 1, :].broadcast_to([128, 1024])), writes=[Rgn])
        S.op("act", lambda e: e.activation(out=lg, in_=rd, func=AF.Exp, scale=-1.0), reads=[Rlg], writes=[Rlg])
        S.op("act", lambda e: e.activation(out=lg, in_=lg, func=AF.Ln, bias=1.0), reads=[Rlg], writes=[Rlg])
        S.op("dve", lambda e: e.tensor_scalar(out=lg, in0=lg, scalar1=-1.0, scalar2=None, op0=ALU.mult), reads=[Rlg], writes=[Rlg])
        for dr in range(2):
            for h in range(4):
                col = dr * 4 + h
                S.op("act", lambda e: e.activation(out=DT[:, col * 128:(col + 1) * 128], in_=cdt[:, (2 * dr) * 128:(2 * dr + 1) * 128],
                                                   func=AF.Exp, scale=lg[:, col:col + 1]), reads=[Rlg, Rc], writes=[Rtab])
                S.op("dve", lambda e: e.tensor_tensor(out=DT[:, col * 128:(col + 1) * 128], in0=DT[:, col * 128:(col + 1) * 128],
                                                      in1=cdt[:, (2 * dr + 1) * 128:(2 * dr + 2) * 128], op=ALU.mult), reads=[Rtab, Rc], writes=[Rtab])
                S.op("act", lambda e: e.activation(out=XI[:, col * 128:(col + 1) * 128], in_=xir[:, dr * 128:(dr + 1) * 128],
                                                   func=AF.Exp, scale=lg[:, col:col + 1]), reads=[Rlg, Rc], writes=[Rtab])
                S.op("act", lambda e: e.activation(out=zg[:, col:col + 1], in_=zc[:, dr:dr + 1], func=AF.Exp, scale=lg[:, col:col + 1]),
                     reads=[Rlg, Rc], writes=[Rtab])
                S.op("act", lambda e: e.activation(out=zg[:, 8 + col:9 + col], in_=lg[:, col:col + 1], func=AF.Exp, scale=128.0),
                     reads=[Rlg, Rc], writes=[Rtab])
        nlc = L // 128; ncc = CTX // 128
        step = [0]
        for dr in range(2):
            for h in range(4):
                S.op("dve", lambda e: e.memset(Sf[h], 0.0), writes=[RS[h]])
                S.op("pool", lambda e: e.memset(Sb[h], 0.0), writes=[RSb[h]])
            if dr == 0:
                order = [(1, L + i * 128) for i in range(ncc)] + [(0, i * 128) for i in range(nlc)]
            else:
                order = [(1, L + i * 128) for i in reversed(range(ncc))] + [(0, i * 128) for i in reversed(range(nlc))]
            for (isctx, tok0) in order:
                need_out = (not isctx) or with_ctx
                sl = step[0] % 2; step[0] += 1
                S.dma("sp", lambda e: e.dma_start(out=kc[sl].rearrange("p (h t) -> p h t", h=4),
                                                  in_=kT_d[:, tok0:tok0 + 128].rearrange("(h d) t -> d h t", d=128)), writes=[Rkc[sl]])
                S.dma("sp", lambda e: e.dma_start(out=vc[sl], in_=v_d[tok0:tok0 + 128, :]), writes=[Rvc[sl]])
                if need_out:
                    S.dma("sp", lambda e: e.dma_start(out=qc[sl].rearrange("p (h t) -> p h t", h=4),
                                                      in_=qT_d[:, tok0:tok0 + 128].rearrange("(h d) t -> d h t", d=128)), writes=[Rqc[sl]])
                    if dr == 1:
                        S.dma("sp", lambda e: e.dma_start(out=ofc[sl], in_=of_d[tok0:tok0 + 128, :]), writes=[Rofc[sl]])
                        S.dma("sp", lambda e: e.dma_start(out=sgc[sl], in_=sg_d[tok0:tok0 + 128, :]), writes=[Rsgc[sl]])
                for h in range(4):
                    col = dr * 4 + h
                    hs = h % 2
                    kh = kc[sl][:, h * 128:(h + 1) * 128]
                    vh = vc[sl][:, h * 256:(h + 1) * 256]
                    if need_out:
                        qh = qc[sl][:, h * 128:(h + 1) * 128]
                        S.op("pe", lambda e: e.matmul(PS[0][:, 0:128], lhsT=kh, rhs=qh, start=True, stop=True),
                             reads=[Rkc[sl], Rqc[sl]], writes=[RB0])
                        yield
                        S.op("dve", lambda e: e.tensor_tensor(out=innb[hs], in0=PS[0][:, 0:128], in1=DT[:, col * 128:(col + 1) * 128], op=ALU.mult),
                             reads=[RB0, Rtab], writes=[Rinnb[hs]])
                        S.op("pool", lambda e: e.tensor_tensor(out=qx[hs], in0=qh, in1=XI[:, col * 128:(col + 1) * 128], op=ALU.mult),
                             reads=[Rqc[sl], Rtab], writes=[Rqx[hs]])
                        yield
                        S.op("pe", lambda e: e.matmul(PS[1][:, hs * 256:(hs + 1) * 256], lhsT=innb[hs], rhs=vh, start=True, stop=False),
                             reads=[Rinnb[hs], Rvc[sl]], writes=[RB1])
                        S.op("pe", lambda e: e.matmul(PS[1][:, hs * 256:(hs + 1) * 256], lhsT=qx[hs], rhs=Sb[h], start=False, stop=True),
                             reads=[Rqx[hs], RSb[h]], writes=[RB1])
                    S.op("pe", lambda e: e.transpose(PSB[0][:, 256 + hs * 128:256 + (hs + 1) * 128], kh, identb), reads=[Rkc[sl], Rid], writes=[RB0])
                    yield
                    S.op("act", lambda e: e.activation(out=kz[hs], in_=PSB[0][:, 256 + hs * 128:256 + (hs + 1) * 128], func=AF.Identity, scale=zg[:, col:col + 1]),
                         reads=[RB0, Rtab], writes=[Rkz[hs]])
                    yield
                    S.op("pe", lambda e: e.matmul(PS[0][:, 256:512], lhsT=kz[hs], rhs=vh, start=True, stop=True),
                         reads=[Rkz[hs], Rvc[sl]], writes=[RB0])
                    yield
                    S.op("dve", lambda e: e.scalar_tensor_tensor(out=Sf[h], in0=Sf[h], scalar=zg[:, 8 + col:9 + col], in1=PS[0][:, 256:512],
                                                                 op0=ALU.mult, op1=ALU.add), reads=[RS[h], RB0, Rtab], writes=[RS[h]])
                    S.op("act", lambda e: e.activation(out=Sb[h], in_=Sf[h], func=AF.Copy), reads=[RS[h]], writes=[RSb[h]])
                    if need_out:
                        if dr == 0:
                            S.op("act", lambda e: e.activation(out=ob[sl][:, h * 256:(h + 1) * 256], in_=PS[1][:, hs * 256:(hs + 1) * 256], func=AF.Copy),
                                 reads=[RB1], writes=[Rob[sl]])
                        else:
                            S.op("dve", lambda e: e.tensor_tensor(out=ob[sl][:, h * 256:(h + 1) * 256], in0=PS[1][:, hs * 256:(hs + 1) * 256],
                                                                  in1=ofc[sl][:, h * 256:(h + 1) * 256], op=ALU.add),
                                 reads=[RB1, Rofc[sl]], writes=[Rob[sl]])
                    yield
                if not need_out:
                    yield
                    continue
                if dr == 0:
                    S.dma("pool", lambda e: e.dma_start(out=of_d[tok0:tok0 + 128, :], in_=ob[sl]), reads=[Rob[sl]])
                    yield
                    continue
                for h in range(4):
                    S.op("act", lambda e: e.activation(out=junk, in_=ob[sl][:, h * 256:(h + 1) * 256], func=AF.Identity, accum_out=st[:, h:h + 1]),
                         reads=[Rob[sl]], writes=[Rjunk, Rst])
                    S.op("act", lambda e: e.activation(out=junk, in_=ob[sl][:, h * 256:(h + 1) * 256], func=AF.Square, accum_out=st[:, 4 + h:5 + h]),
                         reads=[Rob[sl]], writes=[Rjunk, Rst])
                S.op("dve", lambda e: e.tensor_scalar(out=st[:, 0:8], in0=st[:, 0:8], scalar1=1.0 / 256, scalar2=None, op0=ALU.mult), reads=[Rst], writes=[Rst])
                S.op("dve", lambda e: e.tensor_tensor(out=st[:, 8:12], in0=st[:, 0:4], in1=st[:, 0:4], op=ALU.mult), reads=[Rst], writes=[Rst])
                S.op("dve", lambda e: e.tensor_tensor(out=st[:, 8:12], in0=st[:, 4:8], in1=st[:, 8:12], op=ALU.subtract), reads=[Rst], writes=[Rst])
                S.op("act", lambda e: e.activation(out=st[:, 8:12], in_=st[:, 8:12], func=AF.Sqrt, bias=EPS), reads=[Rst], writes=[Rst])
                S.op("dve", lambda e: e.reciprocal(out=st[:, 8:12], in_=st[:, 8:12]), reads=[Rst], writes=[Rst])
                for h in range(4):
                    S.op("dve", lambda e: e.tensor_scalar(out=ob[sl][:, h * 256:(h + 1) * 256], in0=ob[sl][:, h * 256:(h + 1) * 256],
                                                          scalar1=st[:, h:h + 1], scalar2=st[:, 8 + h:9 + h], op0=ALU.subtract, op1=ALU.mult),
                         reads=[Rob[sl], Rst], writes=[Rob[sl]])
                S.op("pool", lambda e: e.tensor_tensor(out=ob[sl], in0=ob[sl], in1=gngt, op=ALU.mult), reads=[Rob[sl], Rgn], writes=[Rob[sl]])
                S.op("dve", lambda e: e.tensor_tensor(out=ybf, in0=ob[sl], in1=sgc[sl], op=ALU.mult), reads=[Rob[sl], Rsgc[sl]], writes=[Rybf])
                for c in range(8):
                    S.op("pe", lambda e: e.transpose(PSB[3][:, c * 128:(c + 1) * 128], ybf[:, c * 128:(c + 1) * 128], identb),
                         reads=[Rybf, Rid], writes=[RP_m])
                S.op("act", lambda e: e.activation(out=mst[sl], in_=PSB[3][:, 0:1024], func=AF.Copy), reads=[RP_m], writes=[Rmst[sl]])
                S.dma("pool", lambda e: e.dma_start(out=mixT_d[0:1024, tok0:tok0 + 128].rearrange("(c p) t -> p c t", p=128),
                                                    in_=mst[sl].rearrange("p (c t) -> p c t", c=8)), reads=[Rmst[sl]])
                yield

    def phase_conv(l, with_ctx, A):
        RPC = Res('rp_conv')
        cw = A.f32(124); cb = A.f32(4); lg_ = A.f32(4); lb_ = A.f32(4); Rcp = Res("convp")
        ones = A.f32(128); Rones = Res("ones")
        pw = A.bf16(4 * 512); Rpw = Res("pw")
        ub = [A.f32(4 * 542) for _ in range(1)]; Rub = [Res("ub%d" % i) for i in range(1)]
        acc = [A.f32(512) for _ in range(4)]; Racc = [Res("acc%d" % i) for i in range(4)]
        sq = [A.f32(512) for _ in range(4)]; Rsq = [Res("sq%d" % i) for i in range(4)]
        rstd = A.f32(512); Rrstd = Res("rstd")
        zb = A.bf16(4 * 512); Rzb = Res("zb")
        stb = [A.bf16(512) for _ in range(2)]; Rstb = [Res("cstb%d" % i) for i in range(2)]
        S.dma("sp", lambda e: e.dma_start(out=cw, in_=cdw[l].rearrange("p c k -> p (c k)")), writes=[Rcp])
        S.dma("sp", lambda e: e.dma_start(out=cb, in_=cdb[l]), writes=[Rcp])
        S.dma("sp", lambda e: e.dma_start(out=lg_, in_=lng[l]), writes=[Rcp])
        S.dma("sp", lambda e: e.dma_start(out=lb_, in_=lnb[l]), writes=[Rcp])
        S.dma("sp", lambda e: e.dma_start(out=pw.rearrange("p (k n) -> p k n", k=4), in_=pwb[l].rearrange("(k p) n -> p k n", p=128)), writes=[Rpw], reads=[RCW("pw%d" % l)])
        S.op("pool", lambda e: e.memset(ones, 1.0 / 512), writes=[Rones])
        tiles = [(0, t0, min(512, L - t0)) for t0 in range(0, L, 512)]
        if with_ctx:
            tiles += [(1, t0, min(512, CTX - t0)) for t0 in range(0, CTX, 512)]
        for ti, (isctx, t0, n) in enumerate(tiles):
            Ls = CTX if isctx else L
            base = L if isctx else 0
            sl = 0
            u3 = ub[sl].rearrange("p (c t) -> p c t", c=4)
            lo = max(t0 - 15, 0); hi = min(t0 + n + 15, Ls)
            off = lo - (t0 - 15)
            if lo != t0 - 15 or hi != t0 + n + 15:
                S.op("pool", lambda e: e.memset(ub[sl], 0.0), writes=[Rub[sl]])
            S.dma("sp", lambda e: e.dma_start(out=u3[:, :, off:off + hi - lo], in_=uT_d[:, base + lo:base + hi].rearrange("(c p) t -> p c t", p=128)),
                  writes=[Rub[sl]])
            for c in range(4):
                S.op("dve", lambda e: e.tensor_scalar(out=acc[c][:, 0:n], in0=u3[:, c, 15:15 + n], scalar1=cw[:, c * 31 + 15:c * 31 + 16],
                                                      scalar2=cb[:, c:c + 1], op0=ALU.mult, op1=ALU.add), reads=[Rub[sl], Rcp], writes=[Racc[c]])
            for k in range(31):
                if k == 15:
                    continue
                for c in range(4):
                    S.op("dve", lambda e: e.scalar_tensor_tensor(out=acc[c][:, 0:n], in0=u3[:, c, k:k + n], scalar=cw[:, c * 31 + k:c * 31 + k + 1],
                                                                 in1=acc[c][:, 0:n], op0=ALU.mult, op1=ALU.add),
                         reads=[Rub[sl], Rcp, Racc[c]], writes=[Racc[c]])
                yield
            for c in range(4):
                S.op("pe", lambda e: e.matmul(PS[6][:, 0:n], lhsT=ones, rhs=acc[c][:, 0:n], start=(c == 0), stop=(c == 3)),
                     reads=[Rones, Racc[c]], writes=[RPC])
            for c in range(4):
                S.op("dve", lambda e: e.tensor_tensor(out=acc[c][:, 0:n], in0=acc[c][:, 0:n], in1=PS[6][:, 0:n], op=ALU.subtract),
                     reads=[RPC, Racc[c]], writes=[Racc[c]])
                S.op("act", lambda e: e.activation(out=sq[c][:, 0:n], in_=acc[c][:, 0:n], func=AF.Square), reads=[Racc[c]], writes=[Rsq[c]])
            for c in range(4):
                S.op("pe", lambda e: e.matmul(PS[6][:, 0:n], lhsT=ones, rhs=sq[c][:, 0:n], start=(c == 0), stop=(c == 3)),
                     reads=[Rones, Rsq[c]], writes=[RPC])
            yield
            S.op("act", lambda e: e.activation(out=rstd[:, 0:n], in_=PS[6][:, 0:n], func=AF.Sqrt, bias=EPS), reads=[RPC], writes=[Rrstd])
            S.op("dve", lambda e: e.reciprocal(out=rstd[:, 0:n], in_=rstd[:, 0:n]), reads=[Rrstd], writes=[Rrstd])
            for c in range(4):
                S.op("dve", lambda e: e.tensor_tensor(out=acc[c][:, 0:n], in0=acc[c][:, 0:n], in1=rstd[:, 0:n], op=ALU.mult),
                     reads=[Rrstd, Racc[c]], writes=[Racc[c]])
                S.op("act", lambda e: e.activation(out=zb[:, c * 512:c * 512 + n], in_=acc[c][:, 0:n], func=AF.Silu,
                                                   scale=lg_[:, c:c + 1], bias=lb_[:, c:c + 1]), reads=[Racc[c], Rcp], writes=[Rzb])
            for co in range(4):
                bk = 6
                for ci in range(4):
                    S.op("pe", lambda e: e.matmul(PS[bk][:, 0:n], lhsT=pw[:, ci * 512 + co * 128:ci * 512 + (co + 1) * 128],
                                                  rhs=zb[:, ci * 512:ci * 512 + n], start=(ci == 0), stop=(ci == 3)),
                         reads=[Rpw, Rzb], writes=[RPC])
                ss_ = co % 2
                S.op("act", lambda e: e.activation(out=stb[ss_][:, 0:n], in_=PS[bk][:, 0:n], func=AF.Copy), reads=[RPC], writes=[Rstb[ss_]])
                S.dma("pool", lambda e: e.dma_start(out=mixT_d[1024 + co * 128:1024 + (co + 1) * 128, base + t0:base + t0 + n], in_=stb[ss_][:, 0:n]),
                      reads=[Rstb[ss_]])
                yield

    def na_zero():
        S.op("pool", lambda e: e.memset(ztile, 0.0), writes=[Rzt])
        S.dma("sp", lambda e: e.dma_start(out=bass.AP(toep_d.tensor, 0, [[2176, 128], [1, 2176]]), in_=ztile), reads=[Rzt])

    def na_diag(l):
        dsem = S.dma_sem()
        for h in range(4):
            for ro in range(15):
                dst = bass.AP(toep_d.tensor, h * 69632 + (ro + 1) * 64 - 15, [[1089, 64], [1, 31]])
                S.dma("sp", lambda e: e.dma_start(out=dst, in_=na_rpb[l, h, ro:ro + 1, :].broadcast_to([64, 31])), sem=dsem)

    def phase_na(l, with_ctx, A):
        RP_sc = Res("rp_sc"); RP_pt = Res("rp_pt"); RP_no = Res("rp_no")
        TB = A.f32(20 * 576); RTB = Res("TB")
        nam = A.f32(5 * 576); Rnam = Res("nam")
        ckT = A.bf16(4 * CTX); Rck = Res("ckT")
        cv = A.bf16((CTX // 128) * 512); Rcv = Res("cv")
        qm = [A.bf16(512) for _ in range(2)]; Rqm = [Res("qm%d" % i) for i in range(2)]
        km = [A.bf16(4 * 576) for _ in range(2)]; Rkm = [Res("km%d" % i) for i in range(2)]
        vm = [A.bf16(5 * 512) for _ in range(2)]; Rvm = [Res("vm%d" % i) for i in range(2)]
        scs = [A.f32(832)] * 2; Rscs = [Res("scs")] * 2
        pb = [A.bf16(832) for _ in range(2)]; Rpb = [Res("pb%d" % i) for i in range(2)]
        pT = [A.bf16(7 * 128) for _ in range(2)]; RpT = [Res("pT%d" % i) for i in range(2)]
        nst = [A.bf16(512) for _ in range(2)]; Rnst = [Res("nst%d" % i) for i in range(2)]
        sm = A.f32(16); Rsm = Res("sm")
        types = na_types(R)
        for h in range(4):
            for ti, (mrep, lo) in enumerate(types):
                idx = h * 5 + ti
                for qr in range(2):
                    ro0 = lo - (2 * mrep + qr) + 7
                    src = bass.AP(toep_d.tensor, h * 69632 + (ro0 + 1) * 64, [[1088, 64], [1, 576]])
                    S.dma("sp", lambda e: e.dma_start(out=TB[qr * 64:(qr + 1) * 64, idx * 576:(idx + 1) * 576], in_=src), writes=[RTB])
        S.dma("sp", lambda e: e.dma_start(out=nam.rearrange("p (a k) -> p a k", a=5), in_=nam_d.rearrange("a p k -> p a k")), writes=[Rnam])
        for h in range(4):
            S.op("dve", lambda e: e.tensor_tensor(out=TB[:, h * 2880:(h + 1) * 2880], in0=TB[:, h * 2880:(h + 1) * 2880], in1=nam, op=ALU.add),
                 reads=[RTB, Rnam], writes=[RTB])
        S.dma("sp", lambda e: e.dma_start(out=ckT.rearrange("p (h t) -> p h t", h=4), in_=nkT_d[:, L:T].rearrange("(h d) t -> d h t", d=128)), writes=[Rck])
        S.dma("sp", lambda e: e.dma_start(out=cv.rearrange("p (c f) -> p c f", f=512), in_=nv_d[L:T, :].rearrange("(c p) f -> p c f", p=128)), writes=[Rcv])
        cnt = [0]

        def na_block(sl, tokdst, segs, vch):
            ntot = sum(s_[1] for s_ in segs)
            for h in range(4):
                i = cnt[0] % 2; cnt[0] += 1
                sc = psum_t[:, 4 * 512:4 * 512 + 1024]
                ops_ = PS[2][:, 0:128]
                o = 0
                for (rf, ncol, bf_) in segs:
                    c0 = 0
                    while c0 < ncol:
                        w_ = min(ncol - c0, 512 - (o % 512))
                        S.op("pe", lambda e: e.matmul(sc[:, o:o + w_], lhsT=qm[sl][:, h * 128:(h + 1) * 128], rhs=rf(h)[:, c0:c0 + w_], start=True, stop=True),
                             reads=[Rqm[sl], Rkm[sl], Rck], writes=[RP_sc])
                        o += w_; c0 += w_
                yield
                o = 0
                for (rf, ncol, bf_) in segs:
                    if bf_ is not None:
                        S.op("dve", lambda e: e.tensor_tensor(out=scs[i][:, o:o + ncol], in0=sc[:, o:o + ncol], in1=bf_(h), op=ALU.add),
                             reads=[RP_sc, RTB], writes=[Rscs[i]])
                    else:
                        S.op("act", lambda e: e.activation(out=scs[i][:, o:o + ncol], in_=sc[:, o:o + ncol], func=AF.Copy),
                             reads=[RP_sc], writes=[Rscs[i]])
                    o += ncol
                yield
                S.op("dve", lambda e: e.reduce_max(out=sm[:, i:i + 1], in_=scs[i][:, 0:ntot], axis=AX.X), reads=[Rscs[i]], writes=[Rsm])
                S.op("dve", lambda e: e.tensor_scalar(out=sm[:, i:i + 1], in0=sm[:, i:i + 1], scalar1=-1.0, scalar2=None, op0=ALU.mult), reads=[Rsm], writes=[Rsm])
                yield
                S.op("act", lambda e: e.activation(out=scs[i][:, 0:ntot], in_=scs[i][:, 0:ntot], func=AF.Exp, bias=sm[:, i:i + 1],
                                                   accum_out=sm[:, 4 + i:5 + i]), reads=[Rscs[i], Rsm], writes=[Rscs[i], Rsm])
                yield
                S.op("dve", lambda e: e.reciprocal(out=sm[:, 4 + i:5 + i], in_=sm[:, 4 + i:5 + i]), reads=[Rsm], writes=[Rsm])
                S.op("dve", lambda e: e.tensor_scalar(out=pb[i][:, 0:ntot], in0=scs[i][:, 0:ntot], scalar1=sm[:, 4 + i:5 + i], scalar2=None, op0=ALU.mult),
                     reads=[Rscs[i], Rsm], writes=[Rpb[i]])
                yield
                bt = 7
                o = 0
                for ci, (vf, nk) in enumerate(vch):
                    S.op("pe", lambda e: e.transpose(PSB[bt][0:nk, ci * 128:(ci + 1) * 128], pb[i][:, o:o + nk], identb),
                         reads=[Rpb[i], Rid], writes=[RP_pt])
                    o += nk
                nch = len(vch)
                yield
                S.op("act", lambda e: e.activation(out=pT[i][:, 0:nch * 128], in_=PSB[bt][:, 0:nch * 128], func=AF.Copy), reads=[RP_pt], writes=[RpT[i]])
                yield
                for ci, (vf, nk) in enumerate(vch):
                    S.op("pe", lambda e: e.matmul(ops_, lhsT=vf(h)[0:nk, :], rhs=pT[i][0:nk, ci * 128:(ci + 1) * 128],
                                                  start=(ci == 0), stop=(ci == nch - 1)), reads=[RpT[i], Rvm[sl], Rcv], writes=[RP_no])
                S.op("dve", lambda e: e.tensor_copy(out=nst[sl][:, h * 128:(h + 1) * 128], in_=ops_), reads=[RP_no], writes=[Rnst[sl]])
                yield
            S.dma("pool", lambda e: e.dma_start(out=mixT_d[1536:2048, tokdst:tokdst + 128].rearrange("(h d) t -> d h t", d=128),
                                                in_=nst[sl].rearrange("p (h t) -> p h t", h=4)), reads=[Rnst[sl]])

        ctx_v = [((lambda h, c=c: cv[:, c * 512 + h * 128:c * 512 + (h + 1) * 128]), 128) for c in range(CTX // 128)]
        ctx_seg = ((lambda h: ckT[:, h * CTX:(h + 1) * CTX]), CTX, None)
        for m in range(R // 2):
            ti, lo = na_type_of(m, R)
            sl = m % 2
            S.dma("sp", lambda e: e.dma_start(out=qm[sl].rearrange("p (h t) -> p h t", h=4),
                                              in_=nqT_d[:, m * 128:(m + 1) * 128].rearrange("(h d) t -> d h t", d=128)), writes=[Rqm[sl]])
            S.dma("sp", lambda e: e.dma_start(out=km[sl].rearrange("p (h t) -> p h t", h=4),
                                              in_=nkT_d[:, lo * 64:lo * 64 + 576].rearrange("(h d) t -> d h t", d=128)), writes=[Rkm[sl]])
            S.dma("sp", lambda e: e.dma_start(out=vm[sl][:, 0:2048].rearrange("p (c f) -> p c f", f=512),
                                              in_=nv_d[lo * 64:lo * 64 + 512, :].rearrange("(c p) f -> p c f", p=128)), writes=[Rvm[sl]])
            S.dma("sp", lambda e: e.dma_start(out=vm[sl][0:64, 2048:2560], in_=nv_d[lo * 64 + 512:lo * 64 + 576, :]), writes=[Rvm[sl]])
            kseg = ((lambda h, sl=sl: km[sl][:, h * 576:(h + 1) * 576]), 576,
                    (lambda h, ti=ti: TB[:, (h * 5 + ti) * 576:(h * 5 + ti + 1) * 576]))
            vch = [((lambda h, c=c, sl=sl: vm[sl][:, c * 512 + h * 128:c * 512 + (h + 1) * 128]), 128) for c in range(4)]
            vch += [((lambda h, sl=sl: vm[sl][:, 2048 + h * 128:2048 + (h + 1) * 128]), 64)]
            yield from na_block(sl, m * 128, [kseg, ctx_seg], vch + ctx_v)
        if with_ctx:
            for qb in range(CTX // 128):
                sl = qb % 2
                S.dma("sp", lambda e: e.dma_start(out=qm[sl].rearrange("p (h t) -> p h t", h=4),
                                                  in_=nqT_d[:, L + qb * 128:L + (qb + 1) * 128].rearrange("(h d) t -> d h t", d=128)), writes=[Rqm[sl]])
                yield from na_block(sl, L + qb * 128, [ctx_seg], ctx_v)

    def phase_ffn(l, with_ctx, xlat, xctx, last):
        A.off = pers_mark
        fw = A.f32(264); fb = A.f32(88); Rfp = Res("ffnp")
        xs = [A.f32(D) for _ in range(4)]; Rxs = [Res("fxs%d" % i) for i in range(4)]
        xn = [A.f32(D) for _ in range(2)]; Rxn = [Res("fxn%d" % i) for i in range(2)]
        hT = A.bf16(16 * 512); RhT = Res("fhT")
        aT = A.bf16(22 * 512); RaT = Res("aT")
        mt = A.bf16(16 * 512); RmT = Res("mT")
        upw = [A.bf16(16 * 256) for _ in range(4)]; Rupw = [Res("upw%d" % i) for i in range(4)]
        wdn = [A.bf16(16 * 512) for _ in range(2)]; Rwdn = [Res("wdn%d" % i) for i in range(2)]
        gt = [A.f32(D) for _ in range(2)]; Rgt = [Res("gt%d" % i) for i in range(2)]
        yv = [A.f32(512) for _ in range(2)]; Ryv = [Res("yv%d" % i) for i in range(2)]
        yg = [A.f32(512) for _ in range(2)]; Ryg = [Res("yg%d" % i) for i in range(2)]
        tmp = [A.f32(512) for _ in range(2)]; Rtmp = [Res("ftmp%d" % i) for i in range(2)]
        ss = A.f32(16); Rss = Res("fss")
        S.dma("sp", lambda e: e.dma_start(out=fw, in_=fdw[l].rearrange("p c k -> p (c k)")), writes=[Rfp])
        S.dma("sp", lambda e: e.dma_start(out=fb, in_=fdb[l]), writes=[Rfp])
        cnt = {"up": 0, "dn": 0, "tmp": 0, "xn": 0, "y": 0}

        def nxt(key, n_):
            v = cnt[key]; cnt[key] = (v + 1) % n_; return v

        tiles = [(0, t0, min(510, L - t0)) for t0 in range(0, L, 510)]
        if with_ctx:
            tiles += [(1, t0, min(510, CTX - t0)) for t0 in range(0, CTX, 510)]
        for (isctx, t0, ni) in tiles:
            cond = isctx
            Ls = CTX if isctx else L
            base = L if isctx else 0
            xsrc = xctx if isctx else xlat
            n = ni + 2
            nb = (n + 127) // 128
            jlo = 1 if t0 == 0 else 0
            jhi = n - 1 if t0 + ni == Ls else n
            nts = [min(128, n - tb * 128) for tb in range(nb)]
            for tb in range(nb):
                r0 = max(tb * 128, jlo); r1 = min(tb * 128 + nts[tb], jhi)
                if r0 != tb * 128 or r1 != tb * 128 + nts[tb]:
                    S.op("pool", lambda e: e.memset(xs[tb], 0.0), writes=[Rxs[tb]])
                if r1 > r0:
                    S.dma("sp", lambda e: e.dma_start(out=xs[tb][r0 - tb * 128:r1 - tb * 128, :], in_=xsrc[t0 - 1 + r0:t0 - 1 + r1, :]), writes=[Rxs[tb]])
            mt3 = mt.rearrange("p (c t) -> p c t", c=16)
            if jlo != 0 or jhi != n:
                S.op("pool", lambda e: e.memset(mt, 0.0), writes=[RmT])
            S.dma("sp", lambda e: e.dma_start(out=mt3[:, :, jlo:jhi], in_=mixT_d[:, base + t0 - 1 + jlo:base + t0 - 1 + jhi].rearrange("(c p) t -> p c t", p=128)),
                  writes=[RmT])
            S.dma("sp", lambda e: e.dma_start(out=gt[0], in_=mods_d[cond:cond + 1, 2 * D:3 * D].broadcast_to([128, D])), writes=[Rgt[0]])
            S.dma("sp", lambda e: e.dma_start(out=gt[1], in_=mods_d[cond:cond + 1, 5 * D:6 * D].broadcast_to([128, D])), writes=[Rgt[1]])
            for nbk in range(4):
                sl = nxt("dn", 2)
                S.dma("sp", lambda e: e.dma_start(out=wdn[sl].rearrange("p (k n) -> p k n", k=16),
                                                  in_=woutb[l][:, nbk * 512:(nbk + 1) * 512].rearrange("(k p) n -> p k n", p=128)), writes=[Rwdn[sl]], reads=[RCW("wout%d" % l)])
                for tb in range(nb):
                    nt = nts[tb]
                    bk = 4 + tb
                    for k in range(16):
                        S.op("pe", lambda e: e.matmul(PS[bk][0:nt, :], lhsT=mt[:, k * 512 + tb * 128:k * 512 + tb * 128 + nt],
                                                      rhs=wdn[sl][:, k * 512:(k + 1) * 512], start=(k == 0), stop=(k == 15)),
                             reads=[RmT, Rwdn[sl]], writes=[RPS[bk]])
                    ti_ = nxt("tmp", 2)
                    S.op("dve", lambda e: e.tensor_tensor(out=tmp[ti_][0:nt, :], in0=PS[bk][0:nt, :], in1=gt[0][0:nt, nbk * 512:(nbk + 1) * 512], op=ALU.mult),
                         reads=[RPS[bk], Rgt[0]], writes=[Rtmp[ti_]])
                    S.op("pool", lambda e: e.tensor_tensor(out=xs[tb][0:nt, nbk * 512:(nbk + 1) * 512], in0=xs[tb][0:nt, nbk * 512:(nbk + 1) * 512],
                                                           in1=tmp[ti_][0:nt, :], op=ALU.add), reads=[Rtmp[ti_], Rxs[tb]], writes=[Rxs[tb]])
            if last:
                S.dma("sp", lambda e: e.dma_start(out=gt[0], in_=final_g.rearrange("(o d) -> o d", o=1).broadcast_to([128, D])), writes=[Rgt[0]])
            for tb in range(nb):
                nt = nts[tb]
                xi_ = nxt("xn", 2)
                S.op("act", lambda e: e.activation(out=xn[xi_][0:nt, :], in_=xs[tb][0:nt, :], func=AF.Square, accum_out=ss[0:nt, tb:tb + 1]),
                     reads=[Rxs[tb]], writes=[Rxn[xi_], Rss])
                S.op("act", lambda e: e.activation(out=ss[0:nt, 4 + tb:5 + tb], in_=ss[0:nt, tb:tb + 1], func=AF.Sqrt, scale=1.0 / D, bias=EPS),
                     reads=[Rss], writes=[Rss])
                S.op("dve", lambda e: e.reciprocal(out=ss[0:nt, 4 + tb:5 + tb], in_=ss[0:nt, 4 + tb:5 + tb]), reads=[Rss], writes=[Rss])
                S.op("dve", lambda e: e.tensor_scalar(out=xn[xi_][0:nt, :], in0=xs[tb][0:nt, :], scalar1=ss[0:nt, 4 + tb:5 + tb], scalar2=None, op0=ALU.mult),
                     reads=[Rxs[tb], Rss], writes=[Rxn[xi_]])
                for c4 in range(4):
                    bk = c4 % 2
                    for cc in range(4):
                        c = c4 * 4 + cc
                        S.op("pe", lambda e: e.transpose(PS[bk][:, cc * 128:cc * 128 + nt], xn[xi_][0:nt, c * 128:(c + 1) * 128], ident[0:nt, 0:nt]),
                             reads=[Rxn[xi_], Rid], writes=[RPS[bk]])
                    for cc in range(4):
                        c = c4 * 4 + cc
                        dst = hT[:, c * 512 + tb * 128:c * 512 + tb * 128 + nt]
                        if cc % 2 == 0:
                            S.op("dve", lambda e: e.tensor_scalar(out=dst, in0=PS[bk][:, cc * 128:cc * 128 + nt], scalar1=G2T[:, 2 * c + cond:2 * c + cond + 1],
                                                                  scalar2=modcol(3, c, cond), op0=ALU.mult, op1=ALU.add),
                                 reads=[RPS[bk], RG, RmodT], writes=[RhT])
                        else:
                            S.op("act", lambda e: e.activation(out=dst, in_=PS[bk][:, cc * 128:cc * 128 + nt], func=AF.Identity,
                                                               scale=G2T[:, 2 * c + cond:2 * c + cond + 1], bias=modcol(3, c, cond)),
                                 reads=[RPS[bk], RG, RmodT], writes=[RhT])
            hT3 = hT.rearrange("p (c t) -> p c t", c=16)
            if jlo == 1:
                S.op("pool", lambda e: e.memset(hT3[:, :, 0:1], 0.0), reads=[RhT], writes=[RhT])
            if jhi == n - 1:
                S.op("pool", lambda e: e.memset(hT3[:, :, n - 1:n], 0.0), reads=[RhT], writes=[RhT])
            for hf in range(2):
                for jj in range(22):
                    c = hf * 22 + jj
                    sub = c % 2
                    if jj % 2 == 0:
                        uv = nxt("up", 4); ug = nxt("up", 4)
                        for (us, c0) in ((uv, (c // 2) * 256), (ug, DFF + (c // 2) * 256)):
                            S.dma("sp", lambda e: e.dma_start(out=upw[us].rearrange("p (k n) -> p k n", k=16),
                                                              in_=upb[l][:, c0:c0 + 256].rearrange("(k p) n -> p k n", p=128)), writes=[Rupw[us]], reads=[RCW("up%d" % l)])
                    pbk = (jj % 2) * 2
                    for (us, bk) in ((uv, pbk), (ug, pbk + 1)):
                        for k in range(16):
                            S.op("pe", lambda e: e.matmul(PS[bk][:, 0:n], lhsT=upw[us][:, k * 256 + sub * 128:k * 256 + (sub + 1) * 128],
                                                          rhs=hT[:, k * 512:k * 512 + n], start=(k == 0), stop=(k == 15)),
                                 reads=[Rupw[us], RhT], writes=[RPS[bk]])
                    yi = nxt("y", 2)
                    for (yy, Ryy, bk, ch) in ((yv[yi], Ryv[yi], pbk, c), (yg[yi], Ryg[yi], pbk + 1, 44 + c)):
                        S.op("act", lambda e: e.activation(out=yy[:, 0:n], in_=PS[bk][:, 0:n], func=AF.Identity, scale=fw[:, ch * 3 + 1:ch * 3 + 2],
                                                           bias=fb[:, ch:ch + 1]), reads=[RPS[bk], Rfp], writes=[Ryy])
                        S.op("dve", lambda e: e.scalar_tensor_tensor(out=yy[:, 1:n], in0=PS[bk][:, 0:n - 1], scalar=fw[:, ch * 3:ch * 3 + 1], in1=yy[:, 1:n],
                                                                     op0=ALU.mult, op1=ALU.add), reads=[RPS[bk], Rfp, Ryy], writes=[Ryy])
                        S.op("dve", lambda e: e.scalar_tensor_tensor(out=yy[:, 0:n - 1], in0=PS[bk][:, 1:n], scalar=fw[:, ch * 3 + 2:ch * 3 + 3], in1=yy[:, 0:n - 1],
                                                                     op0=ALU.mult, op1=ALU.add), reads=[RPS[bk], Rfp, Ryy], writes=[Ryy])
                    S.op("act", lambda e: e.activation(out=yg[yi][:, 0:n], in_=yg[yi][:, 0:n], func=AF.Silu), reads=[Ryg[yi]], writes=[Ryg[yi]])
                    S.op("pool", lambda e: e.tensor_tensor(out=aT[:, jj * 512:jj * 512 + n], in0=yv[yi][:, 0:n], in1=yg[yi][:, 0:n], op=ALU.mult),
                         reads=[Ryv[yi], Ryg[yi]], writes=[RaT])
                for nbk in range(4):
                    for part in range(2):
                        sl = nxt("dn", 2)
                        k0 = hf * 22 + part * 11
                        S.dma("sp", lambda e: e.dma_start(out=wdn[sl][:, 0:11 * 512].rearrange("p (k n) -> p k n", k=11),
                                                          in_=downb[l][k0 * 128:(k0 + 11) * 128, nbk * 512:(nbk + 1) * 512].rearrange("(k p) n -> p k n", p=128)),
                              writes=[Rwdn[sl]], reads=[RCW("down%d" % l)])
                        for tb in range(nb):
                            nt = nts[tb]
                            bk = 4 + tb
                            for kl in range(11):
                                kk = part * 11 + kl
                                S.op("pe", lambda e: e.matmul(PS[bk][0:nt, :], lhsT=aT[:, kk * 512 + tb * 128:kk * 512 + tb * 128 + nt],
                                                              rhs=wdn[sl][:, kl * 512:(kl + 1) * 512], start=(kk == 0), stop=(kk == 21)),
                                     reads=[RaT, Rwdn[sl]], writes=[RPS[bk]])
                    for tb in range(nb):
                        nt = nts[tb]
                        bk = 4 + tb
                        ti_ = nxt("tmp", 2)
                        S.op("dve", lambda e: e.tensor_tensor(out=tmp[ti_][0:nt, :], in0=PS[bk][0:nt, :], in1=gt[1][0:nt, nbk * 512:(nbk + 1) * 512], op=ALU.mult),
                             reads=[RPS[bk], Rgt[1]], writes=[Rtmp[ti_]])
                        S.op("pool", lambda e: e.tensor_tensor(out=xs[tb][0:nt, nbk * 512:(nbk + 1) * 512], in0=xs[tb][0:nt, nbk * 512:(nbk + 1) * 512],
                                                               in1=tmp[ti_][0:nt, :], op=ALU.add), reads=[Rtmp[ti_], Rxs[tb]], writes=[Rxs[tb]])
            for tb in range(nb):
                nt = nts[tb]
                r0 = max(tb * 128, 1); r1 = min(tb * 128 + nt, n - 1)
                if r1 <= r0:
                    continue
                if last:
                    xi_ = nxt("xn", 2)
                    S.op("act", lambda e: e.activation(out=xn[xi_][0:nt, :], in_=xs[tb][0:nt, :], func=AF.Square, accum_out=ss[0:nt, 8 + tb:9 + tb]),
                         reads=[Rxs[tb]], writes=[Rxn[xi_], Rss])
                    S.op("act", lambda e: e.activation(out=ss[0:nt, 12 + tb:13 + tb], in_=ss[0:nt, 8 + tb:9 + tb], func=AF.Sqrt, scale=1.0 / D, bias=EPS),
                         reads=[Rss], writes=[Rss])
                    S.op("dve", lambda e: e.reciprocal(out=ss[0:nt, 12 + tb:13 + tb], in_=ss[0:nt, 12 + tb:13 + tb]), reads=[Rss], writes=[Rss])
                    S.op("dve", lambda e: e.tensor_scalar(out=xn[xi_][0:nt, :], in0=xs[tb][0:nt, :], scalar1=ss[0:nt, 12 + tb:13 + tb], scalar2=None, op0=ALU.mult),
                         reads=[Rxs[tb], Rss], writes=[Rxn[xi_]])
                    S.op("pool", lambda e: e.tensor_tensor(out=xn[xi_][0:nt, :], in0=xn[xi_][0:nt, :], in1=gt[0][0:nt, :], op=ALU.mult),
                         reads=[Rxn[xi_], Rgt[0]], writes=[Rxn[xi_]])
                    S.dma("pool", lambda e: e.dma_start(out=out_d[t0 - 1 + r0:t0 - 1 + r1, :], in_=xn[xi_][r0 - tb * 128:r1 - tb * 128, :]), reads=[Rxn[xi_]])
                else:
                    S.dma("pool", lambda e: e.dma_start(out=xa_d[base + t0 - 1 + r0:base + t0 - 1 + r1, :], in_=xs[tb][r0 - tb * 128:r1 - tb * 128, :]),
                          reads=[Rxs[tb]])
        S.barrier()

    def run_group(items):
        active = [(iter(g_), w_) for g_, w_ in items]
        while active:
            for it in list(active):
                g_, w_ = it
                for _ in range(w_):
                    try:
                        next(g_)
                    except StopIteration:
                        active.remove(it)
                        break
        S.barrier()

    def run(stage="all", nlay=NL):
        for l in range(nlay):
            last = (l == NL - 1)
            xlat = x_in if l == 0 else xa_d[0:L]
            xctx = ctx_in if l == 0 else xa_d[L:T]
            if l == 0:
                for l2 in range(NL):
                    issue_casts(l2)
            na_zero()
            phase_mods(l)
            if stage == "mods" and l == nlay - 1:
                break
            na_diag(l)
            phase_inproj(l, xlat, xctx)
            if stage == "inproj" and l == nlay - 1:
                break
            A.off = pers_mark
            Ana = A.sub(A.n - A.off - 15900 - 9700); Aret = A.sub(15900); Aconv = A.sub(9700)
            import os
            gm = os.environ.get("GMODE", "par")
            if gm == "seq":
                for g_ in (phase_ret(l, not last, Aret), phase_conv(l, not last, Aconv), phase_na(l, not last, Ana)):
                    run_group([(g_, 1)])
            elif gm in ("ret", "conv", "na"):
                run_group([({"ret": phase_ret(l, not last, Aret), "conv": phase_conv(l, not last, Aconv), "na": phase_na(l, not last, Ana)}[gm], 1)])
            else:
                run_group([(phase_ret(l, not last, Aret), 6), (phase_conv(l, not last, Aconv), 1), (phase_na(l, not last, Ana), 4)])
            if stage == "na" and l == nlay - 1:
                break
            phase_ffn(l, not last, xlat, xctx, last)
        S.run_block()

    g.run = run
    g.nc = nc; g.S = S
    g.phase_mods = phase_mods; g.phase_inproj = phase_inproj
    g.names = dict(x_in=x_in, ctx_in=ctx_in, xa_d=xa_d, out_d=out_d)
    g.locals = locals()
    return g


def kernel(**inputs):
    inp = {k: np.asarray(v) for k, v in inputs.items()}
    B, L, _ = inp["x"].shape
    CTX = inp["ctx"].shape[1]
    g = build(L, CTX, NL=2, dbg=False)
    g.run("all", 2)
    in_maps = [host_layout(inp, b, L) for b in range(B)]
    res = run_bass_kernel_spmd(g.nc, in_maps, core_ids=list(range(B)))
    return np.stack([np.asarray(res.results[b]["out"], np.float32) for b in range(B)]).astype(np.float32)
```

```python
import numpy as np
import ml_dtypes
import concourse.bass as bass
import concourse.mybir as mybir
from concourse.bass_utils import run_bass_kernel_spmd

F32 = mybir.dt.float32
BF16 = mybir.dt.bfloat16
AF = mybir.ActivationFunctionType
ALU = mybir.AluOpType
AX = mybir.AxisListType

COMPUTE = ("pe", "act", "dve", "pool")
SAME_ENGINE_WAIT = True


class Res:
    __slots__ = ("name", "w", "r", "lsem", "ssem")

    def __init__(self, name):
        self.name = name
        self.w = {}
        self.r = {}
        self.lsem = None
        self.ssem = None


class _Cap:
    def __init__(self):
        self.call = None

    def __getattr__(self, name):
        def f(*a, **k):
            self.call = (name, a, k)
            return self
        return f


class Rec:
    __slots__ = ("waits", "fn", "inc")

    def __init__(self, waits, fn, inc):
        self.waits = waits
        if fn is not None:
            cap = _Cap()
            fn(cap)
            fn = cap.call
            assert fn is not None
        self.fn = fn
        self.inc = inc


class Sched:
    def __init__(self, nc):
        self.nc = nc
        self.prog = {e: [] for e in ("pe", "act", "dve", "pool", "sp")}
        self.sems = {}
        self.cnt = {}
        self.known = {e: {} for e in self.prog}
        self.last = {e: None for e in COMPUTE}
        self.pending = {e: False for e in COMPUTE}
        self.free_dma_sems = []
        self.live_dma_sems = []
        self.nsem = 0
        self.nobarrier = set()
        self.live_res = []
        for e in COMPUTE:
            self._mk("E_" + e)

    def _mk(self, key):
        self.sems[key] = self.nc.alloc_semaphore(key)
        self.cnt[key] = 0
        self.nsem += 1
        return key

    def dma_sem(self):
        if self.free_dma_sems:
            k = self.free_dma_sems.pop()
        else:
            k = self._mk("D%d" % self.nsem)
        self.live_dma_sems.append(k)
        return k

    def _force(self, key):
        if key.startswith("E_"):
            e = key[2:]
            if self.pending[e]:
                rec = self.last[e]
                assert rec.inc is None
                rec.inc = (key, 1)
                self.cnt[key] += 1
                self.pending[e] = False

    def _waits(self, eng, deps):
        out = []
        kn = self.known[eng]
        for key, val in deps.items():
            if key == "E_" + eng:
                if eng == "pe" or not SAME_ENGINE_WAIT:
                    continue
            if kn.get(key, 0) >= val:
                continue
            self._force(key)
            assert self.cnt[key] >= val, (key, self.cnt[key], val)
            kn[key] = val
            out.append((key, val))
        return out

    @staticmethod
    def _merge(d, s):
        for k, v in s.items():
            if d.get(k, 0) < v:
                d[k] = v

    def _deps(self, reads, writes):
        deps = {}
        for r in reads:
            self._merge(deps, r.w)
        for w in writes:
            self._merge(deps, w.w)
            self._merge(deps, w.r)
        return deps

    def op(self, eng, fn, reads=(), writes=()):
        deps = self._deps(reads, writes)
        waits = self._waits(eng, deps)
        key = "E_" + eng
        rec = Rec(waits, fn, None)
        self.prog[eng].append(rec)
        self.last[eng] = rec
        self.pending[eng] = True
        tok = {key: self.cnt[key] + 1}
        for r in reads:
            self._merge(r.r, tok)
        for w in writes:
            w.w = dict(tok)
            w.r = {}

    def dma(self, queue, fn, reads=(), writes=(), sem=None):
        deps = self._deps(reads, writes)
        waits = self._waits(queue, deps)
        if sem is None:
            if writes:
                w0 = writes[0]
                if w0.lsem is None:
                    w0.lsem = self.dma_sem()
                    self.live_res.append(w0)
                sem = w0.lsem
            else:
                r0 = reads[0]
                if r0.ssem is None:
                    r0.ssem = self.dma_sem()
                    self.live_res.append(r0)
                sem = r0.ssem
        self.cnt[sem] += 16
        tok = {sem: self.cnt[sem]}
        rec = Rec(waits, fn, (sem, 16))
        self.prog[queue].append(rec)
        if queue in COMPUTE:
            pass
        for r in reads:
            self._merge(r.r, tok)
        for w in writes:
            w.w = dict(tok)
            w.r = {}

    def barrier(self, recycle=True, final=False):
        for e in COMPUTE:
            self._force("E_" + e)
        allk = {k: v for k, v in self.cnt.items() if v > 0 and (final or k not in self.nobarrier)}
        for eng in self.prog:
            waits = self._waits_all(eng, allk)
            if waits:
                self.prog[eng].append(Rec(waits, None, None))
        if recycle:
            self.free_dma_sems.extend(self.live_dma_sems)
            self.live_dma_sems = []
            for r_ in self.live_res:
                r_.lsem = None
                r_.ssem = None
            self.live_res = []

    def _waits_all(self, eng, allk):
        out = []
        kn = self.known[eng]
        for key, val in allk.items():
            if key == "E_" + eng:
                continue
            if kn.get(key, 0) >= val:
                continue
            kn[key] = val
            out.append((key, val))
        return out

    def replay(self, eng, e):
        for rec in self.prog[eng]:
            for key, val in rec.waits:
                e.wait_ge(self.sems[key], val)
            if rec.fn is None:
                continue
            name, a, k = rec.fn
            ins = getattr(e, name)(*a, **k)
            if rec.inc is not None:
                ins.then_inc(self.sems[rec.inc[0]], rec.inc[1])

    def run_block(self):
        nc = self.nc
        self.barrier(recycle=False, final=True)
        with nc.Block() as block:
            @block.tensor
            def _(e):
                self.replay("pe", e)

            @block.scalar
            def _(e):
                self.replay("act", e)

            @block.vector
            def _(e):
                self.replay("dve", e)

            @block.gpsimd
            def _(e):
                self.replay("pool", e)

            @block.sync
            def _(e):
                self.replay("sp", e)


class Arena:
    def __init__(self, t, nwords, base=0):
        self.t = t
        self.n = base + nwords
        self.off = base
        self.base = base

    def sub(self, nwords):
        assert self.off + nwords <= self.n, ("arena overflow(sub)", self.off, nwords, self.n)
        a = Arena(self.t, nwords, self.off)
        self.off += nwords
        return a

    def reset(self):
        self.off = 0

    def f32(self, n, parts=128):
        assert self.off + n <= self.n, ("arena overflow", self.off, n, self.n)
        ap = self.t[0:parts, self.off:self.off + n]
        self.off += n
        return ap

    def bf16(self, n, parts=128):
        w = (n + 1) // 2
        assert self.off + w <= self.n, ("arena overflow", self.off, w, self.n)
        ap = self.t[0:parts, self.off:self.off + w].bitcast(BF16)
        self.off += w
        return ap[:, 0:n]


D = 2048
DIN = 5632
DFF = 5632
WCOLS = DIN + 1024
EPS = 1e-6
GRID_W = 64
NEG = -30000.0
CAST_BARRIER = False
import os
NA_OLDPS = bool(int(os.environ.get('NA_OLDPS', '0')))


def host_consts(L):
    c = {}
    c["ident"] = np.eye(128, dtype=np.float32)
    pos = np.arange(L)
    row = (pos // GRID_W).astype(np.float32)
    col = (pos % GRID_W).astype(np.float32)
    nf = 32
    inv = (10000.0 ** (-np.arange(nf, dtype=np.float32) / nf)).astype(np.float32)
    f = np.arange(128)
    p = np.where((f // 64)[:, None] == 0, row[None, :], col[None, :]).astype(np.float32)
    ang = (p * inv[f % 32][:, None]).astype(np.float32)
    sign = np.where((f % 64) < 32, -1.0, 1.0).astype(np.float32)[:, None]
    C = np.cos(ang).astype(np.float32)
    Sg = (sign * np.sin(ang)).astype(np.float32)
    sc = np.float32(128 ** -0.5)
    c["rope"] = np.stack([C * sc, Sg * sc, C, Sg]).astype(np.float32)
    i = np.arange(128)
    jj, ii = np.meshgrid(i, i, indexing="ij")
    dec = np.stack([np.maximum(ii - jj, 0), (ii >= jj), np.maximum(jj - ii, 0), (jj >= ii)]).astype(np.float32)
    c["dec"] = dec
    xirow = np.stack([np.tile((i + 1)[None, :], (128, 1)), np.tile((128 - i)[None, :], (128, 1))]).astype(np.float32)
    c["xirow"] = xirow
    c["zcol"] = np.stack([127 - i, i], axis=1).astype(np.float32)
    R = L // GRID_W
    types = na_types(R)
    nam = np.zeros((5, 128, 576), np.float32)
    cols = np.arange(64)
    cs = np.clip(cols - 8, 0, 64 - 16)
    band = (cols[None, :] >= cs[:, None]) & (cols[None, :] < cs[:, None] + 16)
    for ti, (m, lo) in enumerate(types):
        for qr in range(2):
            r = 2 * m + qr
            w0 = int(np.clip(r - 4, 0, R - 8))
            for kidx in range(9):
                kr = lo + kidx
                ok = (w0 <= kr < w0 + 8)
                blk = np.where(band, 0.0, NEG) if ok else np.full((64, 64), NEG)
                nam[ti, qr * 64:(qr + 1) * 64, kidx * 64:(kidx + 1) * 64] = blk
    c["nam"] = nam
    return c


def na_types(R):
    M = R // 2
    return [(0, 0), (1, 0), (2, 0), (M - 2, R - 9), (M - 1, R - 9)]


def na_type_of(m, R):
    M = R // 2
    if m == 0:
        return 0, 0
    if m == 1:
        return 1, 0
    if m == M - 2:
        return 3, R - 9
    if m == M - 1:
        return 4, R - 9
    return 2, 2 * m - 4


def fm(v, nch):
    s = v.shape[:-1]
    return np.ascontiguousarray(np.moveaxis(v.reshape(s + (nch, 128)), -1, -2))


def host_layout(inp, b, L):
    f32 = np.float32
    o = {}
    o["x"] = np.ascontiguousarray(inp["x"][b], f32)
    o["ctx"] = np.ascontiguousarray(inp["ctx"][b], f32)
    cv = np.stack([inp["c"][b], inp["c_ctx"]], axis=1).astype(f32)
    o["cT"] = np.ascontiguousarray(cv.reshape(16, 128, 2).transpose(1, 0, 2))
    for k in ("w_ada", "b_ada", "w_in", "w_out", "ffn_up", "ffn_down", "conv_pw", "final_g", "na_rpb"):
        o[k] = np.ascontiguousarray(inp[k], f32)
    o["ret_decay"] = np.ascontiguousarray(inp["ret_decay"].reshape(-1, 8), f32)
    o["gng"] = np.ascontiguousarray(inp["ret_gn_g"], f32)
    o["n1g"] = fm(inp["norm1_g"].astype(f32), 16)
    o["n2g"] = fm(inp["norm2_g"].astype(f32), 16)
    o["cdw"] = np.ascontiguousarray(fm(inp["conv_dw_w"].astype(f32), 4).transpose(0, 2, 3, 1))
    o["cdb"] = fm(inp["conv_dw_b"].astype(f32), 4)
    o["lng"] = fm(inp["conv_ln_g"].astype(f32), 4)
    o["lnb"] = fm(inp["conv_ln_b"].astype(f32), 4)
    o["fdw"] = np.ascontiguousarray(fm(inp["ffn_dw_w"].astype(f32), 88).transpose(0, 2, 3, 1))
    o["fdb"] = fm(inp["ffn_dw_b"].astype(f32), 88)
    o.update(host_consts(L))
    return o


class K:
    pass


def build(L, CTX, NL=2, dbg=False, upto=99):
    T = L + CTX
    R = L // GRID_W
    nc = bass.Bass("TRN2", target_bir_lowering=False)
    g = K()

    def din(name, shape, dt=F32):
        return nc.dram_tensor(name, list(shape), dt, kind="ExternalInput").ap()

    def dscr(name, shape, dt):
        return nc.dram_tensor(name, list(shape), dt, kind="ExternalOutput" if dbg else "Internal").ap()

    x_in = din("x", [L, D]); ctx_in = din("ctx", [CTX, D]); cT = din("cT", [128, 16, 2])
    w_ada = din("w_ada", [NL, D, 6 * D]); b_ada = din("b_ada", [NL, 6 * D]); w_in = din("w_in", [NL, D, DIN])
    w_out = din("w_out", [NL, D, D]); ffn_up = din("ffn_up", [NL, D, 2 * DFF]); ffn_down = din("ffn_down", [NL, DFF, D])
    conv_pw = din("conv_pw", [NL, 512, 512]); final_g = din("final_g", [D]); na_rpb = din("na_rpb", [NL, 4, 15, 31])
    ret_decay = din("ret_decay", [NL, 8]); gng = din("gng", [NL, 1024])
    n1g = din("n1g", [NL, 128, 16]); n2g = din("n2g", [NL, 128, 16])
    cdw = din("cdw", [NL, 128, 4, 31]); cdb = din("cdb", [NL, 128, 4]); lng = din("lng", [NL, 128, 4]); lnb = din("lnb", [NL, 128, 4])
    fdw = din("fdw", [NL, 128, 88, 3]); fdb = din("fdb", [NL, 128, 88])
    ident_d = din("ident", [128, 128]); rope_d = din("rope", [4, 128, L]); dec_d = din("dec", [4, 128, 128])
    xirow_d = din("xirow", [2, 128, 128]); zcol_d = din("zcol", [128, 2]); nam_d = din("nam", [5, 128, 576])
    out_d = nc.dram_tensor("out", [L, D], F32, kind="ExternalOutput").ap()

    winb = dscr("winb", [NL, D, WCOLS], BF16); woutb = dscr("woutb", [NL, D, D], BF16)
    upb = dscr("upb", [NL, D, 2 * DFF], BF16); downb = dscr("downb", [NL, DFF, D], BF16)
    pwb = dscr("pwb", [NL, 512, 512], BF16); wadab = dscr("wadab", [NL, D, 6 * D], BF16)
    mods_d = dscr("mods", [2, 6 * D], F32)
    qT_d = dscr("qT", [512, T], BF16); kT_d = dscr("kT", [512, T], BF16); v_d = dscr("v", [T, 1024], BF16)
    sg_d = dscr("sg", [T, 1024], F32); uT_d = dscr("uT", [512, T], F32)
    nqT_d = dscr("nqT", [512, T], BF16); nkT_d = dscr("nkT", [512, T], BF16); nv_d = dscr("nv", [T, 512], BF16)
    of_d = dscr("of", [T, 1024], F32); mixT_d = dscr("mixT", [D, T], BF16)
    xa_d = dscr("xa", [T, D], F32); toep_d = dscr("toep", [4, 64, 17, 64], F32)

    S = Sched(nc)
    NARENA = 52900
    import contextlib
    es = contextlib.ExitStack()
    arena_t = es.enter_context(nc.sbuf_tensor("arena", [128, NARENA], F32))
    psum_t = es.enter_context(nc.psum_tensor("psum", [128, 4096], F32))
    A = Arena(arena_t, NARENA)
    PS = [psum_t[:, i * 512:(i + 1) * 512] for i in range(8)]
    RPS = [Res("ps%d" % i) for i in range(8)]

    ident = A.f32(128); Rid = Res("ident")
    identb = A.bf16(128)
    modT = A.f32(192); RmodT = Res("modT")
    G1T = A.f32(32); G2T = A.f32(32); RG = Res("G")
    ztile = A.f32(2176); Rzt = Res("zt")
    pers_mark = A.off

    S.dma("sp", lambda e: e.dma_start(out=ident, in_=ident_d), writes=[Rid])
    S.op("dve", lambda e: e.tensor_copy(out=identb, in_=ident), reads=[Rid], writes=[Rid])

    RC = {}

    def cast(dst, src, rows, step, key):
        if key not in RC:
            sem = S._mk("C_" + key)
            S.nobarrier.add(sem)
            RC[key] = (Res("cast_" + key), sem)
        rc, sem = RC[key]
        for r0 in range(0, rows, step):
            S.dma("pool", lambda e, r0=r0: e.dma_start(out=dst[r0:r0 + step], in_=src[r0:r0 + step]), sem=sem)
        rc.w = {sem: S.cnt[sem]}

    def issue_casts(l):
        cast(winb[l][:, 0:DIN], w_in[l], D, 256, "win%d" % l)
        cast(pwb[l], conv_pw[l], 512, 512, "pw%d" % l)
        cast(woutb[l], w_out[l], D, 512, "wout%d" % l)
        cast(upb[l], ffn_up[l], D, 128, "up%d" % l)
        cast(downb[l], ffn_down[l], DFF, 512, "down%d" % l)

    def RCW(key):
        return RC[key][0]

    def phase_mods(l):
        A.off = pers_mark
        sT = A.f32(32); RsT = Res("sT")
        sTs = A.f32(32)
        bada2 = A.f32(6 * D, parts=2); Rb = Res("bada2")
        m = A.f32(6 * D, parts=2); Rm = Res("m")
        ng = A.f32(32); Rng = Res("ng")
        wt = [A.f32(16 * 512) for _ in range(2)]; Rwt = [Res("wt%d" % i) for i in range(2)]
        S.dma("sp", lambda e: e.dma_start(out=sT, in_=cT.rearrange("p k m -> p (k m)")), writes=[RsT])
        S.dma("sp", lambda e: e.dma_start(out=bada2, in_=b_ada[l:l + 1, :].broadcast_to([2, 6 * D])), writes=[Rb])
        S.dma("sp", lambda e: e.dma_start(out=ng[:, 0:16], in_=n1g[l]), writes=[Rng])
        S.dma("sp", lambda e: e.dma_start(out=ng[:, 16:32], in_=n2g[l]), writes=[Rng])
        S.op("act", lambda e: e.activation(out=sTs, in_=sT, func=AF.Silu), reads=[RsT], writes=[RsT])
        for nb in range(24):
            sl = nb % 2
            S.dma("sp", lambda e, nb=nb, sl=sl: e.dma_start(
                out=wt[sl].rearrange("p (k n) -> p k n", k=16),
                in_=w_ada[l][:, nb * 512:(nb + 1) * 512].rearrange("(k p) n -> p k n", p=128)), writes=[Rwt[sl]])
            for k in range(16):
                S.op("pe", lambda e, nb=nb, sl=sl, k=k: e.matmul(PS[sl][0:2, :], lhsT=sTs[:, 2 * k:2 * k + 2],
                                                               rhs=wt[sl][:, k * 512:(k + 1) * 512], start=(k == 0), stop=(k == 15)),
                     reads=[RsT, Rwt[sl]], writes=[RPS[sl]])
            S.op("dve", lambda e, nb=nb, sl=sl: e.tensor_tensor(out=m[:, nb * 512:(nb + 1) * 512], in0=PS[sl][0:2, :],
                                                                in1=bada2[:, nb * 512:(nb + 1) * 512], op=ALU.add),
                 reads=[RPS[sl], Rb], writes=[Rm])
        for s0 in (1, 4):
            S.op("dve", lambda e, s0=s0: e.tensor_scalar_add(out=m[:, s0 * D:(s0 + 1) * D], in0=m[:, s0 * D:(s0 + 1) * D], scalar1=1.0),
                 reads=[Rm], writes=[Rm])
        S.dma("pool", lambda e: e.dma_start(out=mods_d, in_=m), reads=[Rm])
        for j in range(96):
            S.op("pe", lambda e, j=j: e.transpose(PS[2][:, 2 * j:2 * j + 2], m[0:2, j * 128:(j + 1) * 128], ident[0:2, 0:2]),
                 reads=[Rm, Rid], writes=[RPS[2]])
        S.op("dve", lambda e: e.tensor_copy(out=modT, in_=PS[2][:, 0:192]), reads=[RPS[2]], writes=[RmodT])
        m3 = modT.rearrange("p (j m) -> p j m", m=2)
        for cond in range(2):
            S.op("dve", lambda e, cond=cond: e.tensor_tensor(out=G1T.rearrange("p (c m) -> p c m", m=2)[:, :, cond],
                                                             in0=ng[:, 0:16], in1=m3[:, 16:32, cond], op=ALU.mult),
                 reads=[RmodT, Rng], writes=[RG])
            S.op("dve", lambda e, cond=cond: e.tensor_tensor(out=G2T.rearrange("p (c m) -> p c m", m=2)[:, :, cond],
                                                             in0=ng[:, 16:32], in1=m3[:, 64:80, cond], op=ALU.mult),
                 reads=[RmodT, Rng], writes=[RG])
        S.barrier()

    def modcol(sec, c, cond):
        j = sec * 16 + c
        return modT[:, 2 * j + cond:2 * j + cond + 1]

    def phase_inproj(l, xlat, xctx):
        A.off = pers_mark
        rp = A.f32(4 * 512); Rrp = Res("rp")
        xs = [A.f32(D) for _ in range(4)]; Rxs = [Res("xs%d" % i) for i in range(4)]
        hT = A.bf16(16 * 512); RhT = Res("hT")
        junk = A.bf16(D); Rjunk = Res("junk")
        wb = [A.bf16(16 * 512) for _ in range(4)]; Rwb = [Res("wb%d" % i) for i in range(4)]
        stb = [A.bf16(512) for _ in range(3)]; Rstb = [Res("stb%d" % i) for i in range(3)]
        stf = [A.f32(512) for _ in range(3)]; Rstf = [Res("stf%d" % i) for i in range(3)]
        tmp = [A.f32(512) for _ in range(4)]; Rtmp = [Res("tmp%d" % i) for i in range(4)]
        ss = A.f32(8); Rss = Res("ss")
        cnt = {"w": 0, "sb": 0, "sf": 0, "tmp": 0, "ps": 0}

        def nxt(key, n):
            v = cnt[key]; cnt[key] = (v + 1) % n; return v

        def load_w(cb):
            sl = nxt("w", 4)
            S.dma("sp", lambda e: e.dma_start(out=wb[sl].rearrange("p (k n) -> p k n", k=16),
                                              in_=winb[l][:, cb * 512:(cb + 1) * 512].rearrange("(k p) n -> p k n", p=128)),
                  writes=[Rwb[sl]], reads=[RCW("win%d" % l)])
            return wb[sl], Rwb[sl]

        def psb():
            i = 2 + nxt("ps", 6)
            return PS[i], RPS[i]

        tiles = [(0, t0, min(512, L - t0)) for t0 in range(0, L, 512)] + [(1, t0, min(512, CTX - t0)) for t0 in range(0, CTX, 512)]
        for (isctx, t0, n) in tiles:
            cond = isctx
            nb = n // 128
            src = xctx if isctx else xlat
            tok0 = L + t0 if isctx else t0
            for tb in range(nb):
                S.dma("sp", lambda e, tb=tb: e.dma_start(out=xs[tb], in_=src[t0 + tb * 128:t0 + (tb + 1) * 128, :]), writes=[Rxs[tb]])
            if not isctx:
                S.dma("sp", lambda e: e.dma_start(out=rp.rearrange("p (a t) -> p a t", a=4)[:, :, 0:n],
                                                  in_=rope_d[:, :, t0:t0 + n].rearrange("a p t -> p a t")), writes=[Rrp])
            for tb in range(nb):
                S.op("act", lambda e, tb=tb: e.activation(out=junk, in_=xs[tb], func=AF.Square, accum_out=ss[:, tb:tb + 1]),
                     reads=[Rxs[tb]], writes=[Rjunk, Rss])
            S.op("act", lambda e: e.activation(out=ss[:, 4:4 + nb], in_=ss[:, 0:nb], func=AF.Sqrt, scale=1.0 / D, bias=EPS),
                 reads=[Rss], writes=[Rss])
            S.op("dve", lambda e: e.reciprocal(out=ss[:, 4:4 + nb], in_=ss[:, 4:4 + nb]), reads=[Rss], writes=[Rss])
            for tb in range(nb):
                S.op("dve", lambda e, tb=tb: e.tensor_scalar(out=xs[tb], in0=xs[tb], scalar1=ss[:, 4 + tb:5 + tb], scalar2=None, op0=ALU.mult),
                     reads=[Rxs[tb], Rss], writes=[Rxs[tb]])
            for c in range(16):
                bk = c % 2
                for tb in range(nb):
                    S.op("pe", lambda e, tb=tb, c=c, bk=bk: e.transpose(PS[bk][:, tb * 128:(tb + 1) * 128], xs[tb][:, c * 128:(c + 1) * 128], ident),
                         reads=[Rxs[tb], Rid], writes=[RPS[bk]])
                if c % 2 == 0:
                    S.op("dve", lambda e, c=c, bk=bk: e.tensor_scalar(out=hT[:, c * 512:c * 512 + n], in0=PS[bk][:, 0:n],
                                                                      scalar1=G1T[:, 2 * c + cond:2 * c + cond + 1], scalar2=modcol(0, c, cond),
                                                                      op0=ALU.mult, op1=ALU.add),
                         reads=[RPS[bk], RG, RmodT], writes=[RhT])
                else:
                    S.op("act", lambda e, c=c, bk=bk: e.activation(out=hT[:, c * 512:c * 512 + n], in_=PS[bk][:, 0:n], func=AF.Identity,
                                                                   scale=G1T[:, 2 * c + cond:2 * c + cond + 1], bias=modcol(0, c, cond)),
                         reads=[RPS[bk], RG, RmodT], writes=[RhT])

            def fm_chunk(wa, Rwa, j):
                ps, Rps = psb()
                for k in range(16):
                    S.op("pe", lambda e, k=k: e.matmul(ps[:, 0:n], lhsT=wa[:, k * 512 + j * 128:k * 512 + (j + 1) * 128],
                                                       rhs=hT[:, k * 512:k * 512 + n], start=(k == 0), stop=(k == 15)),
                         reads=[Rwa, RhT], writes=[Rps])
                return ps, Rps

            def tm_block(wa, Rwa, tb):
                ps, Rps = psb()
                for k in range(16):
                    S.op("pe", lambda e, k=k: e.matmul(ps[:, :], lhsT=hT[:, k * 512 + tb * 128:k * 512 + (tb + 1) * 128],
                                                       rhs=wa[:, k * 512:(k + 1) * 512], start=(k == 0), stop=(k == 15)),
                         reads=[Rwa, RhT], writes=[Rps])
                return ps, Rps

            def store_fm(dst, j, stage, Rst):
                S.dma("pool", lambda e: e.dma_start(out=dst[j * 128:(j + 1) * 128, tok0:tok0 + n], in_=stage[:, 0:n]), reads=[Rst])

            def store_tm(dst, tb, c0, stage, Rst):
                S.dma("pool", lambda e: e.dma_start(out=dst[tok0 + tb * 128:tok0 + (tb + 1) * 128, c0:c0 + 512], in_=stage[:, 0:512]), reads=[Rst])

            for qi, (cb, cbp, dst) in enumerate(((0, 11, qT_d), (1, 12, kT_d))):
                wa, Rwa = load_w(cb)
                if not isctx:
                    psl = nxt("w", 4)
                    wp, Rwp = wb[psl], Rwb[psl]
                    for bb in range(2):
                        S.op("act", lambda e: e.activation(out=wp.rearrange("p (a b e) -> p a b e", b=2, e=32)[:, :, 1 - bb, :],
                                                           in_=wa.rearrange("p (a b e) -> p a b e", b=2, e=32)[:, :, bb, :], func=AF.Copy),
                             reads=[Rwa], writes=[Rwp])
                for j in range(4):
                    pa, Rpa = fm_chunk(wa, Rwa, j)
                    sb = nxt("sb", 3)
                    if not isctx:
                        pb, Rpb = fm_chunk(wp, Rwp, j)
                        t1 = nxt("tmp", 4); t2 = nxt("tmp", 4)
                        S.op("dve", lambda e, t1=t1, pa=pa: e.tensor_tensor(out=tmp[t1][:, 0:n], in0=pa[:, 0:n], in1=rp[:, (2 * qi) * 512:(2 * qi) * 512 + n], op=ALU.mult),
                             reads=[Rpa, Rrp], writes=[Rtmp[t1]])
                        S.op("dve", lambda e, t2=t2, pb=pb: e.tensor_tensor(out=tmp[t2][:, 0:n], in0=pb[:, 0:n], in1=rp[:, (2 * qi + 1) * 512:(2 * qi + 1) * 512 + n], op=ALU.mult),
                             reads=[Rpb, Rrp], writes=[Rtmp[t2]])
                        S.op("pool", lambda e, t1=t1, t2=t2, sb=sb: e.tensor_tensor(out=stb[sb][:, 0:n], in0=tmp[t1][:, 0:n], in1=tmp[t2][:, 0:n], op=ALU.add),
                             reads=[Rtmp[t1], Rtmp[t2]], writes=[Rstb[sb]])
                    else:
                        S.op("act", lambda e, sb=sb, pa=pa: e.activation(out=stb[sb][:, 0:n], in_=pa[:, 0:n], func=AF.Copy,
                                                                         scale=(128 ** -0.5 if qi == 0 else 1.0)),
                             reads=[Rpa], writes=[Rstb[sb]])
                    store_fm(dst, j, stb[sb], Rstb[sb])
            for half in range(2):
                wa, Rwa = load_w(2 + half)
                for tb in range(nb):
                    ps, Rps = tm_block(wa, Rwa, tb)
                    sb = nxt("sb", 3)
                    S.op("act", lambda e, sb=sb, ps=ps: e.activation(out=stb[sb], in_=ps, func=AF.Copy), reads=[Rps], writes=[Rstb[sb]])
                    store_tm(v_d, tb, half * 512, stb[sb], Rstb[sb])
            for half in range(2):
                wa, Rwa = load_w(4 + half)
                for tb in range(nb):
                    ps, Rps = tm_block(wa, Rwa, tb)
                    sf = nxt("sf", 3)
                    S.op("act", lambda e, sf=sf, ps=ps: e.activation(out=stf[sf], in_=ps, func=AF.Silu), reads=[Rps], writes=[Rstf[sf]])
                    store_tm(sg_d, tb, half * 512, stf[sf], Rstf[sf])
            wa, Rwa = load_w(6)
            wp, Rwp = load_w(7)
            for j in range(4):
                pa, Rpa = fm_chunk(wa, Rwa, j)
                pb, Rpb = fm_chunk(wp, Rwp, j)
                t1 = nxt("tmp", 4); sf = nxt("sf", 3)
                S.op("act", lambda e, t1=t1, pb=pb: e.activation(out=tmp[t1][:, 0:n], in_=pb[:, 0:n], func=AF.Sigmoid), reads=[Rpb], writes=[Rtmp[t1]])
                S.op("dve", lambda e, t1=t1, pa=pa, sf=sf: e.tensor_tensor(out=stf[sf][:, 0:n], in0=pa[:, 0:n], in1=tmp[t1][:, 0:n], op=ALU.mult),
                     reads=[Rpa, Rtmp[t1]], writes=[Rstf[sf]])
                store_fm(uT_d, j, stf[sf], Rstf[sf])
            for cb, dst, scl in ((8, nqT_d, 128 ** -0.5), (9, nkT_d, 1.0)):
                wa, Rwa = load_w(cb)
                for j in range(4):
                    pa, Rpa = fm_chunk(wa, Rwa, j)
                    sb = nxt("sb", 3)
                    S.op("act", lambda e, sb=sb, pa=pa, scl=scl: e.activation(out=stb[sb][:, 0:n], in_=pa[:, 0:n], func=AF.Copy, scale=scl),
                         reads=[Rpa], writes=[Rstb[sb]])
                    store_fm(dst, j, stb[sb], Rstb[sb])
            wa, Rwa = load_w(10)
            for tb in range(nb):
                ps, Rps = tm_block(wa, Rwa, tb)
                sb = nxt("sb", 3)
                S.op("dve", lambda e, sb=sb, ps=ps: e.tensor_copy(out=stb[sb], in_=ps), reads=[Rps], writes=[Rstb[sb]])
                store_tm(nv_d, tb, 0, stb[sb], Rstb[sb])
        S.barrier()

    PSB = [p_.bitcast(BF16) for p_ in PS]

    def phase_ret(l, with_ctx, A):
        RB0 = Res("ret_b0"); RB1 = Res("ret_b1"); RP_m = Res("rp_m")
        rd = A.f32(8); lg = A.f32(8); Rlg = Res("lg")
        cdt = A.f32(4 * 128); xir = A.f32(2 * 128); zc = A.f32(2); Rc = Res("retconst")
        DT = A.f32(8 * 128); XI = A.f32(8 * 128); zg = A.f32(16); Rtab = Res("rettab")
        gngt = A.f32(1024); Rgn = Res("gngt")
        Sf = [A.f32(256) for _ in range(4)]; Sb = [A.bf16(256) for _ in range(4)]; RS = [Res("S%d" % h) for h in range(4)]
        RSb = [Res("Sb%d" % h) for h in range(4)]
        qc = [A.bf16(512) for _ in range(2)]; Rqc = [Res("qc%d" % i) for i in range(2)]
        kc = [A.bf16(512) for _ in range(2)]; Rkc = [Res("kc%d" % i) for i in range(2)]
        vc = [A.bf16(1024) for _ in range(2)]; Rvc = [Res("vc%d" % i) for i in range(2)]
        ofc = [A.f32(1024) for _ in range(2)]; Rofc = [Res("ofc%d" % i) for i in range(2)]
        sgc = [A.f32(1024) for _ in range(2)]; Rsgc = [Res("sgc%d" % i) for i in range(2)]
        ob = [A.f32(1024) for _ in range(2)]; Rob = [Res("ob%d" % i) for i in range(2)]
        innb = [A.bf16(128) for _ in range(2)]; Rinnb = [Res("innb%d" % i) for i in range(2)]
        qx = [A.bf16(128) for _ in range(2)]; Rqx = [Res("qx%d" % i) for i in range(2)]
        kz = [A.bf16(128) for _ in range(2)]; Rkz = [Res("kz%d" % i) for i in range(2)]
        ybf = A.bf16(1024); Rybf = Res("ybf")
        mst = [A.bf16(1024) for _ in range(2)]; Rmst = [Res("mst%d" % i) for i in range(2)]
        st = A.f32(16); Rst = Res("gnstat")
        junk = A.f32(256); Rjunk = Res("junk")

        S.dma("sp", lambda e: e.dma_start(out=rd, in_=ret_decay[l:l + 1, :].broadcast_to([128, 8])), writes=[Rlg])
        S.dma("sp", lambda e: e.dma_start(out=cdt.rearrange("p (a i) -> p a i", a=4), in_=dec_d.rearrange("a p i -> p a i")), writes=[Rc])
        S.dma("sp", lambda e: e.dma_start(out=xir.rearrange("p (a i) -> p a i", a=2), in_=xirow_d.rearrange("a p i -> p a i")), writes=[Rc])
        S.dma("sp", lambda e: e.dma_start(out=zc, in_=zcol_d), writes=[Rc])
        S.dma("sp", lambda e: e.dma_start(out=gngt, in_=gng[l:l + 1, :].broadcast_to([128, 1024])), writes=[Rgn])
        S.op("act", lambda e: e.activation(out=lg, in_=rd, func=AF.Exp, scale=-1.0), reads=[Rlg], writes=[Rlg])
        S.op("act", lambda e: e.activation(out=lg, in_=lg, func=AF.Ln, bias=1.0), reads=[Rlg], writes=[Rlg])
        S.op("dve", lambda e: e.tensor_scalar(out=lg, in0=lg, scalar1=-1.0, scalar2=None, op0=ALU.mult), reads=[Rlg], writes=[Rlg])
        for dr in range(2):
            for h in range(4):
                col = dr * 4 + h
                S.op("act", lambda e: e.activation(out=DT[:, col * 128:(col + 1) * 128], in_=cdt[:, (2 * dr) * 128:(2 * dr + 1) * 128],
                                                   func=AF.Exp, scale=lg[:, col:col + 1]), reads=[Rlg, Rc], writes=[Rtab])
                S.op("dve", lambda e: e.tensor_tensor(out=DT[:, col * 128:(col + 1) * 128], in0=DT[:, col * 128:(col + 1) * 128],
                                                      in1=cdt[:, (2 * dr + 1) * 128:(2 * dr + 2) * 128], op=ALU.mult), reads=[Rtab, Rc], writes=[Rtab])
                S.op("act", lambda e: e.activation(out=XI[:, col * 128:(col + 1) * 128], in_=xir[:, dr * 128:(dr + 1) * 128],
                                                   func=AF.Exp, scale=lg[:, col:col + 1]), reads=[Rlg, Rc], writes=[Rtab])
                S.op("act", lambda e: e.activation(out=zg[:, col:col + 1], in_=zc[:, dr:dr + 1], func=AF.Exp, scale=lg[:, col:col + 1]),
                     reads=[Rlg, Rc], writes=[Rtab])
                S.op("act", lambda e: e.activation(out=zg[:, 8 + col:9 + col], in_=lg[:, col:col + 1], func=AF.Exp, scale=128.0),
                     reads=[Rlg, Rc], writes=[Rtab])
        nlc = L // 128; ncc = CTX // 128
        step = [0]
        for dr in range(2):
            for h in range(4):
                S.op("dve", lambda e: e.memset(Sf[h], 0.0), writes=[RS[h]])
                S.op("pool", lambda e: e.memset(Sb[h], 0.0), writes=[RSb[h]])
            if dr == 0:
                order = [(1, L + i * 128) for i in range(ncc)] + [(0, i * 128) for i in range(nlc)]
            else:
                order = [(1, L + i * 128) for i in reversed(range(ncc))] + [(0, i * 128) for i in reversed(range(nlc))]
            for (isctx, tok0) in order:
                need_out = (not isctx) or with_ctx
                sl = step[0] % 2; step[0] += 1
                S.dma("sp", lambda e: e.dma_start(out=kc[sl].rearrange("p (h t) -> p h t", h=4),
                                                  in_=kT_d[:, tok0:tok0 + 128].rearrange("(h d) t -> d h t", d=128)), writes=[Rkc[sl]])
                S.dma("sp", lambda e: e.dma_start(out=vc[sl], in_=v_d[tok0:tok0 + 128, :]), writes=[Rvc[sl]])
                if need_out:
                    S.dma("sp", lambda e: e.dma_start(out=qc[sl].rearrange("p (h t) -> p h t", h=4),
                                                      in_=qT_d[:, tok0:tok0 + 128].rearrange("(h d) t -> d h t", d=128)), writes=[Rqc[sl]])
                    if dr == 1:
                        S.dma("sp", lambda e: e.dma_start(out=ofc[sl], in_=of_d[tok0:tok0 + 128, :]), writes=[Rofc[sl]])
                        S.dma("sp", lambda e: e.dma_start(out=sgc[sl], in_=sg_d[tok0:tok0 + 128, :]), writes=[Rsgc[sl]])
                for h in range(4):
                    col = dr * 4 + h
                    hs = h % 2
                    kh = kc[sl][:, h * 128:(h + 1) * 128]
                    vh = vc[sl][:, h * 256:(h + 1) * 256]
                    if need_out:
                        qh = qc[sl][:, h * 128:(h + 1) * 128]
                        S.op("pe", lambda e: e.matmul(PS[0][:, 0:128], lhsT=kh, rhs=qh, start=True, stop=True),
                             reads=[Rkc[sl], Rqc[sl]], writes=[RB0])
                        yield
                        S.op("dve", lambda e: e.tensor_tensor(out=innb[hs], in0=PS[0][:, 0:128], in1=DT[:, col * 128:(col + 1) * 128], op=ALU.mult),
                             reads=[RB0, Rtab], writes=[Rinnb[hs]])
                        S.op("pool", lambda e: e.tensor_tensor(out=qx[hs], in0=qh, in1=XI[:, col * 128:(col + 1) * 128], op=ALU.mult),
                             reads=[Rqc[sl], Rtab], writes=[Rqx[hs]])
                        yield
                        S.op("pe", lambda e: e.matmul(PS[1][:, hs * 256:(hs + 1) * 256], lhsT=innb[hs], rhs=vh, start=True, stop=False),
                             reads=[Rinnb[hs], Rvc[sl]], writes=[RB1])
                        S.op("pe", lambda e: e.matmul(PS[1][:, hs * 256:(hs + 1) * 256], lhsT=qx[hs], rhs=Sb[h], start=False, stop=True),
                             reads=[Rqx[hs], RSb[h]], writes=[RB1])
                    S.op("pe", lambda e: e.transpose(PSB[0][:, 256 + hs * 128:256 + (hs + 1) * 128], kh, identb), reads=[Rkc[sl], Rid], writes=[RB0])
                    yield
                    S.op("act", lambda e: e.activation(out=kz[hs], in_=PSB[0][:, 256 + hs * 128:256 + (hs + 1) * 128], func=AF.Identity, scale=zg[:, col:col + 1]),
                         reads=[RB0, Rtab], writes=[Rkz[hs]])
                    yield
                    S.op("pe", lambda e: e.matmul(PS[0][:, 256:512], lhsT=kz[hs], rhs=vh, start=True, stop=True),
                         reads=[Rkz[hs], Rvc[sl]], writes=[RB0])
                    yield
                    S.op("dve", lambda e: e.scalar_tensor_tensor(out=Sf[h], in0=Sf[h], scalar=zg[:, 8 + col:9 + col], in1=PS[0][:, 256:512],
                                                                 op0=ALU.mult, op1=ALU.add), reads=[RS[h], RB0, Rtab], writes=[RS[h]])
                    S.op("act", lambda e: e.activation(out=Sb[h], in_=Sf[h], func=AF.Copy), reads=[RS[h]], writes=[RSb[h]])
                    if need_out:
                        if dr == 0:
                            S.op("act", lambda e: e.activation(out=ob[sl][:, h * 256:(h + 1) * 256], in_=PS[1][:, hs * 256:(hs + 1) * 256], func=AF.Copy),
                                 reads=[RB1], writes=[Rob[sl]])
                        else:
                            S.op("dve", lambda e: e.tensor_tensor(out=ob[sl][:, h * 256:(h + 1) * 256], in0=PS[1][:, hs * 256:(hs + 1) * 256],
                                                                  in1=ofc[sl][:, h * 256:(h + 1) * 256], op=ALU.add),
                                 reads=[RB1, Rofc[sl]], writes=[Rob[sl]])
                    yield
                if not need_out:
                    yield
                    continue
                if dr == 0:
                    S.dma("pool", lambda e: e.dma_start(out=of_d[tok0:tok0 + 128, :], in_=ob[sl]), reads=[Rob[sl]])
                    yield
                    continue
                for h in range(4):
                    S.op("act", lambda e: e.activation(out=junk, in_=ob[sl][:, h * 256:(h + 1) * 256], func=AF.Identity, accum_out=st[:, h:h + 1]),
                         reads=[Rob[sl]], writes=[Rjunk, Rst])
                    S.op("act", lambda e: e.activation(out=junk, in_=ob[sl][:, h * 256:(h + 1) * 256], func=AF.Square, accum_out=st[:, 4 + h:5 + h]),
                         reads=[Rob[sl]], writes=[Rjunk, Rst])
                S.op("dve", lambda e: e.tensor_scalar(out=st[:, 0:8], in0=st[:, 0:8], scalar1=1.0 / 256, scalar2=None, op0=ALU.mult), reads=[Rst], writes=[Rst])
                S.op("dve", lambda e: e.tensor_tensor(out=st[:, 8:12], in0=st[:, 0:4], in1=st[:, 0:4], op=ALU.mult), reads=[Rst], writes=[Rst])
                S.op("dve", lambda e: e.tensor_tensor(out=st[:, 8:12], in0=st[:, 4:8], in1=st[:, 8:12], op=ALU.subtract), reads=[Rst], writes=[Rst])
                S.op("act", lambda e: e.activation(out=st[:, 8:12], in_=st[:, 8:12], func=AF.Sqrt, bias=EPS), reads=[Rst], writes=[Rst])
                S.op("dve", lambda e: e.reciprocal(out=st[:, 8:12], in_=st[:, 8:12]), reads=[Rst], writes=[Rst])
                for h in range(4):
                    S.op("dve", lambda e: e.tensor_scalar(out=ob[sl][:, h * 256:(h + 1) * 256], in0=ob[sl][:, h * 256:(h + 1) * 256],
                                                          scalar1=st[:, h:h + 1], scalar2=st[:, 8 + h:9 + h], op0=ALU.subtract, op1=ALU.mult),
                         reads=[Rob[sl], Rst], writes=[Rob[sl]])
                S.op("pool", lambda e: e.tensor_tensor(out=ob[sl], in0=ob[sl], in1=gngt, op=ALU.mult), reads=[Rob[sl], Rgn], writes=[Rob[sl]])
                S.op("dve", lambda e: e.tensor_tensor(out=ybf, in0=ob[sl], in1=sgc[sl], op=ALU.mult), reads=[Rob[sl], Rsgc[sl]], writes=[Rybf])
                for c in range(8):
                    S.op("pe", lambda e: e.transpose(PSB[3][:, c * 128:(c + 1) * 128], ybf[:, c * 128:(c + 1) * 128], identb),
                         reads=[Rybf, Rid], writes=[RP_m])
                S.op("act", lambda e: e.activation(out=mst[sl], in_=PSB[3][:, 0:1024], func=AF.Copy), reads=[RP_m], writes=[Rmst[sl]])
                S.dma("pool", lambda e: e.dma_start(out=mixT_d[0:1024, tok0:tok0 + 128].rearrange("(c p) t -> p c t", p=128),
                                                    in_=mst[sl].rearrange("p (c t) -> p c t", c=8)), reads=[Rmst[sl]])
                yield

    def phase_conv(l, with_ctx, A):
        RPC = Res('rp_conv')
        cw = A.f32(124); cb = A.f32(4); lg_ = A.f32(4); lb_ = A.f32(4); Rcp = Res("convp")
        ones = A.f32(128); Rones = Res("ones")
        pw = A.bf16(4 * 512); Rpw = Res("pw")
        ub = [A.f32(4 * 542) for _ in range(1)]; Rub = [Res("ub%d" % i) for i in range(1)]
        acc = [A.f32(512) for _ in range(4)]; Racc = [Res("acc%d" % i) for i in range(4)]
        sq = [A.f32(512) for _ in range(4)]; Rsq = [Res("sq%d" % i) for i in range(4)]
        rstd = A.f32(512); Rrstd = Res("rstd")
        zb = A.bf16(4 * 512); Rzb = Res("zb")
        stb = [A.bf16(512) for _ in range(2)]; Rstb = [Res("cstb%d" % i) for i in range(2)]
        S.dma("sp", lambda e: e.dma_start(out=cw, in_=cdw[l].rearrange("p c k -> p (c k)")), writes=[Rcp])
        S.dma("sp", lambda e: e.dma_start(out=cb, in_=cdb[l]), writes=[Rcp])
        S.dma("sp", lambda e: e.dma_start(out=lg_, in_=lng[l]), writes=[Rcp])
        S.dma("sp", lambda e: e.dma_start(out=lb_, in_=lnb[l]), writes=[Rcp])
        S.dma("sp", lambda e: e.dma_start(out=pw.rearrange("p (k n) -> p k n", k=4), in_=pwb[l].rearrange("(k p) n -> p k n", p=128)), writes=[Rpw], reads=[RCW("pw%d" % l)])
        S.op("pool", lambda e: e.memset(ones, 1.0 / 512), writes=[Rones])
        tiles = [(0, t0, min(512, L - t0)) for t0 in range(0, L, 512)]
        if with_ctx:
            tiles += [(1, t0, min(512, CTX - t0)) for t0 in range(0, CTX, 512)]
        for ti, (isctx, t0, n) in enumerate(tiles):
            Ls = CTX if isctx else L
            base = L if isctx else 0
            sl = 0
            u3 = ub[sl].rearrange("p (c t) -> p c t", c=4)
            lo = max(t0 - 15, 0); hi = min(t0 + n + 15, Ls)
            off = lo - (t0 - 15)
            if lo != t0 - 15 or hi != t0 + n + 15:
                S.op("pool", lambda e: e.memset(ub[sl], 0.0), writes=[Rub[sl]])
            S.dma("sp", lambda e: e.dma_start(out=u3[:, :, off:off + hi - lo], in_=uT_d[:, base + lo:base + hi].rearrange("(c p) t -> p c t", p=128)),
                  writes=[Rub[sl]])
            for c in range(4):
                S.op("dve", lambda e: e.tensor_scalar(out=acc[c][:, 0:n], in0=u3[:, c, 15:15 + n], scalar1=cw[:, c * 31 + 15:c * 31 + 16],
                                                      scalar2=cb[:, c:c + 1], op0=ALU.mult, op1=ALU.add), reads=[Rub[sl], Rcp], writes=[Racc[c]])
            for k in range(31):
                if k == 15:
                    continue
                for c in range(4):
                    S.op("dve", lambda e: e.scalar_tensor_tensor(out=acc[c][:, 0:n], in0=u3[:, c, k:k + n], scalar=cw[:, c * 31 + k:c * 31 + k + 1],
                                                                 in1=acc[c][:, 0:n], op0=ALU.mult, op1=ALU.add),
                         reads=[Rub[sl], Rcp, Racc[c]], writes=[Racc[c]])
                yield
            for c in range(4):
                S.op("pe", lambda e: e.matmul(PS[6][:, 0:n], lhsT=ones, rhs=acc[c][:, 0:n], start=(c == 0), stop=(c == 3)),
                     reads=[Rones, Racc[c]], writes=[RPC])
            for c in range(4):
                S.op("dve", lambda e: e.tensor_tensor(out=acc[c][:, 0:n], in0=acc[c][:, 0:n], in1=PS[6][:, 0:n], op=ALU.subtract),
                     reads=[RPC, Racc[c]], writes=[Racc[c]])
                S.op("act", lambda e: e.activation(out=sq[c][:, 0:n], in_=acc[c][:, 0:n], func=AF.Square), reads=[Racc[c]], writes=[Rsq[c]])
            for c in range(4):
                S.op("pe", lambda e: e.matmul(PS[6][:, 0:n], lhsT=ones, rhs=sq[c][:, 0:n], start=(c == 0), stop=(c == 3)),
                     reads=[Rones, Rsq[c]], writes=[RPC])
            yield
            S.op("act", lambda e: e.activation(out=rstd[:, 0:n], in_=PS[6][:, 0:n], func=AF.Sqrt, bias=EPS), reads=[RPC], writes=[Rrstd])
            S.op("dve", lambda e: e.reciprocal(out=rstd[:, 0:n], in_=rstd[:, 0:n]), reads=[Rrstd], writes=[Rrstd])
            for c in range(4):
                S.op("dve", lambda e: e.tensor_tensor(out=acc[c][:, 0:n], in0=acc[c][:, 0:n], in1=rstd[:, 0:n], op=ALU.mult),
                     reads=[Rrstd, Racc[c]], writes=[Racc[c]])
                S.op("act", lambda e: e.activation(out=zb[:, c * 512:c * 512 + n], in_=acc[c][:, 0:n], func=AF.Silu,
                                                   scale=lg_[:, c:c + 1], bias=lb_[:, c:c + 1]), reads=[Racc[c], Rcp], writes=[Rzb])
            for co in range(4):
                bk = 6
                for ci in range(4):
                    S.op("pe", lambda e: e.matmul(PS[bk][:, 0:n], lhsT=pw[:, ci * 512 + co * 128:ci * 512 + (co + 1) * 128],
                                                  rhs=zb[:, ci * 512:ci * 512 + n], start=(ci == 0), stop=(ci == 3)),
                         reads=[Rpw, Rzb], writes=[RPC])
                ss_ = co % 2
                S.op("act", lambda e: e.activation(out=stb[ss_][:, 0:n], in_=PS[bk][:, 0:n], func=AF.Copy), reads=[RPC], writes=[Rstb[ss_]])
                S.dma("pool", lambda e: e.dma_start(out=mixT_d[1024 + co * 128:1024 + (co + 1) * 128, base + t0:base + t0 + n], in_=stb[ss_][:, 0:n]),
                      reads=[Rstb[ss_]])
                yield

    def na_zero():
        S.op("pool", lambda e: e.memset(ztile, 0.0), writes=[Rzt])
        S.dma("sp", lambda e: e.dma_start(out=bass.AP(toep_d.tensor, 0, [[2176, 128], [1, 2176]]), in_=ztile), reads=[Rzt])

    def na_diag(l):
        dsem = S.dma_sem()
        for h in range(4):
            for ro in range(15):
                dst = bass.AP(toep_d.tensor, h * 69632 + (ro + 1) * 64 - 15, [[1089, 64], [1, 31]])
                S.dma("sp", lambda e: e.dma_start(out=dst, in_=na_rpb[l, h, ro:ro + 1, :].broadcast_to([64, 31])), sem=dsem)

    def phase_na(l, with_ctx, A):
        RP_sc = Res("rp_sc"); RP_pt = Res("rp_pt"); RP_no = Res("rp_no")
        TB = A.f32(20 * 576); RTB = Res("TB")
        nam = A.f32(5 * 576); Rnam = Res("nam")
        ckT = A.bf16(4 * CTX); Rck = Res("ckT")
        cv = A.bf16((CTX // 128) * 512); Rcv = Res("cv")
        qm = [A.bf16(512) for _ in range(2)]; Rqm = [Res("qm%d" % i) for i in range(2)]
        km = [A.bf16(4 * 576) for _ in range(2)]; Rkm = [Res("km%d" % i) for i in range(2)]
        vm = [A.bf16(5 * 512) for _ in range(2)]; Rvm = [Res("vm%d" % i) for i in range(2)]
        scs = [A.f32(832)] * 2; Rscs = [Res("scs")] * 2
        pb = [A.bf16(832) for _ in range(2)]; Rpb = [Res("pb%d" % i) for i in range(2)]
        pT = [A.bf16(7 * 128) for _ in range(2)]; RpT = [Res("pT%d" % i) for i in range(2)]
        nst = [A.bf16(512) for _ in range(2)]; Rnst = [Res("nst%d" % i) for i in range(2)]
        sm = A.f32(16); Rsm = Res("sm")
        types = na_types(R)
        for h in range(4):
            for ti, (mrep, lo) in enumerate(types):
                idx = h * 5 + ti
                for qr in range(2):
                    ro0 = lo - (2 * mrep + qr) + 7
                    src = bass.AP(toep_d.tensor, h * 69632 + (ro0 + 1) * 64, [[1088, 64], [1, 576]])
                    S.dma("sp", lambda e: e.dma_start(out=TB[qr * 64:(qr + 1) * 64, idx * 576:(idx + 1) * 576], in_=src), writes=[RTB])
        S.dma("sp", lambda e: e.dma_start(out=nam.rearrange("p (a k) -> p a k", a=5), in_=nam_d.rearrange("a p k -> p a k")), writes=[Rnam])
        for h in range(4):
            S.op("dve", lambda e: e.tensor_tensor(out=TB[:, h * 2880:(h + 1) * 2880], in0=TB[:, h * 2880:(h + 1) * 2880], in1=nam, op=ALU.add),
                 reads=[RTB, Rnam], writes=[RTB])
        S.dma("sp", lambda e: e.dma_start(out=ckT.rearrange("p (h t) -> p h t", h=4), in_=nkT_d[:, L:T].rearrange("(h d) t -> d h t", d=128)), writes=[Rck])
        S.dma("sp", lambda e: e.dma_start(out=cv.rearrange("p (c f) -> p c f", f=512), in_=nv_d[L:T, :].rearrange("(c p) f -> p c f", p=128)), writes=[Rcv])
        cnt = [0]

        def na_block(sl, tokdst, segs, vch):
            ntot = sum(s_[1] for s_ in segs)
            for h in range(4):
                i = cnt[0] % 2; cnt[0] += 1
                sc = psum_t[:, 4 * 512:4 * 512 + 1024]
                ops_ = PS[2][:, 0:128]
                o = 0
                for (rf, ncol, bf_) in segs:
                    c0 = 0
                    while c0 < ncol:
                        w_ = min(ncol - c0, 512 - (o % 512))
                        S.op("pe", lambda e: e.matmul(sc[:, o:o + w_], lhsT=qm[sl][:, h * 128:(h + 1) * 128], rhs=rf(h)[:, c0:c0 + w_], start=True, stop=True),
                             reads=[Rqm[sl], Rkm[sl], Rck], writes=[RP_sc])
                        o += w_; c0 += w_
                yield
                o = 0
                for (rf, ncol, bf_) in segs:
                    if bf_ is not None:
                        S.op("dve", lambda e: e.tensor_tensor(out=scs[i][:, o:o + ncol], in0=sc[:, o:o + ncol], in1=bf_(h), op=ALU.add),
                             reads=[RP_sc, RTB], writes=[Rscs[i]])
                    else:
                        S.op("act", lambda e: e.activation(out=scs[i][:, o:o + ncol], in_=sc[:, o:o + ncol], func=AF.Copy),
                             reads=[RP_sc], writes=[Rscs[i]])
                    o += ncol
                yield
                S.op("dve", lambda e: e.reduce_max(out=sm[:, i:i + 1], in_=scs[i][:, 0:ntot], axis=AX.X), reads=[Rscs[i]], writes=[Rsm])
                S.op("dve", lambda e: e.tensor_scalar(out=sm[:, i:i + 1], in0=sm[:, i:i + 1], scalar1=-1.0, scalar2=None, op0=ALU.mult), reads=[Rsm], writes=[Rsm])
                yield
                S.op("act", lambda e: e.activation(out=scs[i][:, 0:ntot], in_=scs[i][:, 0:ntot], func=AF.Exp, bias=sm[:, i:i + 1],
                                                   accum_out=sm[:, 4 + i:5 + i]), reads=[Rscs[i], Rsm], writes=[Rscs[i], Rsm])
                yield
                S.op("dve", lambda e: e.reciprocal(out=sm[:, 4 + i:5 + i], in_=sm[:, 4 + i:5 + i]), reads=[Rsm], writes=[Rsm])
                S.op("dve", lambda e: e.tensor_scalar(out=pb[i][:, 0:ntot], in0=scs[i][:, 0:ntot], scalar1=sm[:, 4 + i:5 + i], scalar2=None, op0=ALU.mult),
                     reads=[Rscs[i], Rsm], writes=[Rpb[i]])
                yield
                bt = 7
                o = 0
                for ci, (vf, nk) in enumerate(vch):
                    S.op("pe", lambda e: e.transpose(PSB[bt][0:nk, ci * 128:(ci + 1) * 128], pb[i][:, o:o + nk], identb),
                         reads=[Rpb[i], Rid], writes=[RP_pt])
                    o += nk
                nch = len(vch)
                yield
                S.op("act", lambda e: e.activation(out=pT[i][:, 0:nch * 128], in_=PSB[bt][:, 0:nch * 128], func=AF.Copy), reads=[RP_pt], writes=[RpT[i]])
                yield
                for ci, (vf, nk) in enumerate(vch):
                    S.op("pe", lambda e: e.matmul(ops_, lhsT=vf(h)[0:nk, :], rhs=pT[i][0:nk, ci * 128:(ci + 1) * 128],
                                                  start=(ci == 0), stop=(ci == nch - 1)), reads=[RpT[i], Rvm[sl], Rcv], writes=[RP_no])
                S.op("dve", lambda e: e.tensor_copy(out=nst[sl][:, h * 128:(h + 1) * 128], in_=ops_), reads=[RP_no], writes=[Rnst[sl]])
                yield
            S.dma("pool", lambda e: e.dma_start(out=mixT_d[1536:2048, tokdst:tokdst + 128].rearrange("(h d) t -> d h t", d=128),
                                                in_=nst[sl].rearrange("p (h t) -> p h t", h=4)), reads=[Rnst[sl]])

        ctx_v = [((lambda h, c=c: cv[:, c * 512 + h * 128:c * 512 + (h + 1) * 128]), 128) for c in range(CTX // 128)]
        ctx_seg = ((lambda h: ckT[:, h * CTX:(h + 1) * CTX]), CTX, None)
        for m in range(R // 2):
            ti, lo = na_type_of(m, R)
            sl = m % 2
            S.dma("sp", lambda e: e.dma_start(out=qm[sl].rearrange("p (h t) -> p h t", h=4),
                                              in_=nqT_d[:, m * 128:(m + 1) * 128].rearrange("(h d) t -> d h t", d=128)), writes=[Rqm[sl]])
            S.dma("sp", lambda e: e.dma_start(out=km[sl].rearrange("p (h t) -> p h t", h=4),
                                              in_=nkT_d[:, lo * 64:lo * 64 + 576].rearrange("(h d) t -> d h t", d=128)), writes=[Rkm[sl]])
            S.dma("sp", lambda e: e.dma_start(out=vm[sl][:, 0:2048].rearrange("p (c f) -> p c f", f=512),
                                              in_=nv_d[lo * 64:lo * 64 + 512, :].rearrange("(c p) f -> p c f", p=128)), writes=[Rvm[sl]])
            S.dma("sp", lambda e: e.dma_start(out=vm[sl][0:64, 2048:2560], in_=nv_d[lo * 64 + 512:lo * 64 + 576, :]), writes=[Rvm[sl]])
            kseg = ((lambda h, sl=sl: km[sl][:, h * 576:(h + 1) * 576]), 576,
                    (lambda h, ti=ti: TB[:, (h * 5 + ti) * 576:(h * 5 + ti + 1) * 576]))
            vch = [((lambda h, c=c, sl=sl: vm[sl][:, c * 512 + h * 128:c * 512 + (h + 1) * 128]), 128) for c in range(4)]
            vch += [((lambda h, sl=sl: vm[sl][:, 2048 + h * 128:2048 + (h + 1) * 128]), 64)]
            yield from na_block(sl, m * 128, [kseg, ctx_seg], vch + ctx_v)
        if with_ctx:
            for qb in range(CTX // 128):
                sl = qb % 2
                S.dma("sp", lambda e: e.dma_start(out=qm[sl].rearrange("p (h t) -> p h t", h=4),
                                                  in_=nqT_d[:, L + qb * 128:L + (qb + 1) * 128].rearrange("(h d) t -> d h t", d=128)), writes=[Rqm[sl]])
                yield from na_block(sl, L + qb * 128, [ctx_seg], ctx_v)

    def phase_ffn(l, with_ctx, xlat, xctx, last):
        A.off = pers_mark
        fw = A.f32(264); fb = A.f32(88); Rfp = Res("ffnp")
        xs = [A.f32(D) for _ in range(4)]; Rxs = [Res("fxs%d" % i) for i in range(4)]
        xn = [A.f32(D) for _ in range(2)]; Rxn = [Res("fxn%d" % i) for i in range(2)]
        hT = A.bf16(16 * 512); RhT = Res("fhT")
        aT = A.bf16(22 * 512); RaT = Res("aT")
        mt = aT[:, 0:16 * 512]; RmT = RaT
        upw = [A.bf16(16 * 256) for _ in range(4)]; Rupw = [Res("upw%d" % i) for i in range(4)]
        wdn = [A.bf16(16 * 512) for _ in range(3)]; Rwdn = [Res("wdn%d" % i) for i in range(3)]
        gt = [A.f32(D) for _ in range(2)]; Rgt = [Res("gt%d" % i) for i in range(2)]
        yv = [A.f32(512) for _ in range(2)]; Ryv = [Res("yv%d" % i) for i in range(2)]
        yg = [A.f32(512) for _ in range(2)]; Ryg = [Res("yg%d" % i) for i in range(2)]
        tmp = [A.f32(512) for _ in range(2)]; Rtmp = [Res("ftmp%d" % i) for i in range(2)]
        ss = A.f32(16); Rss = Res("fss")
        S.dma("sp", lambda e: e.dma_start(out=fw, in_=fdw[l].rearrange("p c k -> p (c k)")), writes=[Rfp])
        S.dma("sp", lambda e: e.dma_start(out=fb, in_=fdb[l]), writes=[Rfp])
        cnt = {"up": 0, "dn": 0, "tmp": 0, "xn": 0, "y": 0}

        def nxt(key, n_):
            v = cnt[key]; cnt[key] = (v + 1) % n_; return v

        tiles = [(0, t0, min(510, L - t0)) for t0 in range(0, L, 510)]
        if with_ctx:
            tiles += [(1, t0, min(510, CTX - t0)) for t0 in range(0, CTX, 510)]
        for (isctx, t0, ni) in tiles:
            cond = isctx
            Ls = CTX if isctx else L
            base = L if isctx else 0
            xsrc = xctx if isctx else xlat
            n = ni + 2
            nb = (n + 127) // 128
            jlo = 1 if t0 == 0 else 0
            jhi = n - 1 if t0 + ni == Ls else n
            nts = [min(128, n - tb * 128) for tb in range(nb)]
            for tb in range(nb):
                r0 = max(tb * 128, jlo); r1 = min(tb * 128 + nts[tb], jhi)
                if r0 != tb * 128 or r1 != tb * 128 + nts[tb]:
                    S.op("pool", lambda e: e.memset(xs[tb], 0.0), writes=[Rxs[tb]])
                if r1 > r0:
                    S.dma("sp", lambda e: e.dma_start(out=xs[tb][r0 - tb * 128:r1 - tb * 128, :], in_=xsrc[t0 - 1 + r0:t0 - 1 + r1, :]), writes=[Rxs[tb]])
            mt3 = mt.rearrange("p (c t) -> p c t", c=16)
            if jlo != 0 or jhi != n:
                S.op("pool", lambda e: e.memset(mt, 0.0), writes=[RmT])
            S.dma("sp", lambda e: e.dma_start(out=mt3[:, :, jlo:jhi], in_=mixT_d[:, base + t0 - 1 + jlo:base + t0 - 1 + jhi].rearrange("(c p) t -> p c t", p=128)),
                  writes=[RmT])
            S.dma("sp", lambda e: e.dma_start(out=gt[0], in_=mods_d[cond:cond + 1, 2 * D:3 * D].broadcast_to([128, D])), writes=[Rgt[0]])
            S.dma("sp", lambda e: e.dma_start(out=gt[1], in_=mods_d[cond:cond + 1, 5 * D:6 * D].broadcast_to([128, D])), writes=[Rgt[1]])
            for nbk in range(4):
                sl = nxt("dn", 3)
                S.dma("sp", lambda e: e.dma_start(out=wdn[sl].rearrange("p (k n) -> p k n", k=16),
                                                  in_=woutb[l][:, nbk * 512:(nbk + 1) * 512].rearrange("(k p) n -> p k n", p=128)), writes=[Rwdn[sl]], reads=[RCW("wout%d" % l)])
                for tb in range(nb):
                    nt = nts[tb]
                    bk = 4 + tb
                    for k in range(16):
                        S.op("pe", lambda e: e.matmul(PS[bk][0:nt, :], lhsT=mt[:, k * 512 + tb * 128:k * 512 + tb * 128 + nt],
                                                      rhs=wdn[sl][:, k * 512:(k + 1) * 512], start=(k == 0), stop=(k == 15)),
                             reads=[RmT, Rwdn[sl]], writes=[RPS[bk]])
                    ti_ = nxt("tmp", 2)
                    S.op("dve", lambda e: e.tensor_tensor(out=tmp[ti_][0:nt, :], in0=PS[bk][0:nt, :], in1=gt[0][0:nt, nbk * 512:(nbk + 1) * 512], op=ALU.mult),
                         reads=[RPS[bk], Rgt[0]], writes=[Rtmp[ti_]])
                    S.op("pool", lambda e: e.tensor_tensor(out=xs[tb][0:nt, nbk * 512:(nbk + 1) * 512], in0=xs[tb][0:nt, nbk * 512:(nbk + 1) * 512],
                                                           in1=tmp[ti_][0:nt, :], op=ALU.add), reads=[Rtmp[ti_], Rxs[tb]], writes=[Rxs[tb]])
            if last:
                S.dma("sp", lambda e: e.dma_start(out=gt[0], in_=final_g.rearrange("(o d) -> o d", o=1).broadcast_to([128, D])), writes=[Rgt[0]])
            for tb in range(nb):
                nt = nts[tb]
                xi_ = nxt("xn", 2)
                S.op("act", lambda e: e.activation(out=xn[xi_][0:nt, :], in_=xs[tb][0:nt, :], func=AF.Square, accum_out=ss[0:nt, tb:tb + 1]),
                     reads=[Rxs[tb]], writes=[Rxn[xi_], Rss])
                S.op("act", lambda e: e.activation(out=ss[0:nt, 4 + tb:5 + tb], in_=ss[0:nt, tb:tb + 1], func=AF.Sqrt, scale=1.0 / D, bias=EPS),
                     reads=[Rss], writes=[Rss])
                S.op("dve", lambda e: e.reciprocal(out=ss[0:nt, 4 + tb:5 + tb], in_=ss[0:nt, 4 + tb:5 + tb]), reads=[Rss], writes=[Rss])
                S.op("dve", lambda e: e.tensor_scalar(out=xn[xi_][0:nt, :], in0=xs[tb][0:nt, :], scalar1=ss[0:nt, 4 + tb:5 + tb], scalar2=None, op0=ALU.mult),
                     reads=[Rxs[tb], Rss], writes=[Rxn[xi_]])
                for c4 in range(4):
                    bk = c4 % 2
                    for cc in range(4):
                        c = c4 * 4 + cc
                        S.op("pe", lambda e: e.transpose(PS[bk][:, cc * 128:cc * 128 + nt], xn[xi_][0:nt, c * 128:(c + 1) * 128], ident[0:nt, 0:nt]),
                             reads=[Rxn[xi_], Rid], writes=[RPS[bk]])
                    for cc in range(4):
                        c = c4 * 4 + cc
                        dst = hT[:, c * 512 + tb * 128:c * 512 + tb * 128 + nt]
                        if cc % 2 == 0:
                            S.op("dve", lambda e: e.tensor_scalar(out=dst, in0=PS[bk][:, cc * 128:cc * 128 + nt], scalar1=G2T[:, 2 * c + cond:2 * c + cond + 1],
                                                                  scalar2=modcol(3, c, cond), op0=ALU.mult, op1=ALU.add),
                                 reads=[RPS[bk], RG, RmodT], writes=[RhT])
                        else:
                            S.op("act", lambda e: e.activation(out=dst, in_=PS[bk][:, cc * 128:cc * 128 + nt], func=AF.Identity,
                                                               scale=G2T[:, 2 * c + cond:2 * c + cond + 1], bias=modcol(3, c, cond)),
                                 reads=[RPS[bk], RG, RmodT], writes=[RhT])
            hT3 = hT.rearrange("p (c t) -> p c t", c=16)
            if jlo == 1:
                S.op("pool", lambda e: e.memset(hT3[:, :, 0:1], 0.0), reads=[RhT], writes=[RhT])
            if jhi == n - 1:
                S.op("pool", lambda e: e.memset(hT3[:, :, n - 1:n], 0.0), reads=[RhT], writes=[RhT])
            for hf in range(2):
                for jj in range(22):
                    c = hf * 22 + jj
                    sub = c % 2
                    if jj % 2 == 0:
                        uv = nxt("up", 4); ug = nxt("up", 4)
                        for (us, c0) in ((uv, (c // 2) * 256), (ug, DFF + (c // 2) * 256)):
                            S.dma("sp", lambda e: e.dma_start(out=upw[us].rearrange("p (k n) -> p k n", k=16),
                                                              in_=upb[l][:, c0:c0 + 256].rearrange("(k p) n -> p k n", p=128)), writes=[Rupw[us]], reads=[RCW("up%d" % l)])
                    pbk = (jj % 2) * 2
                    for (us, bk) in ((uv, pbk), (ug, pbk + 1)):
                        for k in range(16):
                            S.op("pe", lambda e: e.matmul(PS[bk][:, 0:n], lhsT=upw[us][:, k * 256 + sub * 128:k * 256 + (sub + 1) * 128],
                                                          rhs=hT[:, k * 512:k * 512 + n], start=(k == 0), stop=(k == 15)),
                                 reads=[Rupw[us], RhT], writes=[RPS[bk]])
                    yi = nxt("y", 2)
                    for (yy, Ryy, bk, ch) in ((yv[yi], Ryv[yi], pbk, c), (yg[yi], Ryg[yi], pbk + 1, 44 + c)):
                        S.op("act", lambda e: e.activation(out=yy[:, 0:n], in_=PS[bk][:, 0:n], func=AF.Identity, scale=fw[:, ch * 3 + 1:ch * 3 + 2],
                                                           bias=fb[:, ch:ch + 1]), reads=[RPS[bk], Rfp], writes=[Ryy])
                        S.op("dve", lambda e: e.scalar_tensor_tensor(out=yy[:, 1:n], in0=PS[bk][:, 0:n - 1], scalar=fw[:, ch * 3:ch * 3 + 1], in1=yy[:, 1:n],
                                                                     op0=ALU.mult, op1=ALU.add), reads=[RPS[bk], Rfp, Ryy], writes=[Ryy])
                        S.op("dve", lambda e: e.scalar_tensor_tensor(out=yy[:, 0:n - 1], in0=PS[bk][:, 1:n], scalar=fw[:, ch * 3 + 2:ch * 3 + 3], in1=yy[:, 0:n - 1],
                                                                     op0=ALU.mult, op1=ALU.add), reads=[RPS[bk], Rfp, Ryy], writes=[Ryy])
                    S.op("act", lambda e: e.activation(out=yg[yi][:, 0:n], in_=yg[yi][:, 0:n], func=AF.Silu), reads=[Ryg[yi]], writes=[Ryg[yi]])
                    S.op("pool", lambda e: e.tensor_tensor(out=aT[:, jj * 512:jj * 512 + n], in0=yv[yi][:, 0:n], in1=yg[yi][:, 0:n], op=ALU.mult),
                         reads=[Ryv[yi], Ryg[yi]], writes=[RaT])
                for nbk in range(4):
                    for part in range(2):
                        sl = nxt("dn", 3)
                        k0 = hf * 22 + part * 11
                        S.dma("sp", lambda e: e.dma_start(out=wdn[sl][:, 0:11 * 512].rearrange("p (k n) -> p k n", k=11),
                                                          in_=downb[l][k0 * 128:(k0 + 11) * 128, nbk * 512:(nbk + 1) * 512].rearrange("(k p) n -> p k n", p=128)),
                              writes=[Rwdn[sl]], reads=[RCW("down%d" % l)])
                        for tb in range(nb):
                            nt = nts[tb]
                            bk = 4 + tb
                            for kl in range(11):
                                kk = part * 11 + kl
                                S.op("pe", lambda e: e.matmul(PS[bk][0:nt, :], lhsT=aT[:, kk * 512 + tb * 128:kk * 512 + tb * 128 + nt],
                                                              rhs=wdn[sl][:, kl * 512:(kl + 1) * 512], start=(kk == 0), stop=(kk == 21)),
                                     reads=[RaT, Rwdn[sl]], writes=[RPS[bk]])
                    for tb in range(nb):
                        nt = nts[tb]
                        bk = 4 + tb
                        ti_ = nxt("tmp", 2)
                        S.op("dve", lambda e: e.tensor_tensor(out=tmp[ti_][0:nt, :], in0=PS[bk][0:nt, :], in1=gt[1][0:nt, nbk * 512:(nbk + 1) * 512], op=ALU.mult),
                             reads=[RPS[bk], Rgt[1]], writes=[Rtmp[ti_]])
                        S.op("pool", lambda e: e.tensor_tensor(out=xs[tb][0:nt, nbk * 512:(nbk + 1) * 512], in0=xs[tb][0:nt, nbk * 512:(nbk + 1) * 512],
                                                               in1=tmp[ti_][0:nt, :], op=ALU.add), reads=[Rtmp[ti_], Rxs[tb]], writes=[Rxs[tb]])
            for tb in range(nb):
                nt = nts[tb]
                r0 = max(tb * 128, 1); r1 = min(tb * 128 + nt, n - 1)
                if r1 <= r0:
                    continue
                if last:
                    xi_ = nxt("xn", 2)
                    S.op("act", lambda e: e.activation(out=xn[xi_][0:nt, :], in_=xs[tb][0:nt, :], func=AF.Square, accum_out=ss[0:nt, 8 + tb:9 + tb]),
                         reads=[Rxs[tb]], writes=[Rxn[xi_], Rss])
                    S.op("act", lambda e: e.activation(out=ss[0:nt, 12 + tb:13 + tb], in_=ss[0:nt, 8 + tb:9 + tb], func=AF.Sqrt, scale=1.0 / D, bias=EPS),
                         reads=[Rss], writes=[Rss])
                    S.op("dve", lambda e: e.reciprocal(out=ss[0:nt, 12 + tb:13 + tb], in_=ss[0:nt, 12 + tb:13 + tb]), reads=[Rss], writes=[Rss])
                    S.op("dve", lambda e: e.tensor_scalar(out=xn[xi_][0:nt, :], in0=xs[tb][0:nt, :], scalar1=ss[0:nt, 12 + tb:13 + tb], scalar2=None, op0=ALU.mult),
                         reads=[Rxs[tb], Rss], writes=[Rxn[xi_]])
                    S.op("pool", lambda e: e.tensor_tensor(out=xn[xi_][0:nt, :], in0=xn[xi_][0:nt, :], in1=gt[0][0:nt, :], op=ALU.mult),
                         reads=[Rxn[xi_], Rgt[0]], writes=[Rxn[xi_]])
                    S.dma("act", lambda e: e.dma_start(out=out_d[t0 - 1 + r0:t0 - 1 + r1, :], in_=xn[xi_][r0 - tb * 128:r1 - tb * 128, :]), reads=[Rxn[xi_]])
                else:
                    S.dma("act", lambda e: e.dma_start(out=xa_d[base + t0 - 1 + r0:base + t0 - 1 + r1, :], in_=xs[tb][r0 - tb * 128:r1 - tb * 128, :]),
                          reads=[Rxs[tb]])
        S.barrier()

    def run_group(items):
        active = [(iter(g_), w_) for g_, w_ in items]
        while active:
            for it in list(active):
                g_, w_ = it
                for _ in range(w_):
                    try:
                        next(g_)
                    except StopIteration:
                        active.remove(it)
                        break
        S.barrier()

    def run(stage="all", nlay=NL):
        for l in range(nlay):
            last = (l == NL - 1)
            xlat = x_in if l == 0 else xa_d[0:L]
            xctx = ctx_in if l == 0 else xa_d[L:T]
            if l == 0:
                for l2 in range(NL):
                    issue_casts(l2)
            na_zero()
            phase_mods(l)
            if stage == "mods" and l == nlay - 1:
                break
            na_diag(l)
            phase_inproj(l, xlat, xctx)
            if stage == "inproj" and l == nlay - 1:
                break
            A.off = pers_mark
            Ana = A.sub(A.n - A.off - 15900 - 9700); Aret = A.sub(15900); Aconv = A.sub(9700)
            import os
            gm = os.environ.get("GMODE", "par")
            if gm == "seq":
                for g_ in (phase_ret(l, not last, Aret), phase_conv(l, not last, Aconv), phase_na(l, not last, Ana)):
                    run_group([(g_, 1)])
            elif gm in ("ret", "conv", "na"):
                run_group([({"ret": phase_ret(l, not last, Aret), "conv": phase_conv(l, not last, Aconv), "na": phase_na(l, not last, Ana)}[gm], 1)])
            else:
                run_group([(phase_ret(l, not last, Aret), 6), (phase_conv(l, not last, Aconv), 1), (phase_na(l, not last, Ana), 4)])
            if stage == "na" and l == nlay - 1:
                break
            phase_ffn(l, not last, xlat, xctx, last)
        S.run_block()

    g.run = run
    g.nc = nc; g.S = S
    g.phase_mods = phase_mods; g.phase_inproj = phase_inproj
    g.names = dict(x_in=x_in, ctx_in=ctx_in, xa_d=xa_d, out_d=out_d)
    g.locals = locals()
    return g


def kernel(**inputs):
    inp = {k: np.asarray(v) for k, v in inputs.items()}
    B, L, _ = inp["x"].shape
    CTX = inp["ctx"].shape[1]
    g = build(L, CTX, NL=2, dbg=False)
    g.run("all", 2)
    in_maps = [host_layout(inp, b, L) for b in range(B)]
    res = run_bass_kernel_spmd(g.nc, in_maps, core_ids=list(range(B)))
    return np.stack([np.asarray(res.results[b]["out"], np.float32) for b in range(B)]).astype(np.float32)
```

```python
import numpy as np
import ml_dtypes
import concourse.bass as bass
import concourse.mybir as mybir
from concourse.bass_utils import run_bass_kernel_spmd

F32 = mybir.dt.float32
BF16 = mybir.dt.bfloat16
AF = mybir.ActivationFunctionType
ALU = mybir.AluOpType
AX = mybir.AxisListType

COMPUTE = ("pe", "act", "dve", "pool")
SAME_ENGINE_WAIT = True


class Res:
    __slots__ = ("name", "w", "r", "lsem", "ssem")

    def __init__(self, name):
        self.name = name
        self.w = {}
        self.r = {}
        self.lsem = None
        self.ssem = None


class _Cap:
    def __init__(self):
        self.call = None

    def __getattr__(self, name):
        def f(*a, **k):
            self.call = (name, a, k)
            return self
        return f


class Rec:
    __slots__ = ("waits", "fn", "inc")

    def __init__(self, waits, fn, inc):
        self.waits = waits
        if fn is not None:
            cap = _Cap()
            fn(cap)
            fn = cap.call
            assert fn is not None
        self.fn = fn
        self.inc = inc


class Sched:
    def __init__(self, nc):
        self.nc = nc
        self.prog = {e: [] for e in ("pe", "act", "dve", "pool", "sp")}
        self.sems = {}
        self.cnt = {}
        self.known = {e: {} for e in self.prog}
        self.last = {e: None for e in COMPUTE}
        self.pending = {e: False for e in COMPUTE}
        self.free_dma_sems = []
        self.live_dma_sems = []
        self.nsem = 0
        self.nobarrier = set()
        self.live_res = []
        for e in COMPUTE:
            self._mk("E_" + e)

    def _mk(self, key):
        self.sems[key] = self.nc.alloc_semaphore(key)
        self.cnt[key] = 0
        self.nsem += 1
        return key

    def dma_sem(self):
        if self.free_dma_sems:
            k = self.free_dma_sems.pop()
        else:
            k = self._mk("D%d" % self.nsem)
        self.live_dma_sems.append(k)
        return k

    def _force(self, key):
        if key.startswith("E_"):
            e = key[2:]
            if self.pending[e]:
                rec = self.last[e]
                assert rec.inc is None
                rec.inc = (key, 1)
                self.cnt[key] += 1
                self.pending[e] = False

    def _waits(self, eng, deps):
        out = []
        kn = self.known[eng]
        for key, val in deps.items():
            if key == "E_" + eng:
                if eng == "pe" or not SAME_ENGINE_WAIT:
                    continue
            if kn.get(key, 0) >= val:
                continue
            self._force(key)
            assert self.cnt[key] >= val, (key, self.cnt[key], val)
            kn[key] = val
            out.append((key, val))
        return out

    @staticmethod
    def _merge(d, s):
        for k, v in s.items():
            if d.get(k, 0) < v:
                d[k] = v

    def _deps(self, reads, writes):
        deps = {}
        for r in reads:
            self._merge(deps, r.w)
        for w in writes:
            self._merge(deps, w.w)
            self._merge(deps, w.r)
        return deps

    def op(self, eng, fn, reads=(), writes=()):
        deps = self._deps(reads, writes)
        waits = self._waits(eng, deps)
        key = "E_" + eng
        rec = Rec(waits, fn, None)
        self.prog[eng].append(rec)
        self.last[eng] = rec
        self.pending[eng] = True
        tok = {key: self.cnt[key] + 1}
        for r in reads:
            self._merge(r.r, tok)
        for w in writes:
            w.w = dict(tok)
            w.r = {}

    def dma(self, queue, fn, reads=(), writes=(), sem=None):
        deps = self._deps(reads, writes)
        waits = self._waits(queue, deps)
        if sem is None:
            if writes:
                w0 = writes[0]
                if w0.lsem is None:
                    w0.lsem = self.dma_sem()
                    self.live_res.append(w0)
                sem = w0.lsem
            else:
                r0 = reads[0]
                if r0.ssem is None:
                    r0.ssem = self.dma_sem()
                    self.live_res.append(r0)
                sem = r0.ssem
        self.cnt[sem] += 16
        tok = {sem: self.cnt[sem]}
        rec = Rec(waits, fn, (sem, 16))
        self.prog[queue].append(rec)
        if queue in COMPUTE:
            pass
        for r in reads:
            self._merge(r.r, tok)
        for w in writes:
            w.w = dict(tok)
            w.r = {}

    def barrier(self, recycle=True, final=False):
        for e in COMPUTE:
            self._force("E_" + e)
        allk = {k: v for k, v in self.cnt.items() if v > 0 and (final or k not in self.nobarrier)}
        for eng in self.prog:
            waits = self._waits_all(eng, allk)
            if waits:
                self.prog[eng].append(Rec(waits, None, None))
        if recycle:
            self.free_dma_sems.extend(self.live_dma_sems)
            self.live_dma_sems = []
            for r_ in self.live_res:
                r_.lsem = None
                r_.ssem = None
            self.live_res = []

    def _waits_all(self, eng, allk):
        out = []
        kn = self.known[eng]
        for key, val in allk.items():
            if key == "E_" + eng:
                continue
            if kn.get(key, 0) >= val:
                continue
            kn[key] = val
            out.append((key, val))
        return out

    def replay(self, eng, e):
        for rec in self.prog[eng]:
            for key, val in rec.waits:
                e.wait_ge(self.sems[key], val)
            if rec.fn is None:
                continue
            name, a, k = rec.fn
            ins = getattr(e, name)(*a, **k)
            if rec.inc is not None:
                ins.then_inc(self.sems[rec.inc[0]], rec.inc[1])

    def run_block(self):
        nc = self.nc
        self.barrier(recycle=False, final=True)
        with nc.Block() as block:
            @block.tensor
            def _(e):
                self.replay("pe", e)

            @block.scalar
            def _(e):
                self.replay("act", e)

            @block.vector
            def _(e):
                self.replay("dve", e)

            @block.gpsimd
            def _(e):
                self.replay("pool", e)

            @block.sync
            def _(e):
                self.replay("sp", e)


class Arena:
    def __init__(self, t, nwords, base=0):
        self.t = t
        self.n = base + nwords
        self.off = base
        self.base = base

    def sub(self, nwords):
        assert self.off + nwords <= self.n, ("arena overflow(sub)", self.off, nwords, self.n)
        a = Arena(self.t, nwords, self.off)
        self.off += nwords
        return a

    def reset(self):
        self.off = 0

    def f32(self, n, parts=128):
        assert self.off + n <= self.n, ("arena overflow", self.off, n, self.n)
        ap = self.t[0:parts, self.off:self.off + n]
        self.off += n
        return ap

    def bf16(self, n, parts=128):
        w = (n + 1) // 2
        assert self.off + w <= self.n, ("arena overflow", self.off, w, self.n)
        ap = self.t[0:parts, self.off:self.off + w].bitcast(BF16)
        self.off += w
        return ap[:, 0:n]


D = 2048
DIN = 5632
DFF = 5632
WCOLS = DIN + 1024
EPS = 1e-6
GRID_W = 64
NEG = -30000.0
CAST_BARRIER = False
import os
NA_OLDPS = bool(int(os.environ.get('NA_OLDPS', '0')))


def host_consts(L):
    c = {}
    c["ident"] = np.eye(128, dtype=np.float32)
    pos = np.arange(L)
    row = (pos // GRID_W).astype(np.float32)
    col = (pos % GRID_W).astype(np.float32)
    nf = 32
    inv = (10000.0 ** (-np.arange(nf, dtype=np.float32) / nf)).astype(np.float32)
    f = np.arange(128)
    p = np.where((f // 64)[:, None] == 0, row[None, :], col[None, :]).astype(np.float32)
    ang = (p * inv[f % 32][:, None]).astype(np.float32)
    sign = np.where((f % 64) < 32, -1.0, 1.0).astype(np.float32)[:, None]
    C = np.cos(ang).astype(np.float32)
    Sg = (sign * np.sin(ang)).astype(np.float32)
    sc = np.float32(128 ** -0.5)
    c["rope"] = np.stack([C * sc, Sg * sc, C, Sg]).astype(np.float32)
    i = np.arange(128)
    jj, ii = np.meshgrid(i, i, indexing="ij")
    dec = np.stack([np.maximum(ii - jj, 0), (ii >= jj), np.maximum(jj - ii, 0), (jj >= ii)]).astype(np.float32)
    c["dec"] = dec
    xirow = np.stack([np.tile((i + 1)[None, :], (128, 1)), np.tile((128 - i)[None, :], (128, 1))]).astype(np.float32)
    c["xirow"] = xirow
    c["zcol"] = np.stack([127 - i, i], axis=1).astype(np.float32)
    R = L // GRID_W
    types = na_types(R)
    nam = np.zeros((5, 128, 576), np.float32)
    cols = np.arange(64)
    cs = np.clip(cols - 8, 0, 64 - 16)
    band = (cols[None, :] >= cs[:, None]) & (cols[None, :] < cs[:, None] + 16)
    for ti, (m, lo) in enumerate(types):
        for qr in range(2):
            r = 2 * m + qr
            w0 = int(np.clip(r - 4, 0, R - 8))
            for kidx in range(9):
                kr = lo + kidx
                ok = (w0 <= kr < w0 + 8)
                blk = np.where(band, 0.0, NEG) if ok else np.full((64, 64), NEG)
                nam[ti, qr * 64:(qr + 1) * 64, kidx * 64:(kidx + 1) * 64] = blk
    c["nam"] = nam
    return c


def na_types(R):
    M = R // 2
    return [(0, 0), (1, 0), (2, 0), (M - 2, R - 9), (M - 1, R - 9)]


def na_type_of(m, R):
    M = R // 2
    if m == 0:
        return 0, 0
    if m == 1:
        return 1, 0
    if m == M - 2:
        return 3, R - 9
    if m == M - 1:
        return 4, R - 9
    return 2, 2 * m - 4


def row_chunks(r0, r1):
    n = r1 - r0
    big = (n // 16) * 16
    out = []
    if big:
        out.append((r0, r0 + big))
    if n - big:
        out.append((r0 + big, r1))
    return out


def fm(v, nch):
    s = v.shape[:-1]
    return np.ascontiguousarray(np.moveaxis(v.reshape(s + (nch, 128)), -1, -2))


def host_layout(inp, b, L):
    f32 = np.float32
    o = {}
    o["x"] = np.ascontiguousarray(inp["x"][b], f32)
    o["ctx"] = np.ascontiguousarray(inp["ctx"][b], f32)
    cv = np.stack([inp["c"][b], inp["c_ctx"]], axis=1).astype(f32)
    o["cT"] = np.ascontiguousarray(cv.reshape(16, 128, 2).transpose(1, 0, 2))
    for k in ("w_ada", "b_ada", "w_in", "w_out", "ffn_up", "ffn_down", "conv_pw", "final_g", "na_rpb"):
        o[k] = np.ascontiguousarray(inp[k], f32)
    o["ret_decay"] = np.ascontiguousarray(inp["ret_decay"].reshape(-1, 8), f32)
    o["gng"] = np.ascontiguousarray(inp["ret_gn_g"], f32)
    o["n1g"] = fm(inp["norm1_g"].astype(f32), 16)
    o["n2g"] = fm(inp["norm2_g"].astype(f32), 16)
    o["cdw"] = np.ascontiguousarray(fm(inp["conv_dw_w"].astype(f32), 4).transpose(0, 2, 3, 1))
    o["cdb"] = fm(inp["conv_dw_b"].astype(f32), 4)
    o["lng"] = fm(inp["conv_ln_g"].astype(f32), 4)
    o["lnb"] = fm(inp["conv_ln_b"].astype(f32), 4)
    o["fdw"] = np.ascontiguousarray(fm(inp["ffn_dw_w"].astype(f32), 88).transpose(0, 2, 3, 1))
    o["fdb"] = fm(inp["ffn_dw_b"].astype(f32), 88)
    o.update(host_consts(L))
    return o


class K:
    pass


def build(L, CTX, NL=2, dbg=False, upto=99):
    T = L + CTX
    R = L // GRID_W
    nc = bass.Bass("TRN2", target_bir_lowering=False)
    g = K()

    def din(name, shape, dt=F32):
        return nc.dram_tensor(name, list(shape), dt, kind="ExternalInput").ap()

    def dscr(name, shape, dt):
        return nc.dram_tensor(name, list(shape), dt, kind="ExternalOutput" if dbg else "Internal").ap()

    x_in = din("x", [L, D]); ctx_in = din("ctx", [CTX, D]); cT = din("cT", [128, 16, 2])
    w_ada = din("w_ada", [NL, D, 6 * D]); b_ada = din("b_ada", [NL, 6 * D]); w_in = din("w_in", [NL, D, DIN])
    w_out = din("w_out", [NL, D, D]); ffn_up = din("ffn_up", [NL, D, 2 * DFF]); ffn_down = din("ffn_down", [NL, DFF, D])
    conv_pw = din("conv_pw", [NL, 512, 512]); final_g = din("final_g", [D]); na_rpb = din("na_rpb", [NL, 4, 15, 31])
    ret_decay = din("ret_decay", [NL, 8]); gng = din("gng", [NL, 1024])
    n1g = din("n1g", [NL, 128, 16]); n2g = din("n2g", [NL, 128, 16])
    cdw = din("cdw", [NL, 128, 4, 31]); cdb = din("cdb", [NL, 128, 4]); lng = din("lng", [NL, 128, 4]); lnb = din("lnb", [NL, 128, 4])
    fdw = din("fdw", [NL, 128, 88, 3]); fdb = din("fdb", [NL, 128, 88])
    ident_d = din("ident", [128, 128]); rope_d = din("rope", [4, 128, L]); dec_d = din("dec", [4, 128, 128])
    xirow_d = din("xirow", [2, 128, 128]); zcol_d = din("zcol", [128, 2]); nam_d = din("nam", [5, 128, 576])
    out_d = nc.dram_tensor("out", [L, D], F32, kind="ExternalOutput").ap()

    winb = dscr("winb", [NL, D, WCOLS], BF16); woutb = dscr("woutb", [NL, D, D], BF16)
    upb = dscr("upb", [NL, D, 2 * DFF], BF16); downb = dscr("downb", [NL, DFF, D], BF16)
    pwb = dscr("pwb", [NL, 512, 512], BF16); wadab = dscr("wadab", [NL, D, 6 * D], BF16)
    mods_d = dscr("mods", [2, 6 * D], F32)
    qT_d = dscr("qT", [512, T], BF16); kT_d = dscr("kT", [512, T], BF16); v_d = dscr("v", [T, 1024], BF16)
    sg_d = dscr("sg", [T, 1024], F32); uT_d = dscr("uT", [512, T], F32)
    nqT_d = dscr("nqT", [512, T], BF16); nkT_d = dscr("nkT", [512, T], BF16); nv_d = dscr("nv", [T, 512], BF16)
    of_d = dscr("of", [T, 1024], F32); mixT_d = dscr("mixT", [D, T], BF16)
    xa_d = dscr("xa", [T, D], F32); toep_d = dscr("toep", [4, 64, 17, 64], F32)

    S = Sched(nc)
    NARENA = 52900
    import contextlib
    es = contextlib.ExitStack()
    arena_t = es.enter_context(nc.sbuf_tensor("arena", [128, NARENA], F32))
    psum_t = es.enter_context(nc.psum_tensor("psum", [128, 4096], F32))
    A = Arena(arena_t, NARENA)
    PS = [psum_t[:, i * 512:(i + 1) * 512] for i in range(8)]
    RPS = [Res("ps%d" % i) for i in range(8)]

    ident = A.f32(128); Rid = Res("ident")
    identb = A.bf16(128)
    modT = A.f32(192); RmodT = Res("modT")
    G1T = A.f32(32); G2T = A.f32(32); RG = Res("G")
    ztile = A.f32(2176); Rzt = Res("zt")
    pers_mark = A.off

    S.dma("sp", lambda e: e.dma_start(out=ident, in_=ident_d), writes=[Rid])
    S.op("dve", lambda e: e.tensor_copy(out=identb, in_=ident), reads=[Rid], writes=[Rid])

    RC = {}

    def cast(dst, src, rows, step, key):
        if key not in RC:
            sem = S._mk("C_" + key)
            S.nobarrier.add(sem)
            RC[key] = (Res("cast_" + key), sem)
        rc, sem = RC[key]
        for r0 in range(0, rows, step):
            S.dma("pool", lambda e, r0=r0: e.dma_start(out=dst[r0:r0 + step], in_=src[r0:r0 + step]), sem=sem)
        rc.w = {sem: S.cnt[sem]}

    def issue_casts(l, part):
        if part == 0:
            cast(winb[l][:, 0:DIN], w_in[l], D, 256, "win%d" % l)
            cast(pwb[l], conv_pw[l], 512, 512, "pw%d" % l)
            return
        cast(woutb[l], w_out[l], D, 512, "wout%d" % l)
        cast(upb[l], ffn_up[l], D, 128, "up%d" % l)
        cast(downb[l], ffn_down[l], DFF, 512, "down%d" % l)

    def RCW(key):
        return RC[key][0]

    def phase_mods(l):
        A.off = pers_mark
        sT = A.f32(32); RsT = Res("sT")
        sTs = A.f32(32)
        bada2 = A.f32(6 * D, parts=2); Rb = Res("bada2")
        m = A.f32(6 * D, parts=2); Rm = Res("m")
        ng = A.f32(32); Rng = Res("ng")
        wt = [A.f32(16 * 512) for _ in range(2)]; Rwt = [Res("wt%d" % i) for i in range(2)]
        S.dma("sp", lambda e: e.dma_start(out=sT, in_=cT.rearrange("p k m -> p (k m)")), writes=[RsT])
        S.dma("sp", lambda e: e.dma_start(out=bada2, in_=b_ada[l:l + 1, :].broadcast_to([2, 6 * D])), writes=[Rb])
        S.dma("sp", lambda e: e.dma_start(out=ng[:, 0:16], in_=n1g[l]), writes=[Rng])
        S.dma("sp", lambda e: e.dma_start(out=ng[:, 16:32], in_=n2g[l]), writes=[Rng])
        S.op("act", lambda e: e.activation(out=sTs, in_=sT, func=AF.Silu), reads=[RsT], writes=[RsT])
        for nb in range(24):
            sl = nb % 2
            S.dma("sp", lambda e, nb=nb, sl=sl: e.dma_start(
                out=wt[sl].rearrange("p (k n) -> p k n", k=16),
                in_=w_ada[l][:, nb * 512:(nb + 1) * 512].rearrange("(k p) n -> p k n", p=128)), writes=[Rwt[sl]])
            for k in range(16):
                S.op("pe", lambda e, nb=nb, sl=sl, k=k: e.matmul(PS[sl][0:2, :], lhsT=sTs[:, 2 * k:2 * k + 2],
                                                               rhs=wt[sl][:, k * 512:(k + 1) * 512], start=(k == 0), stop=(k == 15)),
                     reads=[RsT, Rwt[sl]], writes=[RPS[sl]])
            S.op("dve", lambda e, nb=nb, sl=sl: e.tensor_tensor(out=m[:, nb * 512:(nb + 1) * 512], in0=PS[sl][0:2, :],
                                                                in1=bada2[:, nb * 512:(nb + 1) * 512], op=ALU.add),
                 reads=[RPS[sl], Rb], writes=[Rm])
        for s0 in (1, 4):
            S.op("dve", lambda e, s0=s0: e.tensor_scalar_add(out=m[:, s0 * D:(s0 + 1) * D], in0=m[:, s0 * D:(s0 + 1) * D], scalar1=1.0),
                 reads=[Rm], writes=[Rm])
        S.dma("pool", lambda e: e.dma_start(out=mods_d, in_=m), reads=[Rm])
        for j in range(96):
            S.op("pe", lambda e, j=j: e.transpose(PS[2][:, 2 * j:2 * j + 2], m[0:2, j * 128:(j + 1) * 128], ident[0:2, 0:2]),
                 reads=[Rm, Rid], writes=[RPS[2]])
        S.op("dve", lambda e: e.tensor_copy(out=modT, in_=PS[2][:, 0:192]), reads=[RPS[2]], writes=[RmodT])
        m3 = modT.rearrange("p (j m) -> p j m", m=2)
        for cond in range(2):
            S.op("dve", lambda e, cond=cond: e.tensor_tensor(out=G1T.rearrange("p (c m) -> p c m", m=2)[:, :, cond],
                                                             in0=ng[:, 0:16], in1=m3[:, 16:32, cond], op=ALU.mult),
                 reads=[RmodT, Rng], writes=[RG])
            S.op("dve", lambda e, cond=cond: e.tensor_tensor(out=G2T.rearrange("p (c m) -> p c m", m=2)[:, :, cond],
                                                             in0=ng[:, 16:32], in1=m3[:, 64:80, cond], op=ALU.mult),
                 reads=[RmodT, Rng], writes=[RG])
        S.barrier()

    def modcol(sec, c, cond):
        j = sec * 16 + c
        return modT[:, 2 * j + cond:2 * j + cond + 1]

    def phase_inproj(l, xlat, xctx):
        A.off = pers_mark
        rp = A.f32(4 * 512); Rrp = Res("rp")
        xs = [A.f32(D) for _ in range(4)]; Rxs = [Res("xs%d" % i) for i in range(4)]
        hT = A.bf16(16 * 512); RhT = Res("hT")
        junk = A.bf16(D); Rjunk = Res("junk")
        wb = [A.bf16(16 * 512) for _ in range(4)]; Rwb = [Res("wb%d" % i) for i in range(4)]
        stb = [A.bf16(512) for _ in range(3)]; Rstb = [Res("stb%d" % i) for i in range(3)]
        stf = [A.f32(512) for _ in range(3)]; Rstf = [Res("stf%d" % i) for i in range(3)]
        tmp = [A.f32(512) for _ in range(4)]; Rtmp = [Res("tmp%d" % i) for i in range(4)]
        ss = A.f32(8); Rss = Res("ss")
        cnt = {"w": 0, "sb": 0, "sf": 0, "tmp": 0, "ps": 0}

        def nxt(key, n):
            v = cnt[key]; cnt[key] = (v + 1) % n; return v

        def load_w(cb):
            sl = nxt("w", 4)
            S.dma("sp", lambda e: e.dma_start(out=wb[sl].rearrange("p (k n) -> p k n", k=16),
                                              in_=winb[l][:, cb * 512:(cb + 1) * 512].rearrange("(k p) n -> p k n", p=128)),
                  writes=[Rwb[sl]], reads=[RCW("win%d" % l)])
            return wb[sl], Rwb[sl]

        def psb():
            i = 2 + nxt("ps", 6)
            return PS[i], RPS[i]

        tiles = [(0, t0, min(512, L - t0)) for t0 in range(0, L, 512)] + [(1, t0, min(512, CTX - t0)) for t0 in range(0, CTX, 512)]
        for (isctx, t0, n) in tiles:
            cond = isctx
            nb = n // 128
            src = xctx if isctx else xlat
            tok0 = L + t0 if isctx else t0
            for tb in range(nb):
                S.dma("sp", lambda e, tb=tb: e.dma_start(out=xs[tb], in_=src[t0 + tb * 128:t0 + (tb + 1) * 128, :]), writes=[Rxs[tb]])
            if not isctx:
                S.dma("sp", lambda e: e.dma_start(out=rp.rearrange("p (a t) -> p a t", a=4)[:, :, 0:n],
                                                  in_=rope_d[:, :, t0:t0 + n].rearrange("a p t -> p a t")), writes=[Rrp])
            for tb in range(nb):
                S.op("act", lambda e, tb=tb: e.activation(out=junk, in_=xs[tb], func=AF.Square, accum_out=ss[:, tb:tb + 1]),
                     reads=[Rxs[tb]], writes=[Rjunk, Rss])
            S.op("act", lambda e: e.activation(out=ss[:, 4:4 + nb], in_=ss[:, 0:nb], func=AF.Sqrt, scale=1.0 / D, bias=EPS),
                 reads=[Rss], writes=[Rss])
            S.op("dve", lambda e: e.reciprocal(out=ss[:, 4:4 + nb], in_=ss[:, 4:4 + nb]), reads=[Rss], writes=[Rss])
            for tb in range(nb):
                S.op("dve", lambda e, tb=tb: e.tensor_scalar(out=xs[tb], in0=xs[tb], scalar1=ss[:, 4 + tb:5 + tb], scalar2=None, op0=ALU.mult),
                     reads=[Rxs[tb], Rss], writes=[Rxs[tb]])
            for c in range(16):
                bk = c % 2
                for tb in range(nb):
                    S.op("pe", lambda e, tb=tb, c=c, bk=bk: e.transpose(PS[bk][:, tb * 128:(tb + 1) * 128], xs[tb][:, c * 128:(c + 1) * 128], ident),
                         reads=[Rxs[tb], Rid], writes=[RPS[bk]])
                if c % 2 == 0:
                    S.op("dve", lambda e, c=c, bk=bk: e.tensor_scalar(out=hT[:, c * 512:c * 512 + n], in0=PS[bk][:, 0:n],
                                                                      scalar1=G1T[:, 2 * c + cond:2 * c + cond + 1], scalar2=modcol(0, c, cond),
                                                                      op0=ALU.mult, op1=ALU.add),
                         reads=[RPS[bk], RG, RmodT], writes=[RhT])
                else:
                    S.op("act", lambda e, c=c, bk=bk: e.activation(out=hT[:, c * 512:c * 512 + n], in_=PS[bk][:, 0:n], func=AF.Identity,
                                                                   scale=G1T[:, 2 * c + cond:2 * c + cond + 1], bias=modcol(0, c, cond)),
                         reads=[RPS[bk], RG, RmodT], writes=[RhT])

            def fm_chunk(wa, Rwa, j):
                ps, Rps = psb()
                for k in range(16):
                    S.op("pe", lambda e, k=k: e.matmul(ps[:, 0:n], lhsT=wa[:, k * 512 + j * 128:k * 512 + (j + 1) * 128],
                                                       rhs=hT[:, k * 512:k * 512 + n], start=(k == 0), stop=(k == 15)),
                         reads=[Rwa, RhT], writes=[Rps])
                return ps, Rps

            def tm_block(wa, Rwa, tb):
                ps, Rps = psb()
                for k in range(16):
                    S.op("pe", lambda e, k=k: e.matmul(ps[:, :], lhsT=hT[:, k * 512 + tb * 128:k * 512 + (tb + 1) * 128],
                                                       rhs=wa[:, k * 512:(k + 1) * 512], start=(k == 0), stop=(k == 15)),
                         reads=[Rwa, RhT], writes=[Rps])
                return ps, Rps

            def store_fm(dst, j, stage, Rst):
                S.dma("pool", lambda e: e.dma_start(out=dst[j * 128:(j + 1) * 128, tok0:tok0 + n], in_=stage[:, 0:n]), reads=[Rst])

            def store_tm(dst, tb, c0, stage, Rst):
                S.dma("pool", lambda e: e.dma_start(out=dst[tok0 + tb * 128:tok0 + (tb + 1) * 128, c0:c0 + 512], in_=stage[:, 0:512]), reads=[Rst])

            for qi, (cb, cbp, dst) in enumerate(((0, 11, qT_d), (1, 12, kT_d))):
                wa, Rwa = load_w(cb)
                if not isctx:
                    psl = nxt("w", 4)
                    wp, Rwp = wb[psl], Rwb[psl]
                    for bb in range(2):
                        S.op("act", lambda e: e.activation(out=wp.rearrange("p (a b e) -> p a b e", b=2, e=32)[:, :, 1 - bb, :],
                                                           in_=wa.rearrange("p (a b e) -> p a b e", b=2, e=32)[:, :, bb, :], func=AF.Copy),
                             reads=[Rwa], writes=[Rwp])
                for j in range(4):
                    pa, Rpa = fm_chunk(wa, Rwa, j)
                    sb = nxt("sb", 3)
                    if not isctx:
                        pb, Rpb = fm_chunk(wp, Rwp, j)
                        t1 = nxt("tmp", 4); t2 = nxt("tmp", 4)
                        S.op("dve", lambda e, t1=t1, pa=pa: e.tensor_tensor(out=tmp[t1][:, 0:n], in0=pa[:, 0:n], in1=rp[:, (2 * qi) * 512:(2 * qi) * 512 + n], op=ALU.mult),
                             reads=[Rpa, Rrp], writes=[Rtmp[t1]])
                        S.op("dve", lambda e, t2=t2, pb=pb: e.tensor_tensor(out=tmp[t2][:, 0:n], in0=pb[:, 0:n], in1=rp[:, (2 * qi + 1) * 512:(2 * qi + 1) * 512 + n], op=ALU.mult),
                             reads=[Rpb, Rrp], writes=[Rtmp[t2]])
                        S.op("pool", lambda e, t1=t1, t2=t2, sb=sb: e.tensor_tensor(out=stb[sb][:, 0:n], in0=tmp[t1][:, 0:n], in1=tmp[t2][:, 0:n], op=ALU.add),
                             reads=[Rtmp[t1], Rtmp[t2]], writes=[Rstb[sb]])
                    else:
                        S.op("act", lambda e, sb=sb, pa=pa: e.activation(out=stb[sb][:, 0:n], in_=pa[:, 0:n], func=AF.Copy,
                                                                         scale=(128 ** -0.5 if qi == 0 else 1.0)),
                             reads=[Rpa], writes=[Rstb[sb]])
                    store_fm(dst, j, stb[sb], Rstb[sb])
            for half in range(2):
                wa, Rwa = load_w(2 + half)
                for tb in range(nb):
                    ps, Rps = tm_block(wa, Rwa, tb)
                    sb = nxt("sb", 3)
                    S.op("act", lambda e, sb=sb, ps=ps: e.activation(out=stb[sb], in_=ps, func=AF.Copy), reads=[Rps], writes=[Rstb[sb]])
                    store_tm(v_d, tb, half * 512, stb[sb], Rstb[sb])
            for half in range(2):
                wa, Rwa = load_w(4 + half)
                for tb in range(nb):
                    ps, Rps = tm_block(wa, Rwa, tb)
                    sf = nxt("sf", 3)
                    S.op("act", lambda e, sf=sf, ps=ps: e.activation(out=stf[sf], in_=ps, func=AF.Silu), reads=[Rps], writes=[Rstf[sf]])
                    store_tm(sg_d, tb, half * 512, stf[sf], Rstf[sf])
            wa, Rwa = load_w(6)
            wp, Rwp = load_w(7)
            for j in range(4):
                pa, Rpa = fm_chunk(wa, Rwa, j)
                pb, Rpb = fm_chunk(wp, Rwp, j)
                t1 = nxt("tmp", 4); sf = nxt("sf", 3)
                S.op("act", lambda e, t1=t1, pb=pb: e.activation(out=tmp[t1][:, 0:n], in_=pb[:, 0:n], func=AF.Sigmoid), reads=[Rpb], writes=[Rtmp[t1]])
                S.op("dve", lambda e, t1=t1, pa=pa, sf=sf: e.tensor_tensor(out=stf[sf][:, 0:n], in0=pa[:, 0:n], in1=tmp[t1][:, 0:n], op=ALU.mult),
                     reads=[Rpa, Rtmp[t1]], writes=[Rstf[sf]])
                store_fm(uT_d, j, stf[sf], Rstf[sf])
            for cb, dst, scl in ((8, nqT_d, 128 ** -0.5), (9, nkT_d, 1.0)):
                wa, Rwa = load_w(cb)
                for j in range(4):
                    pa, Rpa = fm_chunk(wa, Rwa, j)
                    sb = nxt("sb", 3)
                    S.op("act", lambda e, sb=sb, pa=pa, scl=scl: e.activation(out=stb[sb][:, 0:n], in_=pa[:, 0:n], func=AF.Copy, scale=scl),
                         reads=[Rpa], writes=[Rstb[sb]])
                    store_fm(dst, j, stb[sb], Rstb[sb])
            wa, Rwa = load_w(10)
            for tb in range(nb):
                ps, Rps = tm_block(wa, Rwa, tb)
                sb = nxt("sb", 3)
                S.op("dve", lambda e, sb=sb, ps=ps: e.tensor_copy(out=stb[sb], in_=ps), reads=[Rps], writes=[Rstb[sb]])
                store_tm(nv_d, tb, 0, stb[sb], Rstb[sb])
        S.barrier()

    PSB = [p_.bitcast(BF16) for p_ in PS]

    def phase_ret(l, with_ctx, A):
        RB0 = Res("ret_b0"); RB1 = Res("ret_b1"); RP_m = Res("rp_m")
        rd = A.f32(8); lg = A.f32(8); Rlg = Res("lg")
        cdt = A.f32(4 * 128); xir = A.f32(2 * 128); zc = A.f32(2); Rc = Res("retconst")
        DT = A.f32(8 * 128); XI = A.f32(8 * 128); zg = A.f32(16); Rtab = Res("rettab")
        gngt = A.f32(1024); Rgn = Res("gngt")
        Sf = [A.f32(256) for _ in range(4)]; Sb = [A.bf16(256) for _ in range(4)]; RS = [Res("S%d" % h) for h in range(4)]
        RSb = [Res("Sb%d" % h) for h in range(4)]
        qc = [A.bf16(512) for _ in range(2)]; Rqc = [Res("qc%d" % i) for i in range(2)]
        kc = [A.bf16(512) for _ in range(2)]; Rkc = [Res("kc%d" % i) for i in range(2)]
        vc = [A.bf16(1024) for _ in range(2)]; Rvc = [Res("vc%d" % i) for i in range(2)]
        ofc = [A.f32(1024) for _ in range(2)]; Rofc = [Res("ofc%d" % i) for i in range(2)]
        sgc = [A.f32(1024) for _ in range(2)]; Rsgc = [Res("sgc%d" % i) for i in range(2)]
        ob = [A.f32(1024) for _ in range(2)]; Rob = [Res("ob%d" % i) for i in range(2)]
        innb = [A.bf16(128) for _ in range(2)]; Rinnb = [Res("innb%d" % i) for i in range(2)]
        qx = [A.bf16(128) for _ in range(2)]; Rqx = [Res("qx%d" % i) for i in range(2)]
        kz = [A.bf16(128) for _ in range(2)]; Rkz = [Res("kz%d" % i) for i in range(2)]
        ybf = A.bf16(1024); Rybf = Res("ybf")
        mst = [A.bf16(1024) for _ in range(2)]; Rmst = [Res("mst%d" % i) for i in range(2)]
        st = A.f32(16); Rst = Res("gnstat")
        junk = A.f32(256); Rjunk = Res("junk")

        S.dma("sp", lambda e: e.dma_start(out=rd, in_=ret_decay[l:l + 1, :].broadcast_to([128, 8])), writes=[Rlg])
        S.dma("sp", lambda e: e.dma_start(out=cdt.rearrange("p (a i) -> p a i", a=4), in_=dec_d.rearrange("a p i -> p a i")), writes=[Rc])
        S.dma("sp", lambda e: e.dma_start(out=xir.rearrange("p (a i) -> p a i", a=2), in_=xirow_d.rearrange("a p i -> p a i")), writes=[Rc])
        S.dma("sp", lambda e: e.dma_start(out=zc, in_=zcol_d), writes=[Rc])
        S.dma("sp", lambda e: e.dma_start(out=gngt, in_=gng[l:l + 1, :].broadcast_to([128, 1024])), writes=[Rgn])
        S.op("act", lambda e: e.activation(out=lg, in_=rd, func=AF.Exp, scale=-1.0), reads=[Rlg], writes=[Rlg])
        S.op("act", lambda e: e.activation(out=lg, in_=lg, func=AF.Ln, bias=1.0), reads=[Rlg], writes=[Rlg])
        S.op("dve", lambda e: e.tensor_scalar(out=lg, in0=lg, scalar1=-1.0, scalar2=None, op0=ALU.mult), reads=[Rlg], writes=[Rlg])
        for dr in range(2):
            for h in range(4):
                col = dr * 4 + h
                S.op("act", lambda e: e.activation(out=DT[:, col * 128:(col + 1) * 128], in_=cdt[:, (2 * dr) * 128:(2 * dr + 1) * 128],
                                                   func=AF.Exp, scale=lg[:, col:col + 1]), reads=[Rlg, Rc], writes=[Rtab])
                S.op("dve", lambda e: e.tensor_tensor(out=DT[:, col * 128:(col + 1) * 128], in0=DT[:, col * 128:(col + 1) * 128],
                                                      in1=cdt[:, (2 * dr + 1) * 128:(2 * dr + 2) * 128], op=ALU.mult), reads=[Rtab, Rc], writes=[Rtab])
                S.op("act", lambda e: e.activation(out=XI[:, col * 128:(col + 1) * 128], in_=xir[:, dr * 128:(dr + 1) * 128],
                                                   func=AF.Exp, scale=lg[:, col:col + 1]), reads=[Rlg, Rc], writes=[Rtab])
                S.op("act", lambda e: e.activation(out=zg[:, col:col + 1], in_=zc[:, dr:dr + 1], func=AF.Exp, scale=lg[:, col:col + 1]),
                     reads=[Rlg, Rc], writes=[Rtab])
                S.op("act", lambda e: e.activation(out=zg[:, 8 + col:9 + col], in_=lg[:, col:col + 1], func=AF.Exp, scale=128.0),
                     reads=[Rlg, Rc], writes=[Rtab])
        nlc = L // 128; ncc = CTX // 128
        step = [0]
        for dr in range(2):
            for h in range(4):
                S.op("dve", lambda e: e.memset(Sf[h], 0.0), writes=[RS[h]])
                S.op("pool", lambda e: e.memset(Sb[h], 0.0), writes=[RSb[h]])
            if dr == 0:
                order = [(1, L + i * 128) for i in range(ncc)] + [(0, i * 128) for i in range(nlc)]
            else:
                order = [(1, L + i * 128) for i in reversed(range(ncc))] + [(0, i * 128) for i in reversed(range(nlc))]
            for (isctx, tok0) in order:
                need_out = (not isctx) or with_ctx
                sl = step[0] % 2; step[0] += 1
                S.dma("sp", lambda e: e.dma_start(out=kc[sl].rearrange("p (h t) -> p h t", h=4),
                                                  in_=kT_d[:, tok0:tok0 + 128].rearrange("(h d) t -> d h t", d=128)), writes=[Rkc[sl]])
                S.dma("sp", lambda e: e.dma_start(out=vc[sl], in_=v_d[tok0:tok0 + 128, :]), writes=[Rvc[sl]])
                if need_out:
                    S.dma("sp", lambda e: e.dma_start(out=qc[sl].rearrange("p (h t) -> p h t", h=4),
                                                      in_=qT_d[:, tok0:tok0 + 128].rearrange("(h d) t -> d h t", d=128)), writes=[Rqc[sl]])
                    if dr == 1:
                        S.dma("sp", lambda e: e.dma_start(out=ofc[sl], in_=of_d[tok0:tok0 + 128, :]), writes=[Rofc[sl]])
                        S.dma("sp", lambda e: e.dma_start(out=sgc[sl], in_=sg_d[tok0:tok0 + 128, :]), writes=[Rsgc[sl]])
                for h in range(4):
                    col = dr * 4 + h
                    hs = h % 2
                    kh = kc[sl][:, h * 128:(h + 1) * 128]
                    vh = vc[sl][:, h * 256:(h + 1) * 256]
                    if need_out:
                        qh = qc[sl][:, h * 128:(h + 1) * 128]
                        S.op("pe", lambda e: e.matmul(PS[0][:, 0:128], lhsT=kh, rhs=qh, start=True, stop=True),
                             reads=[Rkc[sl], Rqc[sl]], writes=[RB0])
                        yield
                        S.op("dve", lambda e: e.tensor_tensor(out=innb[hs], in0=PS[0][:, 0:128], in1=DT[:, col * 128:(col + 1) * 128], op=ALU.mult),
                             reads=[RB0, Rtab], writes=[Rinnb[hs]])
                        S.op("pool", lambda e: e.tensor_tensor(out=qx[hs], in0=qh, in1=XI[:, col * 128:(col + 1) * 128], op=ALU.mult),
                             reads=[Rqc[sl], Rtab], writes=[Rqx[hs]])
                        yield
                        S.op("pe", lambda e: e.matmul(PS[1][:, hs * 256:(hs + 1) * 256], lhsT=innb[hs], rhs=vh, start=True, stop=False),
                             reads=[Rinnb[hs], Rvc[sl]], writes=[RB1])
                        S.op("pe", lambda e: e.matmul(PS[1][:, hs * 256:(hs + 1) * 256], lhsT=qx[hs], rhs=Sb[h], start=False, stop=True),
                             reads=[Rqx[hs], RSb[h]], writes=[RB1])
                    S.op("pe", lambda e: e.transpose(PSB[0][:, 256 + hs * 128:256 + (hs + 1) * 128], kh, identb), reads=[Rkc[sl], Rid], writes=[RB0])
                    yield
                    S.op("act", lambda e: e.activation(out=kz[hs], in_=PSB[0][:, 256 + hs * 128:256 + (hs + 1) * 128], func=AF.Identity, scale=zg[:, col:col + 1]),
                         reads=[RB0, Rtab], writes=[Rkz[hs]])
                    yield
                    S.op("pe", lambda e: e.matmul(PS[0][:, 256:512], lhsT=kz[hs], rhs=vh, start=True, stop=True),
                         reads=[Rkz[hs], Rvc[sl]], writes=[RB0])
                    yield
                    S.op("dve", lambda e: e.scalar_tensor_tensor(out=Sf[h], in0=Sf[h], scalar=zg[:, 8 + col:9 + col], in1=PS[0][:, 256:512],
                                                                 op0=ALU.mult, op1=ALU.add), reads=[RS[h], RB0, Rtab], writes=[RS[h]])
                    S.op("act", lambda e: e.activation(out=Sb[h], in_=Sf[h], func=AF.Copy), reads=[RS[h]], writes=[RSb[h]])
                    if need_out:
                        if dr == 0:
                            S.op("act", lambda e: e.activation(out=ob[sl][:, h * 256:(h + 1) * 256], in_=PS[1][:, hs * 256:(hs + 1) * 256], func=AF.Copy),
                                 reads=[RB1], writes=[Rob[sl]])
                        else:
                            S.op("dve", lambda e: e.tensor_tensor(out=ob[sl][:, h * 256:(h + 1) * 256], in0=PS[1][:, hs * 256:(hs + 1) * 256],
                                                                  in1=ofc[sl][:, h * 256:(h + 1) * 256], op=ALU.add),
                                 reads=[RB1, Rofc[sl]], writes=[Rob[sl]])
                    yield
                if not need_out:
                    yield
                    continue
                if dr == 0:
                    S.dma("pool", lambda e: e.dma_start(out=of_d[tok0:tok0 + 128, :], in_=ob[sl]), reads=[Rob[sl]])
                    yield
                    continue
                for h in range(4):
                    S.op("act", lambda e: e.activation(out=junk, in_=ob[sl][:, h * 256:(h + 1) * 256], func=AF.Identity, accum_out=st[:, h:h + 1]),
                         reads=[Rob[sl]], writes=[Rjunk, Rst])
                    S.op("act", lambda e: e.activation(out=junk, in_=ob[sl][:, h * 256:(h + 1) * 256], func=AF.Square, accum_out=st[:, 4 + h:5 + h]),
                         reads=[Rob[sl]], writes=[Rjunk, Rst])
                S.op("dve", lambda e: e.tensor_scalar(out=st[:, 0:8], in0=st[:, 0:8], scalar1=1.0 / 256, scalar2=None, op0=ALU.mult), reads=[Rst], writes=[Rst])
                S.op("dve", lambda e: e.tensor_tensor(out=st[:, 8:12], in0=st[:, 0:4], in1=st[:, 0:4], op=ALU.mult), reads=[Rst], writes=[Rst])
                S.op("dve", lambda e: e.tensor_tensor(out=st[:, 8:12], in0=st[:, 4:8], in1=st[:, 8:12], op=ALU.subtract), reads=[Rst], writes=[Rst])
                S.op("act", lambda e: e.activation(out=st[:, 8:12], in_=st[:, 8:12], func=AF.Sqrt, bias=EPS), reads=[Rst], writes=[Rst])
                S.op("dve", lambda e: e.reciprocal(out=st[:, 8:12], in_=st[:, 8:12]), reads=[Rst], writes=[Rst])
                for h in range(4):
                    S.op("dve", lambda e: e.tensor_scalar(out=ob[sl][:, h * 256:(h + 1) * 256], in0=ob[sl][:, h * 256:(h + 1) * 256],
                                                          scalar1=st[:, h:h + 1], scalar2=st[:, 8 + h:9 + h], op0=ALU.subtract, op1=ALU.mult),
                         reads=[Rob[sl], Rst], writes=[Rob[sl]])
                S.op("pool", lambda e: e.tensor_tensor(out=ob[sl], in0=ob[sl], in1=gngt, op=ALU.mult), reads=[Rob[sl], Rgn], writes=[Rob[sl]])
                S.op("dve", lambda e: e.tensor_tensor(out=ybf, in0=ob[sl], in1=sgc[sl], op=ALU.mult), reads=[Rob[sl], Rsgc[sl]], writes=[Rybf])
                for c in range(8):
                    S.op("pe", lambda e: e.transpose(PSB[3][:, c * 128:(c + 1) * 128], ybf[:, c * 128:(c + 1) * 128], identb),
                         reads=[Rybf, Rid], writes=[RP_m])
                S.op("act", lambda e: e.activation(out=mst[sl], in_=PSB[3][:, 0:1024], func=AF.Copy), reads=[RP_m], writes=[Rmst[sl]])
                S.dma("pool", lambda e: e.dma_start(out=mixT_d[0:1024, tok0:tok0 + 128].rearrange("(c p) t -> p c t", p=128),
                                                    in_=mst[sl].rearrange("p (c t) -> p c t", c=8)), reads=[Rmst[sl]])
                yield

    def phase_conv(l, with_ctx, A):
        RPC = Res('rp_conv')
        cw = A.f32(124); cb = A.f32(4); lg_ = A.f32(4); lb_ = A.f32(4); Rcp = Res("convp")
        ones = A.f32(128); Rones = Res("ones")
        pw = A.bf16(4 * 512); Rpw = Res("pw")
        ub = [A.f32(4 * 542) for _ in range(1)]; Rub = [Res("ub%d" % i) for i in range(1)]
        acc = [A.f32(512) for _ in range(4)]; Racc = [Res("acc%d" % i) for i in range(4)]
        sq = [A.f32(512) for _ in range(4)]; Rsq = [Res("sq%d" % i) for i in range(4)]
        rstd = A.f32(512); Rrstd = Res("rstd")
        zb = A.bf16(4 * 512); Rzb = Res("zb")
        stb = [A.bf16(512) for _ in range(2)]; Rstb = [Res("cstb%d" % i) for i in range(2)]
        S.dma("sp", lambda e: e.dma_start(out=cw, in_=cdw[l].rearrange("p c k -> p (c k)")), writes=[Rcp])
        S.dma("sp", lambda e: e.dma_start(out=cb, in_=cdb[l]), writes=[Rcp])
        S.dma("sp", lambda e: e.dma_start(out=lg_, in_=lng[l]), writes=[Rcp])
        S.dma("sp", lambda e: e.dma_start(out=lb_, in_=lnb[l]), writes=[Rcp])
        S.dma("sp", lambda e: e.dma_start(out=pw.rearrange("p (k n) -> p k n", k=4), in_=pwb[l].rearrange("(k p) n -> p k n", p=128)), writes=[Rpw], reads=[RCW("pw%d" % l)])
        S.op("pool", lambda e: e.memset(ones, 1.0 / 512), writes=[Rones])
        tiles = [(0, t0, min(512, L - t0)) for t0 in range(0, L, 512)]
        if with_ctx:
            tiles += [(1, t0, min(512, CTX - t0)) for t0 in range(0, CTX, 512)]
        for ti, (isctx, t0, n) in enumerate(tiles):
            Ls = CTX if isctx else L
            base = L if isctx else 0
            sl = 0
            u3 = ub[sl].rearrange("p (c t) -> p c t", c=4)
            lo = max(t0 - 15, 0); hi = min(t0 + n + 15, Ls)
            off = lo - (t0 - 15)
            if lo != t0 - 15 or hi != t0 + n + 15:
                S.op("pool", lambda e: e.memset(ub[sl], 0.0), writes=[Rub[sl]])
            S.dma("sp", lambda e: e.dma_start(out=u3[:, :, off:off + hi - lo], in_=uT_d[:, base + lo:base + hi].rearrange("(c p) t -> p c t", p=128)),
                  writes=[Rub[sl]])
            for c in range(4):
                S.op("dve", lambda e: e.tensor_scalar(out=acc[c][:, 0:n], in0=u3[:, c, 15:15 + n], scalar1=cw[:, c * 31 + 15:c * 31 + 16],
                                                      scalar2=cb[:, c:c + 1], op0=ALU.mult, op1=ALU.add), reads=[Rub[sl], Rcp], writes=[Racc[c]])
            for k in range(31):
                if k == 15:
                    continue
                for c in range(4):
                    S.op("dve", lambda e: e.scalar_tensor_tensor(out=acc[c][:, 0:n], in0=u3[:, c, k:k + n], scalar=cw[:, c * 31 + k:c * 31 + k + 1],
                                                                 in1=acc[c][:, 0:n], op0=ALU.mult, op1=ALU.add),
                         reads=[Rub[sl], Rcp, Racc[c]], writes=[Racc[c]])
                yield
            for c in range(4):
                S.op("pe", lambda e: e.matmul(PS[6][:, 0:n], lhsT=ones, rhs=acc[c][:, 0:n], start=(c == 0), stop=(c == 3)),
                     reads=[Rones, Racc[c]], writes=[RPC])
            for c in range(4):
                S.op("dve", lambda e: e.tensor_tensor(out=acc[c][:, 0:n], in0=acc[c][:, 0:n], in1=PS[6][:, 0:n], op=ALU.subtract),
                     reads=[RPC, Racc[c]], writes=[Racc[c]])
                S.op("act", lambda e: e.activation(out=sq[c][:, 0:n], in_=acc[c][:, 0:n], func=AF.Square), reads=[Racc[c]], writes=[Rsq[c]])
            for c in range(4):
                S.op("pe", lambda e: e.matmul(PS[6][:, 0:n], lhsT=ones, rhs=sq[c][:, 0:n], start=(c == 0), stop=(c == 3)),
                     reads=[Rones, Rsq[c]], writes=[RPC])
            yield
            S.op("act", lambda e: e.activation(out=rstd[:, 0:n], in_=PS[6][:, 0:n], func=AF.Sqrt, bias=EPS), reads=[RPC], writes=[Rrstd])
            S.op("dve", lambda e: e.reciprocal(out=rstd[:, 0:n], in_=rstd[:, 0:n]), reads=[Rrstd], writes=[Rrstd])
            for c in range(4):
                S.op("dve", lambda e: e.tensor_tensor(out=acc[c][:, 0:n], in0=acc[c][:, 0:n], in1=rstd[:, 0:n], op=ALU.mult),
                     reads=[Rrstd, Racc[c]], writes=[Racc[c]])
                S.op("act", lambda e: e.activation(out=zb[:, c * 512:c * 512 + n], in_=acc[c][:, 0:n], func=AF.Silu,
                                                   scale=lg_[:, c:c + 1], bias=lb_[:, c:c + 1]), reads=[Racc[c], Rcp], writes=[Rzb])
            for co in range(4):
                bk = 6
                for ci in range(4):
                    S.op("pe", lambda e: e.matmul(PS[bk][:, 0:n], lhsT=pw[:, ci * 512 + co * 128:ci * 512 + (co + 1) * 128],
                                                  rhs=zb[:, ci * 512:ci * 512 + n], start=(ci == 0), stop=(ci == 3)),
                         reads=[Rpw, Rzb], writes=[RPC])
                ss_ = co % 2
                S.op("act", lambda e: e.activation(out=stb[ss_][:, 0:n], in_=PS[bk][:, 0:n], func=AF.Copy), reads=[RPC], writes=[Rstb[ss_]])
                S.dma("pool", lambda e: e.dma_start(out=mixT_d[1024 + co * 128:1024 + (co + 1) * 128, base + t0:base + t0 + n], in_=stb[ss_][:, 0:n]),
                      reads=[Rstb[ss_]])
                yield

    def na_zero():
        S.op("pool", lambda e: e.memset(ztile, 0.0), writes=[Rzt])
        S.dma("sp", lambda e: e.dma_start(out=bass.AP(toep_d.tensor, 0, [[2176, 128], [1, 2176]]), in_=ztile), reads=[Rzt])

    def na_diag(l):
        dsem = S.dma_sem()
        for h in range(4):
            for ro in range(15):
                dst = bass.AP(toep_d.tensor, h * 69632 + (ro + 1) * 64 - 15, [[1089, 64], [1, 31]])
                S.dma("sp", lambda e: e.dma_start(out=dst, in_=na_rpb[l, h, ro:ro + 1, :].broadcast_to([64, 31])), sem=dsem)

    def phase_na(l, with_ctx, A):
        RP_sc = Res("rp_sc"); RP_pt = Res("rp_pt"); RP_no = Res("rp_no")
        TB = A.f32(20 * 576); RTB = Res("TB")
        nam = A.f32(5 * 576); Rnam = Res("nam")
        ckT = A.bf16(4 * CTX); Rck = Res("ckT")
        cv = A.bf16((CTX // 128) * 512); Rcv = Res("cv")
        qm = [A.bf16(512) for _ in range(2)]; Rqm = [Res("qm%d" % i) for i in range(2)]
        km = [A.bf16(4 * 576) for _ in range(2)]; Rkm = [Res("km%d" % i) for i in range(2)]
        vm = [A.bf16(5 * 512) for _ in range(2)]; Rvm = [Res("vm%d" % i) for i in range(2)]
        scs = [A.f32(832)] * 2; Rscs = [Res("scs")] * 2
        pb = [A.bf16(832) for _ in range(2)]; Rpb = [Res("pb%d" % i) for i in range(2)]
        pT = [A.bf16(7 * 128) for _ in range(2)]; RpT = [Res("pT%d" % i) for i in range(2)]
        nst = [A.bf16(512) for _ in range(2)]; Rnst = [Res("nst%d" % i) for i in range(2)]
        sm = A.f32(16); Rsm = Res("sm")
        types = na_types(R)
        for h in range(4):
            for ti, (mrep, lo) in enumerate(types):
                idx = h * 5 + ti
                for qr in range(2):
                    ro0 = lo - (2 * mrep + qr) + 7
                    src = bass.AP(toep_d.tensor, h * 69632 + (ro0 + 1) * 64, [[1088, 64], [1, 576]])
                    S.dma("sp", lambda e: e.dma_start(out=TB[qr * 64:(qr + 1) * 64, idx * 576:(idx + 1) * 576], in_=src), writes=[RTB])
        S.dma("sp", lambda e: e.dma_start(out=nam.rearrange("p (a k) -> p a k", a=5), in_=nam_d.rearrange("a p k -> p a k")), writes=[Rnam])
        for h in range(4):
            S.op("dve", lambda e: e.tensor_tensor(out=TB[:, h * 2880:(h + 1) * 2880], in0=TB[:, h * 2880:(h + 1) * 2880], in1=nam, op=ALU.add),
                 reads=[RTB, Rnam], writes=[RTB])
        S.dma("sp", lambda e: e.dma_start(out=ckT.rearrange("p (h t) -> p h t", h=4), in_=nkT_d[:, L:T].rearrange("(h d) t -> d h t", d=128)), writes=[Rck])
        S.dma("sp", lambda e: e.dma_start(out=cv.rearrange("p (c f) -> p c f", f=512), in_=nv_d[L:T, :].rearrange("(c p) f -> p c f", p=128)), writes=[Rcv])
        cnt = [0]

        def na_block(sl, tokdst, segs, vch):
            ntot = sum(s_[1] for s_ in segs)
            for h in range(4):
                i = cnt[0] % 2; cnt[0] += 1
                sc = psum_t[:, 4 * 512:4 * 512 + 1024]
                ops_ = PS[2][:, 0:128]
                o = 0
                for (rf, ncol, bf_) in segs:
                    c0 = 0
                    while c0 < ncol:
                        w_ = min(ncol - c0, 512 - (o % 512))
                        S.op("pe", lambda e: e.matmul(sc[:, o:o + w_], lhsT=qm[sl][:, h * 128:(h + 1) * 128], rhs=rf(h)[:, c0:c0 + w_], start=True, stop=True),
                             reads=[Rqm[sl], Rkm[sl], Rck], writes=[RP_sc])
                        o += w_; c0 += w_
                yield
                o = 0
                for (rf, ncol, bf_) in segs:
                    if bf_ is not None:
                        S.op("dve", lambda e: e.tensor_tensor(out=scs[i][:, o:o + ncol], in0=sc[:, o:o + ncol], in1=bf_(h), op=ALU.add),
                             reads=[RP_sc, RTB], writes=[Rscs[i]])
                    else:
                        S.op("act", lambda e: e.activation(out=scs[i][:, o:o + ncol], in_=sc[:, o:o + ncol], func=AF.Copy),
                             reads=[RP_sc], writes=[Rscs[i]])
                    o += ncol
                yield
                S.op("dve", lambda e: e.reduce_max(out=sm[:, i:i + 1], in_=scs[i][:, 0:ntot], axis=AX.X), reads=[Rscs[i]], writes=[Rsm])
                S.op("dve", lambda e: e.tensor_scalar(out=sm[:, i:i + 1], in0=sm[:, i:i + 1], scalar1=-1.0, scalar2=None, op0=ALU.mult), reads=[Rsm], writes=[Rsm])
                yield
                S.op("act", lambda e: e.activation(out=scs[i][:, 0:ntot], in_=scs[i][:, 0:ntot], func=AF.Exp, bias=sm[:, i:i + 1],
                                                   accum_out=sm[:, 4 + i:5 + i]), reads=[Rscs[i], Rsm], writes=[Rscs[i], Rsm])
                yield
                S.op("dve", lambda e: e.reciprocal(out=sm[:, 4 + i:5 + i], in_=sm[:, 4 + i:5 + i]), reads=[Rsm], writes=[Rsm])
                S.op("dve", lambda e: e.tensor_scalar(out=pb[i][:, 0:ntot], in0=scs[i][:, 0:ntot], scalar1=sm[:, 4 + i:5 + i], scalar2=None, op0=ALU.mult),
                     reads=[Rscs[i], Rsm], writes=[Rpb[i]])
                yield
                bt = 7
                o = 0
                for ci, (vf, nk) in enumerate(vch):
                    S.op("pe", lambda e: e.transpose(PSB[bt][0:nk, ci * 128:(ci + 1) * 128], pb[i][:, o:o + nk], identb),
                         reads=[Rpb[i], Rid], writes=[RP_pt])
                    o += nk
                nch = len(vch)
                yield
                S.op("act", lambda e: e.activation(out=pT[i][:, 0:nch * 128], in_=PSB[bt][:, 0:nch * 128], func=AF.Copy), reads=[RP_pt], writes=[RpT[i]])
                yield
                for ci, (vf, nk) in enumerate(vch):
                    S.op("pe", lambda e: e.matmul(ops_, lhsT=vf(h)[0:nk, :], rhs=pT[i][0:nk, ci * 128:(ci + 1) * 128],
                                                  start=(ci == 0), stop=(ci == nch - 1)), reads=[RpT[i], Rvm[sl], Rcv], writes=[RP_no])
                S.op("dve", lambda e: e.tensor_copy(out=nst[sl][:, h * 128:(h + 1) * 128], in_=ops_), reads=[RP_no], writes=[Rnst[sl]])
                yield
            S.dma("pool", lambda e: e.dma_start(out=mixT_d[1536:2048, tokdst:tokdst + 128].rearrange("(h d) t -> d h t", d=128),
                                                in_=nst[sl].rearrange("p (h t) -> p h t", h=4)), reads=[Rnst[sl]])

        ctx_v = [((lambda h, c=c: cv[:, c * 512 + h * 128:c * 512 + (h + 1) * 128]), 128) for c in range(CTX // 128)]
        ctx_seg = ((lambda h: ckT[:, h * CTX:(h + 1) * CTX]), CTX, None)
        for m in range(R // 2):
            ti, lo = na_type_of(m, R)
            sl = m % 2
            S.dma("sp", lambda e: e.dma_start(out=qm[sl].rearrange("p (h t) -> p h t", h=4),
                                              in_=nqT_d[:, m * 128:(m + 1) * 128].rearrange("(h d) t -> d h t", d=128)), writes=[Rqm[sl]])
            S.dma("sp", lambda e: e.dma_start(out=km[sl].rearrange("p (h t) -> p h t", h=4),
                                              in_=nkT_d[:, lo * 64:lo * 64 + 576].rearrange("(h d) t -> d h t", d=128)), writes=[Rkm[sl]])
            S.dma("sp", lambda e: e.dma_start(out=vm[sl][:, 0:2048].rearrange("p (c f) -> p c f", f=512),
                                              in_=nv_d[lo * 64:lo * 64 + 512, :].rearrange("(c p) f -> p c f", p=128)), writes=[Rvm[sl]])
            S.dma("sp", lambda e: e.dma_start(out=vm[sl][0:64, 2048:2560], in_=nv_d[lo * 64 + 512:lo * 64 + 576, :]), writes=[Rvm[sl]])
            kseg = ((lambda h, sl=sl: km[sl][:, h * 576:(h + 1) * 576]), 576,
                    (lambda h, ti=ti: TB[:, (h * 5 + ti) * 576:(h * 5 + ti + 1) * 576]))
            vch = [((lambda h, c=c, sl=sl: vm[sl][:, c * 512 + h * 128:c * 512 + (h + 1) * 128]), 128) for c in range(4)]
            vch += [((lambda h, sl=sl: vm[sl][:, 2048 + h * 128:2048 + (h + 1) * 128]), 64)]
            yield from na_block(sl, m * 128, [kseg, ctx_seg], vch + ctx_v)
        if with_ctx:
            for qb in range(CTX // 128):
                sl = qb % 2
                S.dma("sp", lambda e: e.dma_start(out=qm[sl].rearrange("p (h t) -> p h t", h=4),
                                                  in_=nqT_d[:, L + qb * 128:L + (qb + 1) * 128].rearrange("(h d) t -> d h t", d=128)), writes=[Rqm[sl]])
                yield from na_block(sl, L + qb * 128, [ctx_seg], ctx_v)

    def phase_ffn(l, with_ctx, xlat, xctx, last):
        A.off = pers_mark
        fw = A.f32(264); fb = A.f32(88); Rfp = Res("ffnp")
        xs = [A.f32(D) for _ in range(4)]; Rxs = [Res("fxs%d" % i) for i in range(4)]
        xn = [A.f32(D) for _ in range(2)]; Rxn = [Res("fxn%d" % i) for i in range(2)]
        hT = A.bf16(16 * 512); RhT = Res("fhT")
        aT = A.bf16(22 * 512); RaT = Res("aT")
        mt = aT[:, 0:16 * 512]; RmT = RaT
        upw = [A.bf16(16 * 256) for _ in range(4)]; Rupw = [Res("upw%d" % i) for i in range(4)]
        wdn = [A.bf16(16 * 512) for _ in range(3)]; Rwdn = [Res("wdn%d" % i) for i in range(3)]
        gt = [A.f32(D) for _ in range(2)]; Rgt = [Res("gt%d" % i) for i in range(2)]
        yv = [A.f32(512) for _ in range(2)]; Ryv = [Res("yv%d" % i) for i in range(2)]
        yg = [A.f32(512) for _ in range(2)]; Ryg = [Res("yg%d" % i) for i in range(2)]
        tmp = [A.f32(512) for _ in range(2)]; Rtmp = [Res("ftmp%d" % i) for i in range(2)]
        ss = A.f32(16); Rss = Res("fss")
        S.dma("sp", lambda e: e.dma_start(out=fw, in_=fdw[l].rearrange("p c k -> p (c k)")), writes=[Rfp])
        S.dma("sp", lambda e: e.dma_start(out=fb, in_=fdb[l]), writes=[Rfp])
        cnt = {"up": 0, "dn": 0, "tmp": 0, "xn": 0, "y": 0}

        def nxt(key, n_):
            v = cnt[key]; cnt[key] = (v + 1) % n_; return v

        tiles = [(0, t0, min(510, L - t0)) for t0 in range(0, L, 510)]
        if with_ctx:
            tiles += [(1, t0, min(510, CTX - t0)) for t0 in range(0, CTX, 510)]
        for (isctx, t0, ni) in tiles:
            cond = isctx
            Ls = CTX if isctx else L
            base = L if isctx else 0
            xsrc = xctx if isctx else xlat
            n = ni + 2
            nb = (n + 127) // 128
            jlo = 1 if t0 == 0 else 0
            jhi = n - 1 if t0 + ni == Ls else n
            nts = [min(128, n - tb * 128) for tb in range(nb)]
            for tb in range(nb):
                r0 = max(tb * 128, jlo); r1 = min(tb * 128 + nts[tb], jhi)
                if r0 != tb * 128 or r1 != tb * 128 + nts[tb]:
                    S.op("pool", lambda e: e.memset(xs[tb], 0.0), writes=[Rxs[tb]])
                for (a_, b_) in row_chunks(r0, r1):
                    S.dma("sp", lambda e: e.dma_start(out=xs[tb][a_ - tb * 128:b_ - tb * 128, :], in_=xsrc[t0 - 1 + a_:t0 - 1 + b_, :]), writes=[Rxs[tb]])
            mt3 = mt.rearrange("p (c t) -> p c t", c=16)
            if jlo != 0 or jhi != n:
                S.op("pool", lambda e: e.memset(mt, 0.0), writes=[RmT])
            S.dma("sp", lambda e: e.dma_start(out=mt3[:, :, jlo:jhi], in_=mixT_d[:, base + t0 - 1 + jlo:base + t0 - 1 + jhi].rearrange("(c p) t -> p c t", p=128)),
                  writes=[RmT])
            S.dma("sp", lambda e: e.dma_start(out=gt[0], in_=mods_d[cond:cond + 1, 2 * D:3 * D].broadcast_to([128, D])), writes=[Rgt[0]])
            S.dma("sp", lambda e: e.dma_start(out=gt[1], in_=mods_d[cond:cond + 1, 5 * D:6 * D].broadcast_to([128, D])), writes=[Rgt[1]])
            for nbk in range(4):
                sl = nxt("dn", 3)
                S.dma("sp", lambda e: e.dma_start(out=wdn[sl].rearrange("p (k n) -> p k n", k=16),
                                                  in_=woutb[l][:, nbk * 512:(nbk + 1) * 512].rearrange("(k p) n -> p k n", p=128)), writes=[Rwdn[sl]], reads=[RCW("wout%d" % l)])
                for tb in range(nb):
                    nt = nts[tb]
                    bk = 4 + tb
                    for k in range(16):
                        S.op("pe", lambda e: e.matmul(PS[bk][0:nt, :], lhsT=mt[:, k * 512 + tb * 128:k * 512 + tb * 128 + nt],
                                                      rhs=wdn[sl][:, k * 512:(k + 1) * 512], start=(k == 0), stop=(k == 15)),
                             reads=[RmT, Rwdn[sl]], writes=[RPS[bk]])
                    ti_ = nxt("tmp", 2)
                    S.op("dve", lambda e: e.tensor_tensor(out=tmp[ti_][0:nt, :], in0=PS[bk][0:nt, :], in1=gt[0][0:nt, nbk * 512:(nbk + 1) * 512], op=ALU.mult),
                         reads=[RPS[bk], Rgt[0]], writes=[Rtmp[ti_]])
                    S.op("pool", lambda e: e.tensor_tensor(out=xs[tb][0:nt, nbk * 512:(nbk + 1) * 512], in0=xs[tb][0:nt, nbk * 512:(nbk + 1) * 512],
                                                           in1=tmp[ti_][0:nt, :], op=ALU.add), reads=[Rtmp[ti_], Rxs[tb]], writes=[Rxs[tb]])
            if last:
                S.dma("sp", lambda e: e.dma_start(out=gt[0], in_=final_g.rearrange("(o d) -> o d", o=1).broadcast_to([128, D])), writes=[Rgt[0]])
            for tb in range(nb):
                nt = nts[tb]
                xi_ = nxt("xn", 2)
                S.op("act", lambda e: e.activation(out=xn[xi_][0:nt, :], in_=xs[tb][0:nt, :], func=AF.Square, accum_out=ss[0:nt, tb:tb + 1]),
                     reads=[Rxs[tb]], writes=[Rxn[xi_], Rss])
                S.op("act", lambda e: e.activation(out=ss[0:nt, 4 + tb:5 + tb], in_=ss[0:nt, tb:tb + 1], func=AF.Sqrt, scale=1.0 / D, bias=EPS),
                     reads=[Rss], writes=[Rss])
                S.op("dve", lambda e: e.reciprocal(out=ss[0:nt, 4 + tb:5 + tb], in_=ss[0:nt, 4 + tb:5 + tb]), reads=[Rss], writes=[Rss])
                S.op("dve", lambda e: e.tensor_scalar(out=xn[xi_][0:nt, :], in0=xs[tb][0:nt, :], scalar1=ss[0:nt, 4 + tb:5 + tb], scalar2=None, op0=ALU.mult),
                     reads=[Rxs[tb], Rss], writes=[Rxn[xi_]])
                for c4 in range(4):
                    bk = c4 % 2
                    for cc in range(4):
                        c = c4 * 4 + cc
                        S.op("pe", lambda e: e.transpose(PS[bk][:, cc * 128:cc * 128 + nt], xn[xi_][0:nt, c * 128:(c + 1) * 128], ident[0:nt, 0:nt]),
                             reads=[Rxn[xi_], Rid], writes=[RPS[bk]])
                    for cc in range(4):
                        c = c4 * 4 + cc
                        dst = hT[:, c * 512 + tb * 128:c * 512 + tb * 128 + nt]
                        if cc % 2 == 0:
                            S.op("dve", lambda e: e.tensor_scalar(out=dst, in0=PS[bk][:, cc * 128:cc * 128 + nt], scalar1=G2T[:, 2 * c + cond:2 * c + cond + 1],
                                                                  scalar2=modcol(3, c, cond), op0=ALU.mult, op1=ALU.add),
                                 reads=[RPS[bk], RG, RmodT], writes=[RhT])
                        else:
                            S.op("act", lambda e: e.activation(out=dst, in_=PS[bk][:, cc * 128:cc * 128 + nt], func=AF.Identity,
                                                               scale=G2T[:, 2 * c + cond:2 * c + cond + 1], bias=modcol(3, c, cond)),
                                 reads=[RPS[bk], RG, RmodT], writes=[RhT])
            hT3 = hT.rearrange("p (c t) -> p c t", c=16)
            if jlo == 1:
                S.op("pool", lambda e: e.memset(hT3[:, :, 0:1], 0.0), reads=[RhT], writes=[RhT])
            if jhi == n - 1:
                S.op("pool", lambda e: e.memset(hT3[:, :, n - 1:n], 0.0), reads=[RhT], writes=[RhT])
            for hf in range(2):
                for jj in range(22):
                    c = hf * 22 + jj
                    sub = c % 2
                    if jj % 2 == 0:
                        uv = nxt("up", 4); ug = nxt("up", 4)
                        for (us, c0) in ((uv, (c // 2) * 256), (ug, DFF + (c // 2) * 256)):
                            S.dma("sp", lambda e: e.dma_start(out=upw[us].rearrange("p (k n) -> p k n", k=16),
                                                              in_=upb[l][:, c0:c0 + 256].rearrange("(k p) n -> p k n", p=128)), writes=[Rupw[us]], reads=[RCW("up%d" % l)])
                    pbk = (jj % 2) * 2
                    for (us, bk) in ((uv, pbk), (ug, pbk + 1)):
                        for k in range(16):
                            S.op("pe", lambda e: e.matmul(PS[bk][:, 0:n], lhsT=upw[us][:, k * 256 + sub * 128:k * 256 + (sub + 1) * 128],
                                                          rhs=hT[:, k * 512:k * 512 + n], start=(k == 0), stop=(k == 15)),
                                 reads=[Rupw[us], RhT], writes=[RPS[bk]])
                    yi = nxt("y", 2)
                    for (yy, Ryy, bk, ch) in ((yv[yi], Ryv[yi], pbk, c), (yg[yi], Ryg[yi], pbk + 1, 44 + c)):
                        S.op("act", lambda e: e.activation(out=yy[:, 0:n], in_=PS[bk][:, 0:n], func=AF.Identity, scale=fw[:, ch * 3 + 1:ch * 3 + 2],
                                                           bias=fb[:, ch:ch + 1]), reads=[RPS[bk], Rfp], writes=[Ryy])
                        S.op("dve", lambda e: e.scalar_tensor_tensor(out=yy[:, 1:n], in0=PS[bk][:, 0:n - 1], scalar=fw[:, ch * 3:ch * 3 + 1], in1=yy[:, 1:n],
                                                                     op0=ALU.mult, op1=ALU.add), reads=[RPS[bk], Rfp, Ryy], writes=[Ryy])
                        S.op("dve", lambda e: e.scalar_tensor_tensor(out=yy[:, 0:n - 1], in0=PS[bk][:, 1:n], scalar=fw[:, ch * 3 + 2:ch * 3 + 3], in1=yy[:, 0:n - 1],
                                                                     op0=ALU.mult, op1=ALU.add), reads=[RPS[bk], Rfp, Ryy], writes=[Ryy])
                    S.op("act", lambda e: e.activation(out=yg[yi][:, 0:n], in_=yg[yi][:, 0:n], func=AF.Silu), reads=[Ryg[yi]], writes=[Ryg[yi]])
                    S.op("pool", lambda e: e.tensor_tensor(out=aT[:, jj * 512:jj * 512 + n], in0=yv[yi][:, 0:n], in1=yg[yi][:, 0:n], op=ALU.mult),
                         reads=[Ryv[yi], Ryg[yi]], writes=[RaT])
                for nbk in range(4):
                    for part in range(2):
                        sl = nxt("dn", 3)
                        k0 = hf * 22 + part * 11
                        S.dma("sp", lambda e: e.dma_start(out=wdn[sl][:, 0:11 * 512].rearrange("p (k n) -> p k n", k=11),
                                                          in_=downb[l][k0 * 128:(k0 + 11) * 128, nbk * 512:(nbk + 1) * 512].rearrange("(k p) n -> p k n", p=128)),
                              writes=[Rwdn[sl]], reads=[RCW("down%d" % l)])
                        for tb in range(nb):
                            nt = nts[tb]
                            bk = 4 + tb
                            for kl in range(11):
                                kk = part * 11 + kl
                                S.op("pe", lambda e: e.matmul(PS[bk][0:nt, :], lhsT=aT[:, kk * 512 + tb * 128:kk * 512 + tb * 128 + nt],
                                                              rhs=wdn[sl][:, kl * 512:(kl + 1) * 512], start=(kk == 0), stop=(kk == 21)),
                                     reads=[RaT, Rwdn[sl]], writes=[RPS[bk]])
                    for tb in range(nb):
                        nt = nts[tb]
                        bk = 4 + tb
                        ti_ = nxt("tmp", 2)
                        S.op("dve", lambda e: e.tensor_tensor(out=tmp[ti_][0:nt, :], in0=PS[bk][0:nt, :], in1=gt[1][0:nt, nbk * 512:(nbk + 1) * 512], op=ALU.mult),
                             reads=[RPS[bk], Rgt[1]], writes=[Rtmp[ti_]])
                        S.op("pool", lambda e: e.tensor_tensor(out=xs[tb][0:nt, nbk * 512:(nbk + 1) * 512], in0=xs[tb][0:nt, nbk * 512:(nbk + 1) * 512],
                                                               in1=tmp[ti_][0:nt, :], op=ALU.add), reads=[Rtmp[ti_], Rxs[tb]], writes=[Rxs[tb]])
            for tb in range(nb):
                nt = nts[tb]
                r0 = max(tb * 128, 1); r1 = min(tb * 128 + nt, n - 1)
                if r1 <= r0:
                    continue
                if last:
                    xi_ = nxt("xn", 2)
                    S.op("act", lambda e: e.activation(out=xn[xi_][0:nt, :], in_=xs[tb][0:nt, :], func=AF.Square, accum_out=ss[0:nt, 8 + tb:9 + tb]),
                         reads=[Rxs[tb]], writes=[Rxn[xi_], Rss])
                    S.op("act", lambda e: e.activation(out=ss[0:nt, 12 + tb:13 + tb], in_=ss[0:nt, 8 + tb:9 + tb], func=AF.Sqrt, scale=1.0 / D, bias=EPS),
                         reads=[Rss], writes=[Rss])
                    S.op("dve", lambda e: e.reciprocal(out=ss[0:nt, 12 + tb:13 + tb], in_=ss[0:nt, 12 + tb:13 + tb]), reads=[Rss], writes=[Rss])
                    S.op("dve", lambda e: e.tensor_scalar(out=xn[xi_][0:nt, :], in0=xs[tb][0:nt, :], scalar1=ss[0:nt, 12 + tb:13 + tb], scalar2=None, op0=ALU.mult),
                         reads=[Rxs[tb], Rss], writes=[Rxn[xi_]])
                    S.op("pool", lambda e: e.tensor_tensor(out=xn[xi_][0:nt, :], in0=xn[xi_][0:nt, :], in1=gt[0][0:nt, :], op=ALU.mult),
                         reads=[Rxn[xi_], Rgt[0]], writes=[Rxn[xi_]])
                    for (a_, b_) in row_chunks(r0, r1):
                        S.dma("act", lambda e: e.dma_start(out=out_d[t0 - 1 + a_:t0 - 1 + b_, :], in_=xn[xi_][a_ - tb * 128:b_ - tb * 128, :]), reads=[Rxn[xi_]])
                else:
                    for (a_, b_) in row_chunks(r0, r1):
                        S.dma("act", lambda e: e.dma_start(out=xa_d[base + t0 - 1 + a_:base + t0 - 1 + b_, :], in_=xs[tb][a_ - tb * 128:b_ - tb * 128, :]),
                              reads=[Rxs[tb]])
        S.barrier()

    def run_group(items):
        active = [(iter(g_), w_) for g_, w_ in items]
        while active:
            for it in list(active):
                g_, w_ = it
                for _ in range(w_):
                    try:
                        next(g_)
                    except StopIteration:
                        active.remove(it)
                        break
        S.barrier()

    def run(stage="all", nlay=NL):
        for l in range(nlay):
            last = (l == NL - 1)
            xlat = x_in if l == 0 else xa_d[0:L]
            xctx = ctx_in if l == 0 else xa_d[L:T]
            if l == 0:
                issue_casts(0, 0)
            na_zero()
            phase_mods(l)
            if l == 0:
                issue_casts(0, 1)
                for l2 in range(1, NL):
                    issue_casts(l2, 0)
                    issue_casts(l2, 1)
            if stage == "mods" and l == nlay - 1:
                break
            na_diag(l)
            phase_inproj(l, xlat, xctx)
            if stage == "inproj" and l == nlay - 1:
                break
            A.off = pers_mark
            Ana = A.sub(A.n - A.off - 15900 - 9700); Aret = A.sub(15900); Aconv = A.sub(9700)
            import os
            gm = os.environ.get("GMODE", "par")
            if gm == "seq":
                for g_ in (phase_ret(l, not last, Aret), phase_conv(l, not last, Aconv), phase_na(l, not last, Ana)):
                    run_group([(g_, 1)])
            elif gm in ("ret", "conv", "na"):
                run_group([({"ret": phase_ret(l, not last, Aret), "conv": phase_conv(l, not last, Aconv), "na": phase_na(l, not last, Ana)}[gm], 1)])
            else:
                run_group([(phase_ret(l, not last, Aret), 6), (phase_conv(l, not last, Aconv), 1), (phase_na(l, not last, Ana), 4)])
            if stage == "na" and l == nlay - 1:
                break
            phase_ffn(l, not last, xlat, xctx, last)
        S.run_block()

    g.run = run
    g.nc = nc; g.S = S
    g.phase_mods = phase_mods; g.phase_inproj = phase_inproj
    g.names = dict(x_in=x_in, ctx_in=ctx_in, xa_d=xa_d, out_d=out_d)
    g.locals = locals()
    return g


def kernel(**inputs):
    inp = {k: np.asarray(v) for k, v in inputs.items()}
    B, L, _ = inp["x"].shape
    CTX = inp["ctx"].shape[1]
    g = build(L, CTX, NL=2, dbg=False)
    g.run("all", 2)
    in_maps = [host_layout(inp, b, L) for b in range(B)]
    res = run_bass_kernel_spmd(g.nc, in_maps, core_ids=list(range(B)))
    return np.stack([np.asarray(res.results[b]["out"], np.float32) for b in range(B)]).astype(np.float32)
```

```python
import numpy as np
import ml_dtypes
import concourse.bass as bass
import concourse.mybir as mybir
from concourse.bass_utils import run_bass_kernel_spmd

F32 = mybir.dt.float32
BF16 = mybir.dt.bfloat16
AF = mybir.ActivationFunctionType
ALU = mybir.AluOpType
AX = mybir.AxisListType

COMPUTE = ("pe", "act", "dve", "pool")
SAME_ENGINE_WAIT = True


class Res:
    __slots__ = ("name", "w", "r", "lsem", "ssem")

    def __init__(self, name):
        self.name = name
        self.w = {}
        self.r = {}
        self.lsem = None
        self.ssem = None


class _Cap:
    def __init__(self):
        self.call = None

    def __getattr__(self, name):
        def f(*a, **k):
            self.call = (name, a, k)
            return self
        return f


class Rec:
    __slots__ = ("waits", "fn", "inc")

    def __init__(self, waits, fn, inc):
        self.waits = waits
        if fn is not None:
            cap = _Cap()
            fn(cap)
            fn = cap.call
            assert fn is not None
        self.fn = fn
        self.inc = inc


class Sched:
    def __init__(self, nc):
        self.nc = nc
        self.prog = {e: [] for e in ("pe", "act", "dve", "pool", "sp")}
        self.sems = {}
        self.cnt = {}
        self.known = {e: {} for e in self.prog}
        self.last = {e: None for e in COMPUTE}
        self.pending = {e: False for e in COMPUTE}
        self.free_dma_sems = []
        self.live_dma_sems = []
        self.nsem = 0
        self.nobarrier = set()
        self.live_res = []
        for e in COMPUTE:
            self._mk("E_" + e)

    def _mk(self, key):
        self.sems[key] = self.nc.alloc_semaphore(key)
        self.cnt[key] = 0
        self.nsem += 1
        return key

    def dma_sem(self):
        if self.free_dma_sems:
            k = self.free_dma_sems.pop()
        else:
            k = self._mk("D%d" % self.nsem)
        self.live_dma_sems.append(k)
        return k

    def _force(self, key):
        if key.startswith("E_"):
            e = key[2:]
            if self.pending[e]:
                rec = self.last[e]
                assert rec.inc is None
                rec.inc = (key, 1)
                self.cnt[key] += 1
                self.pending[e] = False

    def _waits(self, eng, deps):
        out = []
        kn = self.known[eng]
        for key, val in deps.items():
            if key == "E_" + eng:
                if eng == "pe" or not SAME_ENGINE_WAIT:
                    continue
            if kn.get(key, 0) >= val:
                continue
            self._force(key)
            assert self.cnt[key] >= val, (key, self.cnt[key], val)
            kn[key] = val
            out.append((key, val))
        return out

    @staticmethod
    def _merge(d, s):
        for k, v in s.items():
            if d.get(k, 0) < v:
                d[k] = v

    def _deps(self, reads, writes):
        deps = {}
        for r in reads:
            self._merge(deps, r.w)
        for w in writes:
            self._merge(deps, w.w)
            self._merge(deps, w.r)
        return deps

    def op(self, eng, fn, reads=(), writes=()):
        deps = self._deps(reads, writes)
        waits = self._waits(eng, deps)
        key = "E_" + eng
        rec = Rec(waits, fn, None)
        self.prog[eng].append(rec)
        self.last[eng] = rec
        self.pending[eng] = True
        tok = {key: self.cnt[key] + 1}
        for r in reads:
            self._merge(r.r, tok)
        for w in writes:
            w.w = dict(tok)
            w.r = {}

    def dma(self, queue, fn, reads=(), writes=(), sem=None):
        deps = self._deps(reads, writes)
        waits = self._waits(queue, deps)
        if sem is None:
            if writes:
                w0 = writes[0]
                if w0.lsem is None:
                    w0.lsem = self.dma_sem()
                    self.live_res.append(w0)
                sem = w0.lsem
            else:
                r0 = reads[0]
                if r0.ssem is None:
                    r0.ssem = self.dma_sem()
                    self.live_res.append(r0)
                sem = r0.ssem
        self.cnt[sem] += 16
        tok = {sem: self.cnt[sem]}
        rec = Rec(waits, fn, (sem, 16))
        self.prog[queue].append(rec)
        if queue in COMPUTE:
            pass
        for r in reads:
            self._merge(r.r, tok)
        for w in writes:
            w.w = dict(tok)
            w.r = {}

    def barrier(self, recycle=True, final=False):
        for e in COMPUTE:
            self._force("E_" + e)
        allk = {k: v for k, v in self.cnt.items() if v > 0 and (final or k not in self.nobarrier)}
        for eng in self.prog:
            waits = self._waits_all(eng, allk)
            if waits:
                self.prog[eng].append(Rec(waits, None, None))
        if recycle:
            self.free_dma_sems.extend(self.live_dma_sems)
            self.live_dma_sems = []
            for r_ in self.live_res:
                r_.lsem = None
                r_.ssem = None
            self.live_res = []

    def _waits_all(self, eng, allk):
        out = []
        kn = self.known[eng]
        for key, val in allk.items():
            if key == "E_" + eng:
                continue
            if kn.get(key, 0) >= val:
                continue
            kn[key] = val
            out.append((key, val))
        return out

    def replay(self, eng, e):
        for rec in self.prog[eng]:
            for key, val in rec.waits:
                e.wait_ge(self.sems[key], val)
            if rec.fn is None:
                continue
            name, a, k = rec.fn
            ins = getattr(e, name)(*a, **k)
            if rec.inc is not None:
                ins.then_inc(self.sems[rec.inc[0]], rec.inc[1])

    def run_block(self):
        nc = self.nc
        self.barrier(recycle=False, final=True)
        with nc.Block() as block:
            @block.tensor
            def _(e):
                self.replay("pe", e)

            @block.scalar
            def _(e):
                self.replay("act", e)

            @block.vector
            def _(e):
                self.replay("dve", e)

            @block.gpsimd
            def _(e):
                self.replay("pool", e)

            @block.sync
            def _(e):
                self.replay("sp", e)


class Arena:
    def __init__(self, t, nwords, base=0):
        self.t = t
        self.n = base + nwords
        self.off = base
        self.base = base

    def sub(self, nwords):
        assert self.off + nwords <= self.n, ("arena overflow(sub)", self.off, nwords, self.n)
        a = Arena(self.t, nwords, self.off)
        self.off += nwords
        return a

    def reset(self):
        self.off = 0

    def f32(self, n, parts=128):
        assert self.off + n <= self.n, ("arena overflow", self.off, n, self.n)
        ap = self.t[0:parts, self.off:self.off + n]
        self.off += n
        return ap

    def bf16(self, n, parts=128):
        w = (n + 1) // 2
        assert self.off + w <= self.n, ("arena overflow", self.off, w, self.n)
        ap = self.t[0:parts, self.off:self.off + w].bitcast(BF16)
        self.off += w
        return ap[:, 0:n]


D = 2048
DIN = 5632
DFF = 5632
WCOLS = DIN + 1024
EPS = 1e-6
GRID_W = 64
NEG = -30000.0
CAST_BARRIER = False
import os
NA_OLDPS = bool(int(os.environ.get('NA_OLDPS', '0')))


def host_consts(L):
    c = {}
    c["ident"] = np.eye(128, dtype=np.float32)
    pos = np.arange(L)
    row = (pos // GRID_W).astype(np.float32)
    col = (pos % GRID_W).astype(np.float32)
    nf = 32
    inv = (10000.0 ** (-np.arange(nf, dtype=np.float32) / nf)).astype(np.float32)
    f = np.arange(128)
    p = np.where((f // 64)[:, None] == 0, row[None, :], col[None, :]).astype(np.float32)
    ang = (p * inv[f % 32][:, None]).astype(np.float32)
    sign = np.where((f % 64) < 32, -1.0, 1.0).astype(np.float32)[:, None]
    C = np.cos(ang).astype(np.float32)
    Sg = (sign * np.sin(ang)).astype(np.float32)
    sc = np.float32(128 ** -0.5)
    c["rope"] = np.stack([C * sc, Sg * sc, C, Sg]).astype(np.float32)
    i = np.arange(128)
    jj, ii = np.meshgrid(i, i, indexing="ij")
    dec = np.stack([np.maximum(ii - jj, 0), (ii >= jj), np.maximum(jj - ii, 0), (jj >= ii)]).astype(np.float32)
    c["dec"] = dec
    xirow = np.stack([np.tile((i + 1)[None, :], (128, 1)), np.tile((128 - i)[None, :], (128, 1))]).astype(np.float32)
    c["xirow"] = xirow
    c["zcol"] = np.stack([127 - i, i], axis=1).astype(np.float32)
    R = L // GRID_W
    types = na_types(R)
    nam = np.zeros((5, 128, 576), np.float32)
    cols = np.arange(64)
    cs = np.clip(cols - 8, 0, 64 - 16)
    band = (cols[None, :] >= cs[:, None]) & (cols[None, :] < cs[:, None] + 16)
    for ti, (m, lo) in enumerate(types):
        for qr in range(2):
            r = 2 * m + qr
            w0 = int(np.clip(r - 4, 0, R - 8))
            for kidx in range(9):
                kr = lo + kidx
                ok = (w0 <= kr < w0 + 8)
                blk = np.where(band, 0.0, NEG) if ok else np.full((64, 64), NEG)
                nam[ti, qr * 64:(qr + 1) * 64, kidx * 64:(kidx + 1) * 64] = blk
    c["nam"] = nam
    return c


def na_types(R):
    M = R // 2
    return [(0, 0), (1, 0), (2, 0), (M - 2, R - 9), (M - 1, R - 9)]


def na_type_of(m, R):
    M = R // 2
    if m == 0:
        return 0, 0
    if m == 1:
        return 1, 0
    if m == M - 2:
        return 3, R - 9
    if m == M - 1:
        return 4, R - 9
    return 2, 2 * m - 4


def row_chunks(r0, r1):
    n = r1 - r0
    big = (n // 16) * 16
    out = []
    if big:
        out.append((r0, r0 + big))
    if n - big:
        out.append((r0 + big, r1))
    return out


def fm(v, nch):
    s = v.shape[:-1]
    return np.ascontiguousarray(np.moveaxis(v.reshape(s + (nch, 128)), -1, -2))


def host_layout(inp, b, L):
    f32 = np.float32
    o = {}
    o["x"] = np.ascontiguousarray(inp["x"][b], f32)
    o["ctx"] = np.ascontiguousarray(inp["ctx"][b], f32)
    cv = np.stack([inp["c"][b], inp["c_ctx"]], axis=1).astype(f32)
    o["cT"] = np.ascontiguousarray(cv.reshape(16, 128, 2).transpose(1, 0, 2))
    for k in ("w_ada", "b_ada", "w_in", "w_out", "ffn_up", "ffn_down", "conv_pw", "final_g", "na_rpb"):
        o[k] = np.ascontiguousarray(inp[k], f32)
    o["ret_decay"] = np.ascontiguousarray(inp["ret_decay"].reshape(-1, 8), f32)
    o["gng"] = np.ascontiguousarray(inp["ret_gn_g"], f32)
    o["n1g"] = fm(inp["norm1_g"].astype(f32), 16)
    o["n2g"] = fm(inp["norm2_g"].astype(f32), 16)
    o["cdw"] = np.ascontiguousarray(fm(inp["conv_dw_w"].astype(f32), 4).transpose(0, 2, 3, 1))
    o["cdb"] = fm(inp["conv_dw_b"].astype(f32), 4)
    o["lng"] = fm(inp["conv_ln_g"].astype(f32), 4)
    o["lnb"] = fm(inp["conv_ln_b"].astype(f32), 4)
    o["fdw"] = np.ascontiguousarray(fm(inp["ffn_dw_w"].astype(f32), 88).transpose(0, 2, 3, 1))
    o["fdb"] = fm(inp["ffn_dw_b"].astype(f32), 88)
    o.update(host_consts(L))
    return o


class K:
    pass


def build(L, CTX, NL=2, dbg=False, upto=99):
    T = L + CTX
    R = L // GRID_W
    nc = bass.Bass("TRN2", target_bir_lowering=False)
    g = K()

    def din(name, shape, dt=F32):
        return nc.dram_tensor(name, list(shape), dt, kind="ExternalInput").ap()

    def dscr(name, shape, dt):
        return nc.dram_tensor(name, list(shape), dt, kind="ExternalOutput" if dbg else "Internal").ap()

    x_in = din("x", [L, D]); ctx_in = din("ctx", [CTX, D]); cT = din("cT", [128, 16, 2])
    w_ada = din("w_ada", [NL, D, 6 * D]); b_ada = din("b_ada", [NL, 6 * D]); w_in = din("w_in", [NL, D, DIN])
    w_out = din("w_out", [NL, D, D]); ffn_up = din("ffn_up", [NL, D, 2 * DFF]); ffn_down = din("ffn_down", [NL, DFF, D])
    conv_pw = din("conv_pw", [NL, 512, 512]); final_g = din("final_g", [D]); na_rpb = din("na_rpb", [NL, 4, 15, 31])
    ret_decay = din("ret_decay", [NL, 8]); gng = din("gng", [NL, 1024])
    n1g = din("n1g", [NL, 128, 16]); n2g = din("n2g", [NL, 128, 16])
    cdw = din("cdw", [NL, 128, 4, 31]); cdb = din("cdb", [NL, 128, 4]); lng = din("lng", [NL, 128, 4]); lnb = din("lnb", [NL, 128, 4])
    fdw = din("fdw", [NL, 128, 88, 3]); fdb = din("fdb", [NL, 128, 88])
    ident_d = din("ident", [128, 128]); rope_d = din("rope", [4, 128, L]); dec_d = din("dec", [4, 128, 128])
    xirow_d = din("xirow", [2, 128, 128]); zcol_d = din("zcol", [128, 2]); nam_d = din("nam", [5, 128, 576])
    out_d = nc.dram_tensor("out", [L, D], F32, kind="ExternalOutput").ap()

    winb = dscr("winb", [NL, D, WCOLS], BF16); woutb = dscr("woutb", [NL, D, D], BF16)
    upb = dscr("upb", [NL, D, 2 * DFF], BF16); downb = dscr("downb", [NL, DFF, D], BF16)
    pwb = dscr("pwb", [NL, 512, 512], BF16); wadab = dscr("wadab", [NL, D, 6 * D], BF16)
    mods_d = dscr("mods", [2, 6 * D], F32)
    qT_d = dscr("qT", [512, T], BF16); kT_d = dscr("kT", [512, T], BF16); v_d = dscr("v", [T, 1024], BF16)
    sg_d = dscr("sg", [T, 1024], F32); uT_d = dscr("uT", [512, T], F32)
    nqT_d = dscr("nqT", [512, T], BF16); nkT_d = dscr("nkT", [512, T], BF16); nv_d = dscr("nv", [T, 512], BF16)
    of_d = dscr("of", [T, 1024], F32); mixT_d = dscr("mixT", [D, T], BF16)
    xa_d = dscr("xa", [T, D], F32); toep_d = dscr("toep", [4, 64, 17, 64], F32)

    S = Sched(nc)
    NARENA = 52900
    import contextlib
    es = contextlib.ExitStack()
    arena_t = es.enter_context(nc.sbuf_tensor("arena", [128, NARENA], F32))
    psum_t = es.enter_context(nc.psum_tensor("psum", [128, 4096], F32))
    A = Arena(arena_t, NARENA)
    PS = [psum_t[:, i * 512:(i + 1) * 512] for i in range(8)]
    RPS = [Res("ps%d" % i) for i in range(8)]

    ident = A.f32(128); Rid = Res("ident")
    identb = A.bf16(128)
    modT = A.f32(192); RmodT = Res("modT")
    G1T = A.f32(32); G2T = A.f32(32); RG = Res("G")
    ztile = A.f32(2176); Rzt = Res("zt")
    pers_mark = A.off

    S.dma("sp", lambda e: e.dma_start(out=ident, in_=ident_d), writes=[Rid])
    S.op("dve", lambda e: e.tensor_copy(out=identb, in_=ident), reads=[Rid], writes=[Rid])

    RC = {}

    def cast(dst, src, rows, step, key):
        if key not in RC:
            sem = S._mk("C_" + key)
            S.nobarrier.add(sem)
            RC[key] = (Res("cast_" + key), sem)
        rc, sem = RC[key]
        for r0 in range(0, rows, step):
            S.dma("pool", lambda e, r0=r0: e.dma_start(out=dst[r0:r0 + step], in_=src[r0:r0 + step]), sem=sem)
        rc.w = {sem: S.cnt[sem]}

    def issue_casts(l, part):
        if part == 0:
            cast(winb[l][:, 0:DIN], w_in[l], D, 256, "win%d" % l)
            cast(pwb[l], conv_pw[l], 512, 512, "pw%d" % l)
            return
        cast(woutb[l], w_out[l], D, 512, "wout%d" % l)
        cast(upb[l], ffn_up[l], D, 128, "up%d" % l)
        cast(downb[l], ffn_down[l], DFF, 512, "down%d" % l)

    def RCW(key):
        return RC[key][0]

    def phase_mods(l):
        A.off = pers_mark
        sT = A.f32(32); RsT = Res("sT")
        sTs = A.f32(32)
        bada2 = A.f32(6 * D, parts=2); Rb = Res("bada2")
        m = A.f32(6 * D, parts=2); Rm = Res("m")
        ng = A.f32(32); Rng = Res("ng")
        wt = [A.f32(16 * 512) for _ in range(2)]; Rwt = [Res("wt%d" % i) for i in range(2)]
        S.dma("sp", lambda e: e.dma_start(out=sT, in_=cT.rearrange("p k m -> p (k m)")), writes=[RsT])
        S.dma("sp", lambda e: e.dma_start(out=bada2, in_=b_ada[l:l + 1, :].broadcast_to([2, 6 * D])), writes=[Rb])
        S.dma("sp", lambda e: e.dma_start(out=ng[:, 0:16], in_=n1g[l]), writes=[Rng])
        S.dma("sp", lambda e: e.dma_start(out=ng[:, 16:32], in_=n2g[l]), writes=[Rng])
        S.op("act", lambda e: e.activation(out=sTs, in_=sT, func=AF.Silu), reads=[RsT], writes=[RsT])
        for nb in range(24):
            sl = nb % 2
            S.dma("sp", lambda e, nb=nb, sl=sl: e.dma_start(
                out=wt[sl].rearrange("p (k n) -> p k n", k=16),
                in_=w_ada[l][:, nb * 512:(nb + 1) * 512].rearrange("(k p) n -> p k n", p=128)), writes=[Rwt[sl]])
            for k in range(16):
                S.op("pe", lambda e, nb=nb, sl=sl, k=k: e.matmul(PS[sl][0:2, :], lhsT=sTs[:, 2 * k:2 * k + 2],
                                                               rhs=wt[sl][:, k * 512:(k + 1) * 512], start=(k == 0), stop=(k == 15)),
                     reads=[RsT, Rwt[sl]], writes=[RPS[sl]])
            S.op("dve", lambda e, nb=nb, sl=sl: e.tensor_tensor(out=m[:, nb * 512:(nb + 1) * 512], in0=PS[sl][0:2, :],
                                                                in1=bada2[:, nb * 512:(nb + 1) * 512], op=ALU.add),
                 reads=[RPS[sl], Rb], writes=[Rm])
        for s0 in (1, 4):
            S.op("dve", lambda e, s0=s0: e.tensor_scalar_add(out=m[:, s0 * D:(s0 + 1) * D], in0=m[:, s0 * D:(s0 + 1) * D], scalar1=1.0),
                 reads=[Rm], writes=[Rm])
        S.dma("pool", lambda e: e.dma_start(out=mods_d, in_=m), reads=[Rm])
        for j in range(96):
            S.op("pe", lambda e, j=j: e.transpose(PS[2][:, 2 * j:2 * j + 2], m[0:2, j * 128:(j + 1) * 128], ident[0:2, 0:2]),
                 reads=[Rm, Rid], writes=[RPS[2]])
        S.op("dve", lambda e: e.tensor_copy(out=modT, in_=PS[2][:, 0:192]), reads=[RPS[2]], writes=[RmodT])
        m3 = modT.rearrange("p (j m) -> p j m", m=2)
        for cond in range(2):
            S.op("dve", lambda e, cond=cond: e.tensor_tensor(out=G1T.rearrange("p (c m) -> p c m", m=2)[:, :, cond],
                                                             in0=ng[:, 0:16], in1=m3[:, 16:32, cond], op=ALU.mult),
                 reads=[RmodT, Rng], writes=[RG])
            S.op("dve", lambda e, cond=cond: e.tensor_tensor(out=G2T.rearrange("p (c m) -> p c m", m=2)[:, :, cond],
                                                             in0=ng[:, 16:32], in1=m3[:, 64:80, cond], op=ALU.mult),
                 reads=[RmodT, Rng], writes=[RG])
        S.barrier()

    def modcol(sec, c, cond):
        j = sec * 16 + c
        return modT[:, 2 * j + cond:2 * j + cond + 1]

    def phase_inproj(l, xlat, xctx):
        A.off = pers_mark
        rp = A.f32(4 * 512); Rrp = Res("rp")
        xs = [A.f32(D) for _ in range(4)]; Rxs = [Res("xs%d" % i) for i in range(4)]
        hT = A.bf16(16 * 512); RhT = Res("hT")
        junk = A.bf16(D); Rjunk = Res("junk")
        wb = [A.bf16(16 * 512) for _ in range(4)]; Rwb = [Res("wb%d" % i) for i in range(4)]
        stb = [A.bf16(512) for _ in range(3)]; Rstb = [Res("stb%d" % i) for i in range(3)]
        stf = [A.f32(512) for _ in range(3)]; Rstf = [Res("stf%d" % i) for i in range(3)]
        tmp = [A.f32(512) for _ in range(4)]; Rtmp = [Res("tmp%d" % i) for i in range(4)]
        ss = A.f32(8); Rss = Res("ss")
        cnt = {"w": 0, "sb": 0, "sf": 0, "tmp": 0, "ps": 0}

        def nxt(key, n):
            v = cnt[key]; cnt[key] = (v + 1) % n; return v

        def load_w(cb):
            sl = nxt("w", 4)
            S.dma("sp", lambda e: e.dma_start(out=wb[sl].rearrange("p (k n) -> p k n", k=16),
                                              in_=winb[l][:, cb * 512:(cb + 1) * 512].rearrange("(k p) n -> p k n", p=128)),
                  writes=[Rwb[sl]], reads=[RCW("win%d" % l)])
            return wb[sl], Rwb[sl]

        def psb():
            i = 2 + nxt("ps", 6)
            return PS[i], RPS[i]

        tiles = [(0, t0, min(512, L - t0)) for t0 in range(0, L, 512)] + [(1, t0, min(512, CTX - t0)) for t0 in range(0, CTX, 512)]
        for (isctx, t0, n) in tiles:
            cond = isctx
            nb = n // 128
            src = xctx if isctx else xlat
            tok0 = L + t0 if isctx else t0
            for tb in range(nb):
                S.dma("sp", lambda e, tb=tb: e.dma_start(out=xs[tb], in_=src[t0 + tb * 128:t0 + (tb + 1) * 128, :]), writes=[Rxs[tb]])
            if not isctx:
                S.dma("sp", lambda e: e.dma_start(out=rp.rearrange("p (a t) -> p a t", a=4)[:, :, 0:n],
                                                  in_=rope_d[:, :, t0:t0 + n].rearrange("a p t -> p a t")), writes=[Rrp])
            for tb in range(nb):
                S.op("act", lambda e, tb=tb: e.activation(out=junk, in_=xs[tb], func=AF.Square, accum_out=ss[:, tb:tb + 1]),
                     reads=[Rxs[tb]], writes=[Rjunk, Rss])
            S.op("act", lambda e: e.activation(out=ss[:, 4:4 + nb], in_=ss[:, 0:nb], func=AF.Sqrt, scale=1.0 / D, bias=EPS),
                 reads=[Rss], writes=[Rss])
            S.op("dve", lambda e: e.reciprocal(out=ss[:, 4:4 + nb], in_=ss[:, 4:4 + nb]), reads=[Rss], writes=[Rss])
            for tb in range(nb):
                S.op("dve", lambda e, tb=tb: e.tensor_scalar(out=xs[tb], in0=xs[tb], scalar1=ss[:, 4 + tb:5 + tb], scalar2=None, op0=ALU.mult),
                     reads=[Rxs[tb], Rss], writes=[Rxs[tb]])
            for c in range(16):
                bk = c % 2
                for tb in range(nb):
                    S.op("pe", lambda e, tb=tb, c=c, bk=bk: e.transpose(PS[bk][:, tb * 128:(tb + 1) * 128], xs[tb][:, c * 128:(c + 1) * 128], ident),
                         reads=[Rxs[tb], Rid], writes=[RPS[bk]])
                if c % 2 == 0:
                    S.op("dve", lambda e, c=c, bk=bk: e.tensor_scalar(out=hT[:, c * 512:c * 512 + n], in0=PS[bk][:, 0:n],
                                                                      scalar1=G1T[:, 2 * c + cond:2 * c + cond + 1], scalar2=modcol(0, c, cond),
                                                                      op0=ALU.mult, op1=ALU.add),
                         reads=[RPS[bk], RG, RmodT], writes=[RhT])
                else:
                    S.op("act", lambda e, c=c, bk=bk: e.activation(out=hT[:, c * 512:c * 512 + n], in_=PS[bk][:, 0:n], func=AF.Identity,
                                                                   scale=G1T[:, 2 * c + cond:2 * c + cond + 1], bias=modcol(0, c, cond)),
                         reads=[RPS[bk], RG, RmodT], writes=[RhT])

            def fm_chunk(wa, Rwa, j):
                ps, Rps = psb()
                for k in range(16):
                    S.op("pe", lambda e, k=k: e.matmul(ps[:, 0:n], lhsT=wa[:, k * 512 + j * 128:k * 512 + (j + 1) * 128],
                                                       rhs=hT[:, k * 512:k * 512 + n], start=(k == 0), stop=(k == 15)),
                         reads=[Rwa, RhT], writes=[Rps])
                return ps, Rps

            def tm_block(wa, Rwa, tb):
                ps, Rps = psb()
                for k in range(16):
                    S.op("pe", lambda e, k=k: e.matmul(ps[:, :], lhsT=hT[:, k * 512 + tb * 128:k * 512 + (tb + 1) * 128],
                                                       rhs=wa[:, k * 512:(k + 1) * 512], start=(k == 0), stop=(k == 15)),
                         reads=[Rwa, RhT], writes=[Rps])
                return ps, Rps

            def store_fm(dst, j, stage, Rst):
                S.dma("pool", lambda e: e.dma_start(out=dst[j * 128:(j + 1) * 128, tok0:tok0 + n], in_=stage[:, 0:n]), reads=[Rst])

            def store_tm(dst, tb, c0, stage, Rst):
                S.dma("pool", lambda e: e.dma_start(out=dst[tok0 + tb * 128:tok0 + (tb + 1) * 128, c0:c0 + 512], in_=stage[:, 0:512]), reads=[Rst])

            for qi, (cb, cbp, dst) in enumerate(((0, 11, qT_d), (1, 12, kT_d))):
                wa, Rwa = load_w(cb)
                if not isctx:
                    psl = nxt("w", 4)
                    wp, Rwp = wb[psl], Rwb[psl]
                    for bb in range(2):
                        S.op("act", lambda e: e.activation(out=wp.rearrange("p (a b e) -> p a b e", b=2, e=32)[:, :, 1 - bb, :],
                                                           in_=wa.rearrange("p (a b e) -> p a b e", b=2, e=32)[:, :, bb, :], func=AF.Copy),
                             reads=[Rwa], writes=[Rwp])
                for j in range(4):
                    pa, Rpa = fm_chunk(wa, Rwa, j)
                    sb = nxt("sb", 3)
                    if not isctx:
                        pb, Rpb = fm_chunk(wp, Rwp, j)
                        t1 = nxt("tmp", 4); t2 = nxt("tmp", 4)
                        S.op("dve", lambda e, t1=t1, pa=pa: e.tensor_tensor(out=tmp[t1][:, 0:n], in0=pa[:, 0:n], in1=rp[:, (2 * qi) * 512:(2 * qi) * 512 + n], op=ALU.mult),
                             reads=[Rpa, Rrp], writes=[Rtmp[t1]])
                        S.op("dve", lambda e, t2=t2, pb=pb: e.tensor_tensor(out=tmp[t2][:, 0:n], in0=pb[:, 0:n], in1=rp[:, (2 * qi + 1) * 512:(2 * qi + 1) * 512 + n], op=ALU.mult),
                             reads=[Rpb, Rrp], writes=[Rtmp[t2]])
                        S.op("pool", lambda e, t1=t1, t2=t2, sb=sb: e.tensor_tensor(out=stb[sb][:, 0:n], in0=tmp[t1][:, 0:n], in1=tmp[t2][:, 0:n], op=ALU.add),
                             reads=[Rtmp[t1], Rtmp[t2]], writes=[Rstb[sb]])
                    else:
                        S.op("act", lambda e, sb=sb, pa=pa: e.activation(out=stb[sb][:, 0:n], in_=pa[:, 0:n], func=AF.Copy,
                                                                         scale=(128 ** -0.5 if qi == 0 else 1.0)),
                             reads=[Rpa], writes=[Rstb[sb]])
                    store_fm(dst, j, stb[sb], Rstb[sb])
            for half in range(2):
                wa, Rwa = load_w(2 + half)
                for tb in range(nb):
                    ps, Rps = tm_block(wa, Rwa, tb)
                    sb = nxt("sb", 3)
                    S.op("act", lambda e, sb=sb, ps=ps: e.activation(out=stb[sb], in_=ps, func=AF.Copy), reads=[Rps], writes=[Rstb[sb]])
                    store_tm(v_d, tb, half * 512, stb[sb], Rstb[sb])
            for half in range(2):
                wa, Rwa = load_w(4 + half)
                for tb in range(nb):
                    ps, Rps = tm_block(wa, Rwa, tb)
                    sf = nxt("sf", 3)
                    S.op("act", lambda e, sf=sf, ps=ps: e.activation(out=stf[sf], in_=ps, func=AF.Silu), reads=[Rps], writes=[Rstf[sf]])
                    store_tm(sg_d, tb, half * 512, stf[sf], Rstf[sf])
            wa, Rwa = load_w(6)
            wp, Rwp = load_w(7)
            for j in range(4):
                pa, Rpa = fm_chunk(wa, Rwa, j)
                pb, Rpb = fm_chunk(wp, Rwp, j)
                t1 = nxt("tmp", 4); sf = nxt("sf", 3)
                S.op("act", lambda e, t1=t1, pb=pb: e.activation(out=tmp[t1][:, 0:n], in_=pb[:, 0:n], func=AF.Sigmoid), reads=[Rpb], writes=[Rtmp[t1]])
                S.op("dve", lambda e, t1=t1, pa=pa, sf=sf: e.tensor_tensor(out=stf[sf][:, 0:n], in0=pa[:, 0:n], in1=tmp[t1][:, 0:n], op=ALU.mult),
                     reads=[Rpa, Rtmp[t1]], writes=[Rstf[sf]])
                store_fm(uT_d, j, stf[sf], Rstf[sf])
            for cb, dst, scl in ((8, nqT_d, 128 ** -0.5), (9, nkT_d, 1.0)):
                wa, Rwa = load_w(cb)
                for j in range(4):
                    pa, Rpa = fm_chunk(wa, Rwa, j)
                    sb = nxt("sb", 3)
                    S.op("act", lambda e, sb=sb, pa=pa, scl=scl: e.activation(out=stb[sb][:, 0:n], in_=pa[:, 0:n], func=AF.Copy, scale=scl),
                         reads=[Rpa], writes=[Rstb[sb]])
                    store_fm(dst, j, stb[sb], Rstb[sb])
            wa, Rwa = load_w(10)
            for tb in range(nb):
                ps, Rps = tm_block(wa, Rwa, tb)
                sb = nxt("sb", 3)
                S.op("dve", lambda e, sb=sb, ps=ps: e.tensor_copy(out=stb[sb], in_=ps), reads=[Rps], writes=[Rstb[sb]])
                store_tm(nv_d, tb, 0, stb[sb], Rstb[sb])
        S.barrier()

    PSB = [p_.bitcast(BF16) for p_ in PS]

    def phase_ret(l, with_ctx, A):
        RB0 = Res("ret_b0"); RB1 = Res("ret_b1"); RP_m = Res("rp_m")
        rd = A.f32(8); lg = A.f32(8); Rlg = Res("lg")
        cdt = A.f32(4 * 128); xir = A.f32(2 * 128); zc = A.f32(2); Rc = Res("retconst")
        DT = A.f32(8 * 128); XI = A.f32(8 * 128); zg = A.f32(16); Rtab = Res("rettab")
        gngt = A.f32(1024); Rgn = Res("gngt")
        Sf = [A.f32(256) for _ in range(4)]; Sb = [A.bf16(256) for _ in range(4)]; RS = [Res("S%d" % h) for h in range(4)]
        RSb = [Res("Sb%d" % h) for h in range(4)]
        qc = [A.bf16(512) for _ in range(2)]; Rqc = [Res("qc%d" % i) for i in range(2)]
        kc = [A.bf16(512) for _ in range(2)]; Rkc = [Res("kc%d" % i) for i in range(2)]
        vc = [A.bf16(1024) for _ in range(2)]; Rvc = [Res("vc%d" % i) for i in range(2)]
        ofc = [A.f32(1024) for _ in range(2)]; Rofc = [Res("ofc%d" % i) for i in range(2)]
        sgc = [A.f32(1024) for _ in range(2)]; Rsgc = [Res("sgc%d" % i) for i in range(2)]
        ob = [A.f32(1024) for _ in range(2)]; Rob = [Res("ob%d" % i) for i in range(2)]
        innb = [A.bf16(128) for _ in range(2)]; Rinnb = [Res("innb%d" % i) for i in range(2)]
        qx = [A.bf16(128) for _ in range(2)]; Rqx = [Res("qx%d" % i) for i in range(2)]
        kz = [A.bf16(128) for _ in range(2)]; Rkz = [Res("kz%d" % i) for i in range(2)]
        ybf = A.bf16(1024); Rybf = Res("ybf")
        mst = [A.bf16(1024) for _ in range(2)]; Rmst = [Res("mst%d" % i) for i in range(2)]
        st = A.f32(16); Rst = Res("gnstat")
        junk = A.f32(256); Rjunk = Res("junk")

        S.dma("sp", lambda e: e.dma_start(out=rd, in_=ret_decay[l:l + 1, :].broadcast_to([128, 8])), writes=[Rlg])
        S.dma("sp", lambda e: e.dma_start(out=cdt.rearrange("p (a i) -> p a i", a=4), in_=dec_d.rearrange("a p i -> p a i")), writes=[Rc])
        S.dma("sp", lambda e: e.dma_start(out=xir.rearrange("p (a i) -> p a i", a=2), in_=xirow_d.rearrange("a p i -> p a i")), writes=[Rc])
        S.dma("sp", lambda e: e.dma_start(out=zc, in_=zcol_d), writes=[Rc])
        S.dma("sp", lambda e: e.dma_start(out=gngt, in_=gng[l:l + 1, :].broadcast_to([128, 1024])), writes=[Rgn])
        S.op("act", lambda e: e.activation(out=lg, in_=rd, func=AF.Exp, scale=-1.0), reads=[Rlg], writes=[Rlg])
        S.op("act", lambda e: e.activation(out=lg, in_=lg, func=AF.Ln, bias=1.0), reads=[Rlg], writes=[Rlg])
        S.op("dve", lambda e: e.tensor_scalar(out=lg, in0=lg, scalar1=-1.0, scalar2=None, op0=ALU.mult), reads=[Rlg], writes=[Rlg])
        for dr in range(2):
            for h in range(4):
                col = dr * 4 + h
                S.op("act", lambda e: e.activation(out=DT[:, col * 128:(col + 1) * 128], in_=cdt[:, (2 * dr) * 128:(2 * dr + 1) * 128],
                                                   func=AF.Exp, scale=lg[:, col:col + 1]), reads=[Rlg, Rc], writes=[Rtab])
                S.op("dve", lambda e: e.tensor_tensor(out=DT[:, col * 128:(col + 1) * 128], in0=DT[:, col * 128:(col + 1) * 128],
                                                      in1=cdt[:, (2 * dr + 1) * 128:(2 * dr + 2) * 128], op=ALU.mult), reads=[Rtab, Rc], writes=[Rtab])
                S.op("act", lambda e: e.activation(out=XI[:, col * 128:(col + 1) * 128], in_=xir[:, dr * 128:(dr + 1) * 128],
                                                   func=AF.Exp, scale=lg[:, col:col + 1]), reads=[Rlg, Rc], writes=[Rtab])
                S.op("act", lambda e: e.activation(out=zg[:, col:col + 1], in_=zc[:, dr:dr + 1], func=AF.Exp, scale=lg[:, col:col + 1]),
                     reads=[Rlg, Rc], writes=[Rtab])
                S.op("act", lambda e: e.activation(out=zg[:, 8 + col:9 + col], in_=lg[:, col:col + 1], func=AF.Exp, scale=128.0),
                     reads=[Rlg, Rc], writes=[Rtab])
        nlc = L // 128; ncc = CTX // 128
        step = [0]
        for dr in range(2):
            for h in range(4):
                S.op("dve", lambda e: e.memset(Sf[h], 0.0), writes=[RS[h]])
                S.op("pool", lambda e: e.memset(Sb[h], 0.0), writes=[RSb[h]])
            if dr == 0:
                order = [(1, L + i * 128) for i in range(ncc)] + [(0, i * 128) for i in range(nlc)]
            else:
                order = [(1, L + i * 128) for i in reversed(range(ncc))] + [(0, i * 128) for i in reversed(range(nlc))]
            for (isctx, tok0) in order:
                need_out = (not isctx) or with_ctx
                sl = step[0] % 2; step[0] += 1
                S.dma("sp", lambda e: e.dma_start(out=kc[sl].rearrange("p (h t) -> p h t", h=4),
                                                  in_=kT_d[:, tok0:tok0 + 128].rearrange("(h d) t -> d h t", d=128)), writes=[Rkc[sl]])
                S.dma("sp", lambda e: e.dma_start(out=vc[sl], in_=v_d[tok0:tok0 + 128, :]), writes=[Rvc[sl]])
                if need_out:
                    S.dma("sp", lambda e: e.dma_start(out=qc[sl].rearrange("p (h t) -> p h t", h=4),
                                                      in_=qT_d[:, tok0:tok0 + 128].rearrange("(h d) t -> d h t", d=128)), writes=[Rqc[sl]])
                    if dr == 1:
                        S.dma("sp", lambda e: e.dma_start(out=ofc[sl], in_=of_d[tok0:tok0 + 128, :]), writes=[Rofc[sl]])
                        S.dma("sp", lambda e: e.dma_start(out=sgc[sl], in_=sg_d[tok0:tok0 + 128, :]), writes=[Rsgc[sl]])
                for h in range(4):
                    col = dr * 4 + h
                    hs = h % 2
                    kh = kc[sl][:, h * 128:(h + 1) * 128]
                    vh = vc[sl][:, h * 256:(h + 1) * 256]
                    if need_out:
                        qh = qc[sl][:, h * 128:(h + 1) * 128]
                        S.op("pe", lambda e: e.matmul(PS[0][:, 0:128], lhsT=kh, rhs=qh, start=True, stop=True),
                             reads=[Rkc[sl], Rqc[sl]], writes=[RB0])
                        yield
                        S.op("dve", lambda e: e.tensor_tensor(out=innb[hs], in0=PS[0][:, 0:128], in1=DT[:, col * 128:(col + 1) * 128], op=ALU.mult),
                             reads=[RB0, Rtab], writes=[Rinnb[hs]])
                        S.op("pool", lambda e: e.tensor_tensor(out=qx[hs], in0=qh, in1=XI[:, col * 128:(col + 1) * 128], op=ALU.mult),
                             reads=[Rqc[sl], Rtab], writes=[Rqx[hs]])
                        yield
                        S.op("pe", lambda e: e.matmul(PS[1][:, hs * 256:(hs + 1) * 256], lhsT=innb[hs], rhs=vh, start=True, stop=False),
                             reads=[Rinnb[hs], Rvc[sl]], writes=[RB1])
                        S.op("pe", lambda e: e.matmul(PS[1][:, hs * 256:(hs + 1) * 256], lhsT=qx[hs], rhs=Sb[h], start=False, stop=True),
                             reads=[Rqx[hs], RSb[h]], writes=[RB1])
                    S.op("pe", lambda e: e.transpose(PSB[0][:, 256 + hs * 128:256 + (hs + 1) * 128], kh, identb), reads=[Rkc[sl], Rid], writes=[RB0])
                    yield
                    S.op("act", lambda e: e.activation(out=kz[hs], in_=PSB[0][:, 256 + hs * 128:256 + (hs + 1) * 128], func=AF.Identity, scale=zg[:, col:col + 1]),
                         reads=[RB0, Rtab], writes=[Rkz[hs]])
                    yield
                    S.op("pe", lambda e: e.matmul(PS[0][:, 256:512], lhsT=kz[hs], rhs=vh, start=True, stop=True),
                         reads=[Rkz[hs], Rvc[sl]], writes=[RB0])
                    yield
                    S.op("dve", lambda e: e.scalar_tensor_tensor(out=Sf[h], in0=Sf[h], scalar=zg[:, 8 + col:9 + col], in1=PS[0][:, 256:512],
                                                                 op0=ALU.mult, op1=ALU.add), reads=[RS[h], RB0, Rtab], writes=[RS[h]])
                    S.op("act", lambda e: e.activation(out=Sb[h], in_=Sf[h], func=AF.Copy), reads=[RS[h]], writes=[RSb[h]])
                    if need_out:
                        if dr == 0:
                            S.op("act", lambda e: e.activation(out=ob[sl][:, h * 256:(h + 1) * 256], in_=PS[1][:, hs * 256:(hs + 1) * 256], func=AF.Copy),
                                 reads=[RB1], writes=[Rob[sl]])
                        else:
                            S.op("dve", lambda e: e.tensor_tensor(out=ob[sl][:, h * 256:(h + 1) * 256], in0=PS[1][:, hs * 256:(hs + 1) * 256],
                                                                  in1=ofc[sl][:, h * 256:(h + 1) * 256], op=ALU.add),
                                 reads=[RB1, Rofc[sl]], writes=[Rob[sl]])
                    yield
                if not need_out:
                    yield
                    continue
                if dr == 0:
                    S.dma("pool", lambda e: e.dma_start(out=of_d[tok0:tok0 + 128, :], in_=ob[sl]), reads=[Rob[sl]])
                    yield
                    continue
                for h in range(4):
                    S.op("act", lambda e: e.activation(out=junk, in_=ob[sl][:, h * 256:(h + 1) * 256], func=AF.Identity, accum_out=st[:, h:h + 1]),
                         reads=[Rob[sl]], writes=[Rjunk, Rst])
                    S.op("act", lambda e: e.activation(out=junk, in_=ob[sl][:, h * 256:(h + 1) * 256], func=AF.Square, accum_out=st[:, 4 + h:5 + h]),
                         reads=[Rob[sl]], writes=[Rjunk, Rst])
                S.op("dve", lambda e: e.tensor_scalar(out=st[:, 0:8], in0=st[:, 0:8], scalar1=1.0 / 256, scalar2=None, op0=ALU.mult), reads=[Rst], writes=[Rst])
                S.op("dve", lambda e: e.tensor_tensor(out=st[:, 8:12], in0=st[:, 0:4], in1=st[:, 0:4], op=ALU.mult), reads=[Rst], writes=[Rst])
                S.op("dve", lambda e: e.tensor_tensor(out=st[:, 8:12], in0=st[:, 4:8], in1=st[:, 8:12], op=ALU.subtract), reads=[Rst], writes=[Rst])
                S.op("act", lambda e: e.activation(out=st[:, 8:12], in_=st[:, 8:12], func=AF.Sqrt, bias=EPS), reads=[Rst], writes=[Rst])
                S.op("dve", lambda e: e.reciprocal(out=st[:, 8:12], in_=st[:, 8:12]), reads=[Rst], writes=[Rst])
                for h in range(4):
                    S.op("dve", lambda e: e.tensor_scalar(out=ob[sl][:, h * 256:(h + 1) * 256], in0=ob[sl][:, h * 256:(h + 1) * 256],
                                                          scalar1=st[:, h:h + 1], scalar2=st[:, 8 + h:9 + h], op0=ALU.subtract, op1=ALU.mult),
                         reads=[Rob[sl], Rst], writes=[Rob[sl]])
                S.op("pool", lambda e: e.tensor_tensor(out=ob[sl], in0=ob[sl], in1=gngt, op=ALU.mult), reads=[Rob[sl], Rgn], writes=[Rob[sl]])
                S.op("dve", lambda e: e.tensor_tensor(out=ybf, in0=ob[sl], in1=sgc[sl], op=ALU.mult), reads=[Rob[sl], Rsgc[sl]], writes=[Rybf])
                for c in range(8):
                    S.op("pe", lambda e: e.transpose(PSB[3][:, c * 128:(c + 1) * 128], ybf[:, c * 128:(c + 1) * 128], identb),
                         reads=[Rybf, Rid], writes=[RP_m])
                S.op("act", lambda e: e.activation(out=mst[sl], in_=PSB[3][:, 0:1024], func=AF.Copy), reads=[RP_m], writes=[Rmst[sl]])
                S.dma("pool", lambda e: e.dma_start(out=mixT_d[0:1024, tok0:tok0 + 128].rearrange("(c p) t -> p c t", p=128),
                                                    in_=mst[sl].rearrange("p (c t) -> p c t", c=8)), reads=[Rmst[sl]])
                yield

    def phase_conv(l, with_ctx, A):
        RPC = Res('rp_conv')
        cw = A.f32(124); cb = A.f32(4); lg_ = A.f32(4); lb_ = A.f32(4); Rcp = Res("convp")
        ones = A.f32(128); Rones = Res("ones")
        pw = A.bf16(4 * 512); Rpw = Res("pw")
        ub = [A.f32(4 * 542) for _ in range(1)]; Rub = [Res("ub%d" % i) for i in range(1)]
        acc = [A.f32(512) for _ in range(4)]; Racc = [Res("acc%d" % i) for i in range(4)]
        sq = [A.f32(512) for _ in range(4)]; Rsq = [Res("sq%d" % i) for i in range(4)]
        rstd = A.f32(512); Rrstd = Res("rstd")
        zb = A.bf16(4 * 512); Rzb = Res("zb")
        stb = [A.bf16(512) for _ in range(2)]; Rstb = [Res("cstb%d" % i) for i in range(2)]
        S.dma("sp", lambda e: e.dma_start(out=cw, in_=cdw[l].rearrange("p c k -> p (c k)")), writes=[Rcp])
        S.dma("sp", lambda e: e.dma_start(out=cb, in_=cdb[l]), writes=[Rcp])
        S.dma("sp", lambda e: e.dma_start(out=lg_, in_=lng[l]), writes=[Rcp])
        S.dma("sp", lambda e: e.dma_start(out=lb_, in_=lnb[l]), writes=[Rcp])
        S.dma("sp", lambda e: e.dma_start(out=pw.rearrange("p (k n) -> p k n", k=4), in_=pwb[l].rearrange("(k p) n -> p k n", p=128)), writes=[Rpw], reads=[RCW("pw%d" % l)])
        S.op("pool", lambda e: e.memset(ones, 1.0 / 512), writes=[Rones])
        tiles = [(0, t0, min(512, L - t0)) for t0 in range(0, L, 512)]
        if with_ctx:
            tiles += [(1, t0, min(512, CTX - t0)) for t0 in range(0, CTX, 512)]
        for ti, (isctx, t0, n) in enumerate(tiles):
            Ls = CTX if isctx else L
            base = L if isctx else 0
            sl = 0
            u3 = ub[sl].rearrange("p (c t) -> p c t", c=4)
            lo = max(t0 - 15, 0); hi = min(t0 + n + 15, Ls)
            off = lo - (t0 - 15)
            if lo != t0 - 15 or hi != t0 + n + 15:
                S.op("pool", lambda e: e.memset(ub[sl], 0.0), writes=[Rub[sl]])
            S.dma("sp", lambda e: e.dma_start(out=u3[:, :, off:off + hi - lo], in_=uT_d[:, base + lo:base + hi].rearrange("(c p) t -> p c t", p=128)),
                  writes=[Rub[sl]])
            for c in range(4):
                S.op("dve", lambda e: e.tensor_scalar(out=acc[c][:, 0:n], in0=u3[:, c, 15:15 + n], scalar1=cw[:, c * 31 + 15:c * 31 + 16],
                                                      scalar2=cb[:, c:c + 1], op0=ALU.mult, op1=ALU.add), reads=[Rub[sl], Rcp], writes=[Racc[c]])
            for k in range(31):
                if k == 15:
                    continue
                for c in range(4):
                    S.op("dve", lambda e: e.scalar_tensor_tensor(out=acc[c][:, 0:n], in0=u3[:, c, k:k + n], scalar=cw[:, c * 31 + k:c * 31 + k + 1],
                                                                 in1=acc[c][:, 0:n], op0=ALU.mult, op1=ALU.add),
                         reads=[Rub[sl], Rcp, Racc[c]], writes=[Racc[c]])
                yield
            for c in range(4):
                S.op("pe", lambda e: e.matmul(PS[6][:, 0:n], lhsT=ones, rhs=acc[c][:, 0:n], start=(c == 0), stop=(c == 3)),
                     reads=[Rones, Racc[c]], writes=[RPC])
            for c in range(4):
                S.op("dve", lambda e: e.tensor_tensor(out=acc[c][:, 0:n], in0=acc[c][:, 0:n], in1=PS[6][:, 0:n], op=ALU.subtract),
                     reads=[RPC, Racc[c]], writes=[Racc[c]])
                S.op("act", lambda e: e.activation(out=sq[c][:, 0:n], in_=acc[c][:, 0:n], func=AF.Square), reads=[Racc[c]], writes=[Rsq[c]])
            for c in range(4):
                S.op("pe", lambda e: e.matmul(PS[6][:, 0:n], lhsT=ones, rhs=sq[c][:, 0:n], start=(c == 0), stop=(c == 3)),
                     reads=[Rones, Rsq[c]], writes=[RPC])
            yield
            S.op("act", lambda e: e.activation(out=rstd[:, 0:n], in_=PS[6][:, 0:n], func=AF.Sqrt, bias=EPS), reads=[RPC], writes=[Rrstd])
            S.op("dve", lambda e: e.reciprocal(out=rstd[:, 0:n], in_=rstd[:, 0:n]), reads=[Rrstd], writes=[Rrstd])
            for c in range(4):
                S.op("dve", lambda e: e.tensor_tensor(out=acc[c][:, 0:n], in0=acc[c][:, 0:n], in1=rstd[:, 0:n], op=ALU.mult),
                     reads=[Rrstd, Racc[c]], writes=[Racc[c]])
                S.op("act", lambda e: e.activation(out=zb[:, c * 512:c * 512 + n], in_=acc[c][:, 0:n], func=AF.Silu,
                                                   scale=lg_[:, c:c + 1], bias=lb_[:, c:c + 1]), reads=[Racc[c], Rcp], writes=[Rzb])
            for co in range(4):
                bk = 6
                for ci in range(4):
                    S.op("pe", lambda e: e.matmul(PS[bk][:, 0:n], lhsT=pw[:, ci * 512 + co * 128:ci * 512 + (co + 1) * 128],
                                                  rhs=zb[:, ci * 512:ci * 512 + n], start=(ci == 0), stop=(ci == 3)),
                         reads=[Rpw, Rzb], writes=[RPC])
                ss_ = co % 2
                S.op("act", lambda e: e.activation(out=stb[ss_][:, 0:n], in_=PS[bk][:, 0:n], func=AF.Copy), reads=[RPC], writes=[Rstb[ss_]])
                S.dma("pool", lambda e: e.dma_start(out=mixT_d[1024 + co * 128:1024 + (co + 1) * 128, base + t0:base + t0 + n], in_=stb[ss_][:, 0:n]),
                      reads=[Rstb[ss_]])
                yield

    def na_zero():
        S.op("pool", lambda e: e.memset(ztile, 0.0), writes=[Rzt])
        S.dma("sp", lambda e: e.dma_start(out=bass.AP(toep_d.tensor, 0, [[2176, 128], [1, 2176]]), in_=ztile), reads=[Rzt])

    def na_diag(l):
        dsem = S.dma_sem()
        for h in range(4):
            for ro in range(15):
                dst = bass.AP(toep_d.tensor, h * 69632 + (ro + 1) * 64 - 15, [[1089, 64], [1, 31]])
                S.dma("pool", lambda e: e.dma_start(out=dst, in_=na_rpb[l, h, ro:ro + 1, :].broadcast_to([64, 31])), sem=dsem)

    def phase_na(l, with_ctx, A):
        RP_sc = Res("rp_sc"); RP_pt = Res("rp_pt"); RP_no = Res("rp_no")
        TB = A.f32(20 * 576); RTB = Res("TB")
        nam = A.f32(5 * 576); Rnam = Res("nam")
        ckT = A.bf16(4 * CTX); Rck = Res("ckT")
        cv = A.bf16((CTX // 128) * 512); Rcv = Res("cv")
        qm = [A.bf16(512) for _ in range(2)]; Rqm = [Res("qm%d" % i) for i in range(2)]
        km = [A.bf16(4 * 576) for _ in range(2)]; Rkm = [Res("km%d" % i) for i in range(2)]
        vm = [A.bf16(5 * 512) for _ in range(2)]; Rvm = [Res("vm%d" % i) for i in range(2)]
        scs = [A.f32(832)] * 2; Rscs = [Res("scs")] * 2
        pb = [A.bf16(832) for _ in range(2)]; Rpb = [Res("pb%d" % i) for i in range(2)]
        pT = [A.bf16(7 * 128) for _ in range(2)]; RpT = [Res("pT%d" % i) for i in range(2)]
        nst = [A.bf16(512) for _ in range(2)]; Rnst = [Res("nst%d" % i) for i in range(2)]
        sm = A.f32(16); Rsm = Res("sm")
        types = na_types(R)
        for h in range(4):
            for ti, (mrep, lo) in enumerate(types):
                idx = h * 5 + ti
                for qr in range(2):
                    ro0 = lo - (2 * mrep + qr) + 7
                    src = bass.AP(toep_d.tensor, h * 69632 + (ro0 + 1) * 64, [[1088, 64], [1, 576]])
                    S.dma("sp", lambda e: e.dma_start(out=TB[qr * 64:(qr + 1) * 64, idx * 576:(idx + 1) * 576], in_=src), writes=[RTB])
        S.dma("sp", lambda e: e.dma_start(out=nam.rearrange("p (a k) -> p a k", a=5), in_=nam_d.rearrange("a p k -> p a k")), writes=[Rnam])
        for h in range(4):
            S.op("dve", lambda e: e.tensor_tensor(out=TB[:, h * 2880:(h + 1) * 2880], in0=TB[:, h * 2880:(h + 1) * 2880], in1=nam, op=ALU.add),
                 reads=[RTB, Rnam], writes=[RTB])
        S.dma("sp", lambda e: e.dma_start(out=ckT.rearrange("p (h t) -> p h t", h=4), in_=nkT_d[:, L:T].rearrange("(h d) t -> d h t", d=128)), writes=[Rck])
        S.dma("sp", lambda e: e.dma_start(out=cv.rearrange("p (c f) -> p c f", f=512), in_=nv_d[L:T, :].rearrange("(c p) f -> p c f", p=128)), writes=[Rcv])
        cnt = [0]

        def na_block(sl, tokdst, segs, vch):
            ntot = sum(s_[1] for s_ in segs)
            for h in range(4):
                i = cnt[0] % 2; cnt[0] += 1
                sc = psum_t[:, 4 * 512:4 * 512 + 1024]
                ops_ = PS[2][:, 0:128]
                o = 0
                for (rf, ncol, bf_) in segs:
                    c0 = 0
                    while c0 < ncol:
                        w_ = min(ncol - c0, 512 - (o % 512))
                        S.op("pe", lambda e: e.matmul(sc[:, o:o + w_], lhsT=qm[sl][:, h * 128:(h + 1) * 128], rhs=rf(h)[:, c0:c0 + w_], start=True, stop=True),
                             reads=[Rqm[sl], Rkm[sl], Rck], writes=[RP_sc])
                        o += w_; c0 += w_
                yield
                o = 0
                for (rf, ncol, bf_) in segs:
                    if bf_ is not None:
                        S.op("dve", lambda e: e.tensor_tensor(out=scs[i][:, o:o + ncol], in0=sc[:, o:o + ncol], in1=bf_(h), op=ALU.add),
                             reads=[RP_sc, RTB], writes=[Rscs[i]])
                    else:
                        S.op("act", lambda e: e.activation(out=scs[i][:, o:o + ncol], in_=sc[:, o:o + ncol], func=AF.Copy),
                             reads=[RP_sc], writes=[Rscs[i]])
                    o += ncol
                yield
                S.op("dve", lambda e: e.reduce_max(out=sm[:, i:i + 1], in_=scs[i][:, 0:ntot], axis=AX.X), reads=[Rscs[i]], writes=[Rsm])
                S.op("dve", lambda e: e.tensor_scalar(out=sm[:, i:i + 1], in0=sm[:, i:i + 1], scalar1=-1.0, scalar2=None, op0=ALU.mult), reads=[Rsm], writes=[Rsm])
                yield
                S.op("act", lambda e: e.activation(out=scs[i][:, 0:ntot], in_=scs[i][:, 0:ntot], func=AF.Exp, bias=sm[:, i:i + 1],
                                                   accum_out=sm[:, 4 + i:5 + i]), reads=[Rscs[i], Rsm], writes=[Rscs[i], Rsm])
                yield
                S.op("dve", lambda e: e.reciprocal(out=sm[:, 4 + i:5 + i], in_=sm[:, 4 + i:5 + i]), reads=[Rsm], writes=[Rsm])
                S.op("dve", lambda e: e.tensor_scalar(out=pb[i][:, 0:ntot], in0=scs[i][:, 0:ntot], scalar1=sm[:, 4 + i:5 + i], scalar2=None, op0=ALU.mult),
                     reads=[Rscs[i], Rsm], writes=[Rpb[i]])
                yield
                bt = 7
                o = 0
                for ci, (vf, nk) in enumerate(vch):
                    S.op("pe", lambda e: e.transpose(PSB[bt][0:nk, ci * 128:(ci + 1) * 128], pb[i][:, o:o + nk], identb),
                         reads=[Rpb[i], Rid], writes=[RP_pt])
                    o += nk
                nch = len(vch)
                yield
                S.op("act", lambda e: e.activation(out=pT[i][:, 0:nch * 128], in_=PSB[bt][:, 0:nch * 128], func=AF.Copy), reads=[RP_pt], writes=[RpT[i]])
                yield
                for ci, (vf, nk) in enumerate(vch):
                    S.op("pe", lambda e: e.matmul(ops_, lhsT=vf(h)[0:nk, :], rhs=pT[i][0:nk, ci * 128:(ci + 1) * 128],
                                                  start=(ci == 0), stop=(ci == nch - 1)), reads=[RpT[i], Rvm[sl], Rcv], writes=[RP_no])
                S.op("dve", lambda e: e.tensor_copy(out=nst[sl][:, h * 128:(h + 1) * 128], in_=ops_), reads=[RP_no], writes=[Rnst[sl]])
                yield
            S.dma("pool", lambda e: e.dma_start(out=mixT_d[1536:2048, tokdst:tokdst + 128].rearrange("(h d) t -> d h t", d=128),
                                                in_=nst[sl].rearrange("p (h t) -> p h t", h=4)), reads=[Rnst[sl]])

        ctx_v = [((lambda h, c=c: cv[:, c * 512 + h * 128:c * 512 + (h + 1) * 128]), 128) for c in range(CTX // 128)]
        ctx_seg = ((lambda h: ckT[:, h * CTX:(h + 1) * CTX]), CTX, None)
        for m in range(R // 2):
            ti, lo = na_type_of(m, R)
            sl = m % 2
            S.dma("sp", lambda e: e.dma_start(out=qm[sl].rearrange("p (h t) -> p h t", h=4),
                                              in_=nqT_d[:, m * 128:(m + 1) * 128].rearrange("(h d) t -> d h t", d=128)), writes=[Rqm[sl]])
            S.dma("sp", lambda e: e.dma_start(out=km[sl].rearrange("p (h t) -> p h t", h=4),
                                              in_=nkT_d[:, lo * 64:lo * 64 + 576].rearrange("(h d) t -> d h t", d=128)), writes=[Rkm[sl]])
            S.dma("sp", lambda e: e.dma_start(out=vm[sl][:, 0:2048].rearrange("p (c f) -> p c f", f=512),
                                              in_=nv_d[lo * 64:lo * 64 + 512, :].rearrange("(c p) f -> p c f", p=128)), writes=[Rvm[sl]])
            S.dma("sp", lambda e: e.dma_start(out=vm[sl][0:64, 2048:2560], in_=nv_d[lo * 64 + 512:lo * 64 + 576, :]), writes=[Rvm[sl]])
            kseg = ((lambda h, sl=sl: km[sl][:, h * 576:(h + 1) * 576]), 576,
                    (lambda h, ti=ti: TB[:, (h * 5 + ti) * 576:(h * 5 + ti + 1) * 576]))
            vch = [((lambda h, c=c, sl=sl: vm[sl][:, c * 512 + h * 128:c * 512 + (h + 1) * 128]), 128) for c in range(4)]
            vch += [((lambda h, sl=sl: vm[sl][:, 2048 + h * 128:2048 + (h + 1) * 128]), 64)]
            yield from na_block(sl, m * 128, [kseg, ctx_seg], vch + ctx_v)
        if with_ctx:
            for qb in range(CTX // 128):
                sl = qb % 2
                S.dma("sp", lambda e: e.dma_start(out=qm[sl].rearrange("p (h t) -> p h t", h=4),
                                                  in_=nqT_d[:, L + qb * 128:L + (qb + 1) * 128].rearrange("(h d) t -> d h t", d=128)), writes=[Rqm[sl]])
                yield from na_block(sl, L + qb * 128, [ctx_seg], ctx_v)

    def phase_ffn(l, with_ctx, xlat, xctx, last):
        A.off = pers_mark
        fw = A.f32(264); fb = A.f32(88); Rfp = Res("ffnp")
        xs = [A.f32(D) for _ in range(4)]; Rxs = [Res("fxs%d" % i) for i in range(4)]
        xn = [A.f32(D) for _ in range(2)]; Rxn = [Res("fxn%d" % i) for i in range(2)]
        hT = A.bf16(16 * 512); RhT = Res("fhT")
        aT = A.bf16(22 * 512); RaT = Res("aT")
        mt = aT[:, 0:16 * 512]; RmT = RaT
        upw = [A.bf16(16 * 256) for _ in range(4)]; Rupw = [Res("upw%d" % i) for i in range(4)]
        wdn = [A.bf16(16 * 512) for _ in range(3)]; Rwdn = [Res("wdn%d" % i) for i in range(3)]
        gt = [A.f32(D) for _ in range(2)]; Rgt = [Res("gt%d" % i) for i in range(2)]
        yv = [A.f32(512) for _ in range(2)]; Ryv = [Res("yv%d" % i) for i in range(2)]
        yg = [A.f32(512) for _ in range(2)]; Ryg = [Res("yg%d" % i) for i in range(2)]
        tmp = [A.f32(512) for _ in range(2)]; Rtmp = [Res("ftmp%d" % i) for i in range(2)]
        ss = A.f32(16); Rss = Res("fss")
        S.dma("sp", lambda e: e.dma_start(out=fw, in_=fdw[l].rearrange("p c k -> p (c k)")), writes=[Rfp])
        S.dma("sp", lambda e: e.dma_start(out=fb, in_=fdb[l]), writes=[Rfp])
        cnt = {"up": 0, "dn": 0, "tmp": 0, "xn": 0, "y": 0}

        def nxt(key, n_):
            v = cnt[key]; cnt[key] = (v + 1) % n_; return v

        tiles = [(0, t0, min(510, L - t0)) for t0 in range(0, L, 510)]
        if with_ctx:
            tiles += [(1, t0, min(510, CTX - t0)) for t0 in range(0, CTX, 510)]
        for (isctx, t0, ni) in tiles:
            cond = isctx
            Ls = CTX if isctx else L
            base = L if isctx else 0
            xsrc = xctx if isctx else xlat
            n = ni + 2
            nb = (n + 127) // 128
            jlo = 1 if t0 == 0 else 0
            jhi = n - 1 if t0 + ni == Ls else n
            nts = [min(128, n - tb * 128) for tb in range(nb)]
            for tb in range(nb):
                r0 = max(tb * 128, jlo); r1 = min(tb * 128 + nts[tb], jhi)
                if r0 != tb * 128 or r1 != tb * 128 + nts[tb]:
                    S.op("pool", lambda e: e.memset(xs[tb], 0.0), writes=[Rxs[tb]])
                for (a_, b_) in row_chunks(r0, r1):
                    S.dma("sp", lambda e: e.dma_start(out=xs[tb][a_ - tb * 128:b_ - tb * 128, :], in_=xsrc[t0 - 1 + a_:t0 - 1 + b_, :]), writes=[Rxs[tb]])
            mt3 = mt.rearrange("p (c t) -> p c t", c=16)
            if jlo != 0 or jhi != n:
                S.op("pool", lambda e: e.memset(mt, 0.0), writes=[RmT])
            S.dma("sp", lambda e: e.dma_start(out=mt3[:, :, jlo:jhi], in_=mixT_d[:, base + t0 - 1 + jlo:base + t0 - 1 + jhi].rearrange("(c p) t -> p c t", p=128)),
                  writes=[RmT])
            S.dma("sp", lambda e: e.dma_start(out=gt[0], in_=mods_d[cond:cond + 1, 2 * D:3 * D].broadcast_to([128, D])), writes=[Rgt[0]])
            S.dma("sp", lambda e: e.dma_start(out=gt[1], in_=mods_d[cond:cond + 1, 5 * D:6 * D].broadcast_to([128, D])), writes=[Rgt[1]])
            for nbk in range(4):
                sl = nxt("dn", 3)
                S.dma("sp", lambda e: e.dma_start(out=wdn[sl].rearrange("p (k n) -> p k n", k=16),
                                                  in_=woutb[l][:, nbk * 512:(nbk + 1) * 512].rearrange("(k p) n -> p k n", p=128)), writes=[Rwdn[sl]], reads=[RCW("wout%d" % l)])
                for tb in range(nb):
                    nt = nts[tb]
                    bk = 4 + tb
                    for k in range(16):
                        S.op("pe", lambda e: e.matmul(PS[bk][0:nt, :], lhsT=mt[:, k * 512 + tb * 128:k * 512 + tb * 128 + nt],
                                                      rhs=wdn[sl][:, k * 512:(k + 1) * 512], start=(k == 0), stop=(k == 15)),
                             reads=[RmT, Rwdn[sl]], writes=[RPS[bk]])
                    ti_ = nxt("tmp", 2)
                    S.op("dve", lambda e: e.tensor_tensor(out=tmp[ti_][0:nt, :], in0=PS[bk][0:nt, :], in1=gt[0][0:nt, nbk * 512:(nbk + 1) * 512], op=ALU.mult),
                         reads=[RPS[bk], Rgt[0]], writes=[Rtmp[ti_]])
                    S.op("pool", lambda e: e.tensor_tensor(out=xs[tb][0:nt, nbk * 512:(nbk + 1) * 512], in0=xs[tb][0:nt, nbk * 512:(nbk + 1) * 512],
                                                           in1=tmp[ti_][0:nt, :], op=ALU.add), reads=[Rtmp[ti_], Rxs[tb]], writes=[Rxs[tb]])
            if last:
                S.dma("sp", lambda e: e.dma_start(out=gt[0], in_=final_g.rearrange("(o d) -> o d", o=1).broadcast_to([128, D])), writes=[Rgt[0]])
            for tb in range(nb):
                nt = nts[tb]
                xi_ = nxt("xn", 2)
                S.op("act", lambda e: e.activation(out=xn[xi_][0:nt, :], in_=xs[tb][0:nt, :], func=AF.Square, accum_out=ss[0:nt, tb:tb + 1]),
                     reads=[Rxs[tb]], writes=[Rxn[xi_], Rss])
                S.op("act", lambda e: e.activation(out=ss[0:nt, 4 + tb:5 + tb], in_=ss[0:nt, tb:tb + 1], func=AF.Sqrt, scale=1.0 / D, bias=EPS),
                     reads=[Rss], writes=[Rss])
                S.op("dve", lambda e: e.reciprocal(out=ss[0:nt, 4 + tb:5 + tb], in_=ss[0:nt, 4 + tb:5 + tb]), reads=[Rss], writes=[Rss])
                S.op("dve", lambda e: e.tensor_scalar(out=xn[xi_][0:nt, :], in0=xs[tb][0:nt, :], scalar1=ss[0:nt, 4 + tb:5 + tb], scalar2=None, op0=ALU.mult),
                     reads=[Rxs[tb], Rss], writes=[Rxn[xi_]])
                for c4 in range(4):
                    bk = c4 % 2
                    for cc in range(4):
                        c = c4 * 4 + cc
                        S.op("pe", lambda e: e.transpose(PS[bk][:, cc * 128:cc * 128 + nt], xn[xi_][0:nt, c * 128:(c + 1) * 128], ident[0:nt, 0:nt]),
                             reads=[Rxn[xi_], Rid], writes=[RPS[bk]])
                    for cc in range(4):
                        c = c4 * 4 + cc
                        dst = hT[:, c * 512 + tb * 128:c * 512 + tb * 128 + nt]
                        if cc % 2 == 0:
                            S.op("dve", lambda e: e.tensor_scalar(out=dst, in0=PS[bk][:, cc * 128:cc * 128 + nt], scalar1=G2T[:, 2 * c + cond:2 * c + cond + 1],
                                                                  scalar2=modcol(3, c, cond), op0=ALU.mult, op1=ALU.add),
                                 reads=[RPS[bk], RG, RmodT], writes=[RhT])
                        else:
                            S.op("act", lambda e: e.activation(out=dst, in_=PS[bk][:, cc * 128:cc * 128 + nt], func=AF.Identity,
                                                               scale=G2T[:, 2 * c + cond:2 * c + cond + 1], bias=modcol(3, c, cond)),
                                 reads=[RPS[bk], RG, RmodT], writes=[RhT])
            hT3 = hT.rearrange("p (c t) -> p c t", c=16)
            if jlo == 1:
                S.op("pool", lambda e: e.memset(hT3[:, :, 0:1], 0.0), reads=[RhT], writes=[RhT])
            if jhi == n - 1:
                S.op("pool", lambda e: e.memset(hT3[:, :, n - 1:n], 0.0), reads=[RhT], writes=[RhT])
            for hf in range(2):
                for jj in range(22):
                    c = hf * 22 + jj
                    sub = c % 2
                    if jj % 2 == 0:
                        uv = nxt("up", 4); ug = nxt("up", 4)
                        for (us, c0) in ((uv, (c // 2) * 256), (ug, DFF + (c // 2) * 256)):
                            S.dma("sp", lambda e: e.dma_start(out=upw[us].rearrange("p (k n) -> p k n", k=16),
                                                              in_=upb[l][:, c0:c0 + 256].rearrange("(k p) n -> p k n", p=128)), writes=[Rupw[us]], reads=[RCW("up%d" % l)])
                    pbk = (jj % 2) * 2
                    for (us, bk) in ((uv, pbk), (ug, pbk + 1)):
                        for k in range(16):
                            S.op("pe", lambda e: e.matmul(PS[bk][:, 0:n], lhsT=upw[us][:, k * 256 + sub * 128:k * 256 + (sub + 1) * 128],
                                                          rhs=hT[:, k * 512:k * 512 + n], start=(k == 0), stop=(k == 15)),
                                 reads=[Rupw[us], RhT], writes=[RPS[bk]])
                    yi = nxt("y", 2)
                    for (yy, Ryy, bk, ch) in ((yv[yi], Ryv[yi], pbk, c), (yg[yi], Ryg[yi], pbk + 1, 44 + c)):
                        S.op("act", lambda e: e.activation(out=yy[:, 0:n], in_=PS[bk][:, 0:n], func=AF.Identity, scale=fw[:, ch * 3 + 1:ch * 3 + 2],
                                                           bias=fb[:, ch:ch + 1]), reads=[RPS[bk], Rfp], writes=[Ryy])
                        S.op("dve", lambda e: e.scalar_tensor_tensor(out=yy[:, 1:n], in0=PS[bk][:, 0:n - 1], scalar=fw[:, ch * 3:ch * 3 + 1], in1=yy[:, 1:n],
                                                                     op0=ALU.mult, op1=ALU.add), reads=[RPS[bk], Rfp, Ryy], writes=[Ryy])
                        S.op("dve", lambda e: e.scalar_tensor_tensor(out=yy[:, 0:n - 1], in0=PS[bk][:, 1:n], scalar=fw[:, ch * 3 + 2:ch * 3 + 3], in1=yy[:, 0:n - 1],
                                                                     op0=ALU.mult, op1=ALU.add), reads=[RPS[bk], Rfp, Ryy], writes=[Ryy])
                    S.op("act", lambda e: e.activation(out=yg[yi][:, 0:n], in_=yg[yi][:, 0:n], func=AF.Silu), reads=[Ryg[yi]], writes=[Ryg[yi]])
                    S.op("pool", lambda e: e.tensor_tensor(out=aT[:, jj * 512:jj * 512 + n], in0=yv[yi][:, 0:n], in1=yg[yi][:, 0:n], op=ALU.mult),
                         reads=[Ryv[yi], Ryg[yi]], writes=[RaT])
                for nbk in range(4):
                    for part in range(2):
                        sl = nxt("dn", 3)
                        k0 = hf * 22 + part * 11
                        S.dma("sp", lambda e: e.dma_start(out=wdn[sl][:, 0:11 * 512].rearrange("p (k n) -> p k n", k=11),
                                                          in_=downb[l][k0 * 128:(k0 + 11) * 128, nbk * 512:(nbk + 1) * 512].rearrange("(k p) n -> p k n", p=128)),
                              writes=[Rwdn[sl]], reads=[RCW("down%d" % l)])
                        for tb in range(nb):
                            nt = nts[tb]
                            bk = 4 + tb
                            for kl in range(11):
                                kk = part * 11 + kl
                                S.op("pe", lambda e: e.matmul(PS[bk][0:nt, :], lhsT=aT[:, kk * 512 + tb * 128:kk * 512 + tb * 128 + nt],
                                                              rhs=wdn[sl][:, kl * 512:(kl + 1) * 512], start=(kk == 0), stop=(kk == 21)),
                                     reads=[RaT, Rwdn[sl]], writes=[RPS[bk]])
                    for tb in range(nb):
                        nt = nts[tb]
                        bk = 4 + tb
                        ti_ = nxt("tmp", 2)
                        S.op("dve", lambda e: e.tensor_tensor(out=tmp[ti_][0:nt, :], in0=PS[bk][0:nt, :], in1=gt[1][0:nt, nbk * 512:(nbk + 1) * 512], op=ALU.mult),
                             reads=[RPS[bk], Rgt[1]], writes=[Rtmp[ti_]])
                        S.op("pool", lambda e: e.tensor_tensor(out=xs[tb][0:nt, nbk * 512:(nbk + 1) * 512], in0=xs[tb][0:nt, nbk * 512:(nbk + 1) * 512],
                                                               in1=tmp[ti_][0:nt, :], op=ALU.add), reads=[Rtmp[ti_], Rxs[tb]], writes=[Rxs[tb]])
            for tb in range(nb):
                nt = nts[tb]
                r0 = max(tb * 128, 1); r1 = min(tb * 128 + nt, n - 1)
                if r1 <= r0:
                    continue
                if last:
                    xi_ = nxt("xn", 2)
                    S.op("act", lambda e: e.activation(out=xn[xi_][0:nt, :], in_=xs[tb][0:nt, :], func=AF.Square, accum_out=ss[0:nt, 8 + tb:9 + tb]),
                         reads=[Rxs[tb]], writes=[Rxn[xi_], Rss])
                    S.op("act", lambda e: e.activation(out=ss[0:nt, 12 + tb:13 + tb], in_=ss[0:nt, 8 + tb:9 + tb], func=AF.Sqrt, scale=1.0 / D, bias=EPS),
                         reads=[Rss], writes=[Rss])
                    S.op("dve", lambda e: e.reciprocal(out=ss[0:nt, 12 + tb:13 + tb], in_=ss[0:nt, 12 + tb:13 + tb]), reads=[Rss], writes=[Rss])
                    S.op("dve", lambda e: e.tensor_scalar(out=xn[xi_][0:nt, :], in0=xs[tb][0:nt, :], scalar1=ss[0:nt, 12 + tb:13 + tb], scalar2=None, op0=ALU.mult),
                         reads=[Rxs[tb], Rss], writes=[Rxn[xi_]])
                    S.op("pool", lambda e: e.tensor_tensor(out=xn[xi_][0:nt, :], in0=xn[xi_][0:nt, :], in1=gt[0][0:nt, :], op=ALU.mult),
                         reads=[Rxn[xi_], Rgt[0]], writes=[Rxn[xi_]])
                    for (a_, b_) in row_chunks(r0, r1):
                        S.dma("act", lambda e: e.dma_start(out=out_d[t0 - 1 + a_:t0 - 1 + b_, :], in_=xn[xi_][a_ - tb * 128:b_ - tb * 128, :]), reads=[Rxn[xi_]])
                else:
                    for (a_, b_) in row_chunks(r0, r1):
                        S.dma("act", lambda e: e.dma_start(out=xa_d[base + t0 - 1 + a_:base + t0 - 1 + b_, :], in_=xs[tb][a_ - tb * 128:b_ - tb * 128, :]),
                              reads=[Rxs[tb]])
        S.barrier()

    def run_group(items):
        active = [(iter(g_), w_) for g_, w_ in items]
        while active:
            for it in list(active):
                g_, w_ = it
                for _ in range(w_):
                    try:
                        next(g_)
                    except StopIteration:
                        active.remove(it)
                        break
        S.barrier()

    def run(stage="all", nlay=NL):
        for l in range(nlay):
            last = (l == NL - 1)
            xlat = x_in if l == 0 else xa_d[0:L]
            xctx = ctx_in if l == 0 else xa_d[L:T]
            if l == 0:
                issue_casts(0, 0)
            na_zero()
            phase_mods(l)
            na_diag(l)
            if l == 0:
                issue_casts(0, 1)
                for l2 in range(1, NL):
                    issue_casts(l2, 0)
                    issue_casts(l2, 1)
            if stage == "mods" and l == nlay - 1:
                break
            phase_inproj(l, xlat, xctx)
            if stage == "inproj" and l == nlay - 1:
                break
            A.off = pers_mark
            Ana = A.sub(A.n - A.off - 15900 - 9700); Aret = A.sub(15900); Aconv = A.sub(9700)
            import os
            gm = os.environ.get("GMODE", "par")
            if gm == "seq":
                for g_ in (phase_ret(l, not last, Aret), phase_conv(l, not last, Aconv), phase_na(l, not last, Ana)):
                    run_group([(g_, 1)])
            elif gm in ("ret", "conv", "na"):
                run_group([({"ret": phase_ret(l, not last, Aret), "conv": phase_conv(l, not last, Aconv), "na": phase_na(l, not last, Ana)}[gm], 1)])
            else:
                run_group([(phase_ret(l, not last, Aret), 6), (phase_conv(l, not last, Aconv), 1), (phase_na(l, not last, Ana), 4)])
            if stage == "na" and l == nlay - 1:
                break
            phase_ffn(l, not last, xlat, xctx, last)
        S.run_block()

    g.run = run
    g.nc = nc; g.S = S
    g.phase_mods = phase_mods; g.phase_inproj = phase_inproj
    g.names = dict(x_in=x_in, ctx_in=ctx_in, xa_d=xa_d, out_d=out_d)
    g.locals = locals()
    return g


def kernel(**inputs):
    inp = {k: np.asarray(v) for k, v in inputs.items()}
    B, L, _ = inp["x"].shape
    CTX = inp["ctx"].shape[1]
    g = build(L, CTX, NL=2, dbg=False)
    g.run("all", 2)
    in_maps = [host_layout(inp, b, L) for b in range(B)]
    res = run_bass_kernel_spmd(g.nc, in_maps, core_ids=list(range(B)))
    return np.stack([np.asarray(res.results[b]["out"], np.float32) for b in range(B)]).astype(np.float32)
```

```python
import numpy as np
import ml_dtypes
import concourse.bass as bass
import concourse.mybir as mybir
from concourse.bass_utils import run_bass_kernel_spmd

F32 = mybir.dt.float32
BF16 = mybir.dt.bfloat16
AF = mybir.ActivationFunctionType
ALU = mybir.AluOpType
AX = mybir.AxisListType

COMPUTE = ("pe", "act", "dve", "pool")
SAME_ENGINE_WAIT = True


class Res:
    __slots__ = ("name", "w", "r", "lsem", "ssem")

    def __init__(self, name):
        self.name = name
        self.w = {}
        self.r = {}
        self.lsem = None
        self.ssem = None


class _Cap:
    def __init__(self):
        self.call = None

    def __getattr__(self, name):
        def f(*a, **k):
            self.call = (name, a, k)
            return self
        return f


class Rec:
    __slots__ = ("waits", "fn", "inc")

    def __init__(self, waits, fn, inc):
        self.waits = waits
        if fn is not None:
            cap = _Cap()
            fn(cap)
            fn = cap.call
            assert fn is not None
        self.fn = fn
        self.inc = inc


class Sched:
    def __init__(self, nc):
        self.nc = nc
        self.prog = {e: [] for e in ("pe", "act", "dve", "pool", "sp")}
        self.sems = {}
        self.cnt = {}
        self.known = {e: {} for e in self.prog}
        self.last = {e: None for e in COMPUTE}
        self.pending = {e: False for e in COMPUTE}
        self.free_dma_sems = []
        self.live_dma_sems = []
        self.nsem = 0
        self.nobarrier = set()
        self.live_res = []
        for e in COMPUTE:
            self._mk("E_" + e)

    def _mk(self, key):
        self.sems[key] = self.nc.alloc_semaphore(key)
        self.cnt[key] = 0
        self.nsem += 1
        return key

    def dma_sem(self):
        if self.free_dma_sems:
            k = self.free_dma_sems.pop()
        else:
            k = self._mk("D%d" % self.nsem)
        self.live_dma_sems.append(k)
        return k

    def _force(self, key):
        if key.startswith("E_"):
            e = key[2:]
            if self.pending[e]:
                rec = self.last[e]
                assert rec.inc is None
                rec.inc = (key, 1)
                self.cnt[key] += 1
                self.pending[e] = False

    def _waits(self, eng, deps):
        out = []
        kn = self.known[eng]
        for key, val in deps.items():
            if key == "E_" + eng:
                if eng == "pe" or not SAME_ENGINE_WAIT:
                    continue
            if kn.get(key, 0) >= val:
                continue
            self._force(key)
            assert self.cnt[key] >= val, (key, self.cnt[key], val)
            kn[key] = val
            out.append((key, val))
        return out

    @staticmethod
    def _merge(d, s):
        for k, v in s.items():
            if d.get(k, 0) < v:
                d[k] = v

    def _deps(self, reads, writes):
        deps = {}
        for r in reads:
            self._merge(deps, r.w)
        for w in writes:
            self._merge(deps, w.w)
            self._merge(deps, w.r)
        return deps

    def op(self, eng, fn, reads=(), writes=()):
        deps = self._deps(reads, writes)
        waits = self._waits(eng, deps)
        key = "E_" + eng
        rec = Rec(waits, fn, None)
        self.prog[eng].append(rec)
        self.last[eng] = rec
        self.pending[eng] = True
        tok = {key: self.cnt[key] + 1}
        for r in reads:
            self._merge(r.r, tok)
        for w in writes:
            w.w = dict(tok)
            w.r = {}

    def dma(self, queue, fn, reads=(), writes=(), sem=None):
        deps = self._deps(reads, writes)
        waits = self._waits(queue, deps)
        if sem is None:
            if writes:
                w0 = writes[0]
                if w0.lsem is None:
                    w0.lsem = self.dma_sem()
                    self.live_res.append(w0)
                sem = w0.lsem
            else:
                r0 = reads[0]
                if r0.ssem is None:
                    r0.ssem = self.dma_sem()
                    self.live_res.append(r0)
                sem = r0.ssem
        self.cnt[sem] += 16
        tok = {sem: self.cnt[sem]}
        rec = Rec(waits, fn, (sem, 16))
        self.prog[queue].append(rec)
        if queue in COMPUTE:
            pass
        for r in reads:
            self._merge(r.r, tok)
        for w in writes:
            w.w = dict(tok)
            w.r = {}

    def barrier(self, recycle=True, final=False):
        for e in COMPUTE:
            self._force("E_" + e)
        allk = {k: v for k, v in self.cnt.items() if v > 0 and (final or k not in self.nobarrier)}
        for eng in self.prog:
            waits = self._waits_all(eng, allk)
            if waits:
                self.prog[eng].append(Rec(waits, None, None))
        if recycle:
            self.free_dma_sems.extend(self.live_dma_sems)
            self.live_dma_sems = []
            for r_ in self.live_res:
                r_.lsem = None
                r_.ssem = None
            self.live_res = []

    def _waits_all(self, eng, allk):
        out = []
        kn = self.known[eng]
        for key, val in allk.items():
            if key == "E_" + eng:
                continue
            if kn.get(key, 0) >= val:
                continue
            kn[key] = val
            out.append((key, val))
        return out

    def replay(self, eng, e):
        for rec in self.prog[eng]:
            for key, val in rec.waits:
                e.wait_ge(self.sems[key], val)
            if rec.fn is None:
                continue
            name, a, k = rec.fn
            ins = getattr(e, name)(*a, **k)
            if rec.inc is not None:
                ins.then_inc(self.sems[rec.inc[0]], rec.inc[1])

    def run_block(self):
        nc = self.nc
        self.barrier(recycle=False, final=True)
        with nc.Block() as block:
            @block.tensor
            def _(e):
                self.replay("pe", e)

            @block.scalar
            def _(e):
                self.replay("act", e)

            @block.vector
            def _(e):
                self.replay("dve", e)

            @block.gpsimd
            def _(e):
                self.replay("pool", e)

            @block.sync
            def _(e):
                self.replay("sp", e)


class Arena:
    def __init__(self, t, nwords, base=0):
        self.t = t
        self.n = base + nwords
        self.off = base
        self.base = base

    def sub(self, nwords):
        assert self.off + nwords <= self.n, ("arena overflow(sub)", self.off, nwords, self.n)
        a = Arena(self.t, nwords, self.off)
        self.off += nwords
        return a

    def reset(self):
        self.off = 0

    def f32(self, n, parts=128):
        assert self.off + n <= self.n, ("arena overflow", self.off, n, self.n)
        ap = self.t[0:parts, self.off:self.off + n]
        self.off += n
        return ap

    def bf16(self, n, parts=128):
        w = (n + 1) // 2
        assert self.off + w <= self.n, ("arena overflow", self.off, w, self.n)
        ap = self.t[0:parts, self.off:self.off + w].bitcast(BF16)
        self.off += w
        return ap[:, 0:n]


D = 2048
DIN = 5632
DFF = 5632
WCOLS = DIN + 1024
EPS = 1e-6
GRID_W = 64
NEG = -30000.0
CAST_BARRIER = False
import os
NA_OLDPS = bool(int(os.environ.get('NA_OLDPS', '0')))


def host_consts(L):
    c = {}
    c["ident"] = np.eye(128, dtype=np.float32)
    pos = np.arange(L)
    row = (pos // GRID_W).astype(np.float32)
    col = (pos % GRID_W).astype(np.float32)
    nf = 32
    inv = (10000.0 ** (-np.arange(nf, dtype=np.float32) / nf)).astype(np.float32)
    f = np.arange(128)
    p = np.where((f // 64)[:, None] == 0, row[None, :], col[None, :]).astype(np.float32)
    ang = (p * inv[f % 32][:, None]).astype(np.float32)
    sign = np.where((f % 64) < 32, -1.0, 1.0).astype(np.float32)[:, None]
    C = np.cos(ang).astype(np.float32)
    Sg = (sign * np.sin(ang)).astype(np.float32)
    sc = np.float32(128 ** -0.5)
    c["rope"] = np.stack([C * sc, Sg * sc, C, Sg]).astype(np.float32)
    i = np.arange(128)
    jj, ii = np.meshgrid(i, i, indexing="ij")
    dec = np.stack([np.maximum(ii - jj, 0), (ii >= jj), np.maximum(jj - ii, 0), (jj >= ii)]).astype(np.float32)
    c["dec"] = dec
    xirow = np.stack([np.tile((i + 1)[None, :], (128, 1)), np.tile((128 - i)[None, :], (128, 1))]).astype(np.float32)
    c["xirow"] = xirow
    c["zcol"] = np.stack([127 - i, i], axis=1).astype(np.float32)
    R = L // GRID_W
    types = na_types(R)
    nam = np.zeros((5, 128, 576), np.float32)
    cols = np.arange(64)
    cs = np.clip(cols - 8, 0, 64 - 16)
    band = (cols[None, :] >= cs[:, None]) & (cols[None, :] < cs[:, None] + 16)
    for ti, (m, lo) in enumerate(types):
        for qr in range(2):
            r = 2 * m + qr
            w0 = int(np.clip(r - 4, 0, R - 8))
            for kidx in range(9):
                kr = lo + kidx
                ok = (w0 <= kr < w0 + 8)
                blk = np.where(band, 0.0, NEG) if ok else np.full((64, 64), NEG)
                nam[ti, qr * 64:(qr + 1) * 64, kidx * 64:(kidx + 1) * 64] = blk
    c["nam"] = nam
    return c


def na_types(R):
    M = R // 2
    return [(0, 0), (1, 0), (2, 0), (M - 2, R - 9), (M - 1, R - 9)]


def na_type_of(m, R):
    M = R // 2
    if m == 0:
        return 0, 0
    if m == 1:
        return 1, 0
    if m == M - 2:
        return 3, R - 9
    if m == M - 1:
        return 4, R - 9
    return 2, 2 * m - 4


def row_chunks(r0, r1):
    n = r1 - r0
    big = (n // 16) * 16
    out = []
    if big:
        out.append((r0, r0 + big))
    if n - big:
        out.append((r0 + big, r1))
    return out


def fm(v, nch):
    s = v.shape[:-1]
    return np.ascontiguousarray(np.moveaxis(v.reshape(s + (nch, 128)), -1, -2))


def host_layout(inp, b, L):
    f32 = np.float32
    o = {}
    o["x"] = np.ascontiguousarray(inp["x"][b], f32)
    o["ctx"] = np.ascontiguousarray(inp["ctx"][b], f32)
    cv = np.stack([inp["c"][b], inp["c_ctx"]], axis=1).astype(f32)
    o["cT"] = np.ascontiguousarray(cv.reshape(16, 128, 2).transpose(1, 0, 2))
    for k in ("w_ada", "b_ada", "w_in", "w_out", "ffn_up", "ffn_down", "conv_pw", "final_g", "na_rpb"):
        o[k] = np.ascontiguousarray(inp[k], f32)
    o["ret_decay"] = np.ascontiguousarray(inp["ret_decay"].reshape(-1, 8), f32)
    o["gng"] = np.ascontiguousarray(inp["ret_gn_g"], f32)
    o["n1g"] = fm(inp["norm1_g"].astype(f32), 16)
    o["n2g"] = fm(inp["norm2_g"].astype(f32), 16)
    o["cdw"] = np.ascontiguousarray(fm(inp["conv_dw_w"].astype(f32), 4).transpose(0, 2, 3, 1))
    o["cdb"] = fm(inp["conv_dw_b"].astype(f32), 4)
    o["lng"] = fm(inp["conv_ln_g"].astype(f32), 4)
    o["lnb"] = fm(inp["conv_ln_b"].astype(f32), 4)
    o["fdw"] = np.ascontiguousarray(fm(inp["ffn_dw_w"].astype(f32), 88).transpose(0, 2, 3, 1))
    o["fdb"] = fm(inp["ffn_dw_b"].astype(f32), 88)
    o.update(host_consts(L))
    return o


class K:
    pass


def build(L, CTX, NL=2, dbg=False, upto=99):
    T = L + CTX
    R = L // GRID_W
    nc = bass.Bass("TRN2", target_bir_lowering=False)
    g = K()

    def din(name, shape, dt=F32):
        return nc.dram_tensor(name, list(shape), dt, kind="ExternalInput").ap()

    def dscr(name, shape, dt):
        return nc.dram_tensor(name, list(shape), dt, kind="ExternalOutput" if dbg else "Internal").ap()

    x_in = din("x", [L, D]); ctx_in = din("ctx", [CTX, D]); cT = din("cT", [128, 16, 2])
    w_ada = din("w_ada", [NL, D, 6 * D]); b_ada = din("b_ada", [NL, 6 * D]); w_in = din("w_in", [NL, D, DIN])
    w_out = din("w_out", [NL, D, D]); ffn_up = din("ffn_up", [NL, D, 2 * DFF]); ffn_down = din("ffn_down", [NL, DFF, D])
    conv_pw = din("conv_pw", [NL, 512, 512]); final_g = din("final_g", [D]); na_rpb = din("na_rpb", [NL, 4, 15, 31])
    ret_decay = din("ret_decay", [NL, 8]); gng = din("gng", [NL, 1024])
    n1g = din("n1g", [NL, 128, 16]); n2g = din("n2g", [NL, 128, 16])
    cdw = din("cdw", [NL, 128, 4, 31]); cdb = din("cdb", [NL, 128, 4]); lng = din("lng", [NL, 128, 4]); lnb = din("lnb", [NL, 128, 4])
    fdw = din("fdw", [NL, 128, 88, 3]); fdb = din("fdb", [NL, 128, 88])
    ident_d = din("ident", [128, 128]); rope_d = din("rope", [4, 128, L]); dec_d = din("dec", [4, 128, 128])
    xirow_d = din("xirow", [2, 128, 128]); zcol_d = din("zcol", [128, 2]); nam_d = din("nam", [5, 128, 576])
    out_d = nc.dram_tensor("out", [L, D], F32, kind="ExternalOutput").ap()

    winb = dscr("winb", [NL, D, WCOLS], BF16); woutb = dscr("woutb", [NL, D, D], BF16)
    upb = dscr("upb", [NL, D, 2 * DFF], BF16); downb = dscr("downb", [NL, DFF, D], BF16)
    pwb = dscr("pwb", [NL, 512, 512], BF16); wadab = dscr("wadab", [NL, D, 6 * D], BF16)
    mods_d = dscr("mods", [2, 6 * D], F32)
    qT_d = dscr("qT", [512, T], BF16); kT_d = dscr("kT", [512, T], BF16); v_d = dscr("v", [T, 1024], BF16)
    sg_d = dscr("sg", [T, 1024], F32); uT_d = dscr("uT", [512, T], F32)
    nqT_d = dscr("nqT", [512, T], BF16); nkT_d = dscr("nkT", [512, T], BF16); nv_d = dscr("nv", [T, 512], BF16)
    of_d = dscr("of", [T, 1024], F32); mixT_d = dscr("mixT", [D, T], BF16)
    xa_d = dscr("xa", [T, D], F32); toep_d = dscr("toep", [4, 64, 17, 64], F32)

    S = Sched(nc)
    NARENA = 52900
    import contextlib
    es = contextlib.ExitStack()
    arena_t = es.enter_context(nc.sbuf_tensor("arena", [128, NARENA], F32))
    psum_t = es.enter_context(nc.psum_tensor("psum", [128, 4096], F32))
    A = Arena(arena_t, NARENA)
    PS = [psum_t[:, i * 512:(i + 1) * 512] for i in range(8)]
    RPS = [Res("ps%d" % i) for i in range(8)]

    ident = A.f32(128); Rid = Res("ident")
    identb = A.bf16(128)
    modT = A.f32(192); RmodT = Res("modT")
    G1T = A.f32(32); G2T = A.f32(32); RG = Res("G")
    ztile = A.f32(2176); Rzt = Res("zt")
    pers_mark = A.off

    S.dma("sp", lambda e: e.dma_start(out=ident, in_=ident_d), writes=[Rid])
    S.op("dve", lambda e: e.tensor_copy(out=identb, in_=ident), reads=[Rid], writes=[Rid])

    RC = {}

    def cast(dst, src, rows, step, key):
        if key not in RC:
            sem = S._mk("C_" + key)
            S.nobarrier.add(sem)
            RC[key] = (Res("cast_" + key), sem)
        rc, sem = RC[key]
        for r0 in range(0, rows, step):
            S.dma("pool", lambda e, r0=r0: e.dma_start(out=dst[r0:r0 + step], in_=src[r0:r0 + step]), sem=sem)
        rc.w = {sem: S.cnt[sem]}

    def issue_casts(l, part):
        if part == 0:
            cast(winb[l][:, 0:DIN], w_in[l], D, 256, "win%d" % l)
            cast(pwb[l], conv_pw[l], 512, 512, "pw%d" % l)
            return
        cast(woutb[l], w_out[l], D, 512, "wout%d" % l)
        cast(upb[l], ffn_up[l], D, 128, "up%d" % l)
        cast(downb[l], ffn_down[l], DFF, 512, "down%d" % l)

    def RCW(key):
        return RC[key][0]

    def phase_mods(l):
        A.off = pers_mark
        sT = A.f32(32); RsT = Res("sT")
        sTs = A.f32(32)
        bada2 = A.f32(6 * D, parts=2); Rb = Res("bada2")
        m = A.f32(6 * D, parts=2); Rm = Res("m")
        ng = A.f32(32); Rng = Res("ng")
        wt = [A.f32(16 * 512) for _ in range(2)]; Rwt = [Res("wt%d" % i) for i in range(2)]
        S.dma("sp", lambda e: e.dma_start(out=sT, in_=cT.rearrange("p k m -> p (k m)")), writes=[RsT])
        S.dma("sp", lambda e: e.dma_start(out=bada2, in_=b_ada[l:l + 1, :].broadcast_to([2, 6 * D])), writes=[Rb])
        S.dma("sp", lambda e: e.dma_start(out=ng[:, 0:16], in_=n1g[l]), writes=[Rng])
        S.dma("sp", lambda e: e.dma_start(out=ng[:, 16:32], in_=n2g[l]), writes=[Rng])
        S.op("act", lambda e: e.activation(out=sTs, in_=sT, func=AF.Silu), reads=[RsT], writes=[RsT])
        for nb in range(24):
            sl = nb % 2
            S.dma("sp", lambda e, nb=nb, sl=sl: e.dma_start(
                out=wt[sl].rearrange("p (k n) -> p k n", k=16),
                in_=w_ada[l][:, nb * 512:(nb + 1) * 512].rearrange("(k p) n -> p k n", p=128)), writes=[Rwt[sl]])
            for k in range(16):
                S.op("pe", lambda e, nb=nb, sl=sl, k=k: e.matmul(PS[sl][0:2, :], lhsT=sTs[:, 2 * k:2 * k + 2],
                                                               rhs=wt[sl][:, k * 512:(k + 1) * 512], start=(k == 0), stop=(k == 15)),
                     reads=[RsT, Rwt[sl]], writes=[RPS[sl]])
            S.op("dve", lambda e, nb=nb, sl=sl: e.tensor_tensor(out=m[:, nb * 512:(nb + 1) * 512], in0=PS[sl][0:2, :],
                                                                in1=bada2[:, nb * 512:(nb + 1) * 512], op=ALU.add),
                 reads=[RPS[sl], Rb], writes=[Rm])
        for s0 in (1, 4):
            S.op("dve", lambda e, s0=s0: e.tensor_scalar_add(out=m[:, s0 * D:(s0 + 1) * D], in0=m[:, s0 * D:(s0 + 1) * D], scalar1=1.0),
                 reads=[Rm], writes=[Rm])
        S.dma("pool", lambda e: e.dma_start(out=mods_d, in_=m), reads=[Rm])
        for j in range(96):
            S.op("pe", lambda e, j=j: e.transpose(PS[2][:, 2 * j:2 * j + 2], m[0:2, j * 128:(j + 1) * 128], ident[0:2, 0:2]),
                 reads=[Rm, Rid], writes=[RPS[2]])
        S.op("dve", lambda e: e.tensor_copy(out=modT, in_=PS[2][:, 0:192]), reads=[RPS[2]], writes=[RmodT])
        m3 = modT.rearrange("p (j m) -> p j m", m=2)
        for cond in range(2):
            S.op("dve", lambda e, cond=cond: e.tensor_tensor(out=G1T.rearrange("p (c m) -> p c m", m=2)[:, :, cond],
                                                             in0=ng[:, 0:16], in1=m3[:, 16:32, cond], op=ALU.mult),
                 reads=[RmodT, Rng], writes=[RG])
            S.op("dve", lambda e, cond=cond: e.tensor_tensor(out=G2T.rearrange("p (c m) -> p c m", m=2)[:, :, cond],
                                                             in0=ng[:, 16:32], in1=m3[:, 64:80, cond], op=ALU.mult),
                 reads=[RmodT, Rng], writes=[RG])
        S.barrier()

    def modcol(sec, c, cond):
        j = sec * 16 + c
        return modT[:, 2 * j + cond:2 * j + cond + 1]

    def phase_inproj(l, xlat, xctx):
        A.off = pers_mark
        rp = A.f32(4 * 512); Rrp = Res("rp")
        xs = [A.f32(D) for _ in range(4)]; Rxs = [Res("xs%d" % i) for i in range(4)]
        hT = A.bf16(16 * 512); RhT = Res("hT")
        junk = A.bf16(D); Rjunk = Res("junk")
        wb = [A.bf16(16 * 512) for _ in range(4)]; Rwb = [Res("wb%d" % i) for i in range(4)]
        stb = [A.bf16(512) for _ in range(3)]; Rstb = [Res("stb%d" % i) for i in range(3)]
        stf = [A.f32(512) for _ in range(3)]; Rstf = [Res("stf%d" % i) for i in range(3)]
        tmp = [A.f32(512) for _ in range(4)]; Rtmp = [Res("tmp%d" % i) for i in range(4)]
        ss = A.f32(8); Rss = Res("ss")
        cnt = {"w": 0, "sb": 0, "sf": 0, "tmp": 0, "ps": 0}

        def nxt(key, n):
            v = cnt[key]; cnt[key] = (v + 1) % n; return v

        def load_w(cb):
            sl = nxt("w", 4)
            S.dma("sp", lambda e: e.dma_start(out=wb[sl].rearrange("p (k n) -> p k n", k=16),
                                              in_=winb[l][:, cb * 512:(cb + 1) * 512].rearrange("(k p) n -> p k n", p=128)),
                  writes=[Rwb[sl]], reads=[RCW("win%d" % l)])
            return wb[sl], Rwb[sl]

        def psb():
            i = 2 + nxt("ps", 6)
            return PS[i], RPS[i]

        tiles = [(0, t0, min(512, L - t0)) for t0 in range(0, L, 512)] + [(1, t0, min(512, CTX - t0)) for t0 in range(0, CTX, 512)]
        for (isctx, t0, n) in tiles:
            cond = isctx
            nb = n // 128
            src = xctx if isctx else xlat
            tok0 = L + t0 if isctx else t0
            for tb in range(nb):
                S.dma("sp", lambda e, tb=tb: e.dma_start(out=xs[tb], in_=src[t0 + tb * 128:t0 + (tb + 1) * 128, :]), writes=[Rxs[tb]])
            if not isctx:
                S.dma("sp", lambda e: e.dma_start(out=rp.rearrange("p (a t) -> p a t", a=4)[:, :, 0:n],
                                                  in_=rope_d[:, :, t0:t0 + n].rearrange("a p t -> p a t")), writes=[Rrp])
            for tb in range(nb):
                S.op("act", lambda e, tb=tb: e.activation(out=junk, in_=xs[tb], func=AF.Square, accum_out=ss[:, tb:tb + 1]),
                     reads=[Rxs[tb]], writes=[Rjunk, Rss])
            S.op("act", lambda e: e.activation(out=ss[:, 4:4 + nb], in_=ss[:, 0:nb], func=AF.Sqrt, scale=1.0 / D, bias=EPS),
                 reads=[Rss], writes=[Rss])
            S.op("dve", lambda e: e.reciprocal(out=ss[:, 4:4 + nb], in_=ss[:, 4:4 + nb]), reads=[Rss], writes=[Rss])
            for tb in range(nb):
                S.op("dve", lambda e, tb=tb: e.tensor_scalar(out=xs[tb], in0=xs[tb], scalar1=ss[:, 4 + tb:5 + tb], scalar2=None, op0=ALU.mult),
                     reads=[Rxs[tb], Rss], writes=[Rxs[tb]])
            for c in range(16):
                bk = c % 2
                for tb in range(nb):
                    S.op("pe", lambda e, tb=tb, c=c, bk=bk: e.transpose(PS[bk][:, tb * 128:(tb + 1) * 128], xs[tb][:, c * 128:(c + 1) * 128], ident),
                         reads=[Rxs[tb], Rid], writes=[RPS[bk]])
                if c % 2 == 0:
                    S.op("dve", lambda e, c=c, bk=bk: e.tensor_scalar(out=hT[:, c * 512:c * 512 + n], in0=PS[bk][:, 0:n],
                                                                      scalar1=G1T[:, 2 * c + cond:2 * c + cond + 1], scalar2=modcol(0, c, cond),
                                                                      op0=ALU.mult, op1=ALU.add),
                         reads=[RPS[bk], RG, RmodT], writes=[RhT])
                else:
                    S.op("act", lambda e, c=c, bk=bk: e.activation(out=hT[:, c * 512:c * 512 + n], in_=PS[bk][:, 0:n], func=AF.Identity,
                                                                   scale=G1T[:, 2 * c + cond:2 * c + cond + 1], bias=modcol(0, c, cond)),
                         reads=[RPS[bk], RG, RmodT], writes=[RhT])

            def fm_chunk(wa, Rwa, j):
                ps, Rps = psb()
                for k in range(16):
                    S.op("pe", lambda e, k=k: e.matmul(ps[:, 0:n], lhsT=wa[:, k * 512 + j * 128:k * 512 + (j + 1) * 128],
                                                       rhs=hT[:, k * 512:k * 512 + n], start=(k == 0), stop=(k == 15)),
                         reads=[Rwa, RhT], writes=[Rps])
                return ps, Rps

            def tm_block(wa, Rwa, tb):
                ps, Rps = psb()
                for k in range(16):
                    S.op("pe", lambda e, k=k: e.matmul(ps[:, :], lhsT=hT[:, k * 512 + tb * 128:k * 512 + (tb + 1) * 128],
                                                       rhs=wa[:, k * 512:(k + 1) * 512], start=(k == 0), stop=(k == 15)),
                         reads=[Rwa, RhT], writes=[Rps])
                return ps, Rps

            def store_fm(dst, j, stage, Rst):
                S.dma("pool", lambda e: e.dma_start(out=dst[j * 128:(j + 1) * 128, tok0:tok0 + n], in_=stage[:, 0:n]), reads=[Rst])

            def store_tm(dst, tb, c0, stage, Rst):
                S.dma("pool", lambda e: e.dma_start(out=dst[tok0 + tb * 128:tok0 + (tb + 1) * 128, c0:c0 + 512], in_=stage[:, 0:512]), reads=[Rst])

            for qi, (cb, cbp, dst) in enumerate(((0, 11, qT_d), (1, 12, kT_d))):
                wa, Rwa = load_w(cb)
                if not isctx:
                    psl = nxt("w", 4)
                    wp, Rwp = wb[psl], Rwb[psl]
                    for bb in range(2):
                        S.op("act", lambda e: e.activation(out=wp.rearrange("p (a b e) -> p a b e", b=2, e=32)[:, :, 1 - bb, :],
                                                           in_=wa.rearrange("p (a b e) -> p a b e", b=2, e=32)[:, :, bb, :], func=AF.Copy),
                             reads=[Rwa], writes=[Rwp])
                for j in range(4):
                    pa, Rpa = fm_chunk(wa, Rwa, j)
                    sb = nxt("sb", 3)
                    if not isctx:
                        pb, Rpb = fm_chunk(wp, Rwp, j)
                        t1 = nxt("tmp", 4); t2 = nxt("tmp", 4)
                        S.op("dve", lambda e, t1=t1, pa=pa: e.tensor_tensor(out=tmp[t1][:, 0:n], in0=pa[:, 0:n], in1=rp[:, (2 * qi) * 512:(2 * qi) * 512 + n], op=ALU.mult),
                             reads=[Rpa, Rrp], writes=[Rtmp[t1]])
                        S.op("dve", lambda e, t2=t2, pb=pb: e.tensor_tensor(out=tmp[t2][:, 0:n], in0=pb[:, 0:n], in1=rp[:, (2 * qi + 1) * 512:(2 * qi + 1) * 512 + n], op=ALU.mult),
                             reads=[Rpb, Rrp], writes=[Rtmp[t2]])
                        S.op("pool", lambda e, t1=t1, t2=t2, sb=sb: e.tensor_tensor(out=stb[sb][:, 0:n], in0=tmp[t1][:, 0:n], in1=tmp[t2][:, 0:n], op=ALU.add),
                             reads=[Rtmp[t1], Rtmp[t2]], writes=[Rstb[sb]])
                    else:
                        S.op("act", lambda e, sb=sb, pa=pa: e.activation(out=stb[sb][:, 0:n], in_=pa[:, 0:n], func=AF.Copy,
                                                                         scale=(128 ** -0.5 if qi == 0 else 1.0)),
                             reads=[Rpa], writes=[Rstb[sb]])
                    store_fm(dst, j, stb[sb], Rstb[sb])
            for half in range(2):
                wa, Rwa = load_w(2 + half)
                for tb in range(nb):
                    ps, Rps = tm_block(wa, Rwa, tb)
                    sb = nxt("sb", 3)
                    S.op("act", lambda e, sb=sb, ps=ps: e.activation(out=stb[sb], in_=ps, func=AF.Copy), reads=[Rps], writes=[Rstb[sb]])
                    store_tm(v_d, tb, half * 512, stb[sb], Rstb[sb])
            for half in range(2):
                wa, Rwa = load_w(4 + half)
                for tb in range(nb):
                    ps, Rps = tm_block(wa, Rwa, tb)
                    sf = nxt("sf", 3)
                    S.op("act", lambda e, sf=sf, ps=ps: e.activation(out=stf[sf], in_=ps, func=AF.Silu), reads=[Rps], writes=[Rstf[sf]])
                    store_tm(sg_d, tb, half * 512, stf[sf], Rstf[sf])
            wa, Rwa = load_w(6)
            wp, Rwp = load_w(7)
            for j in range(4):
                pa, Rpa = fm_chunk(wa, Rwa, j)
                pb, Rpb = fm_chunk(wp, Rwp, j)
                t1 = nxt("tmp", 4); sf = nxt("sf", 3)
                S.op("act", lambda e, t1=t1, pb=pb: e.activation(out=tmp[t1][:, 0:n], in_=pb[:, 0:n], func=AF.Sigmoid), reads=[Rpb], writes=[Rtmp[t1]])
                S.op("dve", lambda e, t1=t1, pa=pa, sf=sf: e.tensor_tensor(out=stf[sf][:, 0:n], in0=pa[:, 0:n], in1=tmp[t1][:, 0:n], op=ALU.mult),
                     reads=[Rpa, Rtmp[t1]], writes=[Rstf[sf]])
                store_fm(uT_d, j, stf[sf], Rstf[sf])
            for cb, dst, scl in ((8, nqT_d, 128 ** -0.5), (9, nkT_d, 1.0)):
                wa, Rwa = load_w(cb)
                for j in range(4):
                    pa, Rpa = fm_chunk(wa, Rwa, j)
                    sb = nxt("sb", 3)
                    S.op("act", lambda e, sb=sb, pa=pa, scl=scl: e.activation(out=stb[sb][:, 0:n], in_=pa[:, 0:n], func=AF.Copy, scale=scl),
                         reads=[Rpa], writes=[Rstb[sb]])
                    store_fm(dst, j, stb[sb], Rstb[sb])
            wa, Rwa = load_w(10)
            for tb in range(nb):
                ps, Rps = tm_block(wa, Rwa, tb)
                sb = nxt("sb", 3)
                S.op("dve", lambda e, sb=sb, ps=ps: e.tensor_copy(out=stb[sb], in_=ps), reads=[Rps], writes=[Rstb[sb]])
                store_tm(nv_d, tb, 0, stb[sb], Rstb[sb])
        S.barrier()

    PSB = [p_.bitcast(BF16) for p_ in PS]

    def phase_ret(l, with_ctx, A):
        RB0 = Res("ret_b0"); RB1 = Res("ret_b1"); RP_m = Res("rp_m")
        rd = A.f32(8); lg = A.f32(8); Rlg = Res("lg")
        cdt = A.f32(4 * 128); xir = A.f32(2 * 128); zc = A.f32(2); Rc = Res("retconst")
        DT = A.f32(8 * 128); XI = A.f32(8 * 128); zg = A.f32(16); Rtab = Res("rettab")
        gngt = A.f32(1024); Rgn = Res("gngt")
        Sf = [A.f32(256) for _ in range(4)]; Sb = [A.bf16(256) for _ in range(4)]; RS = [Res("S%d" % h) for h in range(4)]
        RSb = [Res("Sb%d" % h) for h in range(4)]
        qc = [A.bf16(512) for _ in range(2)]; Rqc = [Res("qc%d" % i) for i in range(2)]
        kc = [A.bf16(512) for _ in range(2)]; Rkc = [Res("kc%d" % i) for i in range(2)]
        vc = [A.bf16(1024) for _ in range(2)]; Rvc = [Res("vc%d" % i) for i in range(2)]
        ofc = [A.f32(1024) for _ in range(2)]; Rofc = [Res("ofc%d" % i) for i in range(2)]
        sgc = [A.f32(1024) for _ in range(2)]; Rsgc = [Res("sgc%d" % i) for i in range(2)]
        ob = [A.f32(1024) for _ in range(2)]; Rob = [Res("ob%d" % i) for i in range(2)]
        innb = [A.bf16(128) for _ in range(2)]; Rinnb = [Res("innb%d" % i) for i in range(2)]
        qx = [A.bf16(128) for _ in range(2)]; Rqx = [Res("qx%d" % i) for i in range(2)]
        kz = [A.bf16(128) for _ in range(2)]; Rkz = [Res("kz%d" % i) for i in range(2)]
        ybf = A.bf16(1024); Rybf = Res("ybf")
        mst = [A.bf16(1024) for _ in range(2)]; Rmst = [Res("mst%d" % i) for i in range(2)]
        st = A.f32(16); Rst = Res("gnstat")
        junk = A.f32(256); Rjunk = Res("junk")

        S.dma("sp", lambda e: e.dma_start(out=rd, in_=ret_decay[l:l + 1, :].broadcast_to([128, 8])), writes=[Rlg])
        S.dma("sp", lambda e: e.dma_start(out=cdt.rearrange("p (a i) -> p a i", a=4), in_=dec_d.rearrange("a p i -> p a i")), writes=[Rc])
        S.dma("sp", lambda e: e.dma_start(out=xir.rearrange("p (a i) -> p a i", a=2), in_=xirow_d.rearrange("a p i -> p a i")), writes=[Rc])
        S.dma("sp", lambda e: e.dma_start(out=zc, in_=zcol_d), writes=[Rc])
        S.dma("sp", lambda e: e.dma_start(out=gngt, in_=gng[l:l + 1, :].broadcast_to([128, 1024])), writes=[Rgn])
        S.op("act", lambda e: e.activation(out=lg, in_=rd, func=AF.Exp, scale=-1.0), reads=[Rlg], writes=[Rlg])
        S.op("act", lambda e: e.activation(out=lg, in_=lg, func=AF.Ln, bias=1.0), reads=[Rlg], writes=[Rlg])
        S.op("dve", lambda e: e.tensor_scalar(out=lg, in0=lg, scalar1=-1.0, scalar2=None, op0=ALU.mult), reads=[Rlg], writes=[Rlg])
        for dr in range(2):
            for h in range(4):
                col = dr * 4 + h
                S.op("act", lambda e: e.activation(out=DT[:, col * 128:(col + 1) * 128], in_=cdt[:, (2 * dr) * 128:(2 * dr + 1) * 128],
                                                   func=AF.Exp, scale=lg[:, col:col + 1]), reads=[Rlg, Rc], writes=[Rtab])
                S.op("dve", lambda e: e.tensor_tensor(out=DT[:, col * 128:(col + 1) * 128], in0=DT[:, col * 128:(col + 1) * 128],
                                                      in1=cdt[:, (2 * dr + 1) * 128:(2 * dr + 2) * 128], op=ALU.mult), reads=[Rtab, Rc], writes=[Rtab])
                S.op("act", lambda e: e.activation(out=XI[:, col * 128:(col + 1) * 128], in_=xir[:, dr * 128:(dr + 1) * 128],
                                                   func=AF.Exp, scale=lg[:, col:col + 1]), reads=[Rlg, Rc], writes=[Rtab])
                S.op("act", lambda e: e.activation(out=zg[:, col:col + 1], in_=zc[:, dr:dr + 1], func=AF.Exp, scale=lg[:, col:col + 1]),
                     reads=[Rlg, Rc], writes=[Rtab])
                S.op("act", lambda e: e.activation(out=zg[:, 8 + col:9 + col], in_=lg[:, col:col + 1], func=AF.Exp, scale=128.0),
                     reads=[Rlg, Rc], writes=[Rtab])
        nlc = L // 128; ncc = CTX // 128
        step = [0]
        for dr in range(2):
            for h in range(4):
                S.op("dve", lambda e: e.memset(Sf[h], 0.0), writes=[RS[h]])
                S.op("pool", lambda e: e.memset(Sb[h], 0.0), writes=[RSb[h]])
            if dr == 0:
                order = [(1, L + i * 128) for i in range(ncc)] + [(0, i * 128) for i in range(nlc)]
            else:
                order = [(1, L + i * 128) for i in reversed(range(ncc))] + [(0, i * 128) for i in reversed(range(nlc))]
            for (isctx, tok0) in order:
                need_out = (not isctx) or with_ctx
                sl = step[0] % 2; step[0] += 1
                S.dma("sp", lambda e: e.dma_start(out=kc[sl].rearrange("p (h t) -> p h t", h=4),
                                                  in_=kT_d[:, tok0:tok0 + 128].rearrange("(h d) t -> d h t", d=128)), writes=[Rkc[sl]])
                S.dma("sp", lambda e: e.dma_start(out=vc[sl], in_=v_d[tok0:tok0 + 128, :]), writes=[Rvc[sl]])
                if need_out:
                    S.dma("sp", lambda e: e.dma_start(out=qc[sl].rearrange("p (h t) -> p h t", h=4),
                                                      in_=qT_d[:, tok0:tok0 + 128].rearrange("(h d) t -> d h t", d=128)), writes=[Rqc[sl]])
                    if dr == 1:
                        S.dma("sp", lambda e: e.dma_start(out=ofc[sl], in_=of_d[tok0:tok0 + 128, :]), writes=[Rofc[sl]])
                        S.dma("sp", lambda e: e.dma_start(out=sgc[sl], in_=sg_d[tok0:tok0 + 128, :]), writes=[Rsgc[sl]])
                for h in range(4):
                    col = dr * 4 + h
                    hs = h % 2
                    kh = kc[sl][:, h * 128:(h + 1) * 128]
                    vh = vc[sl][:, h * 256:(h + 1) * 256]
                    if need_out:
                        qh = qc[sl][:, h * 128:(h + 1) * 128]
                        S.op("pe", lambda e: e.matmul(PS[0][:, 0:128], lhsT=kh, rhs=qh, start=True, stop=True),
                             reads=[Rkc[sl], Rqc[sl]], writes=[RB0])
                        yield
                        S.op("dve", lambda e: e.tensor_tensor(out=innb[hs], in0=PS[0][:, 0:128], in1=DT[:, col * 128:(col + 1) * 128], op=ALU.mult),
                             reads=[RB0, Rtab], writes=[Rinnb[hs]])
                        S.op("pool", lambda e: e.tensor_tensor(out=qx[hs], in0=qh, in1=XI[:, col * 128:(col + 1) * 128], op=ALU.mult),
                             reads=[Rqc[sl], Rtab], writes=[Rqx[hs]])
                        yield
                        S.op("pe", lambda e: e.matmul(PS[1][:, hs * 256:(hs + 1) * 256], lhsT=innb[hs], rhs=vh, start=True, stop=False),
                             reads=[Rinnb[hs], Rvc[sl]], writes=[RB1])
                        S.op("pe", lambda e: e.matmul(PS[1][:, hs * 256:(hs + 1) * 256], lhsT=qx[hs], rhs=Sb[h], start=False, stop=True),
                             reads=[Rqx[hs], RSb[h]], writes=[RB1])
                    S.op("pe", lambda e: e.transpose(PSB[0][:, 256 + hs * 128:256 + (hs + 1) * 128], kh, identb), reads=[Rkc[sl], Rid], writes=[RB0])
                    yield
                    S.op("act", lambda e: e.activation(out=kz[hs], in_=PSB[0][:, 256 + hs * 128:256 + (hs + 1) * 128], func=AF.Identity, scale=zg[:, col:col + 1]),
                         reads=[RB0, Rtab], writes=[Rkz[hs]])
                    yield
                    S.op("pe", lambda e: e.matmul(PS[0][:, 256:512], lhsT=kz[hs], rhs=vh, start=True, stop=True),
                         reads=[Rkz[hs], Rvc[sl]], writes=[RB0])
                    yield
                    S.op("dve", lambda e: e.scalar_tensor_tensor(out=Sf[h], in0=Sf[h], scalar=zg[:, 8 + col:9 + col], in1=PS[0][:, 256:512],
                                                                 op0=ALU.mult, op1=ALU.add), reads=[RS[h], RB0, Rtab], writes=[RS[h]])
                    S.op("act", lambda e: e.activation(out=Sb[h], in_=Sf[h], func=AF.Copy), reads=[RS[h]], writes=[RSb[h]])
                    if need_out:
                        if dr == 0:
                            S.op("act", lambda e: e.activation(out=ob[sl][:, h * 256:(h + 1) * 256], in_=PS[1][:, hs * 256:(hs + 1) * 256], func=AF.Copy),
                                 reads=[RB1], writes=[Rob[sl]])
                        else:
                            S.op("dve", lambda e: e.tensor_tensor(out=ob[sl][:, h * 256:(h + 1) * 256], in0=PS[1][:, hs * 256:(hs + 1) * 256],
                                                                  in1=ofc[sl][:, h * 256:(h + 1) * 256], op=ALU.add),
                                 reads=[RB1, Rofc[sl]], writes=[Rob[sl]])
                    yield
                if not need_out:
                    yield
                    continue
                if dr == 0:
                    S.dma("pool", lambda e: e.dma_start(out=of_d[tok0:tok0 + 128, :], in_=ob[sl]), reads=[Rob[sl]])
                    yield
                    continue
                for h in range(4):
                    S.op("act", lambda e: e.activation(out=junk, in_=ob[sl][:, h * 256:(h + 1) * 256], func=AF.Identity, accum_out=st[:, h:h + 1]),
                         reads=[Rob[sl]], writes=[Rjunk, Rst])
                    S.op("act", lambda e: e.activation(out=junk, in_=ob[sl][:, h * 256:(h + 1) * 256], func=AF.Square, accum_out=st[:, 4 + h:5 + h]),
                         reads=[Rob[sl]], writes=[Rjunk, Rst])
                S.op("dve", lambda e: e.tensor_scalar(out=st[:, 0:8], in0=st[:, 0:8], scalar1=1.0 / 256, scalar2=None, op0=ALU.mult), reads=[Rst], writes=[Rst])
                S.op("dve", lambda e: e.tensor_tensor(out=st[:, 8:12], in0=st[:, 0:4], in1=st[:, 0:4], op=ALU.mult), reads=[Rst], writes=[Rst])
                S.op("dve", lambda e: e.tensor_tensor(out=st[:, 8:12], in0=st[:, 4:8], in1=st[:, 8:12], op=ALU.subtract), reads=[Rst], writes=[Rst])
                S.op("act", lambda e: e.activation(out=st[:, 8:12], in_=st[:, 8:12], func=AF.Sqrt, bias=EPS), reads=[Rst], writes=[Rst])
                S.op("dve", lambda e: e.reciprocal(out=st[:, 8:12], in_=st[:, 8:12]), reads=[Rst], writes=[Rst])
                for h in range(4):
                    S.op("dve", lambda e: e.tensor_scalar(out=ob[sl][:, h * 256:(h + 1) * 256], in0=ob[sl][:, h * 256:(h + 1) * 256],
                                                          scalar1=st[:, h:h + 1], scalar2=st[:, 8 + h:9 + h], op0=ALU.subtract, op1=ALU.mult),
                         reads=[Rob[sl], Rst], writes=[Rob[sl]])
                S.op("pool", lambda e: e.tensor_tensor(out=ob[sl], in0=ob[sl], in1=gngt, op=ALU.mult), reads=[Rob[sl], Rgn], writes=[Rob[sl]])
                S.op("dve", lambda e: e.tensor_tensor(out=ybf, in0=ob[sl], in1=sgc[sl], op=ALU.mult), reads=[Rob[sl], Rsgc[sl]], writes=[Rybf])
                for c in range(8):
                    S.op("pe", lambda e: e.transpose(PSB[3][:, c * 128:(c + 1) * 128], ybf[:, c * 128:(c + 1) * 128], identb),
                         reads=[Rybf, Rid], writes=[RP_m])
                S.op("act", lambda e: e.activation(out=mst[sl], in_=PSB[3][:, 0:1024], func=AF.Copy), reads=[RP_m], writes=[Rmst[sl]])
                S.dma("pool", lambda e: e.dma_start(out=mixT_d[0:1024, tok0:tok0 + 128].rearrange("(c p) t -> p c t", p=128),
                                                    in_=mst[sl].rearrange("p (c t) -> p c t", c=8)), reads=[Rmst[sl]])
                yield

    def phase_conv(l, with_ctx, A):
        RPC = Res('rp_conv')
        cw = A.f32(124); cb = A.f32(4); lg_ = A.f32(4); lb_ = A.f32(4); Rcp = Res("convp")
        ones = A.f32(128); Rones = Res("ones")
        pw = A.bf16(4 * 512); Rpw = Res("pw")
        ub = [A.f32(4 * 542) for _ in range(1)]; Rub = [Res("ub%d" % i) for i in range(1)]
        acc = [A.f32(512) for _ in range(4)]; Racc = [Res("acc%d" % i) for i in range(4)]
        sq = [A.f32(512) for _ in range(4)]; Rsq = [Res("sq%d" % i) for i in range(4)]
        rstd = A.f32(512); Rrstd = Res("rstd")
        zb = A.bf16(4 * 512); Rzb = Res("zb")
        stb = [A.bf16(512) for _ in range(2)]; Rstb = [Res("cstb%d" % i) for i in range(2)]
        S.dma("sp", lambda e: e.dma_start(out=cw, in_=cdw[l].rearrange("p c k -> p (c k)")), writes=[Rcp])
        S.dma("sp", lambda e: e.dma_start(out=cb, in_=cdb[l]), writes=[Rcp])
        S.dma("sp", lambda e: e.dma_start(out=lg_, in_=lng[l]), writes=[Rcp])
        S.dma("sp", lambda e: e.dma_start(out=lb_, in_=lnb[l]), writes=[Rcp])
        S.dma("sp", lambda e: e.dma_start(out=pw.rearrange("p (k n) -> p k n", k=4), in_=pwb[l].rearrange("(k p) n -> p k n", p=128)), writes=[Rpw], reads=[RCW("pw%d" % l)])
        S.op("pool", lambda e: e.memset(ones, 1.0 / 512), writes=[Rones])
        tiles = [(0, t0, min(512, L - t0)) for t0 in range(0, L, 512)]
        if with_ctx:
            tiles += [(1, t0, min(512, CTX - t0)) for t0 in range(0, CTX, 512)]
        for ti, (isctx, t0, n) in enumerate(tiles):
            Ls = CTX if isctx else L
            base = L if isctx else 0
            sl = 0
            u3 = ub[sl].rearrange("p (c t) -> p c t", c=4)
            lo = max(t0 - 15, 0); hi = min(t0 + n + 15, Ls)
            off = lo - (t0 - 15)
            if lo != t0 - 15 or hi != t0 + n + 15:
                S.op("pool", lambda e: e.memset(ub[sl], 0.0), writes=[Rub[sl]])
            S.dma("sp", lambda e: e.dma_start(out=u3[:, :, off:off + hi - lo], in_=uT_d[:, base + lo:base + hi].rearrange("(c p) t -> p c t", p=128)),
                  writes=[Rub[sl]])
            for c in range(4):
                S.op("dve", lambda e: e.tensor_scalar(out=acc[c][:, 0:n], in0=u3[:, c, 15:15 + n], scalar1=cw[:, c * 31 + 15:c * 31 + 16],
                                                      scalar2=cb[:, c:c + 1], op0=ALU.mult, op1=ALU.add), reads=[Rub[sl], Rcp], writes=[Racc[c]])
            for k in range(31):
                if k == 15:
                    continue
                for c in range(4):
                    S.op("dve", lambda e: e.scalar_tensor_tensor(out=acc[c][:, 0:n], in0=u3[:, c, k:k + n], scalar=cw[:, c * 31 + k:c * 31 + k + 1],
                                                                 in1=acc[c][:, 0:n], op0=ALU.mult, op1=ALU.add),
                         reads=[Rub[sl], Rcp, Racc[c]], writes=[Racc[c]])
                yield
            for c in range(4):
                S.op("pe", lambda e: e.matmul(PS[6][:, 0:n], lhsT=ones, rhs=acc[c][:, 0:n], start=(c == 0), stop=(c == 3)),
                     reads=[Rones, Racc[c]], writes=[RPC])
            for c in range(4):
                S.op("dve", lambda e: e.tensor_tensor(out=acc[c][:, 0:n], in0=acc[c][:, 0:n], in1=PS[6][:, 0:n], op=ALU.subtract),
                     reads=[RPC, Racc[c]], writes=[Racc[c]])
                S.op("act", lambda e: e.activation(out=sq[c][:, 0:n], in_=acc[c][:, 0:n], func=AF.Square), reads=[Racc[c]], writes=[Rsq[c]])
            for c in range(4):
                S.op("pe", lambda e: e.matmul(PS[6][:, 0:n], lhsT=ones, rhs=sq[c][:, 0:n], start=(c == 0), stop=(c == 3)),
                     reads=[Rones, Rsq[c]], writes=[RPC])
            yield
            S.op("act", lambda e: e.activation(out=rstd[:, 0:n], in_=PS[6][:, 0:n], func=AF.Sqrt, bias=EPS), reads=[RPC], writes=[Rrstd])
            S.op("dve", lambda e: e.reciprocal(out=rstd[:, 0:n], in_=rstd[:, 0:n]), reads=[Rrstd], writes=[Rrstd])
            for c in range(4):
                S.op("dve", lambda e: e.tensor_tensor(out=acc[c][:, 0:n], in0=acc[c][:, 0:n], in1=rstd[:, 0:n], op=ALU.mult),
                     reads=[Rrstd, Racc[c]], writes=[Racc[c]])
                S.op("act", lambda e: e.activation(out=zb[:, c * 512:c * 512 + n], in_=acc[c][:, 0:n], func=AF.Silu,
                                                   scale=lg_[:, c:c + 1], bias=lb_[:, c:c + 1]), reads=[Racc[c], Rcp], writes=[Rzb])
            for co in range(4):
                bk = 6
                for ci in range(4):
                    S.op("pe", lambda e: e.matmul(PS[bk][:, 0:n], lhsT=pw[:, ci * 512 + co * 128:ci * 512 + (co + 1) * 128],
                                                  rhs=zb[:, ci * 512:ci * 512 + n], start=(ci == 0), stop=(ci == 3)),
                         reads=[Rpw, Rzb], writes=[RPC])
                ss_ = co % 2
                S.op("act", lambda e: e.activation(out=stb[ss_][:, 0:n], in_=PS[bk][:, 0:n], func=AF.Copy), reads=[RPC], writes=[Rstb[ss_]])
                S.dma("pool", lambda e: e.dma_start(out=mixT_d[1024 + co * 128:1024 + (co + 1) * 128, base + t0:base + t0 + n], in_=stb[ss_][:, 0:n]),
                      reads=[Rstb[ss_]])
                yield

    def na_zero():
        S.op("pool", lambda e: e.memset(ztile, 0.0), writes=[Rzt])
        S.dma("sp", lambda e: e.dma_start(out=bass.AP(toep_d.tensor, 0, [[2176, 128], [1, 2176]]), in_=ztile), reads=[Rzt])

    def na_diag(l):
        dsem = S.dma_sem()
        for h in range(4):
            for ro in range(15):
                dst = bass.AP(toep_d.tensor, h * 69632 + (ro + 1) * 64 - 15, [[1089, 64], [1, 31]])
                S.dma("pool", lambda e: e.dma_start(out=dst, in_=na_rpb[l, h, ro:ro + 1, :].broadcast_to([64, 31])), sem=dsem)

    def phase_na(l, with_ctx, A):
        RP_sc = Res("rp_sc"); RP_pt = Res("rp_pt"); RP_no = Res("rp_no")
        TB = A.f32(20 * 576); RTB = Res("TB")
        nam = A.f32(5 * 576); Rnam = Res("nam")
        ckT = A.bf16(4 * CTX); Rck = Res("ckT")
        cv = A.bf16((CTX // 128) * 512); Rcv = Res("cv")
        qm = [A.bf16(512) for _ in range(2)]; Rqm = [Res("qm%d" % i) for i in range(2)]
        km = [A.bf16(4 * 576) for _ in range(2)]; Rkm = [Res("km%d" % i) for i in range(2)]
        vm = [A.bf16(5 * 512) for _ in range(2)]; Rvm = [Res("vm%d" % i) for i in range(2)]
        scs = [A.f32(832)] * 2; Rscs = [Res("scs")] * 2
        pb = [A.bf16(832) for _ in range(2)]; Rpb = [Res("pb%d" % i) for i in range(2)]
        pT = [A.bf16(7 * 128) for _ in range(2)]; RpT = [Res("pT%d" % i) for i in range(2)]
        nst = [A.bf16(512) for _ in range(2)]; Rnst = [Res("nst%d" % i) for i in range(2)]
        sm = A.f32(16); Rsm = Res("sm")
        types = na_types(R)
        for h in range(4):
            for ti, (mrep, lo) in enumerate(types):
                idx = h * 5 + ti
                for qr in range(2):
                    ro0 = lo - (2 * mrep + qr) + 7
                    src = bass.AP(toep_d.tensor, h * 69632 + (ro0 + 1) * 64, [[1088, 64], [1, 576]])
                    S.dma("sp", lambda e: e.dma_start(out=TB[qr * 64:(qr + 1) * 64, idx * 576:(idx + 1) * 576], in_=src), writes=[RTB])
        S.dma("sp", lambda e: e.dma_start(out=nam.rearrange("p (a k) -> p a k", a=5), in_=nam_d.rearrange("a p k -> p a k")), writes=[Rnam])
        for h in range(4):
            S.op("dve", lambda e: e.tensor_tensor(out=TB[:, h * 2880:(h + 1) * 2880], in0=TB[:, h * 2880:(h + 1) * 2880], in1=nam, op=ALU.add),
                 reads=[RTB, Rnam], writes=[RTB])
        S.dma("sp", lambda e: e.dma_start(out=ckT.rearrange("p (h t) -> p h t", h=4), in_=nkT_d[:, L:T].rearrange("(h d) t -> d h t", d=128)), writes=[Rck])
        S.dma("sp", lambda e: e.dma_start(out=cv.rearrange("p (c f) -> p c f", f=512), in_=nv_d[L:T, :].rearrange("(c p) f -> p c f", p=128)), writes=[Rcv])
        cnt = [0]

        def na_block(sl, tokdst, segs, vch):
            ntot = sum(s_[1] for s_ in segs)
            for h in range(4):
                i = cnt[0] % 2; cnt[0] += 1
                sc = psum_t[:, 4 * 512:4 * 512 + 1024]
                ops_ = PS[2][:, 0:128]
                o = 0
                for (rf, ncol, bf_) in segs:
                    c0 = 0
                    while c0 < ncol:
                        w_ = min(ncol - c0, 512 - (o % 512))
                        S.op("pe", lambda e: e.matmul(sc[:, o:o + w_], lhsT=qm[sl][:, h * 128:(h + 1) * 128], rhs=rf(h)[:, c0:c0 + w_], start=True, stop=True),
                             reads=[Rqm[sl], Rkm[sl], Rck], writes=[RP_sc])
                        o += w_; c0 += w_
                yield
                o = 0
                for (rf, ncol, bf_) in segs:
                    if bf_ is not None:
                        S.op("dve", lambda e: e.tensor_tensor(out=scs[i][:, o:o + ncol], in0=sc[:, o:o + ncol], in1=bf_(h), op=ALU.add),
                             reads=[RP_sc, RTB], writes=[Rscs[i]])
                    else:
                        S.op("act", lambda e: e.activation(out=scs[i][:, o:o + ncol], in_=sc[:, o:o + ncol], func=AF.Copy),
                             reads=[RP_sc], writes=[Rscs[i]])
                    o += ncol
                yield
                S.op("dve", lambda e: e.reduce_max(out=sm[:, i:i + 1], in_=scs[i][:, 0:ntot], axis=AX.X), reads=[Rscs[i]], writes=[Rsm])
                S.op("dve", lambda e: e.tensor_scalar(out=sm[:, i:i + 1], in0=sm[:, i:i + 1], scalar1=-1.0, scalar2=None, op0=ALU.mult), reads=[Rsm], writes=[Rsm])
                yield
                S.op("act", lambda e: e.activation(out=scs[i][:, 0:ntot], in_=scs[i][:, 0:ntot], func=AF.Exp, bias=sm[:, i:i + 1],
                                                   accum_out=sm[:, 4 + i:5 + i]), reads=[Rscs[i], Rsm], writes=[Rscs[i], Rsm])
                yield
                S.op("dve", lambda e: e.reciprocal(out=sm[:, 4 + i:5 + i], in_=sm[:, 4 + i:5 + i]), reads=[Rsm], writes=[Rsm])
                S.op("dve", lambda e: e.tensor_scalar(out=pb[i][:, 0:ntot], in0=scs[i][:, 0:ntot], scalar1=sm[:, 4 + i:5 + i], scalar2=None, op0=ALU.mult),
                     reads=[Rscs[i], Rsm], writes=[Rpb[i]])
                yield
                bt = 7
                o = 0
                for ci, (vf, nk) in enumerate(vch):
                    S.op("pe", lambda e: e.transpose(PSB[bt][0:nk, ci * 128:(ci + 1) * 128], pb[i][:, o:o + nk], identb),
                         reads=[Rpb[i], Rid], writes=[RP_pt])
                    o += nk
                nch = len(vch)
                yield
                S.op("act", lambda e: e.activation(out=pT[i][:, 0:nch * 128], in_=PSB[bt][:, 0:nch * 128], func=AF.Copy), reads=[RP_pt], writes=[RpT[i]])
                yield
                for ci, (vf, nk) in enumerate(vch):
                    S.op("pe", lambda e: e.matmul(ops_, lhsT=vf(h)[0:nk, :], rhs=pT[i][0:nk, ci * 128:(ci + 1) * 128],
                                                  start=(ci == 0), stop=(ci == nch - 1)), reads=[RpT[i], Rvm[sl], Rcv], writes=[RP_no])
                S.op("dve", lambda e: e.tensor_copy(out=nst[sl][:, h * 128:(h + 1) * 128], in_=ops_), reads=[RP_no], writes=[Rnst[sl]])
                yield
            S.dma("pool", lambda e: e.dma_start(out=mixT_d[1536:2048, tokdst:tokdst + 128].rearrange("(h d) t -> d h t", d=128),
                                                in_=nst[sl].rearrange("p (h t) -> p h t", h=4)), reads=[Rnst[sl]])

        ctx_v = [((lambda h, c=c: cv[:, c * 512 + h * 128:c * 512 + (h + 1) * 128]), 128) for c in range(CTX // 128)]
        ctx_seg = ((lambda h: ckT[:, h * CTX:(h + 1) * CTX]), CTX, None)
        for m in range(R // 2):
            ti, lo = na_type_of(m, R)
            sl = m % 2
            S.dma("sp", lambda e: e.dma_start(out=qm[sl].rearrange("p (h t) -> p h t", h=4),
                                              in_=nqT_d[:, m * 128:(m + 1) * 128].rearrange("(h d) t -> d h t", d=128)), writes=[Rqm[sl]])
            S.dma("sp", lambda e: e.dma_start(out=km[sl].rearrange("p (h t) -> p h t", h=4),
                                              in_=nkT_d[:, lo * 64:lo * 64 + 576].rearrange("(h d) t -> d h t", d=128)), writes=[Rkm[sl]])
            S.dma("sp", lambda e: e.dma_start(out=vm[sl][:, 0:2048].rearrange("p (c f) -> p c f", f=512),
                                              in_=nv_d[lo * 64:lo * 64 + 512, :].rearrange("(c p) f -> p c f", p=128)), writes=[Rvm[sl]])
            S.dma("sp", lambda e: e.dma_start(out=vm[sl][0:64, 2048:2560], in_=nv_d[lo * 64 + 512:lo * 64 + 576, :]), writes=[Rvm[sl]])
            kseg = ((lambda h, sl=sl: km[sl][:, h * 576:(h + 1) * 576]), 576,
                    (lambda h, ti=ti: TB[:, (h * 5 + ti) * 576:(h * 5 + ti + 1) * 576]))
            vch = [((lambda h, c=c, sl=sl: vm[sl][:, c * 512 + h * 128:c * 512 + (h + 1) * 128]), 128) for c in range(4)]
            vch += [((lambda h, sl=sl: vm[sl][:, 2048 + h * 128:2048 + (h + 1) * 128]), 64)]
            yield from na_block(sl, m * 128, [kseg, ctx_seg], vch + ctx_v)
        if with_ctx:
            for qb in range(CTX // 128):
                sl = qb % 2
                S.dma("sp", lambda e: e.dma_start(out=qm[sl].rearrange("p (h t) -> p h t", h=4),
                                                  in_=nqT_d[:, L + qb * 128:L + (qb + 1) * 128].rearrange("(h d) t -> d h t", d=128)), writes=[Rqm[sl]])
                yield from na_block(sl, L + qb * 128, [ctx_seg], ctx_v)

    def phase_ffn(l, with_ctx, xlat, xctx, last):
        A.off = pers_mark
        fw = A.f32(264); fb = A.f32(88); Rfp = Res("ffnp")
        xs = [A.f32(D) for _ in range(4)]; Rxs = [Res("fxs%d" % i) for i in range(4)]
        xn = [A.f32(D) for _ in range(2)]; Rxn = [Res("fxn%d" % i) for i in range(2)]
        hT = A.bf16(16 * 512); RhT = Res("fhT")
        aT = A.bf16(22 * 512); RaT = Res("aT")
        mt = aT[:, 0:16 * 512]; RmT = RaT
        upw = [A.bf16(16 * 256) for _ in range(4)]; Rupw = [Res("upw%d" % i) for i in range(4)]
        wdn = [A.bf16(16 * 512) for _ in range(3)]; Rwdn = [Res("wdn%d" % i) for i in range(3)]
        gt = [A.f32(D) for _ in range(2)]; Rgt = [Res("gt%d" % i) for i in range(2)]
        yv = [A.f32(512) for _ in range(2)]; Ryv = [Res("yv%d" % i) for i in range(2)]
        yg = [A.f32(512) for _ in range(2)]; Ryg = [Res("yg%d" % i) for i in range(2)]
        tmp = [A.f32(512) for _ in range(2)]; Rtmp = [Res("ftmp%d" % i) for i in range(2)]
        ss = A.f32(16); Rss = Res("fss")
        S.dma("sp", lambda e: e.dma_start(out=fw, in_=fdw[l].rearrange("p c k -> p (c k)")), writes=[Rfp])
        S.dma("sp", lambda e: e.dma_start(out=fb, in_=fdb[l]), writes=[Rfp])
        cnt = {"up": 0, "dn": 0, "tmp": 0, "xn": 0, "y": 0}

        def nxt(key, n_):
            v = cnt[key]; cnt[key] = (v + 1) % n_; return v

        tiles = [(0, t0, min(510, L - t0)) for t0 in range(0, L, 510)]
        if with_ctx:
            tiles += [(1, t0, min(510, CTX - t0)) for t0 in range(0, CTX, 510)]
        for (isctx, t0, ni) in tiles:
            cond = isctx
            Ls = CTX if isctx else L
            base = L if isctx else 0
            xsrc = xctx if isctx else xlat
            n = ni + 2
            nb = (n + 127) // 128
            jlo = 1 if t0 == 0 else 0
            jhi = n - 1 if t0 + ni == Ls else n
            nts = [min(128, n - tb * 128) for tb in range(nb)]
            for tb in range(nb):
                r0 = max(tb * 128, jlo); r1 = min(tb * 128 + nts[tb], jhi)
                if r0 != tb * 128 or r1 != tb * 128 + nts[tb]:
                    S.op("pool", lambda e: e.memset(xs[tb], 0.0), writes=[Rxs[tb]])
                for (a_, b_) in row_chunks(r0, r1):
                    S.dma("sp", lambda e: e.dma_start(out=xs[tb][a_ - tb * 128:b_ - tb * 128, :], in_=xsrc[t0 - 1 + a_:t0 - 1 + b_, :]), writes=[Rxs[tb]])
            mt3 = mt.rearrange("p (c t) -> p c t", c=16)
            if jlo != 0 or jhi != n:
                S.op("pool", lambda e: e.memset(mt, 0.0), writes=[RmT])
            S.dma("sp", lambda e: e.dma_start(out=mt3[:, :, jlo:jhi], in_=mixT_d[:, base + t0 - 1 + jlo:base + t0 - 1 + jhi].rearrange("(c p) t -> p c t", p=128)),
                  writes=[RmT])
            S.dma("sp", lambda e: e.dma_start(out=gt[0], in_=mods_d[cond:cond + 1, 2 * D:3 * D].broadcast_to([128, D])), writes=[Rgt[0]])
            S.dma("sp", lambda e: e.dma_start(out=gt[1], in_=mods_d[cond:cond + 1, 5 * D:6 * D].broadcast_to([128, D])), writes=[Rgt[1]])
            for nbk in range(4):
                sl = nxt("dn", 3)
                S.dma("sp", lambda e: e.dma_start(out=wdn[sl].rearrange("p (k n) -> p k n", k=16),
                                                  in_=woutb[l][:, nbk * 512:(nbk + 1) * 512].rearrange("(k p) n -> p k n", p=128)), writes=[Rwdn[sl]], reads=[RCW("wout%d" % l)])
                for tb in range(nb):
                    nt = nts[tb]
                    bk = 4 + tb
                    for k in range(16):
                        S.op("pe", lambda e: e.matmul(PS[bk][0:nt, :], lhsT=mt[:, k * 512 + tb * 128:k * 512 + tb * 128 + nt],
                                                      rhs=wdn[sl][:, k * 512:(k + 1) * 512], start=(k == 0), stop=(k == 15)),
                             reads=[RmT, Rwdn[sl]], writes=[RPS[bk]])
                    ti_ = nxt("tmp", 2)
                    S.op("dve", lambda e: e.tensor_tensor(out=tmp[ti_][0:nt, :], in0=PS[bk][0:nt, :], in1=gt[0][0:nt, nbk * 512:(nbk + 1) * 512], op=ALU.mult),
                         reads=[RPS[bk], Rgt[0]], writes=[Rtmp[ti_]])
                    S.op("pool", lambda e: e.tensor_tensor(out=xs[tb][0:nt, nbk * 512:(nbk + 1) * 512], in0=xs[tb][0:nt, nbk * 512:(nbk + 1) * 512],
                                                           in1=tmp[ti_][0:nt, :], op=ALU.add), reads=[Rtmp[ti_], Rxs[tb]], writes=[Rxs[tb]])
            if last:
                S.dma("sp", lambda e: e.dma_start(out=gt[0], in_=final_g.rearrange("(o d) -> o d", o=1).broadcast_to([128, D])), writes=[Rgt[0]])
            for tb in range(nb):
                nt = nts[tb]
                xi_ = nxt("xn", 2)
                S.op("act", lambda e: e.activation(out=xn[xi_][0:nt, :], in_=xs[tb][0:nt, :], func=AF.Square, accum_out=ss[0:nt, tb:tb + 1]),
                     reads=[Rxs[tb]], writes=[Rxn[xi_], Rss])
                S.op("act", lambda e: e.activation(out=ss[0:nt, 4 + tb:5 + tb], in_=ss[0:nt, tb:tb + 1], func=AF.Sqrt, scale=1.0 / D, bias=EPS),
                     reads=[Rss], writes=[Rss])
                S.op("dve", lambda e: e.reciprocal(out=ss[0:nt, 4 + tb:5 + tb], in_=ss[0:nt, 4 + tb:5 + tb]), reads=[Rss], writes=[Rss])
                S.op("dve", lambda e: e.tensor_scalar(out=xn[xi_][0:nt, :], in0=xs[tb][0:nt, :], scalar1=ss[0:nt, 4 + tb:5 + tb], scalar2=None, op0=ALU.mult),
                     reads=[Rxs[tb], Rss], writes=[Rxn[xi_]])
                for c4 in range(4):
                    bk = c4 % 2
                    for cc in range(4):
                        c = c4 * 4 + cc
                        S.op("pe", lambda e: e.transpose(PS[bk][:, cc * 128:cc * 128 + nt], xn[xi_][0:nt, c * 128:(c + 1) * 128], ident[0:nt, 0:nt]),
                             reads=[Rxn[xi_], Rid], writes=[RPS[bk]])
                    for cc in range(4):
                        c = c4 * 4 + cc
                        dst = hT[:, c * 512 + tb * 128:c * 512 + tb * 128 + nt]
                        if cc % 2 == 0:
                            S.op("dve", lambda e: e.tensor_scalar(out=dst, in0=PS[bk][:, cc * 128:cc * 128 + nt], scalar1=G2T[:, 2 * c + cond:2 * c + cond + 1],
                                                                  scalar2=modcol(3, c, cond), op0=ALU.mult, op1=ALU.add),
                                 reads=[RPS[bk], RG, RmodT], writes=[RhT])
                        else:
                            S.op("act", lambda e: e.activation(out=dst, in_=PS[bk][:, cc * 128:cc * 128 + nt], func=AF.Identity,
                                                               scale=G2T[:, 2 * c + cond:2 * c + cond + 1], bias=modcol(3, c, cond)),
                                 reads=[RPS[bk], RG, RmodT], writes=[RhT])
            hT3 = hT.rearrange("p (c t) -> p c t", c=16)
            if jlo == 1:
                S.op("pool", lambda e: e.memset(hT3[:, :, 0:1], 0.0), reads=[RhT], writes=[RhT])
            if jhi == n - 1:
                S.op("pool", lambda e: e.memset(hT3[:, :, n - 1:n], 0.0), reads=[RhT], writes=[RhT])
            for hf in range(2):
                for jj in range(22):
                    c = hf * 22 + jj
                    sub = c % 2
                    if jj % 2 == 0:
                        uv = nxt("up", 4); ug = nxt("up", 4)
                        for (us, c0) in ((uv, (c // 2) * 256), (ug, DFF + (c // 2) * 256)):
                            S.dma("sp", lambda e: e.dma_start(out=upw[us].rearrange("p (k n) -> p k n", k=16),
                                                              in_=upb[l][:, c0:c0 + 256].rearrange("(k p) n -> p k n", p=128)), writes=[Rupw[us]], reads=[RCW("up%d" % l)])
                    pbk = (jj % 2) * 2
                    for (us, bk) in ((uv, pbk), (ug, pbk + 1)):
                        for k in range(16):
                            S.op("pe", lambda e: e.matmul(PS[bk][:, 0:n], lhsT=upw[us][:, k * 256 + sub * 128:k * 256 + (sub + 1) * 128],
                                                          rhs=hT[:, k * 512:k * 512 + n], start=(k == 0), stop=(k == 15)),
                                 reads=[Rupw[us], RhT], writes=[RPS[bk]])
                    yi = nxt("y", 2)
                    for (yy, Ryy, bk, ch) in ((yv[yi], Ryv[yi], pbk, c), (yg[yi], Ryg[yi], pbk + 1, 44 + c)):
                        S.op("act", lambda e: e.activation(out=yy[:, 0:n], in_=PS[bk][:, 0:n], func=AF.Identity, scale=fw[:, ch * 3 + 1:ch * 3 + 2],
                                                           bias=fb[:, ch:ch + 1]), reads=[RPS[bk], Rfp], writes=[Ryy])
                        S.op("dve", lambda e: e.scalar_tensor_tensor(out=yy[:, 1:n], in0=PS[bk][:, 0:n - 1], scalar=fw[:, ch * 3:ch * 3 + 1], in1=yy[:, 1:n],
                                                                     op0=ALU.mult, op1=ALU.add), reads=[RPS[bk], Rfp, Ryy], writes=[Ryy])
                        S.op("dve", lambda e: e.scalar_tensor_tensor(out=yy[:, 0:n - 1], in0=PS[bk][:, 1:n], scalar=fw[:, ch * 3 + 2:ch * 3 + 3], in1=yy[:, 0:n - 1],
                                                                     op0=ALU.mult, op1=ALU.add), reads=[RPS[bk], Rfp, Ryy], writes=[Ryy])
                    S.op("act", lambda e: e.activation(out=yg[yi][:, 0:n], in_=yg[yi][:, 0:n], func=AF.Silu), reads=[Ryg[yi]], writes=[Ryg[yi]])
                    S.op("pool", lambda e: e.tensor_tensor(out=aT[:, jj * 512:jj * 512 + n], in0=yv[yi][:, 0:n], in1=yg[yi][:, 0:n], op=ALU.mult),
                         reads=[Ryv[yi], Ryg[yi]], writes=[RaT])
                for nbk in range(4):
                    for part in range(2):
                        sl = nxt("dn", 3)
                        k0 = hf * 22 + part * 11
                        S.dma("sp", lambda e: e.dma_start(out=wdn[sl][:, 0:11 * 512].rearrange("p (k n) -> p k n", k=11),
                                                          in_=downb[l][k0 * 128:(k0 + 11) * 128, nbk * 512:(nbk + 1) * 512].rearrange("(k p) n -> p k n", p=128)),
                              writes=[Rwdn[sl]], reads=[RCW("down%d" % l)])
                        for tb in range(nb):
                            nt = nts[tb]
                            bk = 4 + tb
                            for kl in range(11):
                                kk = part * 11 + kl
                                S.op("pe", lambda e: e.matmul(PS[bk][0:nt, :], lhsT=aT[:, kk * 512 + tb * 128:kk * 512 + tb * 128 + nt],
                                                              rhs=wdn[sl][:, kl * 512:(kl + 1) * 512], start=(kk == 0), stop=(kk == 21)),
                                     reads=[RaT, Rwdn[sl]], writes=[RPS[bk]])
                    for tb in range(nb):
                        nt = nts[tb]
                        bk = 4 + tb
                        ti_ = nxt("tmp", 2)
                        S.op("dve", lambda e: e.tensor_tensor(out=tmp[ti_][0:nt, :], in0=PS[bk][0:nt, :], in1=gt[1][0:nt, nbk * 512:(nbk + 1) * 512], op=ALU.mult),
                             reads=[RPS[bk], Rgt[1]], writes=[Rtmp[ti_]])
                        S.op("pool", lambda e: e.tensor_tensor(out=xs[tb][0:nt, nbk * 512:(nbk + 1) * 512], in0=xs[tb][0:nt, nbk * 512:(nbk + 1) * 512],
                                                               in1=tmp[ti_][0:nt, :], op=ALU.add), reads=[Rtmp[ti_], Rxs[tb]], writes=[Rxs[tb]])
            for tb in range(nb):
                nt = nts[tb]
                r0 = max(tb * 128, 1); r1 = min(tb * 128 + nt, n - 1)
                if r1 <= r0:
                    continue
                if last:
                    xi_ = nxt("xn", 2)
                    S.op("act", lambda e: e.activation(out=xn[xi_][0:nt, :], in_=xs[tb][0:nt, :], func=AF.Square, accum_out=ss[0:nt, 8 + tb:9 + tb]),
                         reads=[Rxs[tb]], writes=[Rxn[xi_], Rss])
                    S.op("act", lambda e: e.activation(out=ss[0:nt, 12 + tb:13 + tb], in_=ss[0:nt, 8 + tb:9 + tb], func=AF.Sqrt, scale=1.0 / D, bias=EPS),
                         reads=[Rss], writes=[Rss])
                    S.op("dve", lambda e: e.reciprocal(out=ss[0:nt, 12 + tb:13 + tb], in_=ss[0:nt, 12 + tb:13 + tb]), reads=[Rss], writes=[Rss])
                    S.op("dve", lambda e: e.tensor_scalar(out=xn[xi_][0:nt, :], in0=xs[tb][0:nt, :], scalar1=ss[0:nt, 12 + tb:13 + tb], scalar2=None, op0=ALU.mult),
                         reads=[Rxs[tb], Rss], writes=[Rxn[xi_]])
                    S.op("pool", lambda e: e.tensor_tensor(out=xn[xi_][0:nt, :], in0=xn[xi_][0:nt, :], in1=gt[0][0:nt, :], op=ALU.mult),
                         reads=[Rxn[xi_], Rgt[0]], writes=[Rxn[xi_]])
                    for (a_, b_) in row_chunks(r0, r1):
                        S.dma("act", lambda e: e.dma_start(out=out_d[t0 - 1 + a_:t0 - 1 + b_, :], in_=xn[xi_][a_ - tb * 128:b_ - tb * 128, :]), reads=[Rxn[xi_]])
                else:
                    for (a_, b_) in row_chunks(r0, r1):
                        S.dma("act", lambda e: e.dma_start(out=xa_d[base + t0 - 1 + a_:base + t0 - 1 + b_, :], in_=xs[tb][a_ - tb * 128:b_ - tb * 128, :]),
                              reads=[Rxs[tb]])
        S.barrier()

    def run_group(items):
        active = [(iter(g_), w_) for g_, w_ in items]
        while active:
            for it in list(active):
                g_, w_ = it
                for _ in range(w_):
                    try:
                        next(g_)
                    except StopIteration:
                        active.remove(it)
                        break
        S.barrier()

    def run(stage="all", nlay=NL):
        for l in range(nlay):
            last = (l == NL - 1)
            xlat = x_in if l == 0 else xa_d[0:L]
            xctx = ctx_in if l == 0 else xa_d[L:T]
            if l == 0:
                issue_casts(0, 0)
            na_zero()
            phase_mods(l)
            na_diag(l)
            if stage == "mods" and l == nlay - 1:
                break
            phase_inproj(l, xlat, xctx)
            if stage == "inproj" and l == nlay - 1:
                break
            issue_casts(l, 1)
            if l + 1 < NL:
                issue_casts(l + 1, 0)
            A.off = pers_mark
            Ana = A.sub(A.n - A.off - 15900 - 9700); Aret = A.sub(15900); Aconv = A.sub(9700)
            import os
            gm = os.environ.get("GMODE", "par")
            if gm == "seq":
                for g_ in (phase_ret(l, not last, Aret), phase_conv(l, not last, Aconv), phase_na(l, not last, Ana)):
                    run_group([(g_, 1)])
            elif gm in ("ret", "conv", "na"):
                run_group([({"ret": phase_ret(l, not last, Aret), "conv": phase_conv(l, not last, Aconv), "na": phase_na(l, not last, Ana)}[gm], 1)])
            else:
                run_group([(phase_ret(l, not last, Aret), 6), (phase_conv(l, not last, Aconv), 1), (phase_na(l, not last, Ana), 4)])
            if stage == "na" and l == nlay - 1:
                break
            phase_ffn(l, not last, xlat, xctx, last)
        S.run_block()

    g.run = run
    g.nc = nc; g.S = S
    g.phase_mods = phase_mods; g.phase_inproj = phase_inproj
    g.names = dict(x_in=x_in, ctx_in=ctx_in, xa_d=xa_d, out_d=out_d)
    g.locals = locals()
    return g


def kernel(**inputs):
    inp = {k: np.asarray(v) for k, v in inputs.items()}
    B, L, _ = inp["x"].shape
    CTX = inp["ctx"].shape[1]
    g = build(L, CTX, NL=2, dbg=False)
    g.run("all", 2)
    in_maps = [host_layout(inp, b, L) for b in range(B)]
    res = run_bass_kernel_spmd(g.nc, in_maps, core_ids=list(range(B)))
    return np.stack([np.asarray(res.results[b]["out"], np.float32) for b in range(B)]).astype(np.float32)
```

```python
import numpy as np
import ml_dtypes
import concourse.bass as bass
import concourse.mybir as mybir
from concourse.bass_utils import run_bass_kernel_spmd

F32 = mybir.dt.float32
BF16 = mybir.dt.bfloat16
AF = mybir.ActivationFunctionType
ALU = mybir.AluOpType
AX = mybir.AxisListType

COMPUTE = ("pe", "act", "dve", "pool")
SAME_ENGINE_WAIT = True


class Res:
    __slots__ = ("name", "w", "r", "lsem", "ssem")

    def __init__(self, name):
        self.name = name
        self.w = {}
        self.r = {}
        self.lsem = None
        self.ssem = None


class _Cap:
    def __init__(self):
        self.call = None

    def __getattr__(self, name):
        def f(*a, **k):
            self.call = (name, a, k)
            return self
        return f


class Rec:
    __slots__ = ("waits", "fn", "inc")

    def __init__(self, waits, fn, inc):
        self.waits = waits
        if fn is not None:
            cap = _Cap()
            fn(cap)
            fn = cap.call
            assert fn is not None
        self.fn = fn
        self.inc = inc


class Sched:
    def __init__(self, nc):
        self.nc = nc
        self.prog = {e: [] for e in ("pe", "act", "dve", "pool", "sp")}
        self.sems = {}
        self.cnt = {}
        self.known = {e: {} for e in self.prog}
        self.last = {e: None for e in COMPUTE}
        self.pending = {e: False for e in COMPUTE}
        self.free_dma_sems = []
        self.live_dma_sems = []
        self.nsem = 0
        self.nobarrier = set()
        self.live_res = []
        for e in COMPUTE:
            self._mk("E_" + e)

    def _mk(self, key):
        self.sems[key] = self.nc.alloc_semaphore(key)
        self.cnt[key] = 0
        self.nsem += 1
        return key

    def dma_sem(self):
        if self.free_dma_sems:
            k = self.free_dma_sems.pop()
        else:
            k = self._mk("D%d" % self.nsem)
        self.live_dma_sems.append(k)
        return k

    def _force(self, key):
        if key.startswith("E_"):
            e = key[2:]
            if self.pending[e]:
                rec = self.last[e]
                assert rec.inc is None
                rec.inc = (key, 1)
                self.cnt[key] += 1
                self.pending[e] = False

    def _waits(self, eng, deps):
        out = []
        kn = self.known[eng]
        for key, val in deps.items():
            if key == "E_" + eng:
                if eng == "pe" or not SAME_ENGINE_WAIT:
                    continue
            if kn.get(key, 0) >= val:
                continue
            self._force(key)
            assert self.cnt[key] >= val, (key, self.cnt[key], val)
            kn[key] = val
            out.append((key, val))
        return out

    @staticmethod
    def _merge(d, s):
        for k, v in s.items():
            if d.get(k, 0) < v:
                d[k] = v

    def _deps(self, reads, writes):
        deps = {}
        for r in reads:
            self._merge(deps, r.w)
        for w in writes:
            self._merge(deps, w.w)
            self._merge(deps, w.r)
        return deps

    def op(self, eng, fn, reads=(), writes=()):
        deps = self._deps(reads, writes)
        waits = self._waits(eng, deps)
        key = "E_" + eng
        rec = Rec(waits, fn, None)
        self.prog[eng].append(rec)
        self.last[eng] = rec
        self.pending[eng] = True
        tok = {key: self.cnt[key] + 1}
        for r in reads:
            self._merge(r.r, tok)
        for w in writes:
            w.w = dict(tok)
            w.r = {}

    def dma(self, queue, fn, reads=(), writes=(), sem=None):
        deps = self._deps(reads, writes)
        waits = self._waits(queue, deps)
        if sem is None:
            if writes:
                w0 = writes[0]
                if w0.lsem is None:
                    w0.lsem = self.dma_sem()
                    self.live_res.append(w0)
                sem = w0.lsem
            else:
                r0 = reads[0]
                if r0.ssem is None:
                    r0.ssem = self.dma_sem()
                    self.live_res.append(r0)
                sem = r0.ssem
        self.cnt[sem] += 16
        tok = {sem: self.cnt[sem]}
        rec = Rec(waits, fn, (sem, 16))
        self.prog[queue].append(rec)
        if queue in COMPUTE:
            pass
        for r in reads:
            self._merge(r.r, tok)
        for w in writes:
            w.w = dict(tok)
            w.r = {}

    def barrier(self, recycle=True, final=False):
        for e in COMPUTE:
            self._force("E_" + e)
        allk = {k: v for k, v in self.cnt.items() if v > 0 and (final or k not in self.nobarrier)}
        for eng in self.prog:
            waits = self._waits_all(eng, allk)
            if waits:
                self.prog[eng].append(Rec(waits, None, None))
        if recycle:
            self.free_dma_sems.extend(self.live_dma_sems)
            self.live_dma_sems = []
            for r_ in self.live_res:
                r_.lsem = None
                r_.ssem = None
            self.live_res = []

    def _waits_all(self, eng, allk):
        out = []
        kn = self.known[eng]
        for key, val in allk.items():
            if key == "E_" + eng:
                continue
            if kn.get(key, 0) >= val:
                continue
            kn[key] = val
            out.append((key, val))
        return out

    def replay(self, eng, e):
        for rec in self.prog[eng]:
            for key, val in rec.waits:
                e.wait_ge(self.sems[key], val)
            if rec.fn is None:
                continue
            name, a, k = rec.fn
            ins = getattr(e, name)(*a, **k)
            if rec.inc is not None:
                ins.then_inc(self.sems[rec.inc[0]], rec.inc[1])

    def run_block(self):
        nc = self.nc
        self.barrier(recycle=False, final=True)
        with nc.Block() as block:
            @block.tensor
            def _(e):
                self.replay("pe", e)

            @block.scalar
            def _(e):
                self.replay("act", e)

            @block.vector
            def _(e):
                self.replay("dve", e)

            @block.gpsimd
            def _(e):
                self.replay("pool", e)

            @block.sync
            def _(e):
                self.replay("sp", e)


class Arena:
    def __init__(self, t, nwords, base=0):
        self.t = t
        self.n = base + nwords
        self.off = base
        self.base = base

    def sub(self, nwords):
        assert self.off + nwords <= self.n, ("arena overflow(sub)", self.off, nwords, self.n)
        a = Arena(self.t, nwords, self.off)
        self.off += nwords
        return a

    def reset(self):
        self.off = 0

    def f32(self, n, parts=128):
        assert self.off + n <= self.n, ("arena overflow", self.off, n, self.n)
        ap = self.t[0:parts, self.off:self.off + n]
        self.off += n
        return ap

    def bf16(self, n, parts=128):
        w = (n + 1) // 2
        assert self.off + w <= self.n, ("arena overflow", self.off, w, self.n)
        ap = self.t[0:parts, self.off:self.off + w].bitcast(BF16)
        self.off += w
        return ap[:, 0:n]


D = 2048
DIN = 5632
DFF = 5632
WCOLS = DIN + 1024
EPS = 1e-6
GRID_W = 64
NEG = -30000.0
CAST_BARRIER = False
import os
NA_OLDPS = bool(int(os.environ.get('NA_OLDPS', '0')))


def host_consts(L):
    c = {}
    c["ident"] = np.eye(128, dtype=np.float32)
    pos = np.arange(L)
    row = (pos // GRID_W).astype(np.float32)
    col = (pos % GRID_W).astype(np.float32)
    nf = 32
    inv = (10000.0 ** (-np.arange(nf, dtype=np.float32) / nf)).astype(np.float32)
    f = np.arange(128)
    p = np.where((f // 64)[:, None] == 0, row[None, :], col[None, :]).astype(np.float32)
    ang = (p * inv[f % 32][:, None]).astype(np.float32)
    sign = np.where((f % 64) < 32, -1.0, 1.0).astype(np.float32)[:, None]
    C = np.cos(ang).astype(np.float32)
    Sg = (sign * np.sin(ang)).astype(np.float32)
    sc = np.float32(128 ** -0.5)
    c["rope"] = np.stack([C * sc, Sg * sc, C, Sg]).astype(np.float32)
    i = np.arange(128)
    jj, ii = np.meshgrid(i, i, indexing="ij")
    dec = np.stack([np.maximum(ii - jj, 0), (ii >= jj), np.maximum(jj - ii, 0), (jj >= ii)]).astype(np.float32)
    c["dec"] = dec
    xirow = np.stack([np.tile((i + 1)[None, :], (128, 1)), np.tile((128 - i)[None, :], (128, 1))]).astype(np.float32)
    c["xirow"] = xirow
    c["zcol"] = np.stack([127 - i, i], axis=1).astype(np.float32)
    R = L // GRID_W
    types = na_types(R)
    nam = np.zeros((5, 128, 576), np.float32)
    cols = np.arange(64)
    cs = np.clip(cols - 8, 0, 64 - 16)
    band = (cols[None, :] >= cs[:, None]) & (cols[None, :] < cs[:, None] + 16)
    for ti, (m, lo) in enumerate(types):
        for qr in range(2):
            r = 2 * m + qr
            w0 = int(np.clip(r - 4, 0, R - 8))
            for kidx in range(9):
                kr = lo + kidx
                ok = (w0 <= kr < w0 + 8)
                blk = np.where(band, 0.0, NEG) if ok else np.full((64, 64), NEG)
                nam[ti, qr * 64:(qr + 1) * 64, kidx * 64:(kidx + 1) * 64] = blk
    c["nam"] = nam
    return c


def na_types(R):
    M = R // 2
    return [(0, 0), (1, 0), (2, 0), (M - 2, R - 9), (M - 1, R - 9)]


def na_type_of(m, R):
    M = R // 2
    if m == 0:
        return 0, 0
    if m == 1:
        return 1, 0
    if m == M - 2:
        return 3, R - 9
    if m == M - 1:
        return 4, R - 9
    return 2, 2 * m - 4


def row_chunks(r0, r1):
    n = r1 - r0
    big = (n // 16) * 16
    out = []
    if big:
        out.append((r0, r0 + big))
    if n - big:
        out.append((r0 + big, r1))
    return out


def fm(v, nch):
    s = v.shape[:-1]
    return np.ascontiguousarray(np.moveaxis(v.reshape(s + (nch, 128)), -1, -2))


def host_layout(inp, b, L):
    f32 = np.float32
    o = {}
    o["x"] = np.ascontiguousarray(inp["x"][b], f32)
    o["ctx"] = np.ascontiguousarray(inp["ctx"][b], f32)
    cv = np.stack([inp["c"][b], inp["c_ctx"]], axis=1).astype(f32)
    o["cT"] = np.ascontiguousarray(cv.reshape(16, 128, 2).transpose(1, 0, 2))
    for k in ("w_ada", "b_ada", "w_in", "w_out", "ffn_up", "ffn_down", "conv_pw", "final_g", "na_rpb"):
        o[k] = np.ascontiguousarray(inp[k], f32)
    o["ret_decay"] = np.ascontiguousarray(inp["ret_decay"].reshape(-1, 8), f32)
    o["gng"] = np.ascontiguousarray(inp["ret_gn_g"], f32)
    o["n1g"] = fm(inp["norm1_g"].astype(f32), 16)
    o["n2g"] = fm(inp["norm2_g"].astype(f32), 16)
    o["cdw"] = np.ascontiguousarray(fm(inp["conv_dw_w"].astype(f32), 4).transpose(0, 2, 3, 1))
    o["cdb"] = fm(inp["conv_dw_b"].astype(f32), 4)
    o["lng"] = fm(inp["conv_ln_g"].astype(f32), 4)
    o["lnb"] = fm(inp["conv_ln_b"].astype(f32), 4)
    o["fdw"] = np.ascontiguousarray(fm(inp["ffn_dw_w"].astype(f32), 88).transpose(0, 2, 3, 1))
    o["fdb"] = fm(inp["ffn_dw_b"].astype(f32), 88)
    o.update(host_consts(L))
    return o


class K:
    pass


def build(L, CTX, NL=2, dbg=False, upto=99):
    T = L + CTX
    R = L // GRID_W
    nc = bass.Bass("TRN2", target_bir_lowering=False)
    g = K()

    def din(name, shape, dt=F32):
        return nc.dram_tensor(name, list(shape), dt, kind="ExternalInput").ap()

    def dscr(name, shape, dt):
        return nc.dram_tensor(name, list(shape), dt, kind="ExternalOutput" if dbg else "Internal").ap()

    x_in = din("x", [L, D]); ctx_in = din("ctx", [CTX, D]); cT = din("cT", [128, 16, 2])
    w_ada = din("w_ada", [NL, D, 6 * D]); b_ada = din("b_ada", [NL, 6 * D]); w_in = din("w_in", [NL, D, DIN])
    w_out = din("w_out", [NL, D, D]); ffn_up = din("ffn_up", [NL, D, 2 * DFF]); ffn_down = din("ffn_down", [NL, DFF, D])
    conv_pw = din("conv_pw", [NL, 512, 512]); final_g = din("final_g", [D]); na_rpb = din("na_rpb", [NL, 4, 15, 31])
    ret_decay = din("ret_decay", [NL, 8]); gng = din("gng", [NL, 1024])
    n1g = din("n1g", [NL, 128, 16]); n2g = din("n2g", [NL, 128, 16])
    cdw = din("cdw", [NL, 128, 4, 31]); cdb = din("cdb", [NL, 128, 4]); lng = din("lng", [NL, 128, 4]); lnb = din("lnb", [NL, 128, 4])
    fdw = din("fdw", [NL, 128, 88, 3]); fdb = din("fdb", [NL, 128, 88])
    ident_d = din("ident", [128, 128]); rope_d = din("rope", [4, 128, L]); dec_d = din("dec", [4, 128, 128])
    xirow_d = din("xirow", [2, 128, 128]); zcol_d = din("zcol", [128, 2]); nam_d = din("nam", [5, 128, 576])
    out_d = nc.dram_tensor("out", [L, D], F32, kind="ExternalOutput").ap()

    winb = dscr("winb", [NL, D, WCOLS], BF16); woutb = dscr("woutb", [NL, D, D], BF16)
    upb = dscr("upb", [NL, D, 2 * DFF], BF16); downb = dscr("downb", [NL, DFF, D], BF16)
    pwb = dscr("pwb", [NL, 512, 512], BF16); wadab = dscr("wadab", [NL, D, 6 * D], BF16)
    mods_d = dscr("mods", [2, 6 * D], F32)
    qT_d = dscr("qT", [512, T], BF16); kT_d = dscr("kT", [512, T], BF16); v_d = dscr("v", [T, 1024], BF16)
    sg_d = dscr("sg", [T, 1024], F32); uT_d = dscr("uT", [512, T], F32)
    nqT_d = dscr("nqT", [512, T], BF16); nkT_d = dscr("nkT", [512, T], BF16); nv_d = dscr("nv", [T, 512], BF16)
    of_d = dscr("of", [T, 1024], F32); mixT_d = dscr("mixT", [D, T], BF16)
    xa_d = dscr("xa", [T, D], F32); toep_d = dscr("toep", [4, 64, 17, 64], F32)

    S = Sched(nc)
    NARENA = 52900
    import contextlib
    es = contextlib.ExitStack()
    arena_t = es.enter_context(nc.sbuf_tensor("arena", [128, NARENA], F32))
    psum_t = es.enter_context(nc.psum_tensor("psum", [128, 4096], F32))
    A = Arena(arena_t, NARENA)
    PS = [psum_t[:, i * 512:(i + 1) * 512] for i in range(8)]
    RPS = [Res("ps%d" % i) for i in range(8)]

    ident = A.f32(128); Rid = Res("ident")
    identb = A.bf16(128)
    modT = A.f32(192); RmodT = Res("modT")
    G1T = A.f32(32); G2T = A.f32(32); RG = Res("G")
    ztile = A.f32(2176); Rzt = Res("zt")
    pers_mark = A.off

    S.dma("sp", lambda e: e.dma_start(out=ident, in_=ident_d), writes=[Rid])
    S.op("dve", lambda e: e.tensor_copy(out=identb, in_=ident), reads=[Rid], writes=[Rid])

    RC = {}

    def cast(dst, src, rows, step, key):
        if key not in RC:
            sem = S._mk("C_" + key)
            S.nobarrier.add(sem)
            RC[key] = (Res("cast_" + key), sem)
        rc, sem = RC[key]
        for r0 in range(0, rows, step):
            S.dma("pool", lambda e, r0=r0: e.dma_start(out=dst[r0:r0 + step], in_=src[r0:r0 + step]), sem=sem)
        rc.w = {sem: S.cnt[sem]}

    def issue_casts(l, part):
        if part == 0:
            cast(winb[l][:, 0:DIN], w_in[l], D, 256, "win%d" % l)
            cast(pwb[l], conv_pw[l], 512, 512, "pw%d" % l)
            return
        cast(woutb[l], w_out[l], D, 512, "wout%d" % l)
        cast(upb[l], ffn_up[l], D, 128, "up%d" % l)
        cast(downb[l], ffn_down[l], DFF, 512, "down%d" % l)

    def RCW(key):
        return RC[key][0]

    def cast_steps(dst, src, rows, step, key):
        if key not in RC:
            sem = S._mk("C_" + key)
            S.nobarrier.add(sem)
            RC[key] = (Res("cast_" + key), sem)
        rc, sem = RC[key]
        n = len(range(0, rows, step))
        rc.w = {sem: S.cnt[sem] + 16 * n}
        for r0 in range(0, rows, step):
            S.dma("pool", lambda e, r0=r0: e.dma_start(out=dst[r0:r0 + step], in_=src[r0:r0 + step]), sem=sem)
            yield

    def paced_casts(items, pace):
        for (dst, src, rows, step, key) in items:
            for _ in cast_steps(dst, src, rows, step, key):
                for _p in range(pace):
                    yield

    def cast_items(l, part):
        if part == 0:
            return [(winb[l][:, 0:DIN], w_in[l], D, 256, "win%d" % l), (pwb[l], conv_pw[l], 512, 512, "pw%d" % l)]
        return [(woutb[l], w_out[l], D, 512, "wout%d" % l), (upb[l], ffn_up[l], D, 128, "up%d" % l),
                (downb[l], ffn_down[l], DFF, 512, "down%d" % l)]

    def phase_mods(l):
        A.off = pers_mark
        sT = A.f32(32); RsT = Res("sT")
        sTs = A.f32(32)
        bada2 = A.f32(6 * D, parts=2); Rb = Res("bada2")
        m = A.f32(6 * D, parts=2); Rm = Res("m")
        ng = A.f32(32); Rng = Res("ng")
        wt = [A.f32(16 * 512) for _ in range(2)]; Rwt = [Res("wt%d" % i) for i in range(2)]
        S.dma("sp", lambda e: e.dma_start(out=sT, in_=cT.rearrange("p k m -> p (k m)")), writes=[RsT])
        S.dma("sp", lambda e: e.dma_start(out=bada2, in_=b_ada[l:l + 1, :].broadcast_to([2, 6 * D])), writes=[Rb])
        S.dma("sp", lambda e: e.dma_start(out=ng[:, 0:16], in_=n1g[l]), writes=[Rng])
        S.dma("sp", lambda e: e.dma_start(out=ng[:, 16:32], in_=n2g[l]), writes=[Rng])
        S.op("act", lambda e: e.activation(out=sTs, in_=sT, func=AF.Silu), reads=[RsT], writes=[RsT])
        for nb in range(24):
            sl = nb % 2
            S.dma("sp", lambda e, nb=nb, sl=sl: e.dma_start(
                out=wt[sl].rearrange("p (k n) -> p k n", k=16),
                in_=w_ada[l][:, nb * 512:(nb + 1) * 512].rearrange("(k p) n -> p k n", p=128)), writes=[Rwt[sl]])
            for k in range(16):
                S.op("pe", lambda e, nb=nb, sl=sl, k=k: e.matmul(PS[sl][0:2, :], lhsT=sTs[:, 2 * k:2 * k + 2],
                                                               rhs=wt[sl][:, k * 512:(k + 1) * 512], start=(k == 0), stop=(k == 15)),
                     reads=[RsT, Rwt[sl]], writes=[RPS[sl]])
            S.op("dve", lambda e, nb=nb, sl=sl: e.tensor_tensor(out=m[:, nb * 512:(nb + 1) * 512], in0=PS[sl][0:2, :],
                                                                in1=bada2[:, nb * 512:(nb + 1) * 512], op=ALU.add),
                 reads=[RPS[sl], Rb], writes=[Rm])
        for s0 in (1, 4):
            S.op("dve", lambda e, s0=s0: e.tensor_scalar_add(out=m[:, s0 * D:(s0 + 1) * D], in0=m[:, s0 * D:(s0 + 1) * D], scalar1=1.0),
                 reads=[Rm], writes=[Rm])
        S.dma("pool", lambda e: e.dma_start(out=mods_d, in_=m), reads=[Rm])
        for j in range(96):
            S.op("pe", lambda e, j=j: e.transpose(PS[2][:, 2 * j:2 * j + 2], m[0:2, j * 128:(j + 1) * 128], ident[0:2, 0:2]),
                 reads=[Rm, Rid], writes=[RPS[2]])
        S.op("dve", lambda e: e.tensor_copy(out=modT, in_=PS[2][:, 0:192]), reads=[RPS[2]], writes=[RmodT])
        m3 = modT.rearrange("p (j m) -> p j m", m=2)
        for cond in range(2):
            S.op("dve", lambda e, cond=cond: e.tensor_tensor(out=G1T.rearrange("p (c m) -> p c m", m=2)[:, :, cond],
                                                             in0=ng[:, 0:16], in1=m3[:, 16:32, cond], op=ALU.mult),
                 reads=[RmodT, Rng], writes=[RG])
            S.op("dve", lambda e, cond=cond: e.tensor_tensor(out=G2T.rearrange("p (c m) -> p c m", m=2)[:, :, cond],
                                                             in0=ng[:, 16:32], in1=m3[:, 64:80, cond], op=ALU.mult),
                 reads=[RmodT, Rng], writes=[RG])
        S.barrier()

    def modcol(sec, c, cond):
        j = sec * 16 + c
        return modT[:, 2 * j + cond:2 * j + cond + 1]

    def phase_inproj(l, xlat, xctx):
        A.off = pers_mark
        rp = A.f32(4 * 512); Rrp = Res("rp")
        xs = [A.f32(D) for _ in range(4)]; Rxs = [Res("xs%d" % i) for i in range(4)]
        hT = A.bf16(16 * 512); RhT = Res("hT")
        junk = A.bf16(D); Rjunk = Res("junk")
        wb = [A.bf16(16 * 512) for _ in range(4)]; Rwb = [Res("wb%d" % i) for i in range(4)]
        stb = [A.bf16(512) for _ in range(3)]; Rstb = [Res("stb%d" % i) for i in range(3)]
        stf = [A.f32(512) for _ in range(3)]; Rstf = [Res("stf%d" % i) for i in range(3)]
        tmp = [A.f32(512) for _ in range(4)]; Rtmp = [Res("tmp%d" % i) for i in range(4)]
        ss = A.f32(8); Rss = Res("ss")
        cnt = {"w": 0, "sb": 0, "sf": 0, "tmp": 0, "ps": 0}

        def nxt(key, n):
            v = cnt[key]; cnt[key] = (v + 1) % n; return v

        def load_w(cb):
            sl = nxt("w", 4)
            S.dma("sp", lambda e: e.dma_start(out=wb[sl].rearrange("p (k n) -> p k n", k=16),
                                              in_=winb[l][:, cb * 512:(cb + 1) * 512].rearrange("(k p) n -> p k n", p=128)),
                  writes=[Rwb[sl]], reads=[RCW("win%d" % l)])
            return wb[sl], Rwb[sl]

        def psb():
            i = 2 + nxt("ps", 6)
            return PS[i], RPS[i]

        tiles = [(0, t0, min(512, L - t0)) for t0 in range(0, L, 512)] + [(1, t0, min(512, CTX - t0)) for t0 in range(0, CTX, 512)]
        for (isctx, t0, n) in tiles:
            cond = isctx
            nb = n // 128
            src = xctx if isctx else xlat
            tok0 = L + t0 if isctx else t0
            for tb in range(nb):
                S.dma("sp", lambda e, tb=tb: e.dma_start(out=xs[tb], in_=src[t0 + tb * 128:t0 + (tb + 1) * 128, :]), writes=[Rxs[tb]])
            if not isctx:
                S.dma("sp", lambda e: e.dma_start(out=rp.rearrange("p (a t) -> p a t", a=4)[:, :, 0:n],
                                                  in_=rope_d[:, :, t0:t0 + n].rearrange("a p t -> p a t")), writes=[Rrp])
            for tb in range(nb):
                S.op("act", lambda e, tb=tb: e.activation(out=junk, in_=xs[tb], func=AF.Square, accum_out=ss[:, tb:tb + 1]),
                     reads=[Rxs[tb]], writes=[Rjunk, Rss])
            S.op("act", lambda e: e.activation(out=ss[:, 4:4 + nb], in_=ss[:, 0:nb], func=AF.Sqrt, scale=1.0 / D, bias=EPS),
                 reads=[Rss], writes=[Rss])
            S.op("dve", lambda e: e.reciprocal(out=ss[:, 4:4 + nb], in_=ss[:, 4:4 + nb]), reads=[Rss], writes=[Rss])
            for tb in range(nb):
                S.op("dve", lambda e, tb=tb: e.tensor_scalar(out=xs[tb], in0=xs[tb], scalar1=ss[:, 4 + tb:5 + tb], scalar2=None, op0=ALU.mult),
                     reads=[Rxs[tb], Rss], writes=[Rxs[tb]])
            for c in range(16):
                bk = c % 2
                for tb in range(nb):
                    S.op("pe", lambda e, tb=tb, c=c, bk=bk: e.transpose(PS[bk][:, tb * 128:(tb + 1) * 128], xs[tb][:, c * 128:(c + 1) * 128], ident),
                         reads=[Rxs[tb], Rid], writes=[RPS[bk]])
                if c % 2 == 0:
                    S.op("dve", lambda e, c=c, bk=bk: e.tensor_scalar(out=hT[:, c * 512:c * 512 + n], in0=PS[bk][:, 0:n],
                                                                      scalar1=G1T[:, 2 * c + cond:2 * c + cond + 1], scalar2=modcol(0, c, cond),
                                                                      op0=ALU.mult, op1=ALU.add),
                         reads=[RPS[bk], RG, RmodT], writes=[RhT])
                else:
                    S.op("act", lambda e, c=c, bk=bk: e.activation(out=hT[:, c * 512:c * 512 + n], in_=PS[bk][:, 0:n], func=AF.Identity,
                                                                   scale=G1T[:, 2 * c + cond:2 * c + cond + 1], bias=modcol(0, c, cond)),
                         reads=[RPS[bk], RG, RmodT], writes=[RhT])

            def fm_chunk(wa, Rwa, j):
                ps, Rps = psb()
                for k in range(16):
                    S.op("pe", lambda e, k=k: e.matmul(ps[:, 0:n], lhsT=wa[:, k * 512 + j * 128:k * 512 + (j + 1) * 128],
                                                       rhs=hT[:, k * 512:k * 512 + n], start=(k == 0), stop=(k == 15)),
                         reads=[Rwa, RhT], writes=[Rps])
                return ps, Rps

            def tm_block(wa, Rwa, tb):
                ps, Rps = psb()
                for k in range(16):
                    S.op("pe", lambda e, k=k: e.matmul(ps[:, :], lhsT=hT[:, k * 512 + tb * 128:k * 512 + (tb + 1) * 128],
                                                       rhs=wa[:, k * 512:(k + 1) * 512], start=(k == 0), stop=(k == 15)),
                         reads=[Rwa, RhT], writes=[Rps])
                return ps, Rps

            def store_fm(dst, j, stage, Rst):
                S.dma("pool", lambda e: e.dma_start(out=dst[j * 128:(j + 1) * 128, tok0:tok0 + n], in_=stage[:, 0:n]), reads=[Rst])

            def store_tm(dst, tb, c0, stage, Rst):
                S.dma("pool", lambda e: e.dma_start(out=dst[tok0 + tb * 128:tok0 + (tb + 1) * 128, c0:c0 + 512], in_=stage[:, 0:512]), reads=[Rst])

            for qi, (cb, cbp, dst) in enumerate(((0, 11, qT_d), (1, 12, kT_d))):
                wa, Rwa = load_w(cb)
                if not isctx:
                    psl = nxt("w", 4)
                    wp, Rwp = wb[psl], Rwb[psl]
                    for bb in range(2):
                        S.op("act", lambda e: e.activation(out=wp.rearrange("p (a b e) -> p a b e", b=2, e=32)[:, :, 1 - bb, :],
                                                           in_=wa.rearrange("p (a b e) -> p a b e", b=2, e=32)[:, :, bb, :], func=AF.Copy),
                             reads=[Rwa], writes=[Rwp])
                for j in range(4):
                    pa, Rpa = fm_chunk(wa, Rwa, j)
                    sb = nxt("sb", 3)
                    if not isctx:
                        pb, Rpb = fm_chunk(wp, Rwp, j)
                        t1 = nxt("tmp", 4); t2 = nxt("tmp", 4)
                        S.op("dve", lambda e, t1=t1, pa=pa: e.tensor_tensor(out=tmp[t1][:, 0:n], in0=pa[:, 0:n], in1=rp[:, (2 * qi) * 512:(2 * qi) * 512 + n], op=ALU.mult),
                             reads=[Rpa, Rrp], writes=[Rtmp[t1]])
                        S.op("dve", lambda e, t2=t2, pb=pb: e.tensor_tensor(out=tmp[t2][:, 0:n], in0=pb[:, 0:n], in1=rp[:, (2 * qi + 1) * 512:(2 * qi + 1) * 512 + n], op=ALU.mult),
                             reads=[Rpb, Rrp], writes=[Rtmp[t2]])
                        S.op("pool", lambda e, t1=t1, t2=t2, sb=sb: e.tensor_tensor(out=stb[sb][:, 0:n], in0=tmp[t1][:, 0:n], in1=tmp[t2][:, 0:n], op=ALU.add),
                             reads=[Rtmp[t1], Rtmp[t2]], writes=[Rstb[sb]])
                    else:
                        S.op("act", lambda e, sb=sb, pa=pa: e.activation(out=stb[sb][:, 0:n], in_=pa[:, 0:n], func=AF.Copy,
                                                                         scale=(128 ** -0.5 if qi == 0 else 1.0)),
                             reads=[Rpa], writes=[Rstb[sb]])
                    store_fm(dst, j, stb[sb], Rstb[sb])
            for half in range(2):
                wa, Rwa = load_w(2 + half)
                for tb in range(nb):
                    ps, Rps = tm_block(wa, Rwa, tb)
                    sb = nxt("sb", 3)
                    S.op("act", lambda e, sb=sb, ps=ps: e.activation(out=stb[sb], in_=ps, func=AF.Copy), reads=[Rps], writes=[Rstb[sb]])
                    store_tm(v_d, tb, half * 512, stb[sb], Rstb[sb])
            for half in range(2):
                wa, Rwa = load_w(4 + half)
                for tb in range(nb):
                    ps, Rps = tm_block(wa, Rwa, tb)
                    sf = nxt("sf", 3)
                    S.op("act", lambda e, sf=sf, ps=ps: e.activation(out=stf[sf], in_=ps, func=AF.Silu), reads=[Rps], writes=[Rstf[sf]])
                    store_tm(sg_d, tb, half * 512, stf[sf], Rstf[sf])
            wa, Rwa = load_w(6)
            wp, Rwp = load_w(7)
            for j in range(4):
                pa, Rpa = fm_chunk(wa, Rwa, j)
                pb, Rpb = fm_chunk(wp, Rwp, j)
                t1 = nxt("tmp", 4); sf = nxt("sf", 3)
                S.op("act", lambda e, t1=t1, pb=pb: e.activation(out=tmp[t1][:, 0:n], in_=pb[:, 0:n], func=AF.Sigmoid), reads=[Rpb], writes=[Rtmp[t1]])
                S.op("dve", lambda e, t1=t1, pa=pa, sf=sf: e.tensor_tensor(out=stf[sf][:, 0:n], in0=pa[:, 0:n], in1=tmp[t1][:, 0:n], op=ALU.mult),
                     reads=[Rpa, Rtmp[t1]], writes=[Rstf[sf]])
                store_fm(uT_d, j, stf[sf], Rstf[sf])
            for cb, dst, scl in ((8, nqT_d, 128 ** -0.5), (9, nkT_d, 1.0)):
                wa, Rwa = load_w(cb)
                for j in range(4):
                    pa, Rpa = fm_chunk(wa, Rwa, j)
                    sb = nxt("sb", 3)
                    S.op("act", lambda e, sb=sb, pa=pa, scl=scl: e.activation(out=stb[sb][:, 0:n], in_=pa[:, 0:n], func=AF.Copy, scale=scl),
                         reads=[Rpa], writes=[Rstb[sb]])
                    store_fm(dst, j, stb[sb], Rstb[sb])
            wa, Rwa = load_w(10)
            for tb in range(nb):
                ps, Rps = tm_block(wa, Rwa, tb)
                sb = nxt("sb", 3)
                S.op("dve", lambda e, sb=sb, ps=ps: e.tensor_copy(out=stb[sb], in_=ps), reads=[Rps], writes=[Rstb[sb]])
                store_tm(nv_d, tb, 0, stb[sb], Rstb[sb])
        S.barrier()

    PSB = [p_.bitcast(BF16) for p_ in PS]

    def phase_ret(l, with_ctx, A):
        RB0 = Res("ret_b0"); RB1 = Res("ret_b1"); RP_m = Res("rp_m")
        rd = A.f32(8); lg = A.f32(8); Rlg = Res("lg")
        cdt = A.f32(4 * 128); xir = A.f32(2 * 128); zc = A.f32(2); Rc = Res("retconst")
        DT = A.f32(8 * 128); XI = A.f32(8 * 128); zg = A.f32(16); Rtab = Res("rettab")
        gngt = A.f32(1024); Rgn = Res("gngt")
        Sf = [A.f32(256) for _ in range(4)]; Sb = [A.bf16(256) for _ in range(4)]; RS = [Res("S%d" % h) for h in range(4)]
        RSb = [Res("Sb%d" % h) for h in range(4)]
        qc = [A.bf16(512) for _ in range(2)]; Rqc = [Res("qc%d" % i) for i in range(2)]
        kc = [A.bf16(512) for _ in range(2)]; Rkc = [Res("kc%d" % i) for i in range(2)]
        vc = [A.bf16(1024) for _ in range(2)]; Rvc = [Res("vc%d" % i) for i in range(2)]
        ofc = [A.f32(1024) for _ in range(2)]; Rofc = [Res("ofc%d" % i) for i in range(2)]
        sgc = [A.f32(1024) for _ in range(2)]; Rsgc = [Res("sgc%d" % i) for i in range(2)]
        ob = [A.f32(1024) for _ in range(2)]; Rob = [Res("ob%d" % i) for i in range(2)]
        innb = [A.bf16(128) for _ in range(2)]; Rinnb = [Res("innb%d" % i) for i in range(2)]
        qx = [A.bf16(128) for _ in range(2)]; Rqx = [Res("qx%d" % i) for i in range(2)]
        kz = [A.bf16(128) for _ in range(2)]; Rkz = [Res("kz%d" % i) for i in range(2)]
        ybf = A.bf16(1024); Rybf = Res("ybf")
        mst = [A.bf16(1024) for _ in range(2)]; Rmst = [Res("mst%d" % i) for i in range(2)]
        st = A.f32(16); Rst = Res("gnstat")
        junk = A.f32(256); Rjunk = Res("junk")

        S.dma("sp", lambda e: e.dma_start(out=rd, in_=ret_decay[l:l + 1, :].broadcast_to([128, 8])), writes=[Rlg])
        S.dma("sp", lambda e: e.dma_start(out=cdt.rearrange("p (a i) -> p a i", a=4), in_=dec_d.rearrange("a p i -> p a i")), writes=[Rc])
        S.dma("sp", lambda e: e.dma_start(out=xir.rearrange("p (a i) -> p a i", a=2), in_=xirow_d.rearrange("a p i -> p a i")), writes=[Rc])
        S.dma("sp", lambda e: e.dma_start(out=zc, in_=zcol_d), writes=[Rc])
        S.dma("sp", lambda e: e.dma_start(out=gngt, in_=gng[l:l + 1, :].broadcast_to([128, 1024])), writes=[Rgn])
        S.op("act", lambda e: e.activation(out=lg, in_=rd, func=AF.Exp, scale=-1.0), reads=[Rlg], writes=[Rlg])
        S.op("act", lambda e: e.activation(out=lg, in_=lg, func=AF.Ln, bias=1.0), reads=[Rlg], writes=[Rlg])
        S.op("dve", lambda e: e.tensor_scalar(out=lg, in0=lg, scalar1=-1.0, scalar2=None, op0=ALU.mult), reads=[Rlg], writes=[Rlg])
        for dr in range(2):
            for h in range(4):
                col = dr * 4 + h
                S.op("act", lambda e: e.activation(out=DT[:, col * 128:(col + 1) * 128], in_=cdt[:, (2 * dr) * 128:(2 * dr + 1) * 128],
                                                   func=AF.Exp, scale=lg[:, col:col + 1]), reads=[Rlg, Rc], writes=[Rtab])
                S.op("dve", lambda e: e.tensor_tensor(out=DT[:, col * 128:(col + 1) * 128], in0=DT[:, col * 128:(col + 1) * 128],
                                                      in1=cdt[:, (2 * dr + 1) * 128:(2 * dr + 2) * 128], op=ALU.mult), reads=[Rtab, Rc], writes=[Rtab])
                S.op("act", lambda e: e.activation(out=XI[:, col * 128:(col + 1) * 128], in_=xir[:, dr * 128:(dr + 1) * 128],
                                                   func=AF.Exp, scale=lg[:, col:col + 1]), reads=[Rlg, Rc], writes=[Rtab])
                S.op("act", lambda e: e.activation(out=zg[:, col:col + 1], in_=zc[:, dr:dr + 1], func=AF.Exp, scale=lg[:, col:col + 1]),
                     reads=[Rlg, Rc], writes=[Rtab])
                S.op("act", lambda e: e.activation(out=zg[:, 8 + col:9 + col], in_=lg[:, col:col + 1], func=AF.Exp, scale=128.0),
                     reads=[Rlg, Rc], writes=[Rtab])
        nlc = L // 128; ncc = CTX // 128
        step = [0]
        for dr in range(2):
            for h in range(4):
                S.op("dve", lambda e: e.memset(Sf[h], 0.0), writes=[RS[h]])
                S.op("pool", lambda e: e.memset(Sb[h], 0.0), writes=[RSb[h]])
            if dr == 0:
                order = [(1, L + i * 128) for i in range(ncc)] + [(0, i * 128) for i in range(nlc)]
            else:
                order = [(1, L + i * 128) for i in reversed(range(ncc))] + [(0, i * 128) for i in reversed(range(nlc))]
            for (isctx, tok0) in order:
                need_out = (not isctx) or with_ctx
                sl = step[0] % 2; step[0] += 1
                S.dma("sp", lambda e: e.dma_start(out=kc[sl].rearrange("p (h t) -> p h t", h=4),
                                                  in_=kT_d[:, tok0:tok0 + 128].rearrange("(h d) t -> d h t", d=128)), writes=[Rkc[sl]])
                S.dma("sp", lambda e: e.dma_start(out=vc[sl], in_=v_d[tok0:tok0 + 128, :]), writes=[Rvc[sl]])
                if need_out:
                    S.dma("sp", lambda e: e.dma_start(out=qc[sl].rearrange("p (h t) -> p h t", h=4),
                                                      in_=qT_d[:, tok0:tok0 + 128].rearrange("(h d) t -> d h t", d=128)), writes=[Rqc[sl]])
                    if dr == 1:
                        S.dma("sp", lambda e: e.dma_start(out=ofc[sl], in_=of_d[tok0:tok0 + 128, :]), writes=[Rofc[sl]])
                        S.dma("sp", lambda e: e.dma_start(out=sgc[sl], in_=sg_d[tok0:tok0 + 128, :]), writes=[Rsgc[sl]])
                for h in range(4):
                    col = dr * 4 + h
                    hs = h % 2
                    kh = kc[sl][:, h * 128:(h + 1) * 128]
                    vh = vc[sl][:, h * 256:(h + 1) * 256]
                    if need_out:
                        qh = qc[sl][:, h * 128:(h + 1) * 128]
                        S.op("pe", lambda e: e.matmul(PS[0][:, 0:128], lhsT=kh, rhs=qh, start=True, stop=True),
                             reads=[Rkc[sl], Rqc[sl]], writes=[RB0])
                        yield
                        S.op("dve", lambda e: e.tensor_tensor(out=innb[hs], in0=PS[0][:, 0:128], in1=DT[:, col * 128:(col + 1) * 128], op=ALU.mult),
                             reads=[RB0, Rtab], writes=[Rinnb[hs]])
                        S.op("pool", lambda e: e.tensor_tensor(out=qx[hs], in0=qh, in1=XI[:, col * 128:(col + 1) * 128], op=ALU.mult),
                             reads=[Rqc[sl], Rtab], writes=[Rqx[hs]])
                        yield
                        S.op("pe", lambda e: e.matmul(PS[1][:, hs * 256:(hs + 1) * 256], lhsT=innb[hs], rhs=vh, start=True, stop=False),
                             reads=[Rinnb[hs], Rvc[sl]], writes=[RB1])
                        S.op("pe", lambda e: e.matmul(PS[1][:, hs * 256:(hs + 1) * 256], lhsT=qx[hs], rhs=Sb[h], start=False, stop=True),
                             reads=[Rqx[hs], RSb[h]], writes=[RB1])
                    S.op("pe", lambda e: e.transpose(PSB[0][:, 256 + hs * 128:256 + (hs + 1) * 128], kh, identb), reads=[Rkc[sl], Rid], writes=[RB0])
                    yield
                    S.op("act", lambda e: e.activation(out=kz[hs], in_=PSB[0][:, 256 + hs * 128:256 + (hs + 1) * 128], func=AF.Identity, scale=zg[:, col:col + 1]),
                         reads=[RB0, Rtab], writes=[Rkz[hs]])
                    yield
                    S.op("pe", lambda e: e.matmul(PS[0][:, 256:512], lhsT=kz[hs], rhs=vh, start=True, stop=True),
                         reads=[Rkz[hs], Rvc[sl]], writes=[RB0])
                    yield
                    S.op("dve", lambda e: e.scalar_tensor_tensor(out=Sf[h], in0=Sf[h], scalar=zg[:, 8 + col:9 + col], in1=PS[0][:, 256:512],
                                                                 op0=ALU.mult, op1=ALU.add), reads=[RS[h], RB0, Rtab], writes=[RS[h]])
                    S.op("act", lambda e: e.activation(out=Sb[h], in_=Sf[h], func=AF.Copy), reads=[RS[h]], writes=[RSb[h]])
                    if need_out:
                        if dr == 0:
                            S.op("act", lambda e: e.activation(out=ob[sl][:, h * 256:(h + 1) * 256], in_=PS[1][:, hs * 256:(hs + 1) * 256], func=AF.Copy),
                                 reads=[RB1], writes=[Rob[sl]])
                        else:
                            S.op("dve", lambda e: e.tensor_tensor(out=ob[sl][:, h * 256:(h + 1) * 256], in0=PS[1][:, hs * 256:(hs + 1) * 256],
                                                                  in1=ofc[sl][:, h * 256:(h + 1) * 256], op=ALU.add),
                                 reads=[RB1, Rofc[sl]], writes=[Rob[sl]])
                    yield
                if not need_out:
                    yield
                    continue
                if dr == 0:
                    S.dma("pool", lambda e: e.dma_start(out=of_d[tok0:tok0 + 128, :], in_=ob[sl]), reads=[Rob[sl]])
                    yield
                    continue
                for h in range(4):
                    S.op("act", lambda e: e.activation(out=junk, in_=ob[sl][:, h * 256:(h + 1) * 256], func=AF.Identity, accum_out=st[:, h:h + 1]),
                         reads=[Rob[sl]], writes=[Rjunk, Rst])
                    S.op("act", lambda e: e.activation(out=junk, in_=ob[sl][:, h * 256:(h + 1) * 256], func=AF.Square, accum_out=st[:, 4 + h:5 + h]),
                         reads=[Rob[sl]], writes=[Rjunk, Rst])
                S.op("dve", lambda e: e.tensor_scalar(out=st[:, 0:8], in0=st[:, 0:8], scalar1=1.0 / 256, scalar2=None, op0=ALU.mult), reads=[Rst], writes=[Rst])
                S.op("dve", lambda e: e.tensor_tensor(out=st[:, 8:12], in0=st[:, 0:4], in1=st[:, 0:4], op=ALU.mult), reads=[Rst], writes=[Rst])
                S.op("dve", lambda e: e.tensor_tensor(out=st[:, 8:12], in0=st[:, 4:8], in1=st[:, 8:12], op=ALU.subtract), reads=[Rst], writes=[Rst])
                S.op("act", lambda e: e.activation(out=st[:, 8:12], in_=st[:, 8:12], func=AF.Sqrt, bias=EPS), reads=[Rst], writes=[Rst])
                S.op("dve", lambda e: e.reciprocal(out=st[:, 8:12], in_=st[:, 8:12]), reads=[Rst], writes=[Rst])
                for h in range(4):
                    S.op("dve", lambda e: e.tensor_scalar(out=ob[sl][:, h * 256:(h + 1) * 256], in0=ob[sl][:, h * 256:(h + 1) * 256],
                                                          scalar1=st[:, h:h + 1], scalar2=st[:, 8 + h:9 + h], op0=ALU.subtract, op1=ALU.mult),
                         reads=[Rob[sl], Rst], writes=[Rob[sl]])
                S.op("pool", lambda e: e.tensor_tensor(out=ob[sl], in0=ob[sl], in1=gngt, op=ALU.mult), reads=[Rob[sl], Rgn], writes=[Rob[sl]])
                S.op("dve", lambda e: e.tensor_tensor(out=ybf, in0=ob[sl], in1=sgc[sl], op=ALU.mult), reads=[Rob[sl], Rsgc[sl]], writes=[Rybf])
                for c in range(8):
                    S.op("pe", lambda e: e.transpose(PSB[3][:, c * 128:(c + 1) * 128], ybf[:, c * 128:(c + 1) * 128], identb),
                         reads=[Rybf, Rid], writes=[RP_m])
                S.op("act", lambda e: e.activation(out=mst[sl], in_=PSB[3][:, 0:1024], func=AF.Copy), reads=[RP_m], writes=[Rmst[sl]])
                S.dma("pool", lambda e: e.dma_start(out=mixT_d[0:1024, tok0:tok0 + 128].rearrange("(c p) t -> p c t", p=128),
                                                    in_=mst[sl].rearrange("p (c t) -> p c t", c=8)), reads=[Rmst[sl]])
                yield

    def phase_conv(l, with_ctx, A):
        RPC = Res('rp_conv')
        cw = A.f32(124); cb = A.f32(4); lg_ = A.f32(4); lb_ = A.f32(4); Rcp = Res("convp")
        ones = A.f32(128); Rones = Res("ones")
        pw = A.bf16(4 * 512); Rpw = Res("pw")
        ub = [A.f32(4 * 542) for _ in range(1)]; Rub = [Res("ub%d" % i) for i in range(1)]
        acc = [A.f32(512) for _ in range(4)]; Racc = [Res("acc%d" % i) for i in range(4)]
        sq = [A.f32(512) for _ in range(4)]; Rsq = [Res("sq%d" % i) for i in range(4)]
        rstd = A.f32(512); Rrstd = Res("rstd")
        zb = A.bf16(4 * 512); Rzb = Res("zb")
        stb = [A.bf16(512) for _ in range(2)]; Rstb = [Res("cstb%d" % i) for i in range(2)]
        S.dma("sp", lambda e: e.dma_start(out=cw, in_=cdw[l].rearrange("p c k -> p (c k)")), writes=[Rcp])
        S.dma("sp", lambda e: e.dma_start(out=cb, in_=cdb[l]), writes=[Rcp])
        S.dma("sp", lambda e: e.dma_start(out=lg_, in_=lng[l]), writes=[Rcp])
        S.dma("sp", lambda e: e.dma_start(out=lb_, in_=lnb[l]), writes=[Rcp])
        S.dma("sp", lambda e: e.dma_start(out=pw.rearrange("p (k n) -> p k n", k=4), in_=pwb[l].rearrange("(k p) n -> p k n", p=128)), writes=[Rpw], reads=[RCW("pw%d" % l)])
        S.op("pool", lambda e: e.memset(ones, 1.0 / 512), writes=[Rones])
        tiles = [(0, t0, min(512, L - t0)) for t0 in range(0, L, 512)]
        if with_ctx:
            tiles += [(1, t0, min(512, CTX - t0)) for t0 in range(0, CTX, 512)]
        for ti, (isctx, t0, n) in enumerate(tiles):
            Ls = CTX if isctx else L
            base = L if isctx else 0
            sl = 0
            u3 = ub[sl].rearrange("p (c t) -> p c t", c=4)
            lo = max(t0 - 15, 0); hi = min(t0 + n + 15, Ls)
            off = lo - (t0 - 15)
            if lo != t0 - 15 or hi != t0 + n + 15:
                S.op("pool", lambda e: e.memset(ub[sl], 0.0), writes=[Rub[sl]])
            S.dma("sp", lambda e: e.dma_start(out=u3[:, :, off:off + hi - lo], in_=uT_d[:, base + lo:base + hi].rearrange("(c p) t -> p c t", p=128)),
                  writes=[Rub[sl]])
            for c in range(4):
                S.op("dve", lambda e: e.tensor_scalar(out=acc[c][:, 0:n], in0=u3[:, c, 15:15 + n], scalar1=cw[:, c * 31 + 15:c * 31 + 16],
                                                      scalar2=cb[:, c:c + 1], op0=ALU.mult, op1=ALU.add), reads=[Rub[sl], Rcp], writes=[Racc[c]])
            for k in range(31):
                if k == 15:
                    continue
                for c in range(4):
                    S.op("dve", lambda e: e.scalar_tensor_tensor(out=acc[c][:, 0:n], in0=u3[:, c, k:k + n], scalar=cw[:, c * 31 + k:c * 31 + k + 1],
                                                                 in1=acc[c][:, 0:n], op0=ALU.mult, op1=ALU.add),
                         reads=[Rub[sl], Rcp, Racc[c]], writes=[Racc[c]])
                yield
            for c in range(4):
                S.op("pe", lambda e: e.matmul(PS[6][:, 0:n], lhsT=ones, rhs=acc[c][:, 0:n], start=(c == 0), stop=(c == 3)),
                     reads=[Rones, Racc[c]], writes=[RPC])
            for c in range(4):
                S.op("dve", lambda e: e.tensor_tensor(out=acc[c][:, 0:n], in0=acc[c][:, 0:n], in1=PS[6][:, 0:n], op=ALU.subtract),
                     reads=[RPC, Racc[c]], writes=[Racc[c]])
                S.op("act", lambda e: e.activation(out=sq[c][:, 0:n], in_=acc[c][:, 0:n], func=AF.Square), reads=[Racc[c]], writes=[Rsq[c]])
            for c in range(4):
                S.op("pe", lambda e: e.matmul(PS[6][:, 0:n], lhsT=ones, rhs=sq[c][:, 0:n], start=(c == 0), stop=(c == 3)),
                     reads=[Rones, Rsq[c]], writes=[RPC])
            yield
            S.op("act", lambda e: e.activation(out=rstd[:, 0:n], in_=PS[6][:, 0:n], func=AF.Sqrt, bias=EPS), reads=[RPC], writes=[Rrstd])
            S.op("dve", lambda e: e.reciprocal(out=rstd[:, 0:n], in_=rstd[:, 0:n]), reads=[Rrstd], writes=[Rrstd])
            for c in range(4):
                S.op("dve", lambda e: e.tensor_tensor(out=acc[c][:, 0:n], in0=acc[c][:, 0:n], in1=rstd[:, 0:n], op=ALU.mult),
                     reads=[Rrstd, Racc[c]], writes=[Racc[c]])
                S.op("act", lambda e: e.activation(out=zb[:, c * 512:c * 512 + n], in_=acc[c][:, 0:n], func=AF.Silu,
                                                   scale=lg_[:, c:c + 1], bias=lb_[:, c:c + 1]), reads=[Racc[c], Rcp], writes=[Rzb])
            for co in range(4):
                bk = 6
                for ci in range(4):
                    S.op("pe", lambda e: e.matmul(PS[bk][:, 0:n], lhsT=pw[:, ci * 512 + co * 128:ci * 512 + (co + 1) * 128],
                                                  rhs=zb[:, ci * 512:ci * 512 + n], start=(ci == 0), stop=(ci == 3)),
                         reads=[Rpw, Rzb], writes=[RPC])
                ss_ = co % 2
                S.op("act", lambda e: e.activation(out=stb[ss_][:, 0:n], in_=PS[bk][:, 0:n], func=AF.Copy), reads=[RPC], writes=[Rstb[ss_]])
                S.dma("pool", lambda e: e.dma_start(out=mixT_d[1024 + co * 128:1024 + (co + 1) * 128, base + t0:base + t0 + n], in_=stb[ss_][:, 0:n]),
                      reads=[Rstb[ss_]])
                yield

    def na_zero():
        S.op("pool", lambda e: e.memset(ztile, 0.0), writes=[Rzt])
        S.dma("sp", lambda e: e.dma_start(out=bass.AP(toep_d.tensor, 0, [[2176, 128], [1, 2176]]), in_=ztile), reads=[Rzt])

    def na_diag(l):
        dsem = S.dma_sem()
        for h in range(4):
            for ro in range(15):
                dst = bass.AP(toep_d.tensor, h * 69632 + (ro + 1) * 64 - 15, [[1089, 64], [1, 31]])
                S.dma("pool", lambda e: e.dma_start(out=dst, in_=na_rpb[l, h, ro:ro + 1, :].broadcast_to([64, 31])), sem=dsem)

    def phase_na(l, with_ctx, A):
        RP_sc = Res("rp_sc"); RP_pt = Res("rp_pt"); RP_no = Res("rp_no")
        TB = A.f32(20 * 576); RTB = Res("TB")
        nam = A.f32(5 * 576); Rnam = Res("nam")
        ckT = A.bf16(4 * CTX); Rck = Res("ckT")
        cv = A.bf16((CTX // 128) * 512); Rcv = Res("cv")
        qm = [A.bf16(512) for _ in range(2)]; Rqm = [Res("qm%d" % i) for i in range(2)]
        km = [A.bf16(4 * 576) for _ in range(2)]; Rkm = [Res("km%d" % i) for i in range(2)]
        vm = [A.bf16(5 * 512) for _ in range(2)]; Rvm = [Res("vm%d" % i) for i in range(2)]
        scs = [A.f32(832)] * 2; Rscs = [Res("scs")] * 2
        pb = [A.bf16(832) for _ in range(2)]; Rpb = [Res("pb%d" % i) for i in range(2)]
        pT = [A.bf16(7 * 128) for _ in range(2)]; RpT = [Res("pT%d" % i) for i in range(2)]
        nst = [A.bf16(512) for _ in range(2)]; Rnst = [Res("nst%d" % i) for i in range(2)]
        sm = A.f32(16); Rsm = Res("sm")
        types = na_types(R)
        for h in range(4):
            for ti, (mrep, lo) in enumerate(types):
                idx = h * 5 + ti
                for qr in range(2):
                    ro0 = lo - (2 * mrep + qr) + 7
                    src = bass.AP(toep_d.tensor, h * 69632 + (ro0 + 1) * 64, [[1088, 64], [1, 576]])
                    S.dma("sp", lambda e: e.dma_start(out=TB[qr * 64:(qr + 1) * 64, idx * 576:(idx + 1) * 576], in_=src), writes=[RTB])
        S.dma("sp", lambda e: e.dma_start(out=nam.rearrange("p (a k) -> p a k", a=5), in_=nam_d.rearrange("a p k -> p a k")), writes=[Rnam])
        for h in range(4):
            S.op("dve", lambda e: e.tensor_tensor(out=TB[:, h * 2880:(h + 1) * 2880], in0=TB[:, h * 2880:(h + 1) * 2880], in1=nam, op=ALU.add),
                 reads=[RTB, Rnam], writes=[RTB])
        S.dma("sp", lambda e: e.dma_start(out=ckT.rearrange("p (h t) -> p h t", h=4), in_=nkT_d[:, L:T].rearrange("(h d) t -> d h t", d=128)), writes=[Rck])
        S.dma("sp", lambda e: e.dma_start(out=cv.rearrange("p (c f) -> p c f", f=512), in_=nv_d[L:T, :].rearrange("(c p) f -> p c f", p=128)), writes=[Rcv])
        cnt = [0]

        def na_block(sl, tokdst, segs, vch):
            ntot = sum(s_[1] for s_ in segs)
            for h in range(4):
                i = cnt[0] % 2; cnt[0] += 1
                sc = psum_t[:, 4 * 512:4 * 512 + 1024]
                ops_ = PS[2][:, 0:128]
                o = 0
                for (rf, ncol, bf_) in segs:
                    c0 = 0
                    while c0 < ncol:
                        w_ = min(ncol - c0, 512 - (o % 512))
                        S.op("pe", lambda e: e.matmul(sc[:, o:o + w_], lhsT=qm[sl][:, h * 128:(h + 1) * 128], rhs=rf(h)[:, c0:c0 + w_], start=True, stop=True),
                             reads=[Rqm[sl], Rkm[sl], Rck], writes=[RP_sc])
                        o += w_; c0 += w_
                yield
                o = 0
                for (rf, ncol, bf_) in segs:
                    if bf_ is not None:
                        S.op("dve", lambda e: e.tensor_tensor(out=scs[i][:, o:o + ncol], in0=sc[:, o:o + ncol], in1=bf_(h), op=ALU.add),
                             reads=[RP_sc, RTB], writes=[Rscs[i]])
                    else:
                        S.op("act", lambda e: e.activation(out=scs[i][:, o:o + ncol], in_=sc[:, o:o + ncol], func=AF.Copy),
                             reads=[RP_sc], writes=[Rscs[i]])
                    o += ncol
                yield
                S.op("dve", lambda e: e.reduce_max(out=sm[:, i:i + 1], in_=scs[i][:, 0:ntot], axis=AX.X), reads=[Rscs[i]], writes=[Rsm])
                S.op("dve", lambda e: e.tensor_scalar(out=sm[:, i:i + 1], in0=sm[:, i:i + 1], scalar1=-1.0, scalar2=None, op0=ALU.mult), reads=[Rsm], writes=[Rsm])
                yield
                S.op("act", lambda e: e.activation(out=scs[i][:, 0:ntot], in_=scs[i][:, 0:ntot], func=AF.Exp, bias=sm[:, i:i + 1],
                                                   accum_out=sm[:, 4 + i:5 + i]), reads=[Rscs[i], Rsm], writes=[Rscs[i], Rsm])
                yield
                S.op("dve", lambda e: e.reciprocal(out=sm[:, 4 + i:5 + i], in_=sm[:, 4 + i:5 + i]), reads=[Rsm], writes=[Rsm])
                S.op("dve", lambda e: e.tensor_scalar(out=pb[i][:, 0:ntot], in0=scs[i][:, 0:ntot], scalar1=sm[:, 4 + i:5 + i], scalar2=None, op0=ALU.mult),
                     reads=[Rscs[i], Rsm], writes=[Rpb[i]])
                yield
                bt = 7
                o = 0
                for ci, (vf, nk) in enumerate(vch):
                    S.op("pe", lambda e: e.transpose(PSB[bt][0:nk, ci * 128:(ci + 1) * 128], pb[i][:, o:o + nk], identb),
                         reads=[Rpb[i], Rid], writes=[RP_pt])
                    o += nk
                nch = len(vch)
                yield
                S.op("act", lambda e: e.activation(out=pT[i][:, 0:nch * 128], in_=PSB[bt][:, 0:nch * 128], func=AF.Copy), reads=[RP_pt], writes=[RpT[i]])
                yield
                for ci, (vf, nk) in enumerate(vch):
                    S.op("pe", lambda e: e.matmul(ops_, lhsT=vf(h)[0:nk, :], rhs=pT[i][0:nk, ci * 128:(ci + 1) * 128],
                                                  start=(ci == 0), stop=(ci == nch - 1)), reads=[RpT[i], Rvm[sl], Rcv], writes=[RP_no])
                S.op("dve", lambda e: e.tensor_copy(out=nst[sl][:, h * 128:(h + 1) * 128], in_=ops_), reads=[RP_no], writes=[Rnst[sl]])
                yield
            S.dma("pool", lambda e: e.dma_start(out=mixT_d[1536:2048, tokdst:tokdst + 128].rearrange("(h d) t -> d h t", d=128),
                                                in_=nst[sl].rearrange("p (h t) -> p h t", h=4)), reads=[Rnst[sl]])

        ctx_v = [((lambda h, c=c: cv[:, c * 512 + h * 128:c * 512 + (h + 1) * 128]), 128) for c in range(CTX // 128)]
        ctx_seg = ((lambda h: ckT[:, h * CTX:(h + 1) * CTX]), CTX, None)
        for m in range(R // 2):
            ti, lo = na_type_of(m, R)
            sl = m % 2
            S.dma("sp", lambda e: e.dma_start(out=qm[sl].rearrange("p (h t) -> p h t", h=4),
                                              in_=nqT_d[:, m * 128:(m + 1) * 128].rearrange("(h d) t -> d h t", d=128)), writes=[Rqm[sl]])
            S.dma("sp", lambda e: e.dma_start(out=km[sl].rearrange("p (h t) -> p h t", h=4),
                                              in_=nkT_d[:, lo * 64:lo * 64 + 576].rearrange("(h d) t -> d h t", d=128)), writes=[Rkm[sl]])
            S.dma("sp", lambda e: e.dma_start(out=vm[sl][:, 0:2048].rearrange("p (c f) -> p c f", f=512),
                                              in_=nv_d[lo * 64:lo * 64 + 512, :].rearrange("(c p) f -> p c f", p=128)), writes=[Rvm[sl]])
            S.dma("sp", lambda e: e.dma_start(out=vm[sl][0:64, 2048:2560], in_=nv_d[lo * 64 + 512:lo * 64 + 576, :]), writes=[Rvm[sl]])
            kseg = ((lambda h, sl=sl: km[sl][:, h * 576:(h + 1) * 576]), 576,
                    (lambda h, ti=ti: TB[:, (h * 5 + ti) * 576:(h * 5 + ti + 1) * 576]))
            vch = [((lambda h, c=c, sl=sl: vm[sl][:, c * 512 + h * 128:c * 512 + (h + 1) * 128]), 128) for c in range(4)]
            vch += [((lambda h, sl=sl: vm[sl][:, 2048 + h * 128:2048 + (h + 1) * 128]), 64)]
            yield from na_block(sl, m * 128, [kseg, ctx_seg], vch + ctx_v)
        if with_ctx:
            for qb in range(CTX // 128):
                sl = qb % 2
                S.dma("sp", lambda e: e.dma_start(out=qm[sl].rearrange("p (h t) -> p h t", h=4),
                                                  in_=nqT_d[:, L + qb * 128:L + (qb + 1) * 128].rearrange("(h d) t -> d h t", d=128)), writes=[Rqm[sl]])
                yield from na_block(sl, L + qb * 128, [ctx_seg], ctx_v)

    def phase_ffn(l, with_ctx, xlat, xctx, last):
        A.off = pers_mark
        fw = A.f32(264); fb = A.f32(88); Rfp = Res("ffnp")
        xs = [A.f32(D) for _ in range(4)]; Rxs = [Res("fxs%d" % i) for i in range(4)]
        xn = [A.f32(D) for _ in range(2)]; Rxn = [Res("fxn%d" % i) for i in range(2)]
        hT = A.bf16(16 * 512); RhT = Res("fhT")
        aT = A.bf16(22 * 512); RaT = Res("aT")
        mt = aT[:, 0:16 * 512]; RmT = RaT
        upw = [A.bf16(16 * 256) for _ in range(4)]; Rupw = [Res("upw%d" % i) for i in range(4)]
        wdn = [A.bf16(16 * 512) for _ in range(3)]; Rwdn = [Res("wdn%d" % i) for i in range(3)]
        gt = [A.f32(D) for _ in range(2)]; Rgt = [Res("gt%d" % i) for i in range(2)]
        yv = [A.f32(512) for _ in range(2)]; Ryv = [Res("yv%d" % i) for i in range(2)]
        yg = [A.f32(512) for _ in range(2)]; Ryg = [Res("yg%d" % i) for i in range(2)]
        tmp = [A.f32(512) for _ in range(2)]; Rtmp = [Res("ftmp%d" % i) for i in range(2)]
        ss = A.f32(16); Rss = Res("fss")
        S.dma("sp", lambda e: e.dma_start(out=fw, in_=fdw[l].rearrange("p c k -> p (c k)")), writes=[Rfp])
        S.dma("sp", lambda e: e.dma_start(out=fb, in_=fdb[l]), writes=[Rfp])
        cnt = {"up": 0, "dn": 0, "tmp": 0, "xn": 0, "y": 0}

        def nxt(key, n_):
            v = cnt[key]; cnt[key] = (v + 1) % n_; return v

        tiles = [(0, t0, min(510, L - t0)) for t0 in range(0, L, 510)]
        if with_ctx:
            tiles += [(1, t0, min(510, CTX - t0)) for t0 in range(0, CTX, 510)]
        for (isctx, t0, ni) in tiles:
            cond = isctx
            Ls = CTX if isctx else L
            base = L if isctx else 0
            xsrc = xctx if isctx else xlat
            n = ni + 2
            nb = (n + 127) // 128
            jlo = 1 if t0 == 0 else 0
            jhi = n - 1 if t0 + ni == Ls else n
            nts = [min(128, n - tb * 128) for tb in range(nb)]
            for tb in range(nb):
                r0 = max(tb * 128, jlo); r1 = min(tb * 128 + nts[tb], jhi)
                if r0 != tb * 128 or r1 != tb * 128 + nts[tb]:
                    S.op("pool", lambda e: e.memset(xs[tb], 0.0), writes=[Rxs[tb]])
                for (a_, b_) in row_chunks(r0, r1):
                    S.dma("sp", lambda e: e.dma_start(out=xs[tb][a_ - tb * 128:b_ - tb * 128, :], in_=xsrc[t0 - 1 + a_:t0 - 1 + b_, :]), writes=[Rxs[tb]])
            mt3 = mt.rearrange("p (c t) -> p c t", c=16)
            if jlo != 0 or jhi != n:
                S.op("pool", lambda e: e.memset(mt, 0.0), writes=[RmT])
            S.dma("sp", lambda e: e.dma_start(out=mt3[:, :, jlo:jhi], in_=mixT_d[:, base + t0 - 1 + jlo:base + t0 - 1 + jhi].rearrange("(c p) t -> p c t", p=128)),
                  writes=[RmT])
            S.dma("sp", lambda e: e.dma_start(out=gt[0], in_=mods_d[cond:cond + 1, 2 * D:3 * D].broadcast_to([128, D])), writes=[Rgt[0]])
            S.dma("sp", lambda e: e.dma_start(out=gt[1], in_=mods_d[cond:cond + 1, 5 * D:6 * D].broadcast_to([128, D])), writes=[Rgt[1]])
            for nbk in range(4):
                sl = nxt("dn", 3)
                S.dma("sp", lambda e: e.dma_start(out=wdn[sl].rearrange("p (k n) -> p k n", k=16),
                                                  in_=woutb[l][:, nbk * 512:(nbk + 1) * 512].rearrange("(k p) n -> p k n", p=128)), writes=[Rwdn[sl]], reads=[RCW("wout%d" % l)])
                for tb in range(nb):
                    nt = nts[tb]
                    bk = 4 + tb
                    for k in range(16):
                        S.op("pe", lambda e: e.matmul(PS[bk][0:nt, :], lhsT=mt[:, k * 512 + tb * 128:k * 512 + tb * 128 + nt],
                                                      rhs=wdn[sl][:, k * 512:(k + 1) * 512], start=(k == 0), stop=(k == 15)),
                             reads=[RmT, Rwdn[sl]], writes=[RPS[bk]])
                    ti_ = nxt("tmp", 2)
                    S.op("dve", lambda e: e.tensor_tensor(out=tmp[ti_][0:nt, :], in0=PS[bk][0:nt, :], in1=gt[0][0:nt, nbk * 512:(nbk + 1) * 512], op=ALU.mult),
                         reads=[RPS[bk], Rgt[0]], writes=[Rtmp[ti_]])
                    S.op("pool", lambda e: e.tensor_tensor(out=xs[tb][0:nt, nbk * 512:(nbk + 1) * 512], in0=xs[tb][0:nt, nbk * 512:(nbk + 1) * 512],
                                                           in1=tmp[ti_][0:nt, :], op=ALU.add), reads=[Rtmp[ti_], Rxs[tb]], writes=[Rxs[tb]])
            if last:
                S.dma("sp", lambda e: e.dma_start(out=gt[0], in_=final_g.rearrange("(o d) -> o d", o=1).broadcast_to([128, D])), writes=[Rgt[0]])
            for tb in range(nb):
                nt = nts[tb]
                xi_ = nxt("xn", 2)
                S.op("act", lambda e: e.activation(out=xn[xi_][0:nt, :], in_=xs[tb][0:nt, :], func=AF.Square, accum_out=ss[0:nt, tb:tb + 1]),
                     reads=[Rxs[tb]], writes=[Rxn[xi_], Rss])
                S.op("act", lambda e: e.activation(out=ss[0:nt, 4 + tb:5 + tb], in_=ss[0:nt, tb:tb + 1], func=AF.Sqrt, scale=1.0 / D, bias=EPS),
                     reads=[Rss], writes=[Rss])
                S.op("dve", lambda e: e.reciprocal(out=ss[0:nt, 4 + tb:5 + tb], in_=ss[0:nt, 4 + tb:5 + tb]), reads=[Rss], writes=[Rss])
                S.op("dve", lambda e: e.tensor_scalar(out=xn[xi_][0:nt, :], in0=xs[tb][0:nt, :], scalar1=ss[0:nt, 4 + tb:5 + tb], scalar2=None, op0=ALU.mult),
                     reads=[Rxs[tb], Rss], writes=[Rxn[xi_]])
                for c4 in range(4):
                    bk = c4 % 2
                    for cc in range(4):
                        c = c4 * 4 + cc
                        S.op("pe", lambda e: e.transpose(PS[bk][:, cc * 128:cc * 128 + nt], xn[xi_][0:nt, c * 128:(c + 1) * 128], ident[0:nt, 0:nt]),
                             reads=[Rxn[xi_], Rid], writes=[RPS[bk]])
                    for cc in range(4):
                        c = c4 * 4 + cc
                        dst = hT[:, c * 512 + tb * 128:c * 512 + tb * 128 + nt]
                        if cc % 2 == 0:
                            S.op("dve", lambda e: e.tensor_scalar(out=dst, in0=PS[bk][:, cc * 128:cc * 128 + nt], scalar1=G2T[:, 2 * c + cond:2 * c + cond + 1],
                                                                  scalar2=modcol(3, c, cond), op0=ALU.mult, op1=ALU.add),
                                 reads=[RPS[bk], RG, RmodT], writes=[RhT])
                        else:
                            S.op("act", lambda e: e.activation(out=dst, in_=PS[bk][:, cc * 128:cc * 128 + nt], func=AF.Identity,
                                                               scale=G2T[:, 2 * c + cond:2 * c + cond + 1], bias=modcol(3, c, cond)),
                                 reads=[RPS[bk], RG, RmodT], writes=[RhT])
            hT3 = hT.rearrange("p (c t) -> p c t", c=16)
            if jlo == 1:
                S.op("pool", lambda e: e.memset(hT3[:, :, 0:1], 0.0), reads=[RhT], writes=[RhT])
            if jhi == n - 1:
                S.op("pool", lambda e: e.memset(hT3[:, :, n - 1:n], 0.0), reads=[RhT], writes=[RhT])
            for hf in range(2):
                for jj in range(22):
                    c = hf * 22 + jj
                    sub = c % 2
                    if jj % 2 == 0:
                        uv = nxt("up", 4); ug = nxt("up", 4)
                        for (us, c0) in ((uv, (c // 2) * 256), (ug, DFF + (c // 2) * 256)):
                            S.dma("sp", lambda e: e.dma_start(out=upw[us].rearrange("p (k n) -> p k n", k=16),
                                                              in_=upb[l][:, c0:c0 + 256].rearrange("(k p) n -> p k n", p=128)), writes=[Rupw[us]], reads=[RCW("up%d" % l)])
                    pbk = (jj % 2) * 2
                    for (us, bk) in ((uv, pbk), (ug, pbk + 1)):
                        for k in range(16):
                            S.op("pe", lambda e: e.matmul(PS[bk][:, 0:n], lhsT=upw[us][:, k * 256 + sub * 128:k * 256 + (sub + 1) * 128],
                                                          rhs=hT[:, k * 512:k * 512 + n], start=(k == 0), stop=(k == 15)),
                                 reads=[Rupw[us], RhT], writes=[RPS[bk]])
                    yi = nxt("y", 2)
                    for (yy, Ryy, bk, ch) in ((yv[yi], Ryv[yi], pbk, c), (yg[yi], Ryg[yi], pbk + 1, 44 + c)):
                        S.op("act", lambda e: e.activation(out=yy[:, 0:n], in_=PS[bk][:, 0:n], func=AF.Identity, scale=fw[:, ch * 3 + 1:ch * 3 + 2],
                                                           bias=fb[:, ch:ch + 1]), reads=[RPS[bk], Rfp], writes=[Ryy])
                        S.op("dve", lambda e: e.scalar_tensor_tensor(out=yy[:, 1:n], in0=PS[bk][:, 0:n - 1], scalar=fw[:, ch * 3:ch * 3 + 1], in1=yy[:, 1:n],
                                                                     op0=ALU.mult, op1=ALU.add), reads=[RPS[bk], Rfp, Ryy], writes=[Ryy])
                        S.op("dve", lambda e: e.scalar_tensor_tensor(out=yy[:, 0:n - 1], in0=PS[bk][:, 1:n], scalar=fw[:, ch * 3 + 2:ch * 3 + 3], in1=yy[:, 0:n - 1],
                                                                     op0=ALU.mult, op1=ALU.add), reads=[RPS[bk], Rfp, Ryy], writes=[Ryy])
                    S.op("act", lambda e: e.activation(out=yg[yi][:, 0:n], in_=yg[yi][:, 0:n], func=AF.Silu), reads=[Ryg[yi]], writes=[Ryg[yi]])
                    S.op("pool", lambda e: e.tensor_tensor(out=aT[:, jj * 512:jj * 512 + n], in0=yv[yi][:, 0:n], in1=yg[yi][:, 0:n], op=ALU.mult),
                         reads=[Ryv[yi], Ryg[yi]], writes=[RaT])
                for nbk in range(4):
                    for part in range(2):
                        sl = nxt("dn", 3)
                        k0 = hf * 22 + part * 11
                        S.dma("sp", lambda e: e.dma_start(out=wdn[sl][:, 0:11 * 512].rearrange("p (k n) -> p k n", k=11),
                                                          in_=downb[l][k0 * 128:(k0 + 11) * 128, nbk * 512:(nbk + 1) * 512].rearrange("(k p) n -> p k n", p=128)),
                              writes=[Rwdn[sl]], reads=[RCW("down%d" % l)])
                        for tb in range(nb):
                            nt = nts[tb]
                            bk = 4 + tb
                            for kl in range(11):
                                kk = part * 11 + kl
                                S.op("pe", lambda e: e.matmul(PS[bk][0:nt, :], lhsT=aT[:, kk * 512 + tb * 128:kk * 512 + tb * 128 + nt],
                                                              rhs=wdn[sl][:, kl * 512:(kl + 1) * 512], start=(kk == 0), stop=(kk == 21)),
                                     reads=[RaT, Rwdn[sl]], writes=[RPS[bk]])
                    for tb in range(nb):
                        nt = nts[tb]
                        bk = 4 + tb
                        ti_ = nxt("tmp", 2)
                        S.op("dve", lambda e: e.tensor_tensor(out=tmp[ti_][0:nt, :], in0=PS[bk][0:nt, :], in1=gt[1][0:nt, nbk * 512:(nbk + 1) * 512], op=ALU.mult),
                             reads=[RPS[bk], Rgt[1]], writes=[Rtmp[ti_]])
                        S.op("pool", lambda e: e.tensor_tensor(out=xs[tb][0:nt, nbk * 512:(nbk + 1) * 512], in0=xs[tb][0:nt, nbk * 512:(nbk + 1) * 512],
                                                               in1=tmp[ti_][0:nt, :], op=ALU.add), reads=[Rtmp[ti_], Rxs[tb]], writes=[Rxs[tb]])
            for tb in range(nb):
                nt = nts[tb]
                r0 = max(tb * 128, 1); r1 = min(tb * 128 + nt, n - 1)
                if r1 <= r0:
                    continue
                if last:
                    xi_ = nxt("xn", 2)
                    S.op("act", lambda e: e.activation(out=xn[xi_][0:nt, :], in_=xs[tb][0:nt, :], func=AF.Square, accum_out=ss[0:nt, 8 + tb:9 + tb]),
                         reads=[Rxs[tb]], writes=[Rxn[xi_], Rss])
                    S.op("act", lambda e: e.activation(out=ss[0:nt, 12 + tb:13 + tb], in_=ss[0:nt, 8 + tb:9 + tb], func=AF.Sqrt, scale=1.0 / D, bias=EPS),
                         reads=[Rss], writes=[Rss])
                    S.op("dve", lambda e: e.reciprocal(out=ss[0:nt, 12 + tb:13 + tb], in_=ss[0:nt, 12 + tb:13 + tb]), reads=[Rss], writes=[Rss])
                    S.op("dve", lambda e: e.tensor_scalar(out=xn[xi_][0:nt, :], in0=xs[tb][0:nt, :], scalar1=ss[0:nt, 12 + tb:13 + tb], scalar2=None, op0=ALU.mult),
                         reads=[Rxs[tb], Rss], writes=[Rxn[xi_]])
                    S.op("pool", lambda e: e.tensor_tensor(out=xn[xi_][0:nt, :], in0=xn[xi_][0:nt, :], in1=gt[0][0:nt, :], op=ALU.mult),
                         reads=[Rxn[xi_], Rgt[0]], writes=[Rxn[xi_]])
                    for (a_, b_) in row_chunks(r0, r1):
                        S.dma("act", lambda e: e.dma_start(out=out_d[t0 - 1 + a_:t0 - 1 + b_, :], in_=xn[xi_][a_ - tb * 128:b_ - tb * 128, :]), reads=[Rxn[xi_]])
                else:
                    for (a_, b_) in row_chunks(r0, r1):
                        S.dma("act", lambda e: e.dma_start(out=xa_d[base + t0 - 1 + a_:base + t0 - 1 + b_, :], in_=xs[tb][a_ - tb * 128:b_ - tb * 128, :]),
                              reads=[Rxs[tb]])
        S.barrier()

    def run_group(items):
        active = [(iter(g_), w_) for g_, w_ in items]
        while active:
            for it in list(active):
                g_, w_ = it
                for _ in range(w_):
                    try:
                        next(g_)
                    except StopIteration:
                        active.remove(it)
                        break
        S.barrier()

    def run(stage="all", nlay=NL):
        for l in range(nlay):
            last = (l == NL - 1)
            xlat = x_in if l == 0 else xa_d[0:L]
            xctx = ctx_in if l == 0 else xa_d[L:T]
            if l == 0:
                issue_casts(0, 0)
            na_zero()
            phase_mods(l)
            na_diag(l)
            if stage == "mods" and l == nlay - 1:
                break
            phase_inproj(l, xlat, xctx)
            if stage == "inproj" and l == nlay - 1:
                break
            citems = cast_items(l, 1) + (cast_items(l + 1, 0) if l + 1 < NL else [])
            A.off = pers_mark
            Ana = A.sub(A.n - A.off - 15900 - 9700); Aret = A.sub(15900); Aconv = A.sub(9700)
            import os
            gm = os.environ.get("GMODE", "par")
            if gm == "seq":
                for g_ in (phase_ret(l, not last, Aret), phase_conv(l, not last, Aconv), phase_na(l, not last, Ana)):
                    run_group([(g_, 1)])
            elif gm in ("ret", "conv", "na"):
                run_group([({"ret": phase_ret(l, not last, Aret), "conv": phase_conv(l, not last, Aconv), "na": phase_na(l, not last, Ana)}[gm], 1)])
            else:
                run_group([(phase_ret(l, not last, Aret), 6), (phase_conv(l, not last, Aconv), 1), (phase_na(l, not last, Ana), 4),
                           (paced_casts(citems, 5), 1)])
            if stage == "na" and l == nlay - 1:
                break
            phase_ffn(l, not last, xlat, xctx, last)
        S.run_block()

    g.run = run
    g.nc = nc; g.S = S
    g.phase_mods = phase_mods; g.phase_inproj = phase_inproj
    g.names = dict(x_in=x_in, ctx_in=ctx_in, xa_d=xa_d, out_d=out_d)
    g.locals = locals()
    return g


def kernel(**inputs):
    inp = {k: np.asarray(v) for k, v in inputs.items()}
    B, L, _ = inp["x"].shape
    CTX = inp["ctx"].shape[1]
    g = build(L, CTX, NL=2, dbg=False)
    g.run("all", 2)
    in_maps = [host_layout(inp, b, L) for b in range(B)]
    res = run_bass_kernel_spmd(g.nc, in_maps, core_ids=list(range(B)))
    return np.stack([np.asarray(res.results[b]["out"], np.float32) for b in range(B)]).astype(np.float32)
```
